# Optimizing a Trainium2 kernel written in Bass

```python
import math
import jax
import jax.numpy as jnp
from jax import lax
import numpy as np

D_MODEL = 1024
BATCH = 4
SEQ = 8192
DEPTH = 2

GRID_W = 64
CTX_LEN = 256

D_MIX = D_MODEL
BRANCH_W = D_MIX // 4

MLA_HEADS = 4
MLA_NOPE = 64
MLA_ROPE = 32
MLA_V = BRANCH_W // MLA_HEADS
Q_LORA = 192
KV_LORA = 128
ROPE_BASE = 10000.0
Q_BLOCK = 128

POOL_WINDOWS = (2, 4, 8, 16)
N_POOL = 4
POOL_GC = BRANCH_W // N_POOL

S5_H = 16
S5_G = BRANCH_W // S5_H
S5_P = 64
DT_MIN = 1e-3
DT_MAX = 1e-1

SGU_HEADS = 4
SGU_HD = BRANCH_W // SGU_HEADS
CHUNK = 128

IN_SPLITS = (KV_LORA,
             KV_LORA + MLA_ROPE,
             KV_LORA + MLA_ROPE + BRANCH_W,
             KV_LORA + MLA_ROPE + BRANCH_W + Q_LORA,
             KV_LORA + MLA_ROPE + 2 * BRANCH_W + Q_LORA,
             KV_LORA + MLA_ROPE + 3 * BRANCH_W + Q_LORA,
             KV_LORA + MLA_ROPE + 4 * BRANCH_W + Q_LORA)
CTX_STATE_COLS = KV_LORA + MLA_ROPE + BRANCH_W
IN_DIM = KV_LORA + MLA_ROPE + 4 * BRANCH_W + Q_LORA + D_MIX

ALPHA = (2 * DEPTH) ** 0.25
BETA = (8 * DEPTH) ** -0.25
LN_EPS = 1e-6

kernel_name = 'hybrid_mla_pool_s5_sgu_flow_block'


def _ln(x):
    xf = x.astype(jnp.float32)
    xc = xf - jnp.mean(xf, -1, keepdims=True)
    return xc * lax.rsqrt(jnp.mean(xc * xc, -1, keepdims=True) + LN_EPS)


def _rms(x, g):
    xf = x.astype(jnp.float32)
    return xf * lax.rsqrt(jnp.mean(xf * xf, -1, keepdims=True) + LN_EPS) * g


def _axial_rope_tables(rows):
    row = jnp.repeat(jnp.arange(rows, dtype=jnp.float32), GRID_W)
    col = jnp.tile(jnp.arange(GRID_W, dtype=jnp.float32), rows)
    nf = MLA_ROPE // 4
    inv = ROPE_BASE ** (-jnp.arange(nf, dtype=jnp.float32) / nf)
    ang = jnp.stack([row[:, None] * inv, col[:, None] * inv], axis=1)
    return jnp.cos(ang), jnp.sin(ang)


def _rope(x, cos, sin):
    shp = x.shape
    xr = x.reshape(shp[:-1] + (2, 2, MLA_ROPE // 4))
    x1, x2 = xr[..., 0, :], xr[..., 1, :]
    cs, sn = cos[:, None], sin[:, None]
    out = jnp.stack([x1 * cs - x2 * sn, x2 * cs + x1 * sn], axis=-2)
    return out.reshape(shp)


def _mla_q(cq, g_q, w_uq):
    b, n = cq.shape[:2]
    q = (_rms(cq, g_q) @ w_uq).reshape(b, n, MLA_HEADS, MLA_NOPE + MLA_ROPE)
    return q[..., :MLA_NOPE], q[..., MLA_NOPE:]


def _mla_kv(ckv, g_kv, w_ukv):
    b, n = ckv.shape[:2]
    kv = (_rms(ckv, g_kv) @ w_ukv).reshape(b, n, MLA_HEADS, MLA_NOPE + MLA_V)
    return kv[..., :MLA_NOPE], kv[..., MLA_NOPE:]


def _mla_attend(q_nope, q_rope, k_nope, k_rope, v):
    b, n = q_nope.shape[:2]
    nb = n // Q_BLOCK
    scale = (MLA_NOPE + MLA_ROPE) ** -0.5
    vf = v.astype(jnp.float32)

    def blocks(t):
        return t.reshape((b, nb, Q_BLOCK) + t.shape[2:]).swapaxes(0, 1)

    def one(qs):
        qn, qr = qs
        s = (jnp.einsum('bqhd,bkhd->bhqk', qn, k_nope)
             + jnp.einsum('bqhr,bkr->bhqk', qr, k_rope))
        p = jax.nn.softmax(s.astype(jnp.float32) * scale, axis=-1)
        return jnp.einsum('bhqk,bkhd->bqhd', p, vf)

    o = lax.map(one, (blocks(q_nope), blocks(q_rope)))
    return o.swapaxes(0, 1).reshape(b, n, MLA_HEADS * MLA_V)


def _pool_mixer(vin, w_pool, scale):
    b, n, _ = vin.shape
    vf = vin.astype(jnp.float32)
    cs = jnp.concatenate([jnp.zeros((b, 1, BRANCH_W), jnp.float32), jnp.cumsum(vf, axis=1)], axis=1)
    t = jnp.arange(n)
    means = []
    for gi, w in enumerate(POOL_WINDOWS):
        lo = jnp.clip(t - w // 2, 0, n)
        hi = jnp.clip(t + (w - w // 2), 0, n)
        csg = cs[..., gi * POOL_GC:(gi + 1) * POOL_GC]
        tot = jnp.take(csg, hi, axis=1) - jnp.take(csg, lo, axis=1)
        means.append(tot / (hi - lo).astype(jnp.float32)[:, None])
    pooled = jnp.concatenate(means, -1) - vf
    y = jnp.einsum('bngc,gcd->bngd', pooled.reshape(b, n, N_POOL, POOL_GC), w_pool)
    return y.reshape(b, n, BRANCH_W) * scale


def _s5_discretise(lam_re, lam_im, log_dt, b_re, b_im):
    lam_re = lam_re.astype(jnp.float32)
    lam_im = lam_im.astype(jnp.float32)
    dt = jnp.exp(log_dt.astype(jnp.float32))[:, None]
    mag = jnp.exp(lam_re * dt)
    lb_re, lb_im = mag * jnp.cos(lam_im * dt), mag * jnp.sin(lam_im * dt)
    nr, ni = lb_re - 1.0, lb_im
    den = lam_re * lam_re + lam_im * lam_im
    k_re = (nr * lam_re + ni * lam_im) / den
    k_im = (ni * lam_re - nr * lam_im) / den
    bb_re = k_re[..., None] * b_re - k_im[..., None] * b_im
    bb_im = k_re[..., None] * b_im + k_im[..., None] * b_re
    return lb_re, lb_im, bb_re, bb_im


def _cscan_combine(e1, e2):
    a1r, a1i, b1r, b1i = e1
    a2r, a2i, b2r, b2i = e2
    return (a2r * a1r - a2i * a1i, a2r * a1i + a2i * a1r,
            a2r * b1r - a2i * b1i + b2r, a2r * b1i + a2i * b1r + b2i)


def _s5_scan(u, disc, reverse, h0):
    lb_re, lb_im, bb_re, bb_im = disc
    n = u.shape[1]
    bu_re = jnp.einsum('bngh,gph->nbgp', u, bb_re)
    bu_im = jnp.einsum('bngh,gph->nbgp', u, bb_im)
    if h0 is not None:
        i0 = n - 1 if reverse else 0
        h0r, h0i = h0
        bu_re = bu_re.at[i0].add(lb_re * h0r - lb_im * h0i)
        bu_im = bu_im.at[i0].add(lb_re * h0i + lb_im * h0r)
    a_re = jnp.broadcast_to(lb_re, (n, 1) + lb_re.shape)
    a_im = jnp.broadcast_to(lb_im, (n, 1) + lb_im.shape)
    _, _, hr, hi = lax.associative_scan(_cscan_combine, (a_re, a_im, bu_re, bu_im),
                                        reverse=reverse, axis=0)
    return hr, hi


def _s5_readout(hr, hi, c_re, c_im):
    return jnp.einsum('nbgp,ghp->bngh', hr, c_re) - jnp.einsum('nbgp,ghp->bngh', hi, c_im)


def _glu(y, w_glu, b_glu):
    g = jax.nn.gelu(y)
    return g * jax.nn.sigmoid(g @ w_glu + b_glu)


def _sgu(u, v, g, bta, w_s, b_s):
    b, n, _ = v.shape
    vn = _ln(v) * g + bta
    vc = vn.reshape(b, n // CHUNK, CHUNK, SGU_HEADS, SGU_HD)
    mixed = jnp.einsum('hts,bcshd->bcthd', w_s, vc) + b_s.T[:, :, None]
    return u * mixed.reshape(b, n, BRANCH_W)


def _layer(x, ctx, c, c_ctx, cos, sin, p, last):
    b, n, _ = x.shape
    nc = ctx.shape[1]
    mod = jax.nn.silu(c.astype(jnp.float32)) @ p['w_mod'] + p['b_mod']
    shift, scale, gate = jnp.split(mod[:, None, :], 3, axis=-1)
    h = _ln(x) * (1.0 + scale) + shift
    ckv, kr, s5_in, cq, pool_in, su, sv, g = jnp.split(h @ p['w_in'], IN_SPLITS, axis=-1)

    sc = jax.nn.silu(c_ctx.astype(jnp.float32))
    if last:
        mod_c = sc @ p['w_mod'][:, :2 * D_MODEL] + p['b_mod'][:2 * D_MODEL]
        shift_c, scale_c = jnp.split(mod_c, 2)
        hc = _ln(ctx) * (1.0 + scale_c) + shift_c
        ckv_c, kr_c, s5_c = jnp.split(hc @ p['w_in'][:, :CTX_STATE_COLS], IN_SPLITS[:2], axis=-1)
    else:
        mod_c = sc @ p['w_mod'] + p['b_mod']
        shift_c, scale_c, gate_c = jnp.split(mod_c, 3)
        hc = _ln(ctx) * (1.0 + scale_c) + shift_c
        ckv_c, kr_c, s5_c, cq_c, pool_c, su_c, sv_c, g_c = jnp.split(hc @ p['w_in'], IN_SPLITS, axis=-1)

    kn_c, v_c = _mla_kv(ckv_c, p['g_kv'], p['w_ukv'])
    kn_l, v_l = _mla_kv(ckv, p['g_kv'], p['w_ukv'])
    qn_l, qr_l = _mla_q(cq, p['g_q'], p['w_uq'])
    qr_l = _rope(qr_l, cos, sin)
    kr_l = _rope(kr[:, :, None, :], cos, sin)[:, :, 0, :]
    att_l = _mla_attend(qn_l, qr_l,
                        jnp.concatenate([kn_c, kn_l], axis=1),
                        jnp.concatenate([kr_c, kr_l], axis=1),
                        jnp.concatenate([v_c, v_l], axis=1))

    pool_l = _pool_mixer(pool_in, p['w_pool'], p['pool_scale'])

    u_l = s5_in.astype(jnp.float32).reshape(b, n, S5_G, S5_H)
    u_c = s5_c.astype(jnp.float32).reshape(b, nc, S5_G, S5_H)
    y_l = s5_in * p['s5_d']
    if not last:
        y_c = s5_c * p['s5_d']
    for d, rev in enumerate((False, True)):
        disc = _s5_discretise(p['lam_re'][d], p['lam_im'][d], p['log_dt'][d],
                              p['s5_b_re'][d], p['s5_b_im'][d])
        hc_r, hc_i = _s5_scan(u_c, disc, rev, None)
        i_end = 0 if rev else nc - 1
        hl_r, hl_i = _s5_scan(u_l, disc, rev, (hc_r[i_end], hc_i[i_end]))
        y_l = y_l + _s5_readout(hl_r, hl_i, p['s5_c_re'][d], p['s5_c_im'][d]).reshape(b, n, BRANCH_W)
        if not last:
            y_c = y_c + _s5_readout(hc_r, hc_i, p['s5_c_re'][d], p['s5_c_im'][d]).reshape(b, nc, BRANCH_W)
    s5_l = _glu(y_l, p['w_glu'], p['b_glu'])

    sgu_l = _sgu(su, sv, p['sgu_g'], p['sgu_b'], p['w_s'], p['b_s'])

    y = (jnp.concatenate([att_l, pool_l, s5_l, sgu_l], axis=-1) * jax.nn.silu(g)) @ p['w_out']
    x_new = (_ln(ALPHA * x + gate * y) * p['ln_g'] + p['ln_b']).astype(x.dtype)
    if last:
        return x_new, None

    qn_c, qr_c = _mla_q(cq_c, p['g_q'], p['w_uq'])
    att_c = _mla_attend(qn_c, qr_c, kn_c, kr_c, v_c)
    pool_cc = _pool_mixer(pool_c, p['w_pool'], p['pool_scale'])
    s5_cc = _glu(y_c, p['w_glu'], p['b_glu'])
    sgu_c = _sgu(su_c, sv_c, p['sgu_g'], p['sgu_b'], p['w_s'], p['b_s'])
    yc = (jnp.concatenate([att_c, pool_cc, s5_cc, sgu_c], axis=-1) * jax.nn.silu(g_c)) @ p['w_out']
    ctx_new = (_ln(ALPHA * ctx + gate_c * yc) * p['ln_g'] + p['ln_b']).astype(ctx.dtype)
    return x_new, ctx_new


def setup_inputs(seed: int = 0) -> dict:
    key = jax.random.key(seed)
    k = jax.random.split(key, 32)
    L = DEPTH
    f32 = jnp.float32

    def nrm(i, shape, s):
        return jax.random.normal(k[i], shape, f32) * s

    n_idx = jnp.arange(S5_P, dtype=f32)
    return {
        'x': nrm(0, (BATCH, SEQ, D_MODEL), 1.0),
        'c': nrm(1, (BATCH, D_MODEL), 1.0),
        'ctx': nrm(2, (BATCH, CTX_LEN, D_MODEL), 1.0),
        'c_ctx': nrm(3, (D_MODEL,), 1.0),
        'w_mod': nrm(4, (L, D_MODEL, 3 * D_MODEL), 0.25 * D_MODEL ** -0.5),
        'b_mod': nrm(5, (L, 3 * D_MODEL), 0.02),
        'w_in': nrm(6, (L, D_MODEL, IN_DIM), D_MODEL ** -0.5),
        'g_q': 1.0 + nrm(7, (L, Q_LORA), 0.02),
        'w_uq': nrm(8, (L, Q_LORA, MLA_HEADS * (MLA_NOPE + MLA_ROPE)), Q_LORA ** -0.5),
        'g_kv': 1.0 + nrm(9, (L, KV_LORA), 0.02),
        'w_ukv': nrm(10, (L, KV_LORA, MLA_HEADS * (MLA_NOPE + MLA_V)), KV_LORA ** -0.5),
        'w_pool': nrm(11, (L, N_POOL, POOL_GC, POOL_GC), POOL_GC ** -0.5),
        'pool_scale': 1.0 + nrm(12, (L, BRANCH_W), 0.02),
        'lam_re': -0.5 + nrm(13, (L, 2, S5_G, S5_P), 0.01),
        'lam_im': math.pi * n_idx + nrm(14, (L, 2, S5_G, S5_P), 0.01),
        'log_dt': jax.random.uniform(k[15], (L, 2, S5_G), dtype=f32,
                                     minval=math.log(DT_MIN), maxval=math.log(DT_MAX)),
        's5_b_re': nrm(16, (L, 2, S5_G, S5_P, S5_H), (2 * S5_H) ** -0.5),
        's5_b_im': nrm(17, (L, 2, S5_G, S5_P, S5_H), (2 * S5_H) ** -0.5),
        's5_c_re': nrm(18, (L, 2, S5_G, S5_H, S5_P), S5_P ** -0.5),
        's5_c_im': nrm(19, (L, 2, S5_G, S5_H, S5_P), S5_P ** -0.5),
        's5_d': nrm(20, (L, BRANCH_W), 1.0),
        'w_glu': nrm(21, (L, BRANCH_W, BRANCH_W), BRANCH_W ** -0.5),
        'b_glu': nrm(22, (L, BRANCH_W), 0.02),
        'sgu_g': 1.0 + nrm(23, (L, BRANCH_W), 0.02),
        'sgu_b': nrm(24, (L, BRANCH_W), 0.02),
        'w_s': nrm(25, (L, SGU_HEADS, CHUNK, CHUNK), CHUNK ** -0.5),
        'b_s': 1.0 + nrm(26, (L, SGU_HEADS, CHUNK), 0.02),
        'w_out': nrm(27, (L, D_MIX, D_MODEL), BETA * D_MIX ** -0.5),
        'ln_g': 1.0 + nrm(28, (L, D_MODEL), 0.02),
        'ln_b': nrm(29, (L, D_MODEL), 0.02),
    }


def reference(x, c, ctx, c_ctx, w_mod, b_mod, w_in, g_q, w_uq, g_kv, w_ukv, w_pool, pool_scale,
              lam_re, lam_im, log_dt, s5_b_re, s5_b_im, s5_c_re, s5_c_im, s5_d, w_glu, b_glu,
              sgu_g, sgu_b, w_s, b_s, w_out, ln_g, ln_b):
    rows = x.shape[1] // GRID_W
    cos, sin = _axial_rope_tables(rows)
    for l in range(DEPTH):
        p = {
            'w_mod': w_mod[l], 'b_mod': b_mod[l], 'w_in': w_in[l],
            'g_q': g_q[l], 'w_uq': w_uq[l], 'g_kv': g_kv[l], 'w_ukv': w_ukv[l],
            'w_pool': w_pool[l], 'pool_scale': pool_scale[l],
            'lam_re': lam_re[l], 'lam_im': lam_im[l], 'log_dt': log_dt[l],
            's5_b_re': s5_b_re[l], 's5_b_im': s5_b_im[l], 's5_c_re': s5_c_re[l], 's5_c_im': s5_c_im[l],
            's5_d': s5_d[l], 'w_glu': w_glu[l], 'b_glu': b_glu[l],
            'sgu_g': sgu_g[l], 'sgu_b': sgu_b[l], 'w_s': w_s[l], 'b_s': b_s[l],
            'w_out': w_out[l], 'ln_g': ln_g[l], 'ln_b': ln_b[l],
        }
        x, ctx = _layer(x, ctx, c, c_ctx, cos, sin, p, l == DEPTH - 1)
    return x
```

```python
import os
from contextlib import ExitStack
import numpy as np
import concourse.bass as bass
import concourse.mybir as mybir
from concourse.bass_utils import run_bass_kernel_spmd

F32 = mybir.dt.float32
BF16 = mybir.dt.bfloat16
AF = mybir.ActivationFunctionType
ALU = mybir.AluOpType

D = 1024
SEQ = 8192
NCTX = 256
NU = NCTX + SEQ
NE = NU + NCTX
NSC = NE // 8
HALF = SEQ // 2
DEPTH = 2
QB = 256
IN_DIM = 2400
LN_EPS = 1e-6
ALPHA = (2 * DEPTH) ** 0.25
PARAMS = ['w_mod', 'b_mod', 'w_in', 'g_q', 'w_uq', 'g_kv', 'w_ukv', 'w_pool', 'pool_scale', 'lam_re', 'lam_im',
          'log_dt', 's5_b_re', 's5_b_im', 's5_c_re', 's5_c_im', 's5_d', 'w_glu', 'b_glu', 'sgu_g', 'sgu_b', 'w_s',
          'b_s', 'w_out', 'ln_g', 'ln_b']
PSHAPES = {'w_mod': (2, 1024, 3072), 'b_mod': (2, 3072), 'w_in': (2, 1024, 2400), 'g_q': (2, 192), 'w_uq': (2, 192, 384),
           'g_kv': (2, 128), 'w_ukv': (2, 128, 512), 'w_pool': (2, 4, 64, 64), 'pool_scale': (2, 256),
           'lam_re': (2, 2, 16, 64), 'lam_im': (2, 2, 16, 64), 'log_dt': (2, 2, 16), 's5_b_re': (2, 2, 16, 64, 16),
           's5_b_im': (2, 2, 16, 64, 16), 's5_c_re': (2, 2, 16, 16, 64), 's5_c_im': (2, 2, 16, 16, 64), 's5_d': (2, 256),
           'w_glu': (2, 256, 256), 'b_glu': (2, 256), 'sgu_g': (2, 256), 'sgu_b': (2, 256), 'w_s': (2, 4, 128, 128),
           'b_s': (2, 4, 128), 'w_out': (2, 1024, 1024), 'ln_g': (2, 1024), 'ln_b': (2, 1024)}

SAME_ENGINE_SYNC = True


class KB:
    NSLOT = 16

    def __init__(self, nc, stack):
        self.nc = nc
        self.eng = {'pe': nc.tensor, 'act': nc.scalar, 'dve': nc.vector, 'pool': nc.gpsimd, 'sp': nc.sync}
        self.sem = {e: stack.enter_context(nc.semaphore('s_' + e)) for e in self.eng}
        self.cnt = {e: 0 for e in self.eng}
        self.dsem, self.duse = {}, {}
        for q in ('sp', 'pool', 'act'):
            for j in range(self.NSLOT):
                self.dsem[(q, j)] = stack.enter_context(nc.semaphore('d_%s%d' % (q, j)))
                self.duse[(q, j)] = 0
        self.dnext = {'sp': 0, 'pool': 0, 'act': 0}
        self.known = {e: {} for e in self.eng}
        self.lastw, self.readers = {}, {}
        self.ninstr = 0

    def _need(self, E, tok, waits):
        if tok is None:
            return
        if tok[0] == 'c':
            _, e2, n = tok
            if e2 == E and (not SAME_ENGINE_SYNC or E == 'pe'):
                return
            key, val = e2, n
        else:
            _, q, j, n = tok
            key, val = (q, j), n * 16
        if self.known[E].get(key, 0) >= val:
            return
        if waits.get(key, 0) < val:
            waits[key] = val

    def _sync(self, E, R, W, waits=None):
        waits = {} if waits is None else waits
        for r in R:
            self._need(E, self.lastw.get(r), waits)
        for w in W:
            self._need(E, self.lastw.get(w), waits)
            for t in self.readers.get(w, ()):
                self._need(E, t, waits)
        eng = self.eng[E]
        for key, val in waits.items():
            eng.wait_ge(self.sem[key] if isinstance(key, str) else self.dsem[key], val)
            self.known[E][key] = val

    def _commit(self, tok, R, W):
        for r in R:
            self.readers.setdefault(r, []).append(tok)
        for w in W:
            self.lastw[w] = tok
            self.readers[w] = []

    def op(self, E, fn, R=(), W=()):
        W = list(W) + [r for r in R if r.startswith('ps') and r not in W]
        self._sync(E, R, W)
        ins = fn()
        self.cnt[E] += 1
        ins.then_inc(self.sem[E], 1)
        self._commit(('c', E, self.cnt[E]), R, W)
        self.ninstr += 1
        return ins

    def dma(self, out, in_, R=(), W=(), q='sp', **kw):
        j = self.dnext[q]
        self.dnext[q] = (j + 1) % self.NSLOT
        slot = (q, j)
        waits = {}
        if self.duse[slot] > 0:
            self._need(q, ('d', q, j, self.duse[slot]), waits)
        self._sync(q, R, W, waits)
        ins = self.eng[q].dma_start(out=out, in_=in_, **kw)
        ins.then_inc(self.dsem[slot], 16)
        self.duse[slot] += 1
        self._commit(('d', q, j, self.duse[slot]), R, W)
        self.ninstr += 1

    def barrier(self):
        for E, eng in self.eng.items():
            for e2 in self.eng:
                if e2 != E and self.cnt[e2] > self.known[E].get(e2, 0):
                    eng.wait_ge(self.sem[e2], self.cnt[e2])
                    self.known[E][e2] = self.cnt[e2]
            for slot, n in self.duse.items():
                if n > 0 and 16 * n > self.known[E].get(slot, 0):
                    eng.wait_ge(self.dsem[slot], 16 * n)
                    self.known[E][slot] = 16 * n

    def wait_all(self, E='sp'):
        eng = self.eng[E]
        for e2 in self.eng:
            if e2 != E and self.cnt[e2] > 0:
                eng.wait_ge(self.sem[e2], self.cnt[e2])
        for slot, n in self.duse.items():
            if n > 0:
                eng.wait_ge(self.dsem[slot], 16 * n)


def _consts():
    c = {}
    c['ident'] = np.eye(128, dtype=np.float32)
    sw = np.zeros((128, 128), np.float32)
    for p in range(64):
        sw[p, p + 64] = 1.0
        sw[p + 64, p] = 1.0
    c['swapm'] = sw
    rows = SEQ // 64
    t = np.arange(SEQ)
    pos = np.stack([t // 64, t % 64], 0).astype(np.float32)
    inv = (10000.0 ** (-np.arange(8, dtype=np.float32) / 8)).astype(np.float32)
    ang = pos[:, None, :] * inv[None, :, None]
    cos, sin = np.cos(ang).astype(np.float32), np.sin(ang).astype(np.float32)
    rc = np.zeros((32, SEQ), np.float32)
    rs = np.zeros((32, SEQ), np.float32)
    for a in range(2):
        for hf in range(2):
            rc[a * 16 + hf * 8: a * 16 + hf * 8 + 8] = cos[a]
            rs[a * 16 + hf * 8: a * 16 + hf * 8 + 8] = sin[a] * (-1.0 if hf == 0 else 1.0)
    r_ = np.arange(128) // 16
    c['mlow'] = (r_[None, :] >= r_[:, None]).astype(np.float32)
    c['mup'] = (r_[None, :] <= r_[:, None]).astype(np.float32)
    c['ropec'] = rc
    c['ropes'] = rs
    return c


def build(nlayers=DEPTH, dbg=()):
    nc = bass.Bass("TRN2", target_bir_lowering=False)
    din = {}

    def inp(name, shape):
        din[name] = nc.dram_tensor(name, list(shape), F32, kind="ExternalInput").ap()
        return din[name]

    x_in = inp('x', (SEQ, D))
    xq_in = inp('xq', (HALF, D))
    ctx_in = inp('ctx', (NCTX, D))
    msel = inp('msel', (2,))
    ropecq = inp('ropec_q', (32, HALF))
    ropesq = inp('ropes_q', (32, HALF))
    cvec = inp('cvec', (2, D))
    prm = {n: inp(n, PSHAPES[n]) for n in PARAMS}
    cst = {n: inp(n, v.shape) for n, v in _consts().items()}
    out_ap = nc.dram_tensor('out', [HALF, D], F32, kind="ExternalOutput").ap()
    dout = {}

    def dbg_out(name, shape, dt=F32):
        dout[name] = nc.dram_tensor('dbg_' + name, list(shape), dt, kind="ExternalOutput").ap()
        return dout[name]

    scr = lambda name, shape, dt=F32: nc.dram_tensor(name, list(shape), dt, kind="Internal").ap()
    CCR = 512
    x1g = [scr('x1g%d' % c, (2 * CCR, D)) for c in range(HALF // CCR)]

    def x1rows(t):
        r_, w_ = t // HALF, t % HALF
        c_, o_ = w_ // CCR, w_ % CCR
        return x1g[c_][r_ * CCR + o_: r_ * CCR + o_ + 128, :]
    x1h_scr = scr('x1h_scr', (HALF, D))
    ctx1_scr = scr('ctx1_scr', (NCTX, D))
    gate_scr = scr('gate_scr', (2, 2, D))
    pool_scr = [scr('pool_scr_c', (256, NCTX + 16), BF16), scr('pool_scr_l', (256, SEQ + 16), BF16)]
    s5o_scr = scr('s5o_scr', (256, NU), BF16)
    xd_scr = scr('xd_scr', (256, 8, NSC), BF16)
    yd_scr = scr('yd_scr', (256, 8, NSC), BF16)
    ydg_scr = [scr('ydg%d' % c, (128, 8 * NSC), BF16) for c in range(2)]

    with ExitStack() as st:
        k = KB(nc, st)

        uid = [0]

        def T(stk, name, shape, dt):
            uid[0] += 1
            return stk.enter_context(nc.sbuf_tensor('%s_t%d' % (name, uid[0]), list(shape), dt))

        def P(stk, name, shape, dt=F32):
            uid[0] += 1
            return stk.enter_context(nc.psum_tensor('%s_p%d' % (name, uid[0]), list(shape), dt))

        V = nc.vector
        A = nc.scalar
        G = nc.gpsimd
        PE = nc.tensor

        def ln_block(xsrcs, rkeys, col, hT_, B):
            n = len(xsrcs)

            def stage1(s_):
                xb, xk = B['xt'][s_ % 2]
                xn_, xnk = B['xn'][s_ % 2]
                pt_, ptk = B['pt'][s_ % 2]
                sm, smk = B['sm'][s_ % 2]
                st6_ = sm[:, 0:12].rearrange("p (a b) -> p a b", a=2)
                k.dma(xb[:], xsrcs[s_], R=rkeys, W=[xk], q=('sp' if s_ % 2 == 0 else 'pool'))
                for c2 in range(2):
                    k.op('dve', lambda c2=c2: V.bn_stats(st6_[:, c2, :], xb[:, c2 * 512:(c2 + 1) * 512]), R=[xk], W=[smk])
                k.op('dve', lambda: V.bn_aggr(sm[:, 12:14], st6_), R=[smk], W=[smk])
                k.op('act', lambda: A.activation(sm[:, 14:15], sm[:, 13:14], AF.Sqrt, bias=epsb[:, 0:1]), R=[smk, 'epsb'], W=[smk])
                k.op('dve', lambda: V.reciprocal(sm[:, 14:15], sm[:, 14:15]), R=[smk], W=[smk])
                k.op('dve', lambda: V.tensor_scalar(sm[:, 15:16], sm[:, 12:13], sm[:, 14:15], -1.0, op0=ALU.mult, op1=ALU.mult), R=[smk], W=[smk])
                k.op('act', lambda: A.activation(xn_[:], xb[:], AF.Identity, scale=sm[:, 14:15], bias=sm[:, 15:16]), R=[xk, smk], W=[xnk])
                for dc in range(8):
                    k.op('pe', lambda dc=dc: PE.transpose(pt_[:, dc, :], xn_[:, dc * 128:(dc + 1) * 128], ident_b[:]), R=[xnk, 'ident_b'], W=[ptk])

            def stage2(s_):
                pt_, ptk = B['pt'][s_ % 2]
                if B.get('dve_mod'):
                    hv = hT_[:, :, s_ * 128:(s_ + 1) * 128]
                    k.op('dve', lambda: V.tensor_tensor(hv, pt_[:, :, :], sc1T[:, :, col].unsqueeze(2).to_broadcast([128, 8, 128]), op=ALU.mult),
                         R=[ptk, 'sc1T'], W=[B['hTk']])
                    k.op('pool', lambda: G.tensor_tensor(hv, hv, modT[:, 0:8, col].unsqueeze(2).to_broadcast([128, 8, 128]), op=ALU.add),
                         R=[B['hTk'], 'modT'], W=[B['hTk']])
                    return
                for dc in range(8):
                    k.op('act', lambda dc=dc: A.activation(hT_[:, dc, s_ * 128:(s_ + 1) * 128], pt_[:, dc, :], AF.Identity,
                                                          scale=sc1T[:, dc, col:col + 1], bias=modT[:, dc, col:col + 1]),
                         R=[ptk, 'sc1T', 'modT'], W=[B['hTk']])

            stage1(0)
            for s_ in range(n):
                if s_ + 1 < n:
                    stage1(s_ + 1)
                stage2(s_)

        ident_f = T(st, 'ident_f', (128, 128), F32)
        ident_b = T(st, 'ident_b', (128, 128), BF16)
        ones_b = T(st, 'ones_b', (128, 128), BF16)
        epsb = T(st, 'epsb', (128, 1), F32)
        KT = T(st, 'KT', (128, 4, NU), BF16)
        Vt = T(st, 'Vt', (128, NU // 128, 4, 65), BF16)
        modT = T(st, 'modT', (128, 24, 2), F32)
        sc1T = T(st, 'sc1T', (128, 8, 2), F32)
        wukv = T(st, 'wukv', (128, 512), BF16)
        msb = T(st, 'msb', (128, 2), F32)
        recF = T(st, 'recF', (128, 2, 8), F32)
        recL = T(st, 'recL', (128, 2, 8), F32)
        ccsem = st.enter_context(nc.semaphore('ccsem'))
        ccn = [0]

        k.dma(ident_f[:], cst['ident'], W=['ident_f'])
        k.op('dve', lambda: V.tensor_copy(ident_b[:], ident_f[:]), R=['ident_f'], W=['ident_b'])
        k.op('pool', lambda: G.memset(ones_b[:], 1.0), W=['ones_b'])
        k.op('pool', lambda: G.memset(epsb[:], LN_EPS), W=['epsb'])
        k.op('pool', lambda: G.memset(Vt[:, :, :, 64:65], 1.0), W=['Vt'])
        k.dma(msb[:], msel.partition_broadcast(128), W=['msb'])
        k.op('pool', lambda: G.memset(recF[:], 1.0), W=['recF'])
        k.op('pool', lambda: G.memset(recL[:], 1.0), W=['recL'])
        WINS = ((0, 0, 2), (64, 0, 4), (0, 1, 8), (64, 1, 16))
        for (r0, ct, w_) in WINS:
            for p_ in range(w_ // 2):
                c1_ = 1.0 / (p_ + w_ // 2) - 1.0 / w_
                k.op('dve', lambda r0=r0, ct=ct, p_=p_, c1_=c1_, w_=w_: V.tensor_scalar(recF[r0:r0 + 64, ct, p_:p_ + 1], msb[r0:r0 + 64, 0:1], c1_, 1.0 / w_, op0=ALU.mult, op1=ALU.add),
                     R=['msb'], W=['recF'])
            for q_ in range(8 - w_ // 2 + 1, 8):
                c1_ = 1.0 / (8 - q_ + w_ // 2) - 1.0 / w_
                k.op('dve', lambda r0=r0, ct=ct, q_=q_, c1_=c1_, w_=w_: V.tensor_scalar(recL[r0:r0 + 64, ct, q_:q_ + 1], msb[r0:r0 + 64, 1:2], c1_, 1.0 / w_, op0=ALU.mult, op1=ALU.add),
                     R=['msb'], W=['recL'])

        for l in range(nlayers):
            last = (l == DEPTH - 1)
            xin = x_in
            cin = ctx_in if l == 0 else ctx1_scr
            k.barrier()
            with ExitStack() as s0:
                cc = T(s0, 'cc', (128, 8, 2), F32)
                scc = T(s0, 'scc', (128, 8, 2), F32)
                bmodT = T(s0, 'bmodT', (128, 24), F32)
                wm = [T(s0, 'wm%d' % i, (128, 8, 512), F32) for i in range(2)]
                pm = P(s0, 'pm', (128, 24, 2))
                wtmp = T(s0, 'wtmp', (128, 512), F32)
                gkv = T(s0, 'gkv', (128, 1), F32)
                for col in range(2):
                    k.dma(cc[:, :, col], cvec[col].rearrange("(c p) -> p c", p=128), W=['cc'], allow_slow_non_contiguous=True)
                k.dma(bmodT[:], prm['b_mod'][l].rearrange("(c p) -> p c", p=128), W=['bmodT'], allow_slow_non_contiguous=True)
                k.op('act', lambda: A.activation(scc[:], cc[:], AF.Silu), R=['cc'], W=['scc'])
                for blk in range(6):
                    w_ = wm[blk % 2]
                    k.dma(w_[:], prm['w_mod'][l][:, blk * 512:(blk + 1) * 512].rearrange("(c p) n -> p c n", p=128),
                          W=['wm%d' % (blk % 2)], q=('sp' if blk % 2 == 0 else 'pool'))
                    for jj in range(4):
                        jt = blk * 4 + jj
                        for dc in range(8):
                            k.op('pe', lambda w_=w_, jj=jj, jt=jt, dc=dc: PE.matmul(
                                pm[:, jt, :], lhsT=w_[:, dc, jj * 128:(jj + 1) * 128], rhs=scc[:, dc, :],
                                start=(dc == 0), stop=(dc == 7)), R=['wm%d' % (blk % 2), 'scc'], W=['pm'])
                k.op('dve', lambda: V.tensor_tensor(modT[:], pm[:], bmodT[:].unsqueeze(2).to_broadcast([128, 24, 2]), op=ALU.add),
                     R=['pm', 'bmodT'], W=['modT'])
                k.op('dve', lambda: V.tensor_scalar_add(sc1T[:], modT[:, 8:16, :], 1.0), R=['modT'], W=['sc1T'])
                for col in range(2):
                    k.dma(gate_scr[l, col].rearrange("(c p) -> p c", p=128), modT[:, 16:24, col], R=['modT'], W=['gate_scr'],
                          allow_slow_non_contiguous=True)
                k.dma(wtmp[:], prm['w_ukv'][l], W=['wtmp'])
                k.dma(gkv[:], prm['g_kv'][l].rearrange("(p o) -> p o", o=1), W=['gkv'])
                k.op('dve', lambda: V.tensor_scalar_mul(wukv[:], wtmp[:], gkv[:, 0:1]), R=['wtmp', 'gkv'], W=['wukv'])

            if 'mod' in dbg and l == 0:
                d_ = dbg_out('mod', (128, 48))
                k.dma(d_, modT[:].rearrange("p a b -> p (a b)"), R=['modT'], W=['dbgmod'])

            k.barrier()
            if l > 0:
                for q_ in ('sp', 'pool'):
                    k.eng[q_].wait_ge(ccsem, x1_ready)
            with ExitStack() as sA:
                with ExitStack() as sa:
                    wA = T(sa, 'wA', (128, 8, 832), BF16)
                    wst = T(sa, 'wst', (128, 8, 256), F32)
                    xt = [T(sa, 'xt%d' % i, (128, D), F32) for i in range(2)]
                    xn2 = [T(sa, 'xn2_%d' % i, (128, D), BF16) for i in range(2)]
                    sm2 = [T(sa, 'sm2_%d' % i, (128, 16), F32) for i in range(2)]
                    st6 = T(sa, 'st6', (128, 2, 6), F32)
                    mv = T(sa, 'mv', (128, 2), F32)
                    rstd = T(sa, 'rstd', (128, 1), F32)
                    xn = T(sa, 'xn', (128, D), BF16)
                    hf32 = T(sa, 'hf32', (128, 8, 128), F32)
                    hT2 = [T(sa, 'hT%d' % i, (128, 8, 512), BF16) for i in range(2)]
                    sq = T(sa, 'sq', (128, 512), BF16)
                    rms = T(sa, 'rms', (128, 512), F32)
                    ckvn = T(sa, 'ckvn', (128, 512), BF16)
                    rc_t = T(sa, 'rc_t', (128, 512), F32)
                    rs_t = T(sa, 'rs_t', (128, 512), F32)
                    kr1 = T(sa, 'kr1', (128, 512), F32)
                    kr2 = T(sa, 'kr2', (128, 512), F32)
                    plo = T(sa, 'plo', (128, 2, 512), BF16)
                    zpad = T(sa, 'zpad', (128, 2, 8), BF16)
                    xds = T(sa, 'xds', (128, 2, 8, 64), BF16)
                    pt = P(sa, 'pt', (128, 8, 128), BF16)
                    ptb = P(sa, 'ptb', (128, 8, 128), BF16)
                    LNB = {'xt': [(xt[0], 'xt0'), (xt[1], 'xt1')], 'xn': [(xn2[0], 'xn2_0'), (xn2[1], 'xn2_1')],
                           'pt': [(pt, 'pt'), (ptb, 'ptb')], 'sm': [(sm2[0], 'sm2_0'), (sm2[1], 'sm2_1')], 'hTk': 'hT'}
                    ps = [P(sa, 'psA%d' % i, (128, 512)) for i in range(6)]

                    k.op('pool', lambda: G.memset(wA[:, :, 128:320], 0.0), W=['wA'])
                    k.op('pool', lambda: G.memset(zpad[:], 0.0), W=['zpad'])
                    win = prm['w_in'][l].rearrange("(c p) n -> p c n", p=128)
                    k.dma(wst[:, :, 0:160], win[:, :, 0:160], W=['wst'])
                    k.op('dve', lambda: V.tensor_copy(wA[:, :, 0:128], wst[:, :, 0:128]), R=['wst'], W=['wA'])
                    k.op('dve', lambda: V.tensor_copy(wA[:, :, 192:224], wst[:, :, 128:160]), R=['wst'], W=['wA'])
                    for (d0, s0_) in ((0, 8), (8, 0), (16, 24), (24, 16)):
                        k.op('dve', lambda d0=d0, s0_=s0_: V.tensor_copy(wA[:, :, 288 + d0:296 + d0], wst[:, :, 128 + s0_:136 + s0_]),
                             R=['wst'], W=['wA'])
                    k.dma(wst[:], win[:, :, 160:416], R=[], W=['wst'])
                    k.op('dve', lambda: V.tensor_copy(wA[:, :, 320:576], wst[:]), R=['wst'], W=['wA'])
                    k.dma(wst[:], win[:, :, 608:864], R=[], W=['wst'])
                    k.op('dve', lambda: V.tensor_copy(wA[:, :, 576:832], wst[:]), R=['wst'], W=['wA'])
                    for si in range(2):
                        k.dma(pool_scr[si][:, 0:8].rearrange("(c p) n -> p c n", p=128), zpad[:], R=['zpad'], W=['pool_scr'])
                        n_ = NCTX if si == 0 else SEQ
                        k.dma(pool_scr[si][:, 8 + n_:16 + n_].rearrange("(c p) n -> p c n", p=128), zpad[:], R=['zpad'], W=['pool_scr'])

                    blocks = [('ctx', 0, NCTX)] + [('lat', 512 * i, 512) for i in range(SEQ // 512)]
                    xcnt = 0
                    for bidx, (kind, t0, Tn) in enumerate(blocks):
                        hT, hTk = hT2[bidx % 2], 'hT%d' % (bidx % 2)
                        LNB['hTk'] = hTk
                        col = 1 if kind == 'ctx' else 0
                        src = cin if kind == 'ctx' else xin
                        u0 = t0 if kind == 'ctx' else NCTX + t0
                        nsub = Tn // 128
                        xsrcs = [(x1rows(t0 + sb * 128) if (l > 0 and kind == 'lat') else src[t0 + sb * 128: t0 + (sb + 1) * 128, :]) for sb in range(nsub)]
                        ln_block(xsrcs, ['x1w'] if l > 0 else [], col, hT, LNB)
                        def proj(pst, c0, ncol, key):
                            for dc in range(8):
                                k.op('pe', lambda dc=dc: PE.matmul(pst[0:ncol, 0:Tn], lhsT=wA[:, dc, c0:c0 + ncol], rhs=hT[:, dc, 0:Tn],
                                                                  start=(dc == 0), stop=(dc == 7)), R=['wA', hTk], W=[key])
                        proj(ps[0], 0, 128, 'psA0')
                        proj(ps[1], 128, 96, 'psA1')
                        if kind == 'lat':
                            proj(ps[2], 224, 96, 'psA2')
                        k.op('act', lambda: A.activation(sq[:, 0:Tn], ps[0][:, 0:Tn], AF.Square), R=['psA0'], W=['sq'])
                        k.op('pe', lambda: PE.matmul(ps[3][:, 0:Tn], lhsT=ones_b[:], rhs=sq[:, 0:Tn], start=True, stop=True),
                             R=['ones_b', 'sq'], W=['psA3'])
                        k.op('act', lambda: A.activation(rms[:, 0:Tn], ps[3][:, 0:Tn], AF.Sqrt, bias=epsb[:, 0:1], scale=1.0 / 128),
                             R=['psA3', 'epsb'], W=['rms'])
                        k.op('dve', lambda: V.reciprocal(rms[:, 0:Tn], rms[:, 0:Tn]), R=['rms'], W=['rms'])
                        k.op('dve', lambda: V.tensor_tensor(ckvn[:, 0:Tn], ps[0][:, 0:Tn], rms[:, 0:Tn], op=ALU.mult), R=['psA0', 'rms'], W=['ckvn'])
                        for h in range(4):
                            k.op('pe', lambda h=h: PE.matmul(ps[3][0:64, 0:Tn], lhsT=wukv[:, h * 128:h * 128 + 64], rhs=ckvn[:, 0:Tn],
                                                            start=True, stop=True), R=['wukv', 'ckvn'], W=['psA3'])
                            k.op('act', lambda h=h: A.copy(KT[0:64, h, u0:u0 + Tn], ps[3][0:64, 0:Tn]), R=['psA3'], W=['KT%d' % (u0 // 512)])
                        wv = wukv[:].rearrange("p (h t d) -> p h t d", h=4, t=2)[:, :, 1, :]
                        for sb in range(nsub):
                            k.op('pe', lambda sb=sb: PE.matmul(ps[4][:, 0:256].rearrange("p (h d) -> p h d", h=4), lhsT=ckvn[:, sb * 128:(sb + 1) * 128],
                                                              rhs=wv, start=True, stop=True), R=['wukv', 'ckvn'], W=['psA4'])
                            k.op('dve', lambda sb=sb: V.tensor_copy(Vt[:, u0 // 128 + sb, :, 0:64], ps[4][:, 0:256].rearrange("p (h d) -> p h d", h=4)),
                                 R=['psA4'], W=['Vt%d' % (u0 // 512)])
                        if kind == 'lat':
                            k.dma(rc_t[64:96, :], cst['ropec'][:, t0:t0 + 512], W=['rc_t'])
                            k.dma(rs_t[64:96, :], cst['ropes'][:, t0:t0 + 512], W=['rs_t'], q='pool')
                            k.op('dve', lambda: V.tensor_tensor(kr1[64:96, :], ps[1][64:96, :], rc_t[64:96, :], op=ALU.mult), R=['psA1', 'rc_t'], W=['kr1'])
                            k.op('dve', lambda: V.tensor_tensor(kr2[64:96, :], ps[2][64:96, :], rs_t[64:96, :], op=ALU.mult), R=['psA2', 'rs_t'], W=['kr2'])
                            for h in range(4):
                                k.op('pool', lambda h=h: G.tensor_tensor(KT[64:96, h, u0:u0 + Tn], kr1[64:96, 0:Tn], kr2[64:96, 0:Tn], op=ALU.add),
                                     R=['kr1', 'kr2'], W=['KT%d' % (u0 // 512)])
                        else:
                            for h in range(4):
                                k.op('act', lambda h=h: A.copy(KT[64:96, h, u0:u0 + Tn], ps[1][64:96, 0:Tn]), R=['psA1'], W=['KT%d' % (u0 // 512)])
                        for ct in range(2):
                            proj(ps[ct % 2 + 4], 320 + ct * 128, 128, 'psA%d' % (ct % 2 + 4))
                            srcv = ps[ct % 2 + 4][:, 0:Tn].rearrange("p (c i) -> p i c", i=8)
                            k.op('act', lambda ct=ct, srcv=srcv: A.copy(xds[:, ct, :, 0:Tn // 8], srcv), R=['psA%d' % (ct % 2 + 4)], W=['xds'])
                        xdv = xd_scr.rearrange("(c p) i n -> p c i n", p=128)
                        for ct in range(2):
                            k.dma(xdv[:, ct, :, u0 // 8:(u0 + Tn) // 8], xds[:, ct, :, 0:Tn // 8], R=['xds'], W=['xd_scr'])
                            if kind == 'ctx':
                                k.dma(xdv[:, ct, :, NU // 8:NE // 8], xds[:, ct, :, 0:Tn // 8], R=['xds'], W=['xd_scr'], q='pool')
                        for ct in range(2):
                            proj(ps[ct % 2 + 4], 576 + ct * 128, 128, 'psA%d' % (ct % 2 + 4))
                            k.op('act', lambda ct=ct: A.copy(plo[:, ct, 0:Tn], ps[ct % 2 + 4][:, 0:Tn]), R=['psA%d' % (ct % 2 + 4)], W=['plo'])
                        si = 0 if kind == 'ctx' else 1
                        k.dma(pool_scr[si][:, 8 + t0:8 + t0 + Tn].rearrange("(c p) n -> p c n", p=128), plo[:, :, 0:Tn], R=['plo'], W=['pool_scr'])

                if 'A' in dbg and l == 0:
                    d1 = dbg_out('KT', (128, 4 * NU), BF16)
                    k.dma(d1, KT[:].rearrange("p a b -> p (a b)"), R=['KT%d' % i for i in range(17)], W=['dbg1'])
                    d2 = dbg_out('Vt', (128, (NU // 128) * 4 * 65), BF16)
                    k.dma(d2, Vt[:].rearrange("p a b c -> p (a b c)"), R=['Vt%d' % i for i in range(17)], W=['dbg2'])
                    d3 = dbg_out('Xd', (256, 8 * NSC), BF16)
                    k.dma(d3, xd_scr.rearrange("a b c -> a (b c)"), R=['xd_scr'], W=['dbg3'])
                    d4 = dbg_out('poolL', (256, SEQ + 16), BF16)
                    k.dma(d4, pool_scr[1], R=['pool_scr'], W=['dbg4'])
                    break

            k.barrier()
            with ExitStack() as sB:
                Bin8T = T(sB, 'Bin8T', (128, 32, 128), BF16)
                Cout8T = T(sB, 'Cout8T', (128, 32, 128), BF16)
                D8T = T(sB, 'D8T', (128, 32, 128), BF16)
                LPr = T(sB, 'LPr', (128, 32), F32)
                LPi = T(sB, 'LPi', (128, 32), F32)
                sgn = T(sB, 'sgn', (128, 1), F32)
                swap_f = T(sB, 'swap_f', (128, 128), F32)
                s5dT = T(sB, 's5dT', (128, 2), F32)
                bgluT = T(sB, 'bgluT', (128, 2), F32)
                wglu = T(sB, 'wglu', (128, 2, 256), BF16)
                k.dma(swap_f[:], cst['swapm'], W=['swap_f'])
                k.op('pool', lambda: G.memset(sgn[0:64, :], 1.0), W=['sgn'])
                k.op('pool', lambda: G.memset(sgn[64:128, :], -1.0), W=['sgn'])
                k.dma(s5dT[:], prm['s5_d'][l].rearrange("(c p) -> p c", p=128), W=['s5dT'], allow_slow_non_contiguous=True)
                k.dma(bgluT[:], prm['b_glu'][l].rearrange("(c p) -> p c", p=128), W=['bgluT'], allow_slow_non_contiguous=True)
                TK = ['s5t']
                with ExitStack() as sT:
                    def t32(name):
                        return T(sT, name, (128, 32), F32)
                    LR, LI, DT_, ar_, ai_ = t32('LR'), t32('LI'), t32('DT_'), t32('ar_'), t32('ai_')
                    t1, t2, t3, t4 = t32('t1'), t32('t2'), t32('t3'), t32('t4')
                    er, ei, kr_, ki_, den = t32('er'), t32('ei'), t32('kr_'), t32('ki_'), t32('den')
                    lnr, lni = t32('lnr'), t32('lni')
                    halfpi = T(sT, 'halfpi', (128, 1), F32)
                    PWr = T(sT, 'PWr', (128, 16, 32), F32)
                    PWi = T(sT, 'PWi', (128, 16, 32), F32)
                    BR = T(sT, 'BR', (128, 32, 16), F32)
                    BI = T(sT, 'BI', (128, 32, 16), F32)
                    CR = T(sT, 'CR', (128, 32, 16), F32)
                    CI = T(sT, 'CI', (128, 32, 16), F32)
                    bbr = T(sT, 'bbr', (128, 32, 16), F32)
                    bbi = T(sT, 'bbi', (128, 32, 16), F32)
                    u1 = T(sT, 'u1', (128, 16, 16), F32)
                    u2 = T(sT, 'u2', (128, 16, 16), F32)
                    CRn = T(sT, 'CRn', (128, 2, 64), F32)
                    Bin8 = T(sT, 'Bin8', (128, 32, 128), F32)
                    CoutN = T(sT, 'CoutN', (128, 32, 128), F32)
                    wg32 = T(sT, 'wg32', (128, 2, 256), F32)
                    mlow = T(sT, 'mlow', (128, 128), F32)
                    mup = T(sT, 'mup', (128, 128), F32)
                    psT = P(sT, 'psT', (128, 128))

                    def dv(fn):
                        k.op('dve', fn, R=TK, W=TK)

                    def ac(fn):
                        k.op('act', fn, R=TK, W=TK)

                    k.dma(mlow[:], cst['mlow'], W=TK)
                    k.dma(mup[:], cst['mup'], W=TK)
                    k.dma(wg32[:], prm['w_glu'][l].rearrange("(c p) n -> p c n", p=128), W=TK)
                    dv(lambda: V.tensor_copy(wglu[:], wg32[:]))
                    for hf in range(2):
                        sl = slice(hf * 64, hf * 64 + 64)
                        k.dma(LR[sl, :], prm['lam_re'][l].rearrange("d g p -> p (d g)"), W=TK, allow_slow_non_contiguous=True)
                        k.dma(LI[sl, :], prm['lam_im'][l].rearrange("d g p -> p (d g)"), W=TK, allow_slow_non_contiguous=True, q='pool')
                        k.dma(BR[sl], prm['s5_b_re'][l].rearrange("d g p h -> p (d g) h"), W=TK)
                        k.dma(BI[sl], prm['s5_b_im'][l].rearrange("d g p h -> p (d g) h"), W=TK, q='pool')
                    k.dma(DT_[:], prm['log_dt'][l].rearrange("d g -> (d g)").partition_broadcast(128), W=TK)
                    k.op('pool', lambda: G.memset(halfpi[:], float(np.pi / 2)), W=TK)
                    for ci, (cn, Ct) in enumerate((('s5_c_re', CR), ('s5_c_im', CI))):
                        crow = prm[cn][l].rearrange("d g h p -> (d g h) p")
                        for j in range(4):
                            k.dma(CRn[:, 0, :], crow[j * 128:(j + 1) * 128, :], W=TK)
                            k.dma(CRn[:, 1, :], crow[j * 128:(j + 1) * 128, :], W=TK, q='pool')
                            k.op('pe', lambda: PE.transpose(psT[:], CRn[:].rearrange("r a p -> r (a p)"), ident_f[:]), R=TK + ['ident_f'], W=['psT'])
                            k.op('dve', lambda j=j, Ct=Ct: V.tensor_copy(Ct[:, j * 8:(j + 1) * 8, :].rearrange("q a h -> q (a h)"), psT[:]), R=['psT'] + TK, W=TK)
                    ac(lambda: A.activation(DT_[:], DT_[:], AF.Exp))
                    dv(lambda: V.tensor_tensor(ar_[:], LR[:], DT_[:], op=ALU.mult))
                    dv(lambda: V.tensor_tensor(ai_[:], LI[:], DT_[:], op=ALU.mult))
                    ac(lambda: A.activation(t1[:], ar_[:], AF.Exp, scale=1.0 / 16))
                    ac(lambda: A.activation(t2[:], ai_[:], AF.Sin, scale=1.0 / 16, bias=halfpi[:, 0:1]))
                    ac(lambda: A.activation(t3[:], ai_[:], AF.Sin, scale=1.0 / 16))
                    dv(lambda: V.tensor_tensor(er[:], t1[:], t2[:], op=ALU.mult))
                    dv(lambda: V.tensor_tensor(ei[:], t1[:], t3[:], op=ALU.mult))

                    def cmul(or_, oi_, xr, xi, yr, yi, a1=None, a2=None, a3=None, a4=None):
                        a1, a2, a3, a4 = t1[:], t2[:], t3[:], t4[:]
                        dv(lambda: V.tensor_tensor(a1, xr, yr, op=ALU.mult))
                        dv(lambda: V.tensor_tensor(a2, xi, yi, op=ALU.mult))
                        dv(lambda: V.tensor_tensor(a3, a1, a2, op=ALU.subtract))
                        dv(lambda: V.tensor_tensor(a1, xr, yi, op=ALU.mult))
                        dv(lambda: V.tensor_tensor(a2, xi, yr, op=ALU.mult))
                        dv(lambda: V.tensor_tensor(a4, a1, a2, op=ALU.add))
                        dv(lambda: V.tensor_copy(or_, a3))
                        dv(lambda: V.tensor_copy(oi_, a4))

                    for _ in range(4):
                        cmul(er[:], ei[:], er[:], ei[:], er[:], ei[:], t1[:], t2[:], t3[:], t4[:])
                    dv(lambda: V.tensor_scalar_add(lnr[:], er[:], -1.0))
                    dv(lambda: V.tensor_tensor(t1[:], LR[:], LR[:], op=ALU.mult))
                    dv(lambda: V.tensor_tensor(t2[:], LI[:], LI[:], op=ALU.mult))
                    dv(lambda: V.tensor_tensor(den[:], t1[:], t2[:], op=ALU.add))
                    dv(lambda: V.reciprocal(den[:], den[:]))
                    dv(lambda: V.tensor_tensor(t1[:], lnr[:], LR[:], op=ALU.mult))
                    dv(lambda: V.tensor_tensor(t2[:], ei[:], LI[:], op=ALU.mult))
                    dv(lambda: V.tensor_tensor(t1[:], t1[:], t2[:], op=ALU.add))
                    dv(lambda: V.tensor_tensor(kr_[:], t1[:], den[:], op=ALU.mult))
                    dv(lambda: V.tensor_tensor(t1[:], ei[:], LR[:], op=ALU.mult))
                    dv(lambda: V.tensor_tensor(t2[:], lnr[:], LI[:], op=ALU.mult))
                    dv(lambda: V.tensor_tensor(t1[:], t1[:], t2[:], op=ALU.subtract))
                    dv(lambda: V.tensor_tensor(ki_[:], t1[:], den[:], op=ALU.mult))
                    bk = lambda a: a[:].unsqueeze(2).to_broadcast([128, 32, 16])
                    w1 = Bin8[:, :, 0:16]
                    w2 = Bin8[:, :, 16:32]
                    dv(lambda: V.tensor_tensor(w1, BR[:], bk(kr_), op=ALU.mult))
                    dv(lambda: V.tensor_tensor(w2, BI[:], bk(ki_), op=ALU.mult))
                    dv(lambda: V.tensor_tensor(bbr[:], w1, w2, op=ALU.subtract))
                    dv(lambda: V.tensor_tensor(w1, BI[:], bk(kr_), op=ALU.mult))
                    dv(lambda: V.tensor_tensor(w2, BR[:], bk(ki_), op=ALU.mult))
                    dv(lambda: V.tensor_tensor(bbi[:], w1, w2, op=ALU.add))
                    dv(lambda: V.memset(PWr[:, 7, :], 1.0))
                    dv(lambda: V.memset(PWi[:, 7, :], 0.0))
                    for e in range(1, 9):
                        cmul(PWr[:, 7 + e, :], PWi[:, 7 + e, :], PWr[:, 6 + e, :], PWi[:, 6 + e, :], er[:], ei[:])
                    ac(lambda: A.activation(den[:], ar_[:], AF.Exp, scale=-2.0))
                    dv(lambda: V.tensor_tensor(lnr[:], er[:], den[:], op=ALU.mult))
                    dv(lambda: V.tensor_tensor(lni[:], ei[:], den[:], op=ALU.mult))
                    dv(lambda: V.tensor_scalar_mul(lni[:], lni[:], -1.0))
                    for e in range(1, 8):
                        cmul(PWr[:, 7 - e, :], PWi[:, 7 - e, :], PWr[:, 8 - e, :], PWi[:, 8 - e, :], lnr[:], lni[:])
                    dv(lambda: V.tensor_copy(LPr[:], PWr[:, 15, :]))
                    dv(lambda: V.tensor_copy(LPi[:], PWi[:, 15, :]))

                    def ctab(dst, Xr, Xi, d, slot, e, im_sign):
                        gs = slice(d * 16, d * 16 + 8)
                        for hf in range(2):
                            ps_ = slice(hf * 64, hf * 64 + 64)
                            pr = PWr[ps_, 7 + e, gs].unsqueeze(2).to_broadcast([64, 8, 16])
                            pi = PWi[ps_, 7 + e, gs].unsqueeze(2).to_broadcast([64, 8, 16])
                            o = dst[ps_, gs, slot * 16:(slot + 1) * 16]
                            if hf == 0:
                                dv(lambda: V.tensor_tensor(u1[ps_, 0:8], Xr[ps_, gs, :], pr, op=ALU.mult))
                                dv(lambda: V.tensor_tensor(u2[ps_, 0:8], Xi[ps_, gs, :], pi, op=ALU.mult))
                                dv(lambda: V.tensor_tensor(o, u1[ps_, 0:8], u2[ps_, 0:8], op=ALU.subtract))
                            else:
                                dv(lambda: V.tensor_tensor(u1[ps_, 0:8], Xr[ps_, gs, :], pi, op=ALU.mult))
                                dv(lambda: V.tensor_tensor(u2[ps_, 0:8], Xi[ps_, gs, :], pr, op=ALU.mult))
                                dv(lambda: V.tensor_tensor(o, u1[ps_, 0:8], u2[ps_, 0:8], op=ALU.add))
                                if im_sign < 0:
                                    dv(lambda: V.tensor_scalar_mul(o, o, -1.0))

                    for d in range(2):
                        for i in range(8):
                            ctab(Bin8, bbr, bbi, d, i, (7 - i) if d == 0 else i, +1)
                            ctab(Cout8T, CR, CI, d, i, (i + 1) if d == 0 else (8 - i), -1)
                            ctab(CoutN, CR, CI, d, i, (i - 7) if d == 0 else (-i), -1)
                    for dg in [d_ * 16 + g_ for d_ in range(2) for g_ in range(8)]:
                        k.op('pe', lambda dg=dg: PE.transpose(psT[:], Bin8[:, dg, :], ident_f[:]), R=TK + ['ident_f'], W=['psT'])
                        k.op('act', lambda dg=dg: A.copy(Bin8T[:, dg, :], psT[:]), R=['psT'], W=['Bin8T'])
                        k.op('pe', lambda dg=dg: PE.matmul(psT[:], lhsT=Bin8[:, dg, :], rhs=CoutN[:, dg, :], start=True, stop=True), R=TK, W=['psT'])
                        mk = mlow if dg < 16 else mup
                        k.op('dve', lambda dg=dg, mk=mk: V.tensor_tensor(D8T[:, dg, :], psT[:], mk[:], op=ALU.mult), R=['psT'] + TK, W=['D8T'])

                    psd = P(sT, 'psd', (128, 128), BF16)
                    k.op('pe', lambda: PE.transpose(psd[:], ident_b[:], ident_b[:]), R=['ident_b', 'psT'], W=['psd'])

                if 'B0' in dbg and l == 0:
                    for nm, tl in (('Bin8T', Bin8T), ('Cout8T', Cout8T), ('D8T', D8T)):
                        d_ = dbg_out(nm, (128, 32 * 128), BF16)
                        k.dma(d_, tl[:].rearrange("p a b -> p (a b)"), R=[nm], W=['dbg' + nm])
                    d_ = dbg_out('LP', (128, 64))
                    k.dma(d_[:, 0:32], LPr[:], R=TK, W=['dbgLP'])
                    k.dma(d_[:, 32:64], LPi[:], R=TK, W=['dbgLP'])
                    break

                k.barrier()
                with ExitStack() as sS:
                    IM = T(sS, 'IM', (128, 2, NSC), BF16)
                    H = T(sS, 'H', (128, 4, NSC + 4), F32)
                    Hb = T(sS, 'Hb', (128, 4, NSC + 4), BF16)
                    Yo = T(sS, 'Yo', (128, 2, NSC), BF16)
                    ATb = [T(sS, 'AT%d' % i, (128, 11, 4, 128), BF16) for i in range(2)]
                    atmp = T(sS, 'atmp', (128, 128), F32)
                    v2 = T(sS, 'v2', (128, 32), F32)
                    LQr = T(sS, 'LQr', (128, 11, 32), F32)
                    LQi = T(sS, 'LQi', (128, 11, 32), F32)
                    q1, q2 = T(sS, 'q1', (128, 32), F32), T(sS, 'q2', (128, 32), F32)
                    psS = [[P(sS, 'psS%d_%d' % (i, j), (128, 512)) for j in range(4)] for i in range(2)]
                    GO = 2
                    NLEV = int(os.environ.get('DBG_NLEV', 11))
                    NRND = int(os.environ.get('DBG_NRND', 4))
                    DOREAD = int(os.environ.get('DBG_READ', 1))
                    k.op('dve', lambda: V.tensor_copy(LQr[:, 0, :], LPr[:]), R=TK, W=['LQ'])
                    k.op('dve', lambda: V.tensor_copy(LQi[:, 0, :], LPi[:]), R=TK, W=['LQ'])
                    for lev in range(1, NLEV):
                        a_r, a_i = LQr[:, lev - 1, :], LQi[:, lev - 1, :]
                        k.op('dve', lambda: V.tensor_tensor(q1[:], a_r, a_r, op=ALU.mult), R=['LQ'], W=['q1'])
                        k.op('dve', lambda: V.tensor_tensor(q2[:], a_i, a_i, op=ALU.mult), R=['LQ'], W=['q2'])
                        k.op('dve', lambda lev=lev: V.tensor_tensor(LQr[:, lev, :], q1[:], q2[:], op=ALU.subtract), R=['q1', 'q2'], W=['LQ'])
                        k.op('dve', lambda: V.tensor_tensor(q1[:], a_r, a_i, op=ALU.mult), R=['LQ'], W=['q1'])
                        k.op('dve', lambda lev=lev: V.tensor_scalar_mul(LQi[:, lev, :], q1[:], 2.0), R=['q1'], W=['LQ'])
                    if int(os.environ.get('DBG_MS', 1)):
                        k.op('dve', lambda: V.memset(H[:], 0.0), W=['H%d' % j for j in range(4)])
                        k.op('pool', lambda: G.memset(Hb[:], 0.0), W=['Hb%d' % j for j in range(4)])
                    xdg = xd_scr.rearrange("(g h) i n -> g h i n", h=16)
                    ydg = yd_scr.rearrange("(g h) i n -> g h i n", h=16)
                    CB = [(0, 512), (512, 1024), (1024, NSC)]
                    for rnd in range(NRND):
                        gl_ = [2 * rnd, 2 * rnd + 1]
                        combos = [(d, g) for g in gl_ for d in range(2)]
                        if int(os.environ.get('DBG_IMZ', 0)):
                            k.op('pool', lambda: G.memset(IM[:], 0.5), W=['IM'])
                        for gi, g in enumerate(gl_ if int(os.environ.get('DBG_IM', 1)) else []):
                            for i in range(8):
                                k.dma(IM[16 * i:16 * i + 16, gi, :], xdg[g, :, i, :], R=['xd_scr'], W=['IM'], q=('sp' if i % 2 == 0 else 'pool'))
                        ATc, ATk = ATb[rnd % 2], 'AT%d' % (rnd % 2)
                        for lev in range(NLEV):
                            k.op('dve', lambda lev=lev: V.tensor_scalar_mul(v2[:], LQi[:, lev, :], sgn[:, 0:1]), R=['LQ', 'sgn'], W=['v2'])
                            for j, (d, g) in enumerate(combos):
                                dg = d * 16 + g
                                k.op('dve', lambda lev=lev, dg=dg: V.tensor_scalar_mul(atmp[:], ident_f[:], LQr[:, lev, dg:dg + 1]), R=['LQ', 'ident_f'], W=['atmp'])
                                k.op('dve', lambda lev=lev, dg=dg, j=j: V.scalar_tensor_tensor(ATc[:, lev, j, :], swap_f[:], v2[:, dg:dg + 1], atmp[:],
                                                                                              op0=ALU.mult, op1=ALU.add), R=['swap_f', 'v2', 'atmp'], W=[ATk])
                        for bi_, (c0, c1) in enumerate(CB):
                            pss = psS[bi_ % 2]
                            for j, (d, g) in enumerate(combos):
                                pk = 'psS%d_%d' % (bi_ % 2, j)
                                k.op('pe', lambda j=j, d=d, g=g: PE.matmul(pss[j][:, 0:c1 - c0], lhsT=Bin8T[:, d * 16 + g, :], rhs=IM[:, j // 2, c0:c1],
                                                                            start=True, stop=True), R=['Bin8T', 'IM'], W=[pk])
                                k.op('dve', lambda j=j: V.tensor_copy(H[:, j, GO + c0:GO + c1], pss[j][:, 0:c1 - c0]), R=[pk], W=['H%d' % j])
                                k.op('act', lambda j=j: A.copy(Hb[:, j, GO + c0:GO + c1], H[:, j, GO + c0:GO + c1]), R=['H%d' % j], W=['Hb%d' % j])
                        for lev in range(NLEV):
                            s_ = 1 << lev
                            nb = (NSC - s_ + 511) // 512
                            for bi_ in range(nb):
                                pss = psS[bi_ % 2]
                                hi_f = NSC - bi_ * 512
                                lo_f = max(s_, hi_f - 512)
                                lo_b = bi_ * 512
                                hi_b = min(NSC - s_, lo_b + 512)
                                for j, (d, g) in enumerate(combos):
                                    pk = 'psS%d_%d' % (bi_ % 2, j)
                                    lo_, hi_ = (lo_f, hi_f) if d == 0 else (lo_b, hi_b)
                                    sh_ = -s_ if d == 0 else s_
                                    k.op('pe', lambda j=j: PE.matmul(pss[j][:, 0:hi_ - lo_], lhsT=ATc[:, lev, j, :], rhs=Hb[:, j, GO + lo_ + sh_:GO + hi_ + sh_],
                                                                    start=True, stop=True), R=[ATk, 'Hb%d' % j], W=[pk])
                                    k.op('dve', lambda j=j, lo_=lo_, hi_=hi_: V.tensor_tensor(H[:, j, GO + lo_:GO + hi_], H[:, j, GO + lo_:GO + hi_], pss[j][:, 0:hi_ - lo_], op=ALU.add),
                                         R=[pk, 'H%d' % j], W=['H%d' % j])
                                    if j % 2 == 0:
                                        k.op('act', lambda j=j, lo_=lo_, hi_=hi_: A.copy(Hb[:, j, GO + lo_:GO + hi_], H[:, j, GO + lo_:GO + hi_]), R=['H%d' % j], W=['Hb%d' % j])
                                    else:
                                        k.op('pool', lambda j=j, lo_=lo_, hi_=hi_: G.tensor_copy(Hb[:, j, GO + lo_:GO + hi_], H[:, j, GO + lo_:GO + hi_]), R=['H%d' % j], W=['Hb%d' % j])
                        RB = [(0, 32, (0,)), (32, 544, (0, 1)), (544, 1056, (0, 1)), (1056, NSC, (1,))]
                        for bi_, (c0, c1, dirs) in enumerate(RB):
                            pss = psS[bi_ % 2]
                            for gi, g in enumerate(gl_):
                                pk = 'psS%d_%d' % (bi_ % 2, gi)
                                nmm = 2 * len(dirs)
                                mi = 0
                                for d in dirs:
                                    j = 2 * gi + d
                                    sh = -1 if d == 0 else 1
                                    k.op('pe', lambda gi=gi, d=d, g=g, mi=mi: PE.matmul(pss[gi][:, 0:c1 - c0], lhsT=D8T[:, d * 16 + g, :], rhs=IM[:, gi, c0:c1],
                                                                                    start=(mi == 0), stop=False), R=['D8T', 'IM'], W=[pk])
                                    mi += 1
                                    k.op('pe', lambda gi=gi, d=d, g=g, j=j, sh=sh, mi=mi: PE.matmul(pss[gi][:, 0:c1 - c0], lhsT=Cout8T[:, d * 16 + g, :],
                                                                                                rhs=Hb[:, j, GO + c0 + sh:GO + c1 + sh],
                                                                                                start=False, stop=(mi == nmm - 1)), R=['Cout8T', 'Hb%d' % j], W=[pk])
                                    mi += 1
                                k.op('act', lambda gi=gi: A.copy(Yo[:, gi, c0:c1], pss[gi][:, 0:c1 - c0]), R=[pk], W=['Yo'])
                        for gi, g in enumerate(gl_ if int(os.environ.get('DBG_YD', 1)) else []):
                            for i in range(8):
                                k.dma(ydg[g, :, i, :], Yo[16 * i:16 * i + 16, gi, :], R=['Yo'], W=['yd_scr'], q=('sp' if i % 2 == 0 else 'pool'))

                if nlayers > 1 or int(os.environ.get('DBG_CC', 0)):
                    k.barrier()
                    k._sync('pool', ['yd_scr'], ['ydg'])
                    ydf = yd_scr.rearrange("c i n -> c (i n)")
                    for c_ in range(2):
                        G.collective_compute("AllGather", ALU.bypass, replica_groups=[[0, 1], [2, 3], [4, 5], [6, 7]],
                                             ins=[ydf[c_ * 64:(c_ + 1) * 64, :]], outs=[ydg_scr[c_]]).then_inc(ccsem)
                    ccn[0] += 2
                    for eng in k.eng.values():
                        eng.wait_ge(ccsem, ccn[0])
                    with ExitStack() as sX:
                        ya = T(sX, 'ya', (128, 8 * NSC), BF16)
                        yb = T(sX, 'yb', (128, 8 * NSC), BF16)
                        for c_ in range(2):
                            k.dma(ya[c_ * 64:(c_ + 1) * 64, :], ydg_scr[c_][64:128, :], W=['ya'], q=('sp' if c_ == 0 else 'pool'))
                            k.dma(yb[c_ * 64:(c_ + 1) * 64, :], ydg_scr[c_][0:64, :], W=['yb'], q=('sp' if c_ == 0 else 'pool'))
                        k.op('dve', lambda: V.tensor_scalar_mul(ya[:], ya[:], msb[:, 0:1]), R=['ya', 'msb'], W=['ya'])
                        k.op('dve', lambda: V.scalar_tensor_tensor(ya[:], yb[:], msb[:, 1:2], ya[:], op0=ALU.mult, op1=ALU.add), R=['ya', 'yb', 'msb'], W=['ya'])
                        k.dma(ydf[128:256, :], ya[:], R=['ya'], W=['yd_scr'])
                    k.barrier()

                if 'B1' in dbg and l == 0:
                    d_ = dbg_out('yd', (256, 8 * NSC), BF16)
                    if int(os.environ.get('DBG_YDD', 1)):
                        k.dma(d_, yd_scr.rearrange("a b c -> a (b c)"), R=['yd_scr'], W=['dbgyd'])
                    break

                k.barrier()
                with ExitStack() as sG:
                    xdt = T(sG, 'xdt', (128, 2, 8, 64), BF16)
                    ydt = T(sG, 'ydt', (128, 2, 8, 64), BF16)
                    yd2 = T(sG, 'yd2', (128, 2, 8, 64), BF16)
                    yf = T(sG, 'yf', (128, 2, 512), F32)
                    g1 = T(sG, 'g1', (128, 2, 512), F32)
                    g2 = T(sG, 'g2', (128, 2, 512), F32)
                    glb = T(sG, 'glb', (128, 2, 512), BF16)
                    sgm = T(sG, 'sgm', (128, 2, 512), F32)
                    s5ob = T(sG, 's5ob', (128, 2, 512), BF16)
                    psG = [P(sG, 'psG%d' % i, (128, 512)) for i in range(2)]
                    xdv = xd_scr.rearrange("(c p) i n -> p c i n", p=128)
                    ydv = yd_scr.rearrange("(c p) i n -> p c i n", p=128)
                    xdtB = T(sG, 'xdtB', (128, 2, 8, 64), BF16)
                    ydtB = T(sG, 'ydtB', (128, 2, 8, 64), BF16)
                    blocks = [('ctx', 0, NCTX)] + [('lat', 512 * i, 512) for i in range(HALF // 512)]
                    for (kind, t0, Tn) in blocks:
                        u0 = t0 if kind == 'ctx' else NCTX + t0
                        n8 = Tn // 8
                        for ct in range(2):
                            k.dma(xdt[:, ct, :, 0:n8], xdv[:, ct, :, u0 // 8:u0 // 8 + n8], R=['xd_scr'], W=['xdt'])
                            k.dma(ydt[:, ct, :, 0:n8], ydv[:, ct, :, u0 // 8:u0 // 8 + n8], R=['yd_scr'], W=['ydt'], q='pool')
                            if kind == 'lat':
                                uB = u0 + HALF
                                k.dma(xdtB[:, ct, :, 0:n8], xdv[:, ct, :, uB // 8:uB // 8 + n8], R=['xd_scr'], W=['xdtB'], q='pool')
                                k.dma(ydtB[:, ct, :, 0:n8], ydv[:, ct, :, uB // 8:uB // 8 + n8], R=['yd_scr'], W=['ydtB'])
                            if kind == 'ctx':
                                k.dma(yd2[:, ct, :, 0:n8], ydv[:, ct, :, NU // 8:NU // 8 + n8], R=['yd_scr'], W=['yd2'])
                        if kind == 'ctx':
                            k.op('pool', lambda: G.tensor_tensor(ydt[:, :, :, 0:n8], ydt[:, :, :, 0:n8], yd2[:, :, :, 0:n8], op=ALU.add), R=['ydt', 'yd2'], W=['ydt'])
                        else:
                            for (ta_, tb_, ka_, kb_) in ((xdt, xdtB, 'xdt', 'xdtB'), (ydt, ydtB, 'ydt', 'ydtB')):
                                k.op('dve', lambda ta_=ta_: V.tensor_scalar_mul(ta_[:], ta_[:], msb[:, 0:1]), R=[ka_, 'msb'], W=[ka_])
                                k.op('dve', lambda ta_=ta_, tb_=tb_: V.scalar_tensor_tensor(ta_[:], tb_[:], msb[:, 1:2], ta_[:], op0=ALU.mult, op1=ALU.add), R=[ka_, kb_, 'msb'], W=[ka_])
                        for ct in range(2):
                            k.op('dve', lambda ct=ct: V.scalar_tensor_tensor(yf[:, ct, 0:Tn].rearrange("p (c i) -> p i c", i=8), xdt[:, ct, :, 0:n8], s5dT[:, ct:ct + 1],
                                                                             ydt[:, ct, :, 0:n8], op0=ALU.mult, op1=ALU.add), R=['xdt', 'ydt', 's5dT'], W=['yf'])
                        if 'B2y' in dbg and l == 0:
                            if kind == 'ctx':
                                dyl = dbg_out('yl', (256, NU))
                            k.dma(dyl[:, u0:u0 + Tn].rearrange("(c p) n -> p c n", p=128), yf[:, :, 0:Tn], R=['yf'], W=['dbgyl'])
                        k.op('pool', lambda: G.tensor_tensor(g1[:, :, 0:Tn], yf[:, :, 0:Tn], yf[:, :, 0:Tn], op=ALU.mult), R=['yf'], W=['g1'])
                        k.op('dve', lambda: V.tensor_scalar(g1[:, :, 0:Tn], g1[:, :, 0:Tn], 0.044715, 1.0, op0=ALU.mult, op1=ALU.add), R=['g1'], W=['g1'])
                        k.op('pool', lambda: G.tensor_tensor(g2[:, :, 0:Tn], g1[:, :, 0:Tn], yf[:, :, 0:Tn], op=ALU.mult), R=['g1', 'yf'], W=['g2'])
                        k.op('act', lambda: A.activation(g2[:, :, 0:Tn], g2[:, :, 0:Tn], AF.Sigmoid, scale=1.5957691216057308), R=['g2'], W=['g2'])
                        k.op('dve', lambda: V.tensor_tensor(glb[:, :, 0:Tn], yf[:, :, 0:Tn], g2[:, :, 0:Tn], op=ALU.mult), R=['yf', 'g2'], W=['glb'])
                        for m in range(2):
                            for kt in range(2):
                                k.op('pe', lambda m=m, kt=kt: PE.matmul(psG[m][:, 0:Tn], lhsT=wglu[:, kt, m * 128:(m + 1) * 128], rhs=glb[:, kt, 0:Tn],
                                                                        start=(kt == 0), stop=(kt == 1)), R=['wglu', 'glb'], W=['psG%d' % m])
                            k.op('act', lambda m=m: A.activation(sgm[:, m, 0:Tn], psG[m][:, 0:Tn], AF.Sigmoid, bias=bgluT[:, m:m + 1]), R=['psG%d' % m, 'bgluT'], W=['sgm'])
                        k.op('dve', lambda: V.tensor_tensor(s5ob[:, :, 0:Tn], glb[:, :, 0:Tn], sgm[:, :, 0:Tn], op=ALU.mult), R=['glb', 'sgm'], W=['s5ob'])
                        k.dma(s5o_scr[:, u0:u0 + Tn].rearrange("(c p) n -> p c n", p=128), s5ob[:, :, 0:Tn], R=['s5ob'], W=['s5o_scr'])

                if 'B2' in dbg and l == 0:
                    d_ = dbg_out('s5o', (256, NU), BF16)
                    k.dma(d_, s5o_scr, R=['s5o_scr'], W=['dbgs5o'])
                    break

            k.barrier()
            with ExitStack() as sC:
                wC = T(sC, 'wC', (128, 8, 1728), BF16)
                wout = T(sC, 'wout', (128, 8, 1024), BF16)
                wqh = T(sC, 'wqh', (128, 2, 4, 96), BF16)
                wqr = T(sC, 'wqr', (128, 2, 4, 96), BF16)
                wpl = T(sC, 'wpl', (128, 2, 128), BF16)
                wsT = T(sC, 'wsT', (128, 4, 128), BF16)
                bsb = T(sC, 'bsb', (128, 4, 128), F32)
                gate_bc = T(sC, 'gate_bc', (128, D), F32)
                lng_bc = T(sC, 'lng_bc', (128, D), F32)
                lnb_bc = T(sC, 'lnb_bc', (128, D), F32)
                colv = T(sC, 'colv', (128, 8), F32)
                winv = T(sC, 'winv', (128, 2), F32)
                with ExitStack() as sw:
                    wst2 = [T(sw, 'wstC%d' % i, (128, 8, 256), F32) for i in range(2)]
                    wq32 = T(sw, 'wq32', (128, 2, 384), F32)
                    wp32 = T(sw, 'wp32', (128, 2, 128), F32)
                    ws32 = T(sw, 'ws32', (128, 128), F32)
                    psw = P(sw, 'psw', (128, 128))
                    win = prm['w_in'][l].rearrange("(c p) n -> p c n", p=128)
                    segs = [(416, 192, 0), (864, 256, 192), (1120, 256, 448), (1376, 256, 704), (1632, 256, 960), (1888, 256, 1216), (2144, 256, 1472)]
                    for si_, (c0, n, d0) in enumerate(segs):
                        wst, wk = wst2[si_ % 2], 'wstC%d' % (si_ % 2)
                        k.dma(wst[:, :, 0:n], win[:, :, c0:c0 + n], W=[wk], q=('sp' if si_ % 2 == 0 else 'pool'))
                        k.op('dve' if si_ % 2 == 0 else 'pool', lambda n=n, d0=d0, wst=wst, si_=si_: (V if si_ % 2 == 0 else G).tensor_copy(wC[:, :, d0:d0 + n], wst[:, :, 0:n]), R=[wk], W=['wC'])
                    wo = prm['w_out'][l].rearrange("(c p) n -> p c n", p=128)
                    for j in range(4):
                        wst, wk = wst2[(j + 1) % 2], 'wstC%d' % ((j + 1) % 2)
                        k.dma(wst[:], wo[:, :, j * 256:(j + 1) * 256], W=[wk], q=('sp' if j % 2 == 0 else 'pool'))
                        k.op('dve' if j % 2 == 0 else 'pool', lambda j=j, wst=wst: (V if j % 2 == 0 else G).tensor_copy(wout[:, :, j * 256:(j + 1) * 256], wst[:]), R=[wk], W=['wout'])
                    for ci, nm in enumerate(('sgu_g', 'sgu_b', 'pool_scale')):
                        k.dma(colv[:, 2 * ci:2 * ci + 2], prm[nm][l].rearrange("(c p) -> p c", p=128), W=['colv'], allow_slow_non_contiguous=True)
                    k.dma(colv[:, 6:7], prm['g_q'][l][0:128].rearrange("(p o) -> p o", o=1), W=['colv'])
                    k.dma(colv[0:64, 7:8], prm['g_q'][l][128:192].rearrange("(p o) -> p o", o=1), W=['colv'])
                    k.op('pool', lambda: G.memset(winv[0:64, 0:1], 1.0 / 2), W=['winv'])
                    k.op('pool', lambda: G.memset(winv[64:128, 0:1], 1.0 / 4), W=['winv'])
                    k.op('pool', lambda: G.memset(winv[0:64, 1:2], 1.0 / 8), W=['winv'])
                    k.op('pool', lambda: G.memset(winv[64:128, 1:2], 1.0 / 16), W=['winv'])
                    k.dma(bsb[:].rearrange("p h t -> p (h t)"), prm['b_s'][l].rearrange("h t -> (h t)").partition_broadcast(128), W=['bsb'])
                    k.dma(lng_bc[:], prm['ln_g'][l].partition_broadcast(128), W=['lng_bc'])
                    k.dma(lnb_bc[:], prm['ln_b'][l].partition_broadcast(128), W=['lnb_bc'], q='pool')
                    k.op('pool', lambda: G.memset(wq32[:], 0.0), W=['wq32'])
                    k.dma(wq32[:, 0, :], prm['w_uq'][l][0:128, :], W=['wq32'])
                    k.dma(wq32[0:64, 1, :], prm['w_uq'][l][128:192, :], W=['wq32'])
                    k.op('pool', lambda: G.memset(wqr[:], 0.0), W=['wqr'])
                    for kt in range(2):
                        rows = slice(0, 128) if kt == 0 else slice(0, 64)
                        k.op('dve', lambda kt=kt, rows=rows: V.tensor_scalar_mul(wq32[rows, kt, :], wq32[rows, kt, :], colv[rows, 6 + kt:7 + kt]), R=['wq32', 'colv'], W=['wq32'])
                    k.op('dve', lambda: V.tensor_copy(wqh[:].rearrange("p a h c -> p a (h c)"), wq32[:]), R=['wq32'], W=['wqh'])
                    wq4 = wq32[:].rearrange("p a (h c) -> p a h c", h=4)
                    for (d0, s0_) in ((0, 8), (8, 0), (16, 24), (24, 16)):
                        k.op('dve', lambda d0=d0, s0_=s0_: V.tensor_copy(wqr[:, :, :, 64 + d0:72 + d0], wq4[:, :, :, 64 + s0_:72 + s0_]), R=['wq32'], W=['wqr'])
                    k.op('pool', lambda: G.memset(wp32[:], 0.0), W=['wp32'])
                    for g in range(4):
                        r0 = (g % 2) * 64
                        k.dma(wp32[r0:r0 + 64, g // 2, r0:r0 + 64], prm['w_pool'][l][g], W=['wp32'])
                    k.op('dve', lambda: V.tensor_copy(wpl[:], wp32[:]), R=['wp32'], W=['wpl'])
                    for h in range(4):
                        k.dma(ws32[:], prm['w_s'][l][h], W=['ws32'])
                        k.op('pe', lambda: PE.transpose(psw[:], ws32[:], ident_f[:]), R=['ws32', 'ident_f'], W=['psw'])
                        k.op('act', lambda h=h: A.copy(wsT[:, h, :], psw[:]), R=['psw'], W=['wsT'])
                    psd2 = P(sw, 'psd2', (128, 128), BF16)
                    k.op('pe', lambda: PE.transpose(psd2[:], ident_b[:], ident_b[:]), R=['ident_b', 'psw'], W=['psd2'])
                k.barrier()

                with ExitStack() as sc:
                    xt = [T(sc, 'xtC0', (128, D), F32)] * 2
                    st6 = T(sc, 'st6C', (128, 2, 6), F32)
                    mv = T(sc, 'mvC', (128, 2), F32)
                    rstd = T(sc, 'rstdC', (128, 1), F32)
                    xn = T(sc, 'xnC', (128, D), BF16)
                    hT = T(sc, 'hTC', (128, 8, QB), BF16)
                    sq = T(sc, 'sqC', (128, 2, QB), BF16)
                    rms = T(sc, 'rmsC', (128, QB), F32)
                    QT = T(sc, 'QT', (128, 4, QB), BF16)
                    rc_t = T(sc, 'rc_tC', (128, QB), F32)
                    rs_t = T(sc, 'rs_tC', (128, QB), F32)
                    su = T(sc, 'su', (128, 2, QB), BF16)
                    svb = T(sc, 'svb', (128, 2, QB), BF16)
                    cqn = svb
                    mean = T(sc, 'mean', (128, QB), F32)
                    var = T(sc, 'var', (128, QB), F32)
                    q1t, q2t = mean, var
                    vnb = T(sc, 'vnb', (128, 2, QB), BF16)
                    vc = T(sc, 'vc', (128, 256), BF16)
                    mx = T(sc, 'mx', (128, 128), F32)
                    sg = T(sc, 'sg', (128, 8, QB), BF16)
                    PT = [T(sc, 'PT%d' % i, (128, 2 * QB), BF16) for i in range(3)]
                    osb = T(sc, 'osb', (128, 4, 65), F32)
                    rec = T(sc, 'rec', (128, 4, 1), F32)
                    att = T(sc, 'att', (128, 4, 256), BF16)
                    catg = sg
                    pin = T(sc, 'pin', (128, 2, QB + 16), BF16)
                    pa = T(sc, 'pa', (128, 2, QB + 16), F32)
                    svf = pa
                    pb = T(sc, 'pb', (128, 2, QB + 16), F32)
                    pld = T(sc, 'pld', (128, 2, QB), BF16)
                    s5t = T(sc, 's5t', (128, 2, QB), BF16)
                    rr = T(sc, 'rr', (128, D), F32)
                    hf32 = rr[:].rearrange("p (a b) -> p a b", a=8)
                    pt = P(sc, 'ptC', (128, 8, 128), BF16)
                    psC = [P(sc, 'psC%d' % i, (128, 512)) for i in range(4)]
                    psS_ = [P(sc, 'psSc%d' % i, (128, 512)) for i in range(2)]
                    psO = P(sc, 'psO', (128, 4, 65))
                    sm2c = [T(sc, 'sm2c_%d' % i, (128, 16), F32) for i in range(2)]
                    ptc2 = psC[3][:].bitcast(BF16).rearrange("p (a b) -> p a b", a=8)
                    LNC = {'xt': [(xt[0], 'xtC0'), (rr, 'rr')], 'xn': [(xn, 'xn'), (xn, 'xn')],
                           'pt': [(pt, 'ptC'), (ptc2, 'psC3')], 'sm': [(sm2c[0], 'sm2c_0'), (sm2c[1], 'sm2c_1')], 'hTk': 'hT', 'dve_mod': True}
                    assert QB <= 512 and NCTX <= QB
                    SCALE = 96 ** -0.5

                    qblocks = [('lat', QB * i, QB) for i in range(HALF // QB)]
                    if not last:
                        qblocks = [('ctx', 0, NCTX)] + qblocks
                    NQB = int(os.environ.get('DBG_NQB', len(qblocks)))
                    xcnt = 0

                    def do_ln(bi_):
                        kind_, t0_, Tn_ = qblocks[bi_]
                        src_ = cin if kind_ == 'ctx' else (xq_in if l == 0 else x1h_scr)
                        ln_block([src_[t0_ + sb_ * 128: t0_ + (sb_ + 1) * 128, :] for sb_ in range(Tn_ // 128)], ['x1w'],
                                 1 if kind_ == 'ctx' else 0, hT, LNC)

                    gcol = [None]
                    do_ln(0)
                    for bidx, (kind, t0, Tn) in enumerate(qblocks[:NQB]):
                        col = 1 if kind == 'ctx' else 0
                        src = cin if kind == 'ctx' else (xq_in if l == 0 else x1h_scr)
                        dst = (ctx1_scr if kind == 'ctx' else (out_ap if last else x1h_scr))
                        u0 = t0 if kind == 'ctx' else NCTX + t0
                        nsub = Tn // 128
                        nkt = (NCTX // 128) if kind == 'ctx' else (NU // 128)
                        if gcol[0] != col:
                            k.dma(gate_bc[:], gate_scr[l, col].partition_broadcast(128), R=['gate_scr'], W=['gate_bc'])
                            gcol[0] = col

                        def proj(pst, key, c0, ncol):
                            for dc in range(8):
                                k.op('pe', lambda dc=dc: PE.matmul(pst[0:ncol, 0:Tn], lhsT=wC[:, dc, c0:c0 + ncol], rhs=hT[:, dc, 0:Tn],
                                                                  start=(dc == 0), stop=(dc == 7)), R=['wC', 'hT'], W=[key])
                        proj(psC[0], 'psC0', 0, 128)
                        proj(psC[1], 'psC1', 128, 64)
                        k.op('act', lambda: A.activation(sq[:, 0, 0:Tn], psC[0][:, 0:Tn], AF.Square), R=['psC0'], W=['sq'])
                        k.op('act', lambda: A.activation(sq[0:64, 1, 0:Tn], psC[1][0:64, 0:Tn], AF.Square), R=['psC1'], W=['sq'])
                        k.op('pe', lambda: PE.matmul(psC[2][:, 0:Tn], lhsT=ones_b[:, :], rhs=sq[:, 0, 0:Tn], start=True, stop=False), R=['ones_b', 'sq'], W=['psC2'])
                        k.op('pe', lambda: PE.matmul(psC[2][:, 0:Tn], lhsT=ones_b[0:64, :], rhs=sq[0:64, 1, 0:Tn], start=False, stop=True), R=['ones_b', 'sq'], W=['psC2'])
                        k.op('act', lambda: A.activation(rms[:, 0:Tn], psC[2][:, 0:Tn], AF.Sqrt, bias=epsb[:, 0:1], scale=1.0 / 192), R=['psC2', 'epsb'], W=['rms'])
                        k.op('dve', lambda: V.reciprocal(rms[:, 0:Tn], rms[:, 0:Tn]), R=['rms'], W=['rms'])
                        k.op('dve', lambda: V.tensor_tensor(cqn[:, 0, 0:Tn], psC[0][:, 0:Tn], rms[:, 0:Tn], op=ALU.mult), R=['psC0', 'rms'], W=['svb'])
                        k.op('dve', lambda: V.tensor_tensor(cqn[0:64, 1, 0:Tn], psC[1][0:64, 0:Tn], rms[0:64, 0:Tn], op=ALU.mult), R=['psC1', 'rms'], W=['svb'])
                        if kind == 'lat':
                            k.dma(rc_t[64:96, :], ropecq[:, t0:t0 + Tn], W=['rc_t'])
                            k.dma(rs_t[64:96, :], ropesq[:, t0:t0 + Tn], W=['rs_t'], q='pool')
                        for h in range(4):
                            pq, pqk = psC[h % 2], 'psC%d' % (h % 2)
                            pr, prk = psC[2 + h % 2], 'psC%d' % (2 + h % 2)
                            k.op('pe', lambda h=h: PE.matmul(pq[0:96, 0:Tn], lhsT=wqh[:, 0, h, :], rhs=cqn[:, 0, 0:Tn], start=True, stop=False), R=['wqh', 'svb'], W=[pqk])
                            k.op('pe', lambda h=h: PE.matmul(pq[0:96, 0:Tn], lhsT=wqh[0:64, 1, h, :], rhs=cqn[0:64, 1, 0:Tn], start=False, stop=True), R=['wqh', 'svb'], W=[pqk])
                            k.op('act', lambda h=h: A.copy(QT[0:64, h, 0:Tn], pq[0:64, 0:Tn]), R=[pqk], W=['QT'])
                            if kind == 'lat':
                                k.op('pe', lambda h=h: PE.matmul(pr[0:96, 0:Tn], lhsT=wqr[:, 0, h, :], rhs=cqn[:, 0, 0:Tn], start=True, stop=False), R=['wqr', 'svb'], W=[prk])
                                k.op('pe', lambda h=h: PE.matmul(pr[0:96, 0:Tn], lhsT=wqr[0:64, 1, h, :], rhs=cqn[0:64, 1, 0:Tn], start=False, stop=True), R=['wqr', 'svb'], W=[prk])
                                k.op('dve', lambda: V.tensor_tensor(q1t[64:96, 0:Tn], pq[64:96, 0:Tn], rc_t[64:96, 0:Tn], op=ALU.mult), R=[pqk, 'rc_t'], W=['mean'])
                                k.op('dve', lambda: V.tensor_tensor(q2t[64:96, 0:Tn], pr[64:96, 0:Tn], rs_t[64:96, 0:Tn], op=ALU.mult), R=[prk, 'rs_t'], W=['var'])
                                k.op('pool', lambda h=h: G.tensor_tensor(QT[64:96, h, 0:Tn], q1t[64:96, 0:Tn], q2t[64:96, 0:Tn], op=ALU.add), R=['mean', 'var'], W=['QT'])
                            else:
                                k.op('act', lambda h=h: A.copy(QT[64:96, h, 0:Tn], pq[64:96, 0:Tn]), R=[pqk], W=['QT'])
                        for ct in range(2):
                            proj(psC[ct], 'psC%d' % ct, 192 + ct * 128, 128)
                            k.op('act', lambda ct=ct: A.copy(su[:, ct, 0:Tn], psC[ct][:, 0:Tn]), R=['psC%d' % ct], W=['su'])
                        for ct in range(2):
                            proj(psC[ct], 'psC%d' % ct, 448 + ct * 128, 128)
                            k.op('act', lambda ct=ct: A.copy(svf[:, ct, 0:Tn], psC[ct][:, 0:Tn]), R=['psC%d' % ct], W=['pa'])
                            k.op('dve', lambda ct=ct: V.tensor_copy(svb[:, ct, 0:Tn], svf[:, ct, 0:Tn]), R=['pa'], W=['svb'])
                            k.op('pool', lambda ct=ct: G.tensor_tensor(sq[:, ct, 0:Tn], svf[:, ct, 0:Tn], svf[:, ct, 0:Tn], op=ALU.mult), R=['pa'], W=['sq'])
                        for ct in range(2):
                            k.op('pe', lambda ct=ct: PE.matmul(psC[2][:, 0:Tn], lhsT=ones_b[:], rhs=svb[:, ct, 0:Tn], start=(ct == 0), stop=(ct == 1)), R=['ones_b', 'svb'], W=['psC2'])
                        for ct in range(2):
                            k.op('pe', lambda ct=ct: PE.matmul(psC[3][:, 0:Tn], lhsT=ones_b[:], rhs=sq[:, ct, 0:Tn], start=(ct == 0), stop=(ct == 1)), R=['ones_b', 'sq'], W=['psC3'])
                        k.op('act', lambda: A.activation(mean[:, 0:Tn], psC[2][:, 0:Tn], AF.Copy, scale=1.0 / 256), R=['psC2'], W=['mean'])
                        k.op('dve', lambda: V.tensor_tensor(var[:, 0:Tn], mean[:, 0:Tn], mean[:, 0:Tn], op=ALU.mult), R=['mean'], W=['var'])
                        k.op('dve', lambda: V.scalar_tensor_tensor(var[:, 0:Tn], psC[3][:, 0:Tn], 1.0 / 256, var[:, 0:Tn], op0=ALU.mult, op1=ALU.subtract), R=['psC3', 'var'], W=['var'])
                        k.op('act', lambda: A.activation(var[:, 0:Tn], var[:, 0:Tn], AF.Sqrt, bias=epsb[:, 0:1]), R=['var', 'epsb'], W=['var'])
                        k.op('dve', lambda: V.reciprocal(var[:, 0:Tn], var[:, 0:Tn]), R=['var'], W=['var'])
                        for ct in range(2):
                            k.op('dve', lambda ct=ct: V.tensor_tensor(svf[:, ct, 0:Tn], svf[:, ct, 0:Tn], mean[:, 0:Tn], op=ALU.subtract), R=['pa', 'mean'], W=['pa'])
                            k.op('pool', lambda ct=ct: G.tensor_tensor(svf[:, ct, 0:Tn], svf[:, ct, 0:Tn], var[:, 0:Tn], op=ALU.mult), R=['pa', 'var'], W=['pa'])
                            k.op('dve', lambda ct=ct: V.tensor_scalar(vnb[:, ct, 0:Tn], svf[:, ct, 0:Tn], colv[:, ct:ct + 1], colv[:, 2 + ct:3 + ct], op0=ALU.mult, op1=ALU.add),
                                 R=['pa', 'colv'], W=['vnb'])
                        for gt in range(8):
                            pg, pgk = psC[gt % 4], 'psC%d' % (gt % 4)
                            proj(pg, pgk, 704 + gt * 128, 128)
                            k.op('act', lambda gt=gt, pg=pg: A.activation(sg[:, gt, 0:Tn], pg[:, 0:Tn], AF.Silu), R=[pgk], W=['sg'])
                        if bidx + 1 < min(NQB, len(qblocks)):
                            do_ln(bidx + 1)
                        sbufs = [(psS_[0], 'psSc0'), (psS_[1], 'psSc1'), (psC[0], 'psC0'), (psC[1], 'psC1')]
                        npair = nkt // 2

                        def score(h, kp):
                            pS, pSk = sbufs[kp % 4]
                            for j_ in range(2):
                                kt_ = 2 * kp + j_
                                k.op('pe', lambda j_=j_, kt_=kt_: PE.matmul(pS[:, j_ * Tn:(j_ + 1) * Tn], lhsT=KT[0:96, h, kt_ * 128:(kt_ + 1) * 128], rhs=QT[0:96, h, 0:Tn],
                                                                           start=True, stop=True, skip_group_check=True), R=['QT', 'KTall'], W=[pSk])
                        for h in range(4 if int(os.environ.get('DBG_ATT', 1)) else 0):
                            pO = psO if h % 2 == 0 else psC[3][:, 0:260].rearrange("p (a b) -> p a b", a=4)
                            pOk = 'psO' if h % 2 == 0 else 'psC3'
                            for kp in range(min(3, npair)):
                                score(h, kp)
                            for kp in range(npair):
                                pS, pSk = sbufs[kp % 4]
                                P_, Pk = PT[kp % 3], 'PT%d' % (kp % 3)
                                if kp + 3 < npair:
                                    score(h, kp + 3)
                                k.op('act', lambda pS=pS, P_=P_: A.activation(P_[:, 0:2 * Tn], pS[:, 0:2 * Tn], AF.Exp, scale=SCALE), R=[pSk], W=[Pk])
                                for j_ in range(2):
                                    kt_ = 2 * kp + j_
                                    for sb in range(nsub):
                                        first = (kp == 0 and j_ == 0 and sb == 0)
                                        k.op('pe', lambda h=h, kt_=kt_, j_=j_, sb=sb, P_=P_, first=first: PE.matmul(
                                            pO[:, sb, :], lhsT=P_[:, j_ * Tn + sb * 128:j_ * Tn + (sb + 1) * 128], rhs=Vt[:, kt_, h, :],
                                            start=first, stop=(kp == npair - 1 and j_ == 1), skip_group_check=True), R=[Pk, 'Vtall'], W=[pOk])
                            k.op('act', lambda: A.copy(osb[:, 0:nsub, :], pO[:, 0:nsub, :]), R=[pOk], W=['osb'])
                            k.op('dve', lambda: V.reciprocal(rec[:, 0:nsub, :], osb[:, 0:nsub, 64:65]), R=['osb'], W=['rec'])
                            k.op('dve', lambda h=h: V.tensor_tensor(att[:, 0:nsub, h * 64:(h + 1) * 64], osb[:, 0:nsub, 0:64], rec[:, 0:nsub, :].to_broadcast([128, nsub, 64]), op=ALU.mult),
                                 R=['osb', 'rec'], W=['att'])
                        if 'Catt' in dbg and l == 0 and kind == 'lat' and t0 == 0:
                            d_ = dbg_out('att', (128, 4 * 256), BF16)
                            k.dma(d_, att[:].rearrange("p a b -> p (a b)"), R=['att'], W=['dbgatt'])
                        for sb in range(nsub):
                            for ft in range(2):
                                k.op('pe', lambda sb=sb, ft=ft: PE.transpose(pt[:, ft, :], att[:, sb, ft * 128:(ft + 1) * 128], ident_b[:]), R=['att', 'ident_b'], W=['ptC'])
                            k.op('dve', lambda sb=sb: V.tensor_tensor(catg[:, 0:2, sb * 128:(sb + 1) * 128], pt[:, 0:2, :], sg[:, 0:2, sb * 128:(sb + 1) * 128], op=ALU.mult),
                                 R=['ptC', 'sg'], W=['sg'])
                        si = 0 if kind == 'ctx' else 1
                        nseq = NCTX if kind == 'ctx' else SEQ
                        L_ = Tn + 16
                        k.dma(pin[:, :, 0:L_], pool_scr[si][:, t0:t0 + L_].rearrange("(c p) n -> p c n", p=128), R=['pool_scr'], W=['pin'])
                        if kind == 'lat':
                            pinB = pb[:].bitcast(BF16)[:, :, 0:L_]
                            k.dma(pinB, pool_scr[si][:, HALF + t0:HALF + t0 + L_].rearrange("(c p) n -> p c n", p=128), R=['pool_scr'], W=['pb'], q='pool')
                            k.op('dve', lambda: V.tensor_scalar_mul(pin[:, :, 0:L_], pin[:, :, 0:L_], msb[:, 0:1]), R=['pin', 'msb'], W=['pin'])
                            k.op('dve', lambda: V.scalar_tensor_tensor(pin[:, :, 0:L_], pinB, msb[:, 1:2], pin[:, :, 0:L_], op0=ALU.mult, op1=ALU.add), R=['pin', 'pb', 'msb'], W=['pin'])
                        k.op('dve', lambda: V.tensor_tensor(pa[:, :, 0:L_ - 1], pin[:, :, 0:L_ - 1], pin[:, :, 1:L_], op=ALU.add), R=['pin'], W=['pa'])
                        o_ = slice(8, 8 + Tn)
                        k.op('pool', lambda: G.tensor_copy(pb[0:64, 0, o_], pa[0:64, 0, 7:7 + Tn]), R=['pa'], W=['pb'])
                        k.op('pool', lambda: G.tensor_tensor(pb[64:128, 0, o_], pa[64:128, 0, 6:6 + Tn], pa[64:128, 0, 8:8 + Tn], op=ALU.add), R=['pa'], W=['pb'])
                        k.op('dve', lambda: V.tensor_tensor(pb[:, 1, 0:L_ - 3], pa[:, 1, 0:L_ - 3], pa[:, 1, 2:L_ - 1], op=ALU.add), R=['pa'], W=['pb'])
                        k.op('dve', lambda: V.tensor_tensor(pa[64:128, 1, 0:L_ - 7], pb[64:128, 1, 0:L_ - 7], pb[64:128, 1, 4:L_ - 3], op=ALU.add), R=['pb'], W=['pa'])
                        k.op('pool', lambda: G.tensor_tensor(pa[0:64, 1, o_], pb[0:64, 1, 4:4 + Tn], pb[0:64, 1, 8:8 + Tn], op=ALU.add), R=['pb'], W=['pa'])
                        k.op('dve', lambda: V.tensor_tensor(pb[64:128, 1, o_], pa[64:128, 1, 0:Tn], pa[64:128, 1, 8:8 + Tn], op=ALU.add), R=['pa'], W=['pb'])
                        k.op('pool', lambda: G.tensor_copy(pb[0:64, 1, o_], pa[0:64, 1, o_]), R=['pa'], W=['pb'])
                        for ct in range(2):
                            k.op('dve', lambda ct=ct: V.scalar_tensor_tensor(pld[:, ct, 0:Tn], pb[:, ct, o_], winv[:, ct:ct + 1], pin[:, ct, o_], op0=ALU.mult, op1=ALU.subtract),
                                 R=['pb', 'pin', 'winv'], W=['pld'])
                        wins = ((0, 0, 2), (64, 0, 4), (0, 1, 8), (64, 1, 16))
                        if t0 == 0:
                            for (r0, ct, w_) in wins:
                                for p_ in range(w_ // 2):
                                    rcp = (1.0 / float(p_ + w_ // 2)) if kind == 'ctx' else recF[r0:r0 + 64, ct, p_:p_ + 1]
                                    k.op('dve', lambda r0=r0, ct=ct, p_=p_, rcp=rcp: V.scalar_tensor_tensor(pld[r0:r0 + 64, ct, p_:p_ + 1], pb[r0:r0 + 64, ct, 8 + p_:9 + p_], rcp,
                                                                                                       pin[r0:r0 + 64, ct, 8 + p_:9 + p_], op0=ALU.mult, op1=ALU.subtract),
                                         R=['pb', 'pin', 'recF'], W=['pld'])
                        if t0 + Tn == (NCTX if kind == 'ctx' else HALF):
                            for (r0, ct, w_) in wins:
                                for p_ in range(Tn - w_ // 2 + 1, Tn):
                                    rcp = (1.0 / float(Tn - p_ + w_ // 2)) if kind == 'ctx' else recL[r0:r0 + 64, ct, p_ - (Tn - 8):p_ - (Tn - 8) + 1]
                                    k.op('dve', lambda r0=r0, ct=ct, p_=p_, rcp=rcp: V.scalar_tensor_tensor(pld[r0:r0 + 64, ct, p_:p_ + 1], pb[r0:r0 + 64, ct, 8 + p_:9 + p_], rcp,
                                                                                                       pin[r0:r0 + 64, ct, 8 + p_:9 + p_], op0=ALU.mult, op1=ALU.subtract),
                                         R=['pb', 'pin', 'recL'], W=['pld'])
                        for ct in range(2):
                            k.op('pe', lambda ct=ct: PE.matmul(psC[ct][:, 0:Tn], lhsT=wpl[:, ct, :], rhs=pld[:, ct, 0:Tn], start=True, stop=True), R=['wpl', 'pld'], W=['psC%d' % ct])
                            k.op('dve', lambda ct=ct: V.scalar_tensor_tensor(catg[:, 2 + ct, 0:Tn], psC[ct][:, 0:Tn], colv[:, 4 + ct:5 + ct], sg[:, 2 + ct, 0:Tn], op0=ALU.mult, op1=ALU.mult),
                                 R=['psC%d' % ct, 'colv', 'sg'], W=['sg'])
                        k.dma(s5t[:, :, 0:Tn], s5o_scr[:, u0:u0 + Tn].rearrange("(c p) n -> p c n", p=128), R=['s5o_scr'], W=['s5t'], q='pool')
                        k.op('pool', lambda: G.tensor_tensor(catg[:, 4:6, 0:Tn], s5t[:, :, 0:Tn], sg[:, 4:6, 0:Tn], op=ALU.mult), R=['s5t', 'sg'], W=['sg'])
                        for sb in range(nsub):
                            cs = slice(sb * 128, (sb + 1) * 128)
                            for ft in range(2):
                                k.op('pe', lambda ft=ft, cs=cs: PE.transpose(pt[:, 2 + ft, :], vnb[:, ft, cs], ident_b[:]), R=['vnb', 'ident_b'], W=['ptC'])
                            k.op('act', lambda: A.copy(vc[:].rearrange("p (a b) -> p a b", a=2), pt[:, 2:4, :]), R=['ptC'], W=['vc'])
                            for h in range(4):
                                ft, r0 = h // 2, (h % 2) * 64
                                pm_, pmk = psC[2 + h % 2], 'psC%d' % (2 + h % 2)
                                k.op('pe', lambda h=h, ft=ft: PE.matmul(pm_[:, 0:128], lhsT=vc[:, ft * 128:(ft + 1) * 128], rhs=wsT[:, h, :], start=True, stop=True), R=['vc', 'wsT'], W=[pmk])
                                k.op('dve', lambda h=h, r0=r0: V.tensor_tensor(mx[r0:r0 + 64, :], pm_[r0:r0 + 64, 0:128], bsb[r0:r0 + 64, h, :], op=ALU.add), R=[pmk, 'bsb'], W=['mx'])
                                k.op('pool', lambda ft=ft, r0=r0, cs=cs: G.tensor_tensor(mx[r0:r0 + 64, :], mx[r0:r0 + 64, :], su[r0:r0 + 64, ft, cs], op=ALU.mult), R=['mx', 'su'], W=['mx'])
                                k.op('dve', lambda ft=ft, r0=r0, cs=cs: V.tensor_tensor(catg[r0:r0 + 64, 6 + ft, cs], mx[r0:r0 + 64, :], sg[r0:r0 + 64, 6 + ft, cs], op=ALU.mult),
                                     R=['mx', 'sg'], W=['sg'])
                        if 'Ccat' in dbg and l == 0 and kind == 'lat' and t0 == 0:
                            d_ = dbg_out('catg', (128, 8 * QB), BF16)
                            k.dma(d_, catg[:].rearrange("p a b -> p (a b)"), R=['sg'], W=['dbgcat'])
                        for sb in range(nsub):
                            cs = slice(sb * 128, (sb + 1) * 128)
                            xb = xt[xcnt % 2]
                            xk = 'xtC0'
                            xcnt += 1
                            k.dma(xb[:], src[t0 + sb * 128: t0 + (sb + 1) * 128, :], W=[xk], q=('sp' if sb % 2 == 0 else 'pool'))
                            for nh in range(2):
                                for kt in range(8):
                                    k.op('pe', lambda nh=nh, kt=kt, cs=cs: PE.matmul(psC[nh][:, :], lhsT=catg[:, kt, cs], rhs=wout[:, kt, nh * 512:(nh + 1) * 512],
                                                                                    start=(kt == 0), stop=(kt == 7)), R=['sg', 'wout'], W=['psC%d' % nh])
                                k.op('dve', lambda nh=nh: V.tensor_tensor(rr[:, nh * 512:(nh + 1) * 512], psC[nh][:, :], gate_bc[:, nh * 512:(nh + 1) * 512], op=ALU.mult),
                                     R=['psC%d' % nh, 'gate_bc'], W=['rr'])
                            k.op('dve', lambda xb=xb: V.scalar_tensor_tensor(rr[:], xb[:], float(ALPHA), rr[:], op0=ALU.mult, op1=ALU.add), R=[xk, 'rr'], W=['rr'])
                            for c2 in range(2):
                                k.op('dve', lambda c2=c2: V.bn_stats(st6[:, c2, :], rr[:, c2 * 512:(c2 + 1) * 512]), R=['rr'], W=['st6'])
                            k.op('dve', lambda: V.bn_aggr(mv[:], st6[:]), R=['st6'], W=['mv'])
                            k.op('act', lambda: A.activation(rstd[:], mv[:, 1:2], AF.Sqrt, bias=epsb[:, 0:1]), R=['mv', 'epsb'], W=['rstd'])
                            k.op('dve', lambda: V.reciprocal(rstd[:], rstd[:]), R=['rstd'], W=['rstd'])
                            k.op('dve', lambda: V.scalar_tensor_tensor(rr[:], rr[:], mv[:, 0:1], lng_bc[:], op0=ALU.subtract, op1=ALU.mult), R=['rr', 'mv', 'lng_bc'], W=['rr'])
                            k.op('dve', lambda xb=xb: V.scalar_tensor_tensor(xb[:], rr[:], rstd[:, 0:1], lnb_bc[:], op0=ALU.mult, op1=ALU.add), R=['rr', 'rstd', 'lnb_bc'], W=[xk])
                            k.dma(dst[t0 + sb * 128: t0 + (sb + 1) * 128, :], xb[:], R=[xk], W=['x1w'])

            if l == 0 and nlayers > 1:
                k._sync('pool', ['x1w'], ['x1full'])
                for c_ in range(HALF // CCR):
                    G.collective_compute("AllGather", ALU.bypass, replica_groups=[[0, 1], [2, 3], [4, 5], [6, 7]],
                                         ins=[x1h_scr[c_ * CCR:(c_ + 1) * CCR, :]], outs=[x1g[c_]]).then_inc(ccsem)
                ccn[0] += HALF // CCR
                x1_ready = ccn[0]
            if 'x1' in dbg and l == 0:
                d_ = dbg_out('x1', (SEQ, D))
                k.dma(d_[0:HALF, :], x1h_scr, R=['x1w'], W=['dbgx1'])
                d_ = dbg_out('ctx1', (NCTX, D))
                k.dma(d_, ctx1_scr, R=['x1w'], W=['dbgc1'])
                break

        k.wait_all('sp')
    return nc, dout


def _in_maps(inputs, cores):
    cst = _consts()
    shared = {n: np.ascontiguousarray(inputs[n], dtype=np.float32) for n in PARAMS}
    shared.update(cst)
    maps = []
    for i in cores:
        b, hf = i // 2, i % 2
        sl = slice(hf * HALF, (hf + 1) * HALF)
        m = dict(shared)
        m['x'] = np.ascontiguousarray(inputs['x'][b], dtype=np.float32)
        m['xq'] = np.ascontiguousarray(inputs['x'][b][sl], dtype=np.float32)
        m['ctx'] = np.ascontiguousarray(inputs['ctx'][b], dtype=np.float32)
        m['cvec'] = np.ascontiguousarray(np.stack([inputs['c'][b], inputs['c_ctx']], 0), dtype=np.float32)
        m['msel'] = np.array([1.0 - hf, float(hf)], np.float32)
        if hf:
            for n_ in ('lam_re', 'lam_im', 'log_dt', 's5_b_re', 's5_b_im', 's5_c_re', 's5_c_im'):
                m[n_] = np.ascontiguousarray(np.roll(shared[n_], -8, axis=2))
            w_in = shared['w_in'].copy()
            w_in[:, :, 160:416] = np.roll(w_in[:, :, 160:416], -128, axis=2)
            w_in[:, :, 1888:2144] = np.roll(w_in[:, :, 1888:2144], -128, axis=2)
            m['w_in'] = w_in
            m['s5_d'] = np.ascontiguousarray(np.roll(shared['s5_d'], -128, axis=1))
            m['b_glu'] = np.ascontiguousarray(np.roll(shared['b_glu'], -128, axis=1))
            m['w_glu'] = np.ascontiguousarray(np.roll(np.roll(shared['w_glu'], -128, axis=1), -128, axis=2))
            w_out = shared['w_out'].copy()
            w_out[:, 512:768, :] = np.roll(w_out[:, 512:768, :], -128, axis=1)
            m['w_out'] = w_out
        m['ropec_q'] = np.ascontiguousarray(cst['ropec'][:, sl])
        m['ropes_q'] = np.ascontiguousarray(cst['ropes'][:, sl])
        maps.append(m)
    return maps


def kernel(**inputs):
    nc, _ = build()
    cores = list(range(8))
    res = run_bass_kernel_spmd(nc, _in_maps(inputs, cores), core_ids=cores)
    out = np.empty((4, SEQ, D), np.float32)
    for i in cores:
        b, hf = i // 2, i % 2
        out[b, hf * HALF:(hf + 1) * HALF] = np.asarray(res.results[i]['out'], dtype=np.float32)
    return out
```

```python
import os
from contextlib import ExitStack
import numpy as np
import concourse.bass as bass
import concourse.mybir as mybir
from concourse.bass_utils import run_bass_kernel_spmd

F32 = mybir.dt.float32
BF16 = mybir.dt.bfloat16
AF = mybir.ActivationFunctionType
ALU = mybir.AluOpType

D = 1024
SEQ = 8192
NCTX = 256
NU = NCTX + SEQ
NE = NU + NCTX
NSC = NE // 8
HALF = SEQ // 2
DEPTH = 2
QB = 256
IN_DIM = 2400
LN_EPS = 1e-6
ALPHA = (2 * DEPTH) ** 0.25
PARAMS = ['w_mod', 'b_mod', 'w_in', 'g_q', 'w_uq', 'g_kv', 'w_ukv', 'w_pool', 'pool_scale', 'lam_re', 'lam_im',
          'log_dt', 's5_b_re', 's5_b_im', 's5_c_re', 's5_c_im', 's5_d', 'w_glu', 'b_glu', 'sgu_g', 'sgu_b', 'w_s',
          'b_s', 'w_out', 'ln_g', 'ln_b']
PSHAPES = {'w_mod': (2, 1024, 3072), 'b_mod': (2, 3072), 'w_in': (2, 1024, 2400), 'g_q': (2, 192), 'w_uq': (2, 192, 384),
           'g_kv': (2, 128), 'w_ukv': (2, 128, 512), 'w_pool': (2, 4, 64, 64), 'pool_scale': (2, 256),
           'lam_re': (2, 2, 16, 64), 'lam_im': (2, 2, 16, 64), 'log_dt': (2, 2, 16), 's5_b_re': (2, 2, 16, 64, 16),
           's5_b_im': (2, 2, 16, 64, 16), 's5_c_re': (2, 2, 16, 16, 64), 's5_c_im': (2, 2, 16, 16, 64), 's5_d': (2, 256),
           'w_glu': (2, 256, 256), 'b_glu': (2, 256), 'sgu_g': (2, 256), 'sgu_b': (2, 256), 'w_s': (2, 4, 128, 128),
           'b_s': (2, 4, 128), 'w_out': (2, 1024, 1024), 'ln_g': (2, 1024), 'ln_b': (2, 1024)}

SAME_ENGINE_SYNC = True


class KB:
    NSLOT = 8

    def __init__(self, nc, stack):
        self.nc = nc
        self.eng = {'pe': nc.tensor, 'act': nc.scalar, 'dve': nc.vector, 'pool': nc.gpsimd, 'sp': nc.sync}
        self.sem = {e: stack.enter_context(nc.semaphore('s_' + e)) for e in self.eng}
        self.cnt = {e: 0 for e in self.eng}
        self.dsem, self.duse = {}, {}
        for q in ('sp', 'pool', 'act'):
            for j in range(self.NSLOT):
                self.dsem[(q, j)] = stack.enter_context(nc.semaphore('d_%s%d' % (q, j)))
                self.duse[(q, j)] = 0
        self.dnext = {'sp': 0, 'pool': 0, 'act': 0}
        self.known = {e: {} for e in self.eng}
        self.lastw, self.readers = {}, {}
        self.ninstr = 0

    def _need(self, E, tok, waits):
        if tok is None:
            return
        if tok[0] == 'c':
            _, e2, n = tok
            if e2 == E and (not SAME_ENGINE_SYNC or E == 'pe'):
                return
            key, val = e2, n
        else:
            _, q, j, n = tok
            key, val = (q, j), n * 16
        if self.known[E].get(key, 0) >= val:
            return
        if waits.get(key, 0) < val:
            waits[key] = val

    def _sync(self, E, R, W, waits=None):
        waits = {} if waits is None else waits
        for r in R:
            self._need(E, self.lastw.get(r), waits)
        for w in W:
            self._need(E, self.lastw.get(w), waits)
            for t in self.readers.get(w, ()):
                self._need(E, t, waits)
        eng = self.eng[E]
        for key, val in waits.items():
            eng.wait_ge(self.sem[key] if isinstance(key, str) else self.dsem[key], val)
            self.known[E][key] = val

    def _commit(self, tok, R, W):
        for r in R:
            self.readers.setdefault(r, []).append(tok)
        for w in W:
            self.lastw[w] = tok
            self.readers[w] = []

    def op(self, E, fn, R=(), W=()):
        W = list(W) + [r for r in R if r.startswith('ps') and r not in W]
        self._sync(E, R, W)
        ins = fn()
        self.cnt[E] += 1
        ins.then_inc(self.sem[E], 1)
        self._commit(('c', E, self.cnt[E]), R, W)
        self.ninstr += 1
        return ins

    def dma(self, out, in_, R=(), W=(), q='sp', **kw):
        j = self.dnext[q]
        self.dnext[q] = (j + 1) % self.NSLOT
        slot = (q, j)
        waits = {}
        if self.duse[slot] > 0:
            self._need(q, ('d', q, j, self.duse[slot]), waits)
        self._sync(q, R, W, waits)
        ins = self.eng[q].dma_start(out=out, in_=in_, **kw)
        ins.then_inc(self.dsem[slot], 16)
        self.duse[slot] += 1
        self._commit(('d', q, j, self.duse[slot]), R, W)
        self.ninstr += 1

    def barrier(self):
        for E, eng in self.eng.items():
            for e2 in self.eng:
                if e2 != E and self.cnt[e2] > self.known[E].get(e2, 0):
                    eng.wait_ge(self.sem[e2], self.cnt[e2])
                    self.known[E][e2] = self.cnt[e2]
            for slot, n in self.duse.items():
                if n > 0 and 16 * n > self.known[E].get(slot, 0):
                    eng.wait_ge(self.dsem[slot], 16 * n)
                    self.known[E][slot] = 16 * n

    def wait_all(self, E='sp'):
        eng = self.eng[E]
        for e2 in self.eng:
            if e2 != E and self.cnt[e2] > 0:
                eng.wait_ge(self.sem[e2], self.cnt[e2])
        for slot, n in self.duse.items():
            if n > 0:
                eng.wait_ge(self.dsem[slot], 16 * n)


def _consts():
    c = {}
    c['ident'] = np.eye(128, dtype=np.float32)
    sw = np.zeros((128, 128), np.float32)
    for p in range(64):
        sw[p, p + 64] = 1.0
        sw[p + 64, p] = 1.0
    c['swapm'] = sw
    rows = SEQ // 64
    t = np.arange(SEQ)
    pos = np.stack([t // 64, t % 64], 0).astype(np.float32)
    inv = (10000.0 ** (-np.arange(8, dtype=np.float32) / 8)).astype(np.float32)
    ang = pos[:, None, :] * inv[None, :, None]
    cos, sin = np.cos(ang).astype(np.float32), np.sin(ang).astype(np.float32)
    rc = np.zeros((32, SEQ), np.float32)
    rs = np.zeros((32, SEQ), np.float32)
    for a in range(2):
        for hf in range(2):
            rc[a * 16 + hf * 8: a * 16 + hf * 8 + 8] = cos[a]
            rs[a * 16 + hf * 8: a * 16 + hf * 8 + 8] = sin[a] * (-1.0 if hf == 0 else 1.0)
    r_ = np.arange(128) // 16
    c['mlow'] = (r_[None, :] >= r_[:, None]).astype(np.float32)
    c['mup'] = (r_[None, :] <= r_[:, None]).astype(np.float32)
    c['ropec'] = rc
    c['ropes'] = rs
    return c


def build(nlayers=DEPTH, dbg=()):
    nc = bass.Bass("TRN2", target_bir_lowering=False)
    din = {}

    def inp(name, shape):
        din[name] = nc.dram_tensor(name, list(shape), F32, kind="ExternalInput").ap()
        return din[name]

    x_in = inp('x', (SEQ, D))
    xq_in = inp('xq', (HALF, D))
    ctx_in = inp('ctx', (NCTX, D))
    msel = inp('msel', (2,))
    ropecq = inp('ropec_q', (32, HALF))
    ropesq = inp('ropes_q', (32, HALF))
    cvec = inp('cvec', (2, D))
    prm = {n: inp(n, PSHAPES[n]) for n in PARAMS}
    cst = {n: inp(n, v.shape) for n, v in _consts().items()}
    out_ap = nc.dram_tensor('out', [HALF, D], F32, kind="ExternalOutput").ap()
    dout = {}

    def dbg_out(name, shape, dt=F32):
        dout[name] = nc.dram_tensor('dbg_' + name, list(shape), dt, kind="ExternalOutput").ap()
        return dout[name]

    scr = lambda name, shape, dt=F32: nc.dram_tensor(name, list(shape), dt, kind="Internal").ap()
    CCR = 512
    x1g = [scr('x1g%d' % c, (2 * CCR, D)) for c in range(HALF // CCR)]

    def x1rows(t):
        r_, w_ = t // HALF, t % HALF
        c_, o_ = w_ // CCR, w_ % CCR
        return x1g[c_][r_ * CCR + o_: r_ * CCR + o_ + 128, :]
    x1h_scr = scr('x1h_scr', (HALF, D))
    ctx1_scr = scr('ctx1_scr', (NCTX, D))
    gate_scr = scr('gate_scr', (2, 2, D))
    pool_scr = [scr('pool_scr_c', (256, NCTX + 16), BF16), scr('pool_scr_l', (256, SEQ + 16), BF16)]
    s5o_scr = scr('s5o_scr', (256, NU), BF16)
    xd_scr = scr('xd_scr', (256, 8, NSC), BF16)
    yd_scr = scr('yd_scr', (256, 8, NSC), BF16)
    ydg_scr = [scr('ydg%d' % c, (128, 8 * NSC), BF16) for c in range(2)]

    with ExitStack() as st:
        k = KB(nc, st)

        uid = [0]

        def T(stk, name, shape, dt):
            uid[0] += 1
            return stk.enter_context(nc.sbuf_tensor('%s_t%d' % (name, uid[0]), list(shape), dt))

        def P(stk, name, shape, dt=F32):
            uid[0] += 1
            return stk.enter_context(nc.psum_tensor('%s_p%d' % (name, uid[0]), list(shape), dt))

        V = nc.vector
        A = nc.scalar
        G = nc.gpsimd
        PE = nc.tensor

        def ln_block(xsrcs, rkeys, col, hT_, B):
            n = len(xsrcs)

            def stage1(s_):
                xb, xk = B['xt'][s_ % 2]
                xn_, xnk = B['xn'][s_ % 2]
                pt_, ptk = B['pt'][s_ % 2]
                sm, smk = B['sm'][s_ % 2]
                st6_ = sm[:, 0:12].rearrange("p (a b) -> p a b", a=2)
                k.dma(xb[:], xsrcs[s_], R=rkeys, W=[xk], q=('sp' if s_ % 2 == 0 else 'pool'))
                for c2 in range(2):
                    k.op('dve', lambda c2=c2: V.bn_stats(st6_[:, c2, :], xb[:, c2 * 512:(c2 + 1) * 512]), R=[xk], W=[smk])
                k.op('dve', lambda: V.bn_aggr(sm[:, 12:14], st6_), R=[smk], W=[smk])
                k.op('act', lambda: A.activation(sm[:, 14:15], sm[:, 13:14], AF.Sqrt, bias=epsb[:, 0:1]), R=[smk, 'epsb'], W=[smk])
                k.op('dve', lambda: V.reciprocal(sm[:, 14:15], sm[:, 14:15]), R=[smk], W=[smk])
                k.op('dve', lambda: V.tensor_scalar(sm[:, 15:16], sm[:, 12:13], sm[:, 14:15], -1.0, op0=ALU.mult, op1=ALU.mult), R=[smk], W=[smk])
                k.op('act', lambda: A.activation(xn_[:], xb[:], AF.Identity, scale=sm[:, 14:15], bias=sm[:, 15:16]), R=[xk, smk], W=[xnk])
                for dc in range(8):
                    k.op('pe', lambda dc=dc: PE.transpose(pt_[:, dc, :], xn_[:, dc * 128:(dc + 1) * 128], ident_b[:]), R=[xnk, 'ident_b'], W=[ptk])

            def stage2(s_):
                pt_, ptk = B['pt'][s_ % 2]
                if B.get('dve_mod'):
                    hv = hT_[:, :, s_ * 128:(s_ + 1) * 128]
                    k.op('dve', lambda: V.tensor_tensor(hv, pt_[:, :, :], sc1T[:, :, col].unsqueeze(2).to_broadcast([128, 8, 128]), op=ALU.mult),
                         R=[ptk, 'sc1T'], W=[B['hTk']])
                    k.op('pool', lambda: G.tensor_tensor(hv, hv, modT[:, 0:8, col].unsqueeze(2).to_broadcast([128, 8, 128]), op=ALU.add),
                         R=[B['hTk'], 'modT'], W=[B['hTk']])
                    return
                for dc in range(8):
                    k.op('act', lambda dc=dc: A.activation(hT_[:, dc, s_ * 128:(s_ + 1) * 128], pt_[:, dc, :], AF.Identity,
                                                          scale=sc1T[:, dc, col:col + 1], bias=modT[:, dc, col:col + 1]),
                         R=[ptk, 'sc1T', 'modT'], W=[B['hTk']])

            stage1(0)
            for s_ in range(n):
                if s_ + 1 < n:
                    stage1(s_ + 1)
                stage2(s_)

        ident_f = T(st, 'ident_f', (128, 128), F32)
        ident_b = T(st, 'ident_b', (128, 128), BF16)
        ones_b = T(st, 'ones_b', (128, 128), BF16)
        epsb = T(st, 'epsb', (128, 1), F32)
        KT = T(st, 'KT', (128, 4, NU), BF16)
        Vt = T(st, 'Vt', (128, NU // 128, 4, 65), BF16)
        modT = T(st, 'modT', (128, 24, 2), F32)
        sc1T = T(st, 'sc1T', (128, 8, 2), F32)
        wukv = T(st, 'wukv', (128, 512), BF16)
        msb = T(st, 'msb', (128, 2), F32)
        recF = T(st, 'recF', (128, 2, 8), F32)
        recL = T(st, 'recL', (128, 2, 8), F32)
        ccsem = st.enter_context(nc.semaphore('ccsem'))
        ccn = [0]

        k.dma(ident_f[:], cst['ident'], W=['ident_f'])
        k.op('dve', lambda: V.tensor_copy(ident_b[:], ident_f[:]), R=['ident_f'], W=['ident_b'])
        k.op('pool', lambda: G.memset(ones_b[:], 1.0), W=['ones_b'])
        k.op('pool', lambda: G.memset(epsb[:], LN_EPS), W=['epsb'])
        k.op('pool', lambda: G.memset(Vt[:, :, :, 64:65], 1.0), W=['Vt'])
        k.dma(msb[:], msel.partition_broadcast(128), W=['msb'])
        k.op('pool', lambda: G.memset(recF[:], 1.0), W=['recF'])
        k.op('pool', lambda: G.memset(recL[:], 1.0), W=['recL'])
        WINS = ((0, 0, 2), (64, 0, 4), (0, 1, 8), (64, 1, 16))
        for (r0, ct, w_) in WINS:
            for p_ in range(w_ // 2):
                c1_ = 1.0 / (p_ + w_ // 2) - 1.0 / w_
                k.op('dve', lambda r0=r0, ct=ct, p_=p_, c1_=c1_, w_=w_: V.tensor_scalar(recF[r0:r0 + 64, ct, p_:p_ + 1], msb[r0:r0 + 64, 0:1], c1_, 1.0 / w_, op0=ALU.mult, op1=ALU.add),
                     R=['msb'], W=['recF'])
            for q_ in range(8 - w_ // 2 + 1, 8):
                c1_ = 1.0 / (8 - q_ + w_ // 2) - 1.0 / w_
                k.op('dve', lambda r0=r0, ct=ct, q_=q_, c1_=c1_, w_=w_: V.tensor_scalar(recL[r0:r0 + 64, ct, q_:q_ + 1], msb[r0:r0 + 64, 1:2], c1_, 1.0 / w_, op0=ALU.mult, op1=ALU.add),
                     R=['msb'], W=['recL'])

        for l in range(nlayers):
            last = (l == DEPTH - 1)
            xin = x_in
            cin = ctx_in if l == 0 else ctx1_scr
            k.barrier()
            with ExitStack() as s0:
                cc = T(s0, 'cc', (128, 8, 2), F32)
                scc = T(s0, 'scc', (128, 8, 2), F32)
                bmodT = T(s0, 'bmodT', (128, 24), F32)
                wm = [T(s0, 'wm%d' % i, (128, 8, 512), F32) for i in range(2)]
                pm = P(s0, 'pm', (128, 24, 2))
                wtmp = T(s0, 'wtmp', (128, 512), F32)
                gkv = T(s0, 'gkv', (128, 1), F32)
                for col in range(2):
                    k.dma(cc[:, :, col], cvec[col].rearrange("(c p) -> p c", p=128), W=['cc'], allow_slow_non_contiguous=True)
                k.dma(bmodT[:], prm['b_mod'][l].rearrange("(c p) -> p c", p=128), W=['bmodT'], allow_slow_non_contiguous=True)
                k.op('act', lambda: A.activation(scc[:], cc[:], AF.Silu), R=['cc'], W=['scc'])
                for blk in range(6):
                    w_ = wm[blk % 2]
                    k.dma(w_[:], prm['w_mod'][l][:, blk * 512:(blk + 1) * 512].rearrange("(c p) n -> p c n", p=128),
                          W=['wm%d' % (blk % 2)], q=('sp' if blk % 2 == 0 else 'pool'))
                    for jj in range(4):
                        jt = blk * 4 + jj
                        for dc in range(8):
                            k.op('pe', lambda w_=w_, jj=jj, jt=jt, dc=dc: PE.matmul(
                                pm[:, jt, :], lhsT=w_[:, dc, jj * 128:(jj + 1) * 128], rhs=scc[:, dc, :],
                                start=(dc == 0), stop=(dc == 7)), R=['wm%d' % (blk % 2), 'scc'], W=['pm'])
                k.op('dve', lambda: V.tensor_tensor(modT[:], pm[:], bmodT[:].unsqueeze(2).to_broadcast([128, 24, 2]), op=ALU.add),
                     R=['pm', 'bmodT'], W=['modT'])
                k.op('dve', lambda: V.tensor_scalar_add(sc1T[:], modT[:, 8:16, :], 1.0), R=['modT'], W=['sc1T'])
                for col in range(2):
                    k.dma(gate_scr[l, col].rearrange("(c p) -> p c", p=128), modT[:, 16:24, col], R=['modT'], W=['gate_scr'],
                          allow_slow_non_contiguous=True)
                k.dma(wtmp[:], prm['w_ukv'][l], W=['wtmp'])
                k.dma(gkv[:], prm['g_kv'][l].rearrange("(p o) -> p o", o=1), W=['gkv'])
                k.op('dve', lambda: V.tensor_scalar_mul(wukv[:], wtmp[:], gkv[:, 0:1]), R=['wtmp', 'gkv'], W=['wukv'])

            if 'mod' in dbg and l == 0:
                d_ = dbg_out('mod', (128, 48))
                k.dma(d_, modT[:].rearrange("p a b -> p (a b)"), R=['modT'], W=['dbgmod'])

            k.barrier()
            if l > 0:
                for q_ in ('sp', 'pool'):
                    k.eng[q_].wait_ge(ccsem, x1_ready)
            with ExitStack() as sA:
                with ExitStack() as sa:
                    wA = T(sa, 'wA', (128, 8, 832), BF16)
                    wst = T(sa, 'wst', (128, 8, 256), F32)
                    xt = [T(sa, 'xt%d' % i, (128, D), F32) for i in range(2)]
                    xn2 = [T(sa, 'xn2_%d' % i, (128, D), BF16) for i in range(2)]
                    sm2 = [T(sa, 'sm2_%d' % i, (128, 16), F32) for i in range(2)]
                    st6 = T(sa, 'st6', (128, 2, 6), F32)
                    mv = T(sa, 'mv', (128, 2), F32)
                    rstd = T(sa, 'rstd', (128, 1), F32)
                    xn = T(sa, 'xn', (128, D), BF16)
                    hf32 = T(sa, 'hf32', (128, 8, 128), F32)
                    hT2 = [T(sa, 'hT%d' % i, (128, 8, 512), BF16) for i in range(2)]
                    sq = T(sa, 'sq', (128, 512), BF16)
                    rms = T(sa, 'rms', (128, 512), F32)
                    ckvn = T(sa, 'ckvn', (128, 512), BF16)
                    rc_t = T(sa, 'rc_t', (128, 512), F32)
                    rs_t = T(sa, 'rs_t', (128, 512), F32)
                    kr1 = T(sa, 'kr1', (128, 512), F32)
                    kr2 = T(sa, 'kr2', (128, 512), F32)
                    plo = T(sa, 'plo', (128, 2, 512), BF16)
                    zpad = T(sa, 'zpad', (128, 2, 8), BF16)
                    xds = T(sa, 'xds', (128, 2, 8, 64), BF16)
                    pt = P(sa, 'pt', (128, 8, 128), BF16)
                    ptb = P(sa, 'ptb', (128, 8, 128), BF16)
                    LNB = {'xt': [(xt[0], 'xt0'), (xt[1], 'xt1')], 'xn': [(xn2[0], 'xn2_0'), (xn2[1], 'xn2_1')],
                           'pt': [(pt, 'pt'), (ptb, 'ptb')], 'sm': [(sm2[0], 'sm2_0'), (sm2[1], 'sm2_1')], 'hTk': 'hT'}
                    ps = [P(sa, 'psA%d' % i, (128, 512)) for i in range(6)]

                    k.op('pool', lambda: G.memset(wA[:, :, 128:320], 0.0), W=['wA'])
                    k.op('pool', lambda: G.memset(zpad[:], 0.0), W=['zpad'])
                    win = prm['w_in'][l].rearrange("(c p) n -> p c n", p=128)
                    k.dma(wst[:, :, 0:160], win[:, :, 0:160], W=['wst'])
                    k.op('dve', lambda: V.tensor_copy(wA[:, :, 0:128], wst[:, :, 0:128]), R=['wst'], W=['wA'])
                    k.op('dve', lambda: V.tensor_copy(wA[:, :, 192:224], wst[:, :, 128:160]), R=['wst'], W=['wA'])
                    for (d0, s0_) in ((0, 8), (8, 0), (16, 24), (24, 16)):
                        k.op('dve', lambda d0=d0, s0_=s0_: V.tensor_copy(wA[:, :, 288 + d0:296 + d0], wst[:, :, 128 + s0_:136 + s0_]),
                             R=['wst'], W=['wA'])
                    k.dma(wst[:], win[:, :, 160:416], R=[], W=['wst'])
                    k.op('dve', lambda: V.tensor_copy(wA[:, :, 320:576], wst[:]), R=['wst'], W=['wA'])
                    k.dma(wst[:], win[:, :, 608:864], R=[], W=['wst'])
                    k.op('dve', lambda: V.tensor_copy(wA[:, :, 576:832], wst[:]), R=['wst'], W=['wA'])
                    for si in range(2):
                        k.dma(pool_scr[si][:, 0:8].rearrange("(c p) n -> p c n", p=128), zpad[:], R=['zpad'], W=['pool_scr'])
                        n_ = NCTX if si == 0 else SEQ
                        k.dma(pool_scr[si][:, 8 + n_:16 + n_].rearrange("(c p) n -> p c n", p=128), zpad[:], R=['zpad'], W=['pool_scr'])

                    blocks = [('ctx', 0, NCTX)] + [('lat', 512 * i, 512) for i in range(SEQ // 512)]
                    xcnt = 0
                    for bidx, (kind, t0, Tn) in enumerate(blocks):
                        hT, hTk = hT2[bidx % 2], 'hT%d' % (bidx % 2)
                        LNB['hTk'] = hTk
                        col = 1 if kind == 'ctx' else 0
                        src = cin if kind == 'ctx' else xin
                        u0 = t0 if kind == 'ctx' else NCTX + t0
                        nsub = Tn // 128
                        xsrcs = [(x1rows(t0 + sb * 128) if (l > 0 and kind == 'lat') else src[t0 + sb * 128: t0 + (sb + 1) * 128, :]) for sb in range(nsub)]
                        ln_block(xsrcs, ['x1w'] if l > 0 else [], col, hT, LNB)
                        def proj(pst, c0, ncol, key):
                            for dc in range(8):
                                k.op('pe', lambda dc=dc: PE.matmul(pst[0:ncol, 0:Tn], lhsT=wA[:, dc, c0:c0 + ncol], rhs=hT[:, dc, 0:Tn],
                                                                  start=(dc == 0), stop=(dc == 7)), R=['wA', hTk], W=[key])
                        proj(ps[0], 0, 128, 'psA0')
                        proj(ps[1], 128, 96, 'psA1')
                        if kind == 'lat':
                            proj(ps[2], 224, 96, 'psA2')
                        k.op('act', lambda: A.activation(sq[:, 0:Tn], ps[0][:, 0:Tn], AF.Square), R=['psA0'], W=['sq'])
                        k.op('pe', lambda: PE.matmul(ps[3][:, 0:Tn], lhsT=ones_b[:], rhs=sq[:, 0:Tn], start=True, stop=True),
                             R=['ones_b', 'sq'], W=['psA3'])
                        k.op('act', lambda: A.activation(rms[:, 0:Tn], ps[3][:, 0:Tn], AF.Sqrt, bias=epsb[:, 0:1], scale=1.0 / 128),
                             R=['psA3', 'epsb'], W=['rms'])
                        k.op('dve', lambda: V.reciprocal(rms[:, 0:Tn], rms[:, 0:Tn]), R=['rms'], W=['rms'])
                        k.op('dve', lambda: V.tensor_tensor(ckvn[:, 0:Tn], ps[0][:, 0:Tn], rms[:, 0:Tn], op=ALU.mult), R=['psA0', 'rms'], W=['ckvn'])
                        for h in range(4):
                            k.op('pe', lambda h=h: PE.matmul(ps[3][0:64, 0:Tn], lhsT=wukv[:, h * 128:h * 128 + 64], rhs=ckvn[:, 0:Tn],
                                                            start=True, stop=True), R=['wukv', 'ckvn'], W=['psA3'])
                            k.op('act', lambda h=h: A.copy(KT[0:64, h, u0:u0 + Tn], ps[3][0:64, 0:Tn]), R=['psA3'], W=['KT%d' % (u0 // 512)])
                        wv = wukv[:].rearrange("p (h t d) -> p h t d", h=4, t=2)[:, :, 1, :]
                        for sb in range(nsub):
                            k.op('pe', lambda sb=sb: PE.matmul(ps[4][:, 0:256].rearrange("p (h d) -> p h d", h=4), lhsT=ckvn[:, sb * 128:(sb + 1) * 128],
                                                              rhs=wv, start=True, stop=True), R=['wukv', 'ckvn'], W=['psA4'])
                            k.op('dve', lambda sb=sb: V.tensor_copy(Vt[:, u0 // 128 + sb, :, 0:64], ps[4][:, 0:256].rearrange("p (h d) -> p h d", h=4)),
                                 R=['psA4'], W=['Vt%d' % (u0 // 512)])
                        if kind == 'lat':
                            k.dma(rc_t[64:96, :], cst['ropec'][:, t0:t0 + 512], W=['rc_t'])
                            k.dma(rs_t[64:96, :], cst['ropes'][:, t0:t0 + 512], W=['rs_t'], q='pool')
                            k.op('dve', lambda: V.tensor_tensor(kr1[64:96, :], ps[1][64:96, :], rc_t[64:96, :], op=ALU.mult), R=['psA1', 'rc_t'], W=['kr1'])
                            k.op('dve', lambda: V.tensor_tensor(kr2[64:96, :], ps[2][64:96, :], rs_t[64:96, :], op=ALU.mult), R=['psA2', 'rs_t'], W=['kr2'])
                            for h in range(4):
                                k.op('pool', lambda h=h: G.tensor_tensor(KT[64:96, h, u0:u0 + Tn], kr1[64:96, 0:Tn], kr2[64:96, 0:Tn], op=ALU.add),
                                     R=['kr1', 'kr2'], W=['KT%d' % (u0 // 512)])
                        else:
                            for h in range(4):
                                k.op('act', lambda h=h: A.copy(KT[64:96, h, u0:u0 + Tn], ps[1][64:96, 0:Tn]), R=['psA1'], W=['KT%d' % (u0 // 512)])
                        for ct in range(2):
                            proj(ps[ct % 2 + 4], 320 + ct * 128, 128, 'psA%d' % (ct % 2 + 4))
                            srcv = ps[ct % 2 + 4][:, 0:Tn].rearrange("p (c i) -> p i c", i=8)
                            k.op('act', lambda ct=ct, srcv=srcv: A.copy(xds[:, ct, :, 0:Tn // 8], srcv), R=['psA%d' % (ct % 2 + 4)], W=['xds'])
                        xdv = xd_scr.rearrange("(c p) i n -> p c i n", p=128)
                        for ct in range(2):
                            k.dma(xdv[:, ct, :, u0 // 8:(u0 + Tn) // 8], xds[:, ct, :, 0:Tn // 8], R=['xds'], W=['xd_scr'])
                            if kind == 'ctx':
                                k.dma(xdv[:, ct, :, NU // 8:NE // 8], xds[:, ct, :, 0:Tn // 8], R=['xds'], W=['xd_scr'], q='pool')
                        for ct in range(2):
                            proj(ps[ct % 2 + 4], 576 + ct * 128, 128, 'psA%d' % (ct % 2 + 4))
                            k.op('act', lambda ct=ct: A.copy(plo[:, ct, 0:Tn], ps[ct % 2 + 4][:, 0:Tn]), R=['psA%d' % (ct % 2 + 4)], W=['plo'])
                        si = 0 if kind == 'ctx' else 1
                        k.dma(pool_scr[si][:, 8 + t0:8 + t0 + Tn].rearrange("(c p) n -> p c n", p=128), plo[:, :, 0:Tn], R=['plo'], W=['pool_scr'])

                if 'A' in dbg and l == 0:
                    d1 = dbg_out('KT', (128, 4 * NU), BF16)
                    k.dma(d1, KT[:].rearrange("p a b -> p (a b)"), R=['KT%d' % i for i in range(17)], W=['dbg1'])
                    d2 = dbg_out('Vt', (128, (NU // 128) * 4 * 65), BF16)
                    k.dma(d2, Vt[:].rearrange("p a b c -> p (a b c)"), R=['Vt%d' % i for i in range(17)], W=['dbg2'])
                    d3 = dbg_out('Xd', (256, 8 * NSC), BF16)
                    k.dma(d3, xd_scr.rearrange("a b c -> a (b c)"), R=['xd_scr'], W=['dbg3'])
                    d4 = dbg_out('poolL', (256, SEQ + 16), BF16)
                    k.dma(d4, pool_scr[1], R=['pool_scr'], W=['dbg4'])
                    break

            k.barrier()
            with ExitStack() as sB:
                Bin8T = T(sB, 'Bin8T', (128, 32, 128), BF16)
                Cout8T = T(sB, 'Cout8T', (128, 32, 128), BF16)
                D8T = T(sB, 'D8T', (128, 32, 128), BF16)
                LPr = T(sB, 'LPr', (128, 32), F32)
                LPi = T(sB, 'LPi', (128, 32), F32)
                sgn = T(sB, 'sgn', (128, 1), F32)
                swap_f = T(sB, 'swap_f', (128, 128), F32)
                s5dT = T(sB, 's5dT', (128, 2), F32)
                bgluT = T(sB, 'bgluT', (128, 2), F32)
                wglu = T(sB, 'wglu', (128, 2, 256), BF16)
                k.dma(swap_f[:], cst['swapm'], W=['swap_f'])
                k.op('pool', lambda: G.memset(sgn[0:64, :], 1.0), W=['sgn'])
                k.op('pool', lambda: G.memset(sgn[64:128, :], -1.0), W=['sgn'])
                k.dma(s5dT[:], prm['s5_d'][l].rearrange("(c p) -> p c", p=128), W=['s5dT'], allow_slow_non_contiguous=True)
                k.dma(bgluT[:], prm['b_glu'][l].rearrange("(c p) -> p c", p=128), W=['bgluT'], allow_slow_non_contiguous=True)
                TK = ['s5t']
                with ExitStack() as sT:
                    def t32(name):
                        return T(sT, name, (128, 32), F32)
                    LR, LI, DT_, ar_, ai_ = t32('LR'), t32('LI'), t32('DT_'), t32('ar_'), t32('ai_')
                    t1, t2, t3, t4 = t32('t1'), t32('t2'), t32('t3'), t32('t4')
                    er, ei, kr_, ki_, den = t32('er'), t32('ei'), t32('kr_'), t32('ki_'), t32('den')
                    lnr, lni = t32('lnr'), t32('lni')
                    halfpi = T(sT, 'halfpi', (128, 1), F32)
                    PWr = T(sT, 'PWr', (128, 16, 32), F32)
                    PWi = T(sT, 'PWi', (128, 16, 32), F32)
                    BR = T(sT, 'BR', (128, 32, 16), F32)
                    BI = T(sT, 'BI', (128, 32, 16), F32)
                    CR = T(sT, 'CR', (128, 32, 16), F32)
                    CI = T(sT, 'CI', (128, 32, 16), F32)
                    bbr = T(sT, 'bbr', (128, 32, 16), F32)
                    bbi = T(sT, 'bbi', (128, 32, 16), F32)
                    u1 = T(sT, 'u1', (128, 16, 16), F32)
                    u2 = T(sT, 'u2', (128, 16, 16), F32)
                    CRn = T(sT, 'CRn', (128, 2, 64), F32)
                    Bin8 = T(sT, 'Bin8', (128, 32, 128), F32)
                    CoutN = T(sT, 'CoutN', (128, 32, 128), F32)
                    wg32 = T(sT, 'wg32', (128, 2, 256), F32)
                    mlow = T(sT, 'mlow', (128, 128), F32)
                    mup = T(sT, 'mup', (128, 128), F32)
                    psT = P(sT, 'psT', (128, 128))

                    def dv(fn):
                        k.op('dve', fn, R=TK, W=TK)

                    def ac(fn):
                        k.op('act', fn, R=TK, W=TK)

                    k.dma(mlow[:], cst['mlow'], W=TK)
                    k.dma(mup[:], cst['mup'], W=TK)
                    k.dma(wg32[:], prm['w_glu'][l].rearrange("(c p) n -> p c n", p=128), W=TK)
                    dv(lambda: V.tensor_copy(wglu[:], wg32[:]))
                    for hf in range(2):
                        sl = slice(hf * 64, hf * 64 + 64)
                        k.dma(LR[sl, :], prm['lam_re'][l].rearrange("d g p -> p (d g)"), W=TK, allow_slow_non_contiguous=True)
                        k.dma(LI[sl, :], prm['lam_im'][l].rearrange("d g p -> p (d g)"), W=TK, allow_slow_non_contiguous=True, q='pool')
                        k.dma(BR[sl], prm['s5_b_re'][l].rearrange("d g p h -> p (d g) h"), W=TK)
                        k.dma(BI[sl], prm['s5_b_im'][l].rearrange("d g p h -> p (d g) h"), W=TK, q='pool')
                    k.dma(DT_[:], prm['log_dt'][l].rearrange("d g -> (d g)").partition_broadcast(128), W=TK)
                    k.op('pool', lambda: G.memset(halfpi[:], float(np.pi / 2)), W=TK)
                    for ci, (cn, Ct) in enumerate((('s5_c_re', CR), ('s5_c_im', CI))):
                        crow = prm[cn][l].rearrange("d g h p -> (d g h) p")
                        for j in range(4):
                            k.dma(CRn[:, 0, :], crow[j * 128:(j + 1) * 128, :], W=TK)
                            k.dma(CRn[:, 1, :], crow[j * 128:(j + 1) * 128, :], W=TK, q='pool')
                            k.op('pe', lambda: PE.transpose(psT[:], CRn[:].rearrange("r a p -> r (a p)"), ident_f[:]), R=TK + ['ident_f'], W=['psT'])
                            k.op('dve', lambda j=j, Ct=Ct: V.tensor_copy(Ct[:, j * 8:(j + 1) * 8, :].rearrange("q a h -> q (a h)"), psT[:]), R=['psT'] + TK, W=TK)
                    ac(lambda: A.activation(DT_[:], DT_[:], AF.Exp))
                    dv(lambda: V.tensor_tensor(ar_[:], LR[:], DT_[:], op=ALU.mult))
                    dv(lambda: V.tensor_tensor(ai_[:], LI[:], DT_[:], op=ALU.mult))
                    ac(lambda: A.activation(t1[:], ar_[:], AF.Exp, scale=1.0 / 16))
                    ac(lambda: A.activation(t2[:], ai_[:], AF.Sin, scale=1.0 / 16, bias=halfpi[:, 0:1]))
                    ac(lambda: A.activation(t3[:], ai_[:], AF.Sin, scale=1.0 / 16))
                    dv(lambda: V.tensor_tensor(er[:], t1[:], t2[:], op=ALU.mult))
                    dv(lambda: V.tensor_tensor(ei[:], t1[:], t3[:], op=ALU.mult))

                    def cmul(or_, oi_, xr, xi, yr, yi, a1=None, a2=None, a3=None, a4=None):
                        a1, a2, a3, a4 = t1[:], t2[:], t3[:], t4[:]
                        dv(lambda: V.tensor_tensor(a1, xr, yr, op=ALU.mult))
                        dv(lambda: V.tensor_tensor(a2, xi, yi, op=ALU.mult))
                        dv(lambda: V.tensor_tensor(a3, a1, a2, op=ALU.subtract))
                        dv(lambda: V.tensor_tensor(a1, xr, yi, op=ALU.mult))
                        dv(lambda: V.tensor_tensor(a2, xi, yr, op=ALU.mult))
                        dv(lambda: V.tensor_tensor(a4, a1, a2, op=ALU.add))
                        dv(lambda: V.tensor_copy(or_, a3))
                        dv(lambda: V.tensor_copy(oi_, a4))

                    for _ in range(4):
                        cmul(er[:], ei[:], er[:], ei[:], er[:], ei[:], t1[:], t2[:], t3[:], t4[:])
                    dv(lambda: V.tensor_scalar_add(lnr[:], er[:], -1.0))
                    dv(lambda: V.tensor_tensor(t1[:], LR[:], LR[:], op=ALU.mult))
                    dv(lambda: V.tensor_tensor(t2[:], LI[:], LI[:], op=ALU.mult))
                    dv(lambda: V.tensor_tensor(den[:], t1[:], t2[:], op=ALU.add))
                    dv(lambda: V.reciprocal(den[:], den[:]))
                    dv(lambda: V.tensor_tensor(t1[:], lnr[:], LR[:], op=ALU.mult))
                    dv(lambda: V.tensor_tensor(t2[:], ei[:], LI[:], op=ALU.mult))
                    dv(lambda: V.tensor_tensor(t1[:], t1[:], t2[:], op=ALU.add))
                    dv(lambda: V.tensor_tensor(kr_[:], t1[:], den[:], op=ALU.mult))
                    dv(lambda: V.tensor_tensor(t1[:], ei[:], LR[:], op=ALU.mult))
                    dv(lambda: V.tensor_tensor(t2[:], lnr[:], LI[:], op=ALU.mult))
                    dv(lambda: V.tensor_tensor(t1[:], t1[:], t2[:], op=ALU.subtract))
                    dv(lambda: V.tensor_tensor(ki_[:], t1[:], den[:], op=ALU.mult))
                    bk = lambda a: a[:].unsqueeze(2).to_broadcast([128, 32, 16])
                    w1 = Bin8[:, :, 0:16]
                    w2 = Bin8[:, :, 16:32]
                    dv(lambda: V.tensor_tensor(w1, BR[:], bk(kr_), op=ALU.mult))
                    dv(lambda: V.tensor_tensor(w2, BI[:], bk(ki_), op=ALU.mult))
                    dv(lambda: V.tensor_tensor(bbr[:], w1, w2, op=ALU.subtract))
                    dv(lambda: V.tensor_tensor(w1, BI[:], bk(kr_), op=ALU.mult))
                    dv(lambda: V.tensor_tensor(w2, BR[:], bk(ki_), op=ALU.mult))
                    dv(lambda: V.tensor_tensor(bbi[:], w1, w2, op=ALU.add))
                    dv(lambda: V.memset(PWr[:, 7, :], 1.0))
                    dv(lambda: V.memset(PWi[:, 7, :], 0.0))
                    for e in range(1, 9):
                        cmul(PWr[:, 7 + e, :], PWi[:, 7 + e, :], PWr[:, 6 + e, :], PWi[:, 6 + e, :], er[:], ei[:])
                    ac(lambda: A.activation(den[:], ar_[:], AF.Exp, scale=-2.0))
                    dv(lambda: V.tensor_tensor(lnr[:], er[:], den[:], op=ALU.mult))
                    dv(lambda: V.tensor_tensor(lni[:], ei[:], den[:], op=ALU.mult))
                    dv(lambda: V.tensor_scalar_mul(lni[:], lni[:], -1.0))
                    for e in range(1, 8):
                        cmul(PWr[:, 7 - e, :], PWi[:, 7 - e, :], PWr[:, 8 - e, :], PWi[:, 8 - e, :], lnr[:], lni[:])
                    dv(lambda: V.tensor_copy(LPr[:], PWr[:, 15, :]))
                    dv(lambda: V.tensor_copy(LPi[:], PWi[:, 15, :]))

                    def ctab(dst, Xr, Xi, d, slot, e, im_sign):
                        gs = slice(d * 16, d * 16 + 8)
                        for hf in range(2):
                            ps_ = slice(hf * 64, hf * 64 + 64)
                            pr = PWr[ps_, 7 + e, gs].unsqueeze(2).to_broadcast([64, 8, 16])
                            pi = PWi[ps_, 7 + e, gs].unsqueeze(2).to_broadcast([64, 8, 16])
                            o = dst[ps_, gs, slot * 16:(slot + 1) * 16]
                            en_, E_ = ('dve', V) if hf == 0 else ('pool', G)
                            ck = ['ctab%d' % hf]

                            def cop(fn):
                                k.op(en_, fn, R=TK + ck, W=ck)
                            if hf == 0:
                                cop(lambda: E_.tensor_tensor(u1[ps_, 0:8], Xr[ps_, gs, :], pr, op=ALU.mult))
                                cop(lambda: E_.tensor_tensor(u2[ps_, 0:8], Xi[ps_, gs, :], pi, op=ALU.mult))
                                cop(lambda: E_.tensor_tensor(o, u1[ps_, 0:8], u2[ps_, 0:8], op=ALU.subtract))
                            else:
                                cop(lambda: E_.tensor_tensor(u1[ps_, 0:8], Xr[ps_, gs, :], pi, op=ALU.mult))
                                cop(lambda: E_.tensor_tensor(u2[ps_, 0:8], Xi[ps_, gs, :], pr, op=ALU.mult))
                                cop(lambda: E_.tensor_tensor(o, u1[ps_, 0:8], u2[ps_, 0:8], op=ALU.add))
                                if im_sign < 0:
                                    cop(lambda: E_.tensor_scalar_mul(o, o, -1.0))

                    for d in range(2):
                        for i in range(8):
                            ctab(Bin8, bbr, bbi, d, i, (7 - i) if d == 0 else i, +1)
                            ctab(Cout8T, CR, CI, d, i, (i + 1) if d == 0 else (8 - i), -1)
                            ctab(CoutN, CR, CI, d, i, (i - 7) if d == 0 else (-i), -1)
                    for dg in [d_ * 16 + g_ for d_ in range(2) for g_ in range(8)]:
                        k.op('pe', lambda dg=dg: PE.transpose(psT[:], Bin8[:, dg, :], ident_f[:]), R=TK + ['ident_f', 'ctab0', 'ctab1'], W=['psT'])
                        k.op('act', lambda dg=dg: A.copy(Bin8T[:, dg, :], psT[:]), R=['psT'], W=['Bin8T'])
                        k.op('pe', lambda dg=dg: PE.matmul(psT[:], lhsT=Bin8[:, dg, :], rhs=CoutN[:, dg, :], start=True, stop=True), R=TK + ['ctab0', 'ctab1'], W=['psT'])
                        mk = mlow if dg < 16 else mup
                        k.op('dve', lambda dg=dg, mk=mk: V.tensor_tensor(D8T[:, dg, :], psT[:], mk[:], op=ALU.mult), R=['psT'] + TK, W=['D8T'])

                    psd = P(sT, 'psd', (128, 128), BF16)
                    k.op('pe', lambda: PE.transpose(psd[:], ident_b[:], ident_b[:]), R=['ident_b', 'psT'], W=['psd'])

                if 'B0' in dbg and l == 0:
                    for nm, tl in (('Bin8T', Bin8T), ('Cout8T', Cout8T), ('D8T', D8T)):
                        d_ = dbg_out(nm, (128, 32 * 128), BF16)
                        k.dma(d_, tl[:].rearrange("p a b -> p (a b)"), R=[nm], W=['dbg' + nm])
                    d_ = dbg_out('LP', (128, 64))
                    k.dma(d_[:, 0:32], LPr[:], R=TK, W=['dbgLP'])
                    k.dma(d_[:, 32:64], LPi[:], R=TK, W=['dbgLP'])
                    break

                k.barrier()
                with ExitStack() as sS:
                    IM = T(sS, 'IM', (128, 2, NSC), BF16)
                    H = T(sS, 'H', (128, 4, NSC + 4), F32)
                    Hb = T(sS, 'Hb', (128, 4, NSC + 4), BF16)
                    Yo = T(sS, 'Yo', (128, 2, NSC), BF16)
                    ATb = [T(sS, 'AT%d' % i, (128, 11, 4, 128), BF16) for i in range(2)]
                    atmp = T(sS, 'atmp', (128, 128), F32)
                    v2 = T(sS, 'v2', (128, 32), F32)
                    LQr = T(sS, 'LQr', (128, 11, 32), F32)
                    LQi = T(sS, 'LQi', (128, 11, 32), F32)
                    q1, q2 = T(sS, 'q1', (128, 32), F32), T(sS, 'q2', (128, 32), F32)
                    psS = [[P(sS, 'psS%d_%d' % (i, j), (128, 512)) for j in range(4)] for i in range(2)]
                    GO = 2
                    NLEV = int(os.environ.get('DBG_NLEV', 11))
                    NRND = int(os.environ.get('DBG_NRND', 4))
                    DOREAD = int(os.environ.get('DBG_READ', 1))
                    k.op('dve', lambda: V.tensor_copy(LQr[:, 0, :], LPr[:]), R=TK, W=['LQ'])
                    k.op('dve', lambda: V.tensor_copy(LQi[:, 0, :], LPi[:]), R=TK, W=['LQ'])
                    for lev in range(1, NLEV):
                        a_r, a_i = LQr[:, lev - 1, :], LQi[:, lev - 1, :]
                        k.op('dve', lambda: V.tensor_tensor(q1[:], a_r, a_r, op=ALU.mult), R=['LQ'], W=['q1'])
                        k.op('dve', lambda: V.tensor_tensor(q2[:], a_i, a_i, op=ALU.mult), R=['LQ'], W=['q2'])
                        k.op('dve', lambda lev=lev: V.tensor_tensor(LQr[:, lev, :], q1[:], q2[:], op=ALU.subtract), R=['q1', 'q2'], W=['LQ'])
                        k.op('dve', lambda: V.tensor_tensor(q1[:], a_r, a_i, op=ALU.mult), R=['LQ'], W=['q1'])
                        k.op('dve', lambda lev=lev: V.tensor_scalar_mul(LQi[:, lev, :], q1[:], 2.0), R=['q1'], W=['LQ'])
                    if int(os.environ.get('DBG_MS', 1)):
                        k.op('dve', lambda: V.memset(H[:], 0.0), W=['H%d' % j for j in range(4)])
                        k.op('pool', lambda: G.memset(Hb[:], 0.0), W=['Hb%d' % j for j in range(4)])
                    xdg = xd_scr.rearrange("(g h) i n -> g h i n", h=16)
                    ydg = yd_scr.rearrange("(g h) i n -> g h i n", h=16)
                    CB = [(0, 512), (512, 1024), (1024, NSC)]
                    for rnd in range(NRND):
                        gl_ = [2 * rnd, 2 * rnd + 1]
                        combos = [(d, g) for g in gl_ for d in range(2)]
                        if int(os.environ.get('DBG_IMZ', 0)):
                            k.op('pool', lambda: G.memset(IM[:], 0.5), W=['IM'])
                        for gi, g in enumerate(gl_ if int(os.environ.get('DBG_IM', 1)) else []):
                            for i in range(8):
                                k.dma(IM[16 * i:16 * i + 16, gi, :], xdg[g, :, i, :], R=['xd_scr'], W=['IM'], q=('sp' if i % 2 == 0 else 'pool'))
                        ATc, ATk = ATb[rnd % 2], 'AT%d' % (rnd % 2)
                        for lev in range(NLEV):
                            k.op('dve', lambda lev=lev: V.tensor_scalar_mul(v2[:], LQi[:, lev, :], sgn[:, 0:1]), R=['LQ', 'sgn'], W=['v2'])
                            for j, (d, g) in enumerate(combos):
                                dg = d * 16 + g
                                k.op('dve', lambda lev=lev, dg=dg: V.tensor_scalar_mul(atmp[:], ident_f[:], LQr[:, lev, dg:dg + 1]), R=['LQ', 'ident_f'], W=['atmp'])
                                k.op('dve', lambda lev=lev, dg=dg, j=j: V.scalar_tensor_tensor(ATc[:, lev, j, :], swap_f[:], v2[:, dg:dg + 1], atmp[:],
                                                                                              op0=ALU.mult, op1=ALU.add), R=['swap_f', 'v2', 'atmp'], W=[ATk])
                        for bi_, (c0, c1) in enumerate(CB):
                            pss = psS[bi_ % 2]
                            for j, (d, g) in enumerate(combos):
                                pk = 'psS%d_%d' % (bi_ % 2, j)
                                k.op('pe', lambda j=j, d=d, g=g: PE.matmul(pss[j][:, 0:c1 - c0], lhsT=Bin8T[:, d * 16 + g, :], rhs=IM[:, j // 2, c0:c1],
                                                                            start=True, stop=True), R=['Bin8T', 'IM'], W=[pk])
                                k.op('dve', lambda j=j: V.tensor_copy(H[:, j, GO + c0:GO + c1], pss[j][:, 0:c1 - c0]), R=[pk], W=['H%d' % j])
                                k.op('act', lambda j=j: A.copy(Hb[:, j, GO + c0:GO + c1], H[:, j, GO + c0:GO + c1]), R=['H%d' % j], W=['Hb%d' % j])
                        for lev in range(NLEV):
                            s_ = 1 << lev
                            nb = (NSC - s_ + 511) // 512
                            for bi_ in range(nb):
                                pss = psS[bi_ % 2]
                                hi_f = NSC - bi_ * 512
                                lo_f = max(s_, hi_f - 512)
                                lo_b = bi_ * 512
                                hi_b = min(NSC - s_, lo_b + 512)
                                for j, (d, g) in enumerate(combos):
                                    pk = 'psS%d_%d' % (bi_ % 2, j)
                                    lo_, hi_ = (lo_f, hi_f) if d == 0 else (lo_b, hi_b)
                                    sh_ = -s_ if d == 0 else s_
                                    k.op('pe', lambda j=j: PE.matmul(pss[j][:, 0:hi_ - lo_], lhsT=ATc[:, lev, j, :], rhs=Hb[:, j, GO + lo_ + sh_:GO + hi_ + sh_],
                                                                    start=True, stop=True), R=[ATk, 'Hb%d' % j], W=[pk])
                                    k.op('dve', lambda j=j, lo_=lo_, hi_=hi_: V.tensor_tensor(H[:, j, GO + lo_:GO + hi_], H[:, j, GO + lo_:GO + hi_], pss[j][:, 0:hi_ - lo_], op=ALU.add),
                                         R=[pk, 'H%d' % j], W=['H%d' % j])
                                    if j % 2 == 0:
                                        k.op('act', lambda j=j, lo_=lo_, hi_=hi_: A.copy(Hb[:, j, GO + lo_:GO + hi_], H[:, j, GO + lo_:GO + hi_]), R=['H%d' % j], W=['Hb%d' % j])
                                    else:
                                        k.op('pool', lambda j=j, lo_=lo_, hi_=hi_: G.tensor_copy(Hb[:, j, GO + lo_:GO + hi_], H[:, j, GO + lo_:GO + hi_]), R=['H%d' % j], W=['Hb%d' % j])
                        RB = [(0, 32, (0,)), (32, 544, (0, 1)), (544, 1056, (0, 1)), (1056, NSC, (1,))]
                        for bi_, (c0, c1, dirs) in enumerate(RB):
                            pss = psS[bi_ % 2]
                            for gi, g in enumerate(gl_):
                                pk = 'psS%d_%d' % (bi_ % 2, gi)
                                nmm = 2 * len(dirs)
                                mi = 0
                                for d in dirs:
                                    j = 2 * gi + d
                                    sh = -1 if d == 0 else 1
                                    k.op('pe', lambda gi=gi, d=d, g=g, mi=mi: PE.matmul(pss[gi][:, 0:c1 - c0], lhsT=D8T[:, d * 16 + g, :], rhs=IM[:, gi, c0:c1],
                                                                                    start=(mi == 0), stop=False), R=['D8T', 'IM'], W=[pk])
                                    mi += 1
                                    k.op('pe', lambda gi=gi, d=d, g=g, j=j, sh=sh, mi=mi: PE.matmul(pss[gi][:, 0:c1 - c0], lhsT=Cout8T[:, d * 16 + g, :],
                                                                                                rhs=Hb[:, j, GO + c0 + sh:GO + c1 + sh],
                                                                                                start=False, stop=(mi == nmm - 1)), R=['Cout8T', 'Hb%d' % j], W=[pk])
                                    mi += 1
                                k.op('act', lambda gi=gi: A.copy(Yo[:, gi, c0:c1], pss[gi][:, 0:c1 - c0]), R=[pk], W=['Yo'])
                        for gi, g in enumerate(gl_ if int(os.environ.get('DBG_YD', 1)) else []):
                            for i in range(8):
                                k.dma(ydg[g, :, i, :], Yo[16 * i:16 * i + 16, gi, :], R=['Yo'], W=['yd_scr'], q=('sp' if i % 2 == 0 else 'pool'))

                if nlayers > 1 or int(os.environ.get('DBG_CC', 0)):
                    k.barrier()
                    k._sync('pool', ['yd_scr'], ['ydg'])
                    ydf = yd_scr.rearrange("c i n -> c (i n)")
                    for c_ in range(2):
                        G.collective_compute("AllGather", ALU.bypass, replica_groups=[[0, 1], [2, 3], [4, 5], [6, 7]],
                                             ins=[ydf[c_ * 64:(c_ + 1) * 64, :]], outs=[ydg_scr[c_]]).then_inc(ccsem)
                    ccn[0] += 2
                    for eng in k.eng.values():
                        eng.wait_ge(ccsem, ccn[0])
                    with ExitStack() as sX:
                        ya = T(sX, 'ya', (128, 8 * NSC), BF16)
                        yb = T(sX, 'yb', (128, 8 * NSC), BF16)
                        for c_ in range(2):
                            k.dma(ya[c_ * 64:(c_ + 1) * 64, :], ydg_scr[c_][64:128, :], W=['ya'], q=('sp' if c_ == 0 else 'pool'))
                            k.dma(yb[c_ * 64:(c_ + 1) * 64, :], ydg_scr[c_][0:64, :], W=['yb'], q=('sp' if c_ == 0 else 'pool'))
                        k.op('dve', lambda: V.tensor_scalar_mul(ya[:], ya[:], msb[:, 0:1]), R=['ya', 'msb'], W=['ya'])
                        k.op('dve', lambda: V.scalar_tensor_tensor(ya[:], yb[:], msb[:, 1:2], ya[:], op0=ALU.mult, op1=ALU.add), R=['ya', 'yb', 'msb'], W=['ya'])
                        k.dma(ydf[128:256, :], ya[:], R=['ya'], W=['yd_scr'])
                    k.barrier()

                if 'B1' in dbg and l == 0:
                    d_ = dbg_out('yd', (256, 8 * NSC), BF16)
                    if int(os.environ.get('DBG_YDD', 1)):
                        k.dma(d_, yd_scr.rearrange("a b c -> a (b c)"), R=['yd_scr'], W=['dbgyd'])
                    break

                k.barrier()
                with ExitStack() as sG:
                    xdt = T(sG, 'xdt', (128, 2, 8, 64), BF16)
                    ydt = T(sG, 'ydt', (128, 2, 8, 64), BF16)
                    yd2 = T(sG, 'yd2', (128, 2, 8, 64), BF16)
                    yf = T(sG, 'yf', (128, 2, 512), F32)
                    g1 = T(sG, 'g1', (128, 2, 512), F32)
                    g2 = T(sG, 'g2', (128, 2, 512), F32)
                    glb = T(sG, 'glb', (128, 2, 512), BF16)
                    sgm = T(sG, 'sgm', (128, 2, 512), F32)
                    s5ob = T(sG, 's5ob', (128, 2, 512), BF16)
                    psG = [P(sG, 'psG%d' % i, (128, 512)) for i in range(2)]
                    xdv = xd_scr.rearrange("(c p) i n -> p c i n", p=128)
                    ydv = yd_scr.rearrange("(c p) i n -> p c i n", p=128)
                    xdtB = T(sG, 'xdtB', (128, 2, 8, 64), BF16)
                    ydtB = T(sG, 'ydtB', (128, 2, 8, 64), BF16)
                    blocks = [('ctx', 0, NCTX)] + [('lat', 512 * i, 512) for i in range(HALF // 512)]
                    for (kind, t0, Tn) in blocks:
                        u0 = t0 if kind == 'ctx' else NCTX + t0
                        n8 = Tn // 8
                        for ct in range(2):
                            k.dma(xdt[:, ct, :, 0:n8], xdv[:, ct, :, u0 // 8:u0 // 8 + n8], R=['xd_scr'], W=['xdt'])
                            k.dma(ydt[:, ct, :, 0:n8], ydv[:, ct, :, u0 // 8:u0 // 8 + n8], R=['yd_scr'], W=['ydt'], q='pool')
                            if kind == 'lat':
                                uB = u0 + HALF
                                k.dma(xdtB[:, ct, :, 0:n8], xdv[:, ct, :, uB // 8:uB // 8 + n8], R=['xd_scr'], W=['xdtB'], q='pool')
                                k.dma(ydtB[:, ct, :, 0:n8], ydv[:, ct, :, uB // 8:uB // 8 + n8], R=['yd_scr'], W=['ydtB'])
                            if kind == 'ctx':
                                k.dma(yd2[:, ct, :, 0:n8], ydv[:, ct, :, NU // 8:NU // 8 + n8], R=['yd_scr'], W=['yd2'])
                        if kind == 'ctx':
                            k.op('pool', lambda: G.tensor_tensor(ydt[:, :, :, 0:n8], ydt[:, :, :, 0:n8], yd2[:, :, :, 0:n8], op=ALU.add), R=['ydt', 'yd2'], W=['ydt'])
                        else:
                            for (ta_, tb_, ka_, kb_) in ((xdt, xdtB, 'xdt', 'xdtB'), (ydt, ydtB, 'ydt', 'ydtB')):
                                k.op('dve', lambda ta_=ta_: V.tensor_scalar_mul(ta_[:], ta_[:], msb[:, 0:1]), R=[ka_, 'msb'], W=[ka_])
                                k.op('dve', lambda ta_=ta_, tb_=tb_: V.scalar_tensor_tensor(ta_[:], tb_[:], msb[:, 1:2], ta_[:], op0=ALU.mult, op1=ALU.add), R=[ka_, kb_, 'msb'], W=[ka_])
                        for ct in range(2):
                            k.op('dve', lambda ct=ct: V.scalar_tensor_tensor(yf[:, ct, 0:Tn].rearrange("p (c i) -> p i c", i=8), xdt[:, ct, :, 0:n8], s5dT[:, ct:ct + 1],
                                                                             ydt[:, ct, :, 0:n8], op0=ALU.mult, op1=ALU.add), R=['xdt', 'ydt', 's5dT'], W=['yf'])
                        if 'B2y' in dbg and l == 0:
                            if kind == 'ctx':
                                dyl = dbg_out('yl', (256, NU))
                            k.dma(dyl[:, u0:u0 + Tn].rearrange("(c p) n -> p c n", p=128), yf[:, :, 0:Tn], R=['yf'], W=['dbgyl'])
                        k.op('pool', lambda: G.tensor_tensor(g1[:, :, 0:Tn], yf[:, :, 0:Tn], yf[:, :, 0:Tn], op=ALU.mult), R=['yf'], W=['g1'])
                        k.op('dve', lambda: V.tensor_scalar(g1[:, :, 0:Tn], g1[:, :, 0:Tn], 0.044715, 1.0, op0=ALU.mult, op1=ALU.add), R=['g1'], W=['g1'])
                        k.op('pool', lambda: G.tensor_tensor(g2[:, :, 0:Tn], g1[:, :, 0:Tn], yf[:, :, 0:Tn], op=ALU.mult), R=['g1', 'yf'], W=['g2'])
                        k.op('act', lambda: A.activation(g2[:, :, 0:Tn], g2[:, :, 0:Tn], AF.Sigmoid, scale=1.5957691216057308), R=['g2'], W=['g2'])
                        k.op('dve', lambda: V.tensor_tensor(glb[:, :, 0:Tn], yf[:, :, 0:Tn], g2[:, :, 0:Tn], op=ALU.mult), R=['yf', 'g2'], W=['glb'])
                        for m in range(2):
                            for kt in range(2):
                                k.op('pe', lambda m=m, kt=kt: PE.matmul(psG[m][:, 0:Tn], lhsT=wglu[:, kt, m * 128:(m + 1) * 128], rhs=glb[:, kt, 0:Tn],
                                                                        start=(kt == 0), stop=(kt == 1)), R=['wglu', 'glb'], W=['psG%d' % m])
                            k.op('act', lambda m=m: A.activation(sgm[:, m, 0:Tn], psG[m][:, 0:Tn], AF.Sigmoid, bias=bgluT[:, m:m + 1]), R=['psG%d' % m, 'bgluT'], W=['sgm'])
                        k.op('dve', lambda: V.tensor_tensor(s5ob[:, :, 0:Tn], glb[:, :, 0:Tn], sgm[:, :, 0:Tn], op=ALU.mult), R=['glb', 'sgm'], W=['s5ob'])
                        k.dma(s5o_scr[:, u0:u0 + Tn].rearrange("(c p) n -> p c n", p=128), s5ob[:, :, 0:Tn], R=['s5ob'], W=['s5o_scr'])

                if 'B2' in dbg and l == 0:
                    d_ = dbg_out('s5o', (256, NU), BF16)
                    k.dma(d_, s5o_scr, R=['s5o_scr'], W=['dbgs5o'])
                    break

            k.barrier()
            with ExitStack() as sC:
                wC = T(sC, 'wC', (128, 8, 1728), BF16)
                wout = T(sC, 'wout', (128, 8, 1024), BF16)
                wqh = T(sC, 'wqh', (128, 2, 4, 96), BF16)
                wqr = T(sC, 'wqr', (128, 2, 4, 96), BF16)
                wpl = T(sC, 'wpl', (128, 2, 128), BF16)
                wsT = T(sC, 'wsT', (128, 4, 128), BF16)
                bsb = T(sC, 'bsb', (128, 4, 128), F32)
                gate_bc = T(sC, 'gate_bc', (128, D), F32)
                lng_bc = T(sC, 'lng_bc', (128, D), F32)
                lnb_bc = T(sC, 'lnb_bc', (128, D), F32)
                colv = T(sC, 'colv', (128, 8), F32)
                winv = T(sC, 'winv', (128, 2), F32)
                with ExitStack() as sw:
                    wst2 = [T(sw, 'wstC%d' % i, (128, 8, 256), F32) for i in range(2)]
                    wq32 = T(sw, 'wq32', (128, 2, 384), F32)
                    wp32 = T(sw, 'wp32', (128, 2, 128), F32)
                    ws32 = T(sw, 'ws32', (128, 128), F32)
                    psw = P(sw, 'psw', (128, 128))
                    win = prm['w_in'][l].rearrange("(c p) n -> p c n", p=128)
                    segs = [(416, 192, 0), (864, 256, 192), (1120, 256, 448), (1376, 256, 704), (1632, 256, 960), (1888, 256, 1216), (2144, 256, 1472)]
                    for si_, (c0, n, d0) in enumerate(segs):
                        wst, wk = wst2[si_ % 2], 'wstC%d' % (si_ % 2)
                        k.dma(wst[:, :, 0:n], win[:, :, c0:c0 + n], W=[wk], q=('sp' if si_ % 2 == 0 else 'pool'))
                        k.op('dve' if si_ % 2 == 0 else 'pool', lambda n=n, d0=d0, wst=wst, si_=si_: (V if si_ % 2 == 0 else G).tensor_copy(wC[:, :, d0:d0 + n], wst[:, :, 0:n]), R=[wk], W=['wC'])
                    wo = prm['w_out'][l].rearrange("(c p) n -> p c n", p=128)
                    for j in range(4):
                        wst, wk = wst2[(j + 1) % 2], 'wstC%d' % ((j + 1) % 2)
                        k.dma(wst[:], wo[:, :, j * 256:(j + 1) * 256], W=[wk], q=('sp' if j % 2 == 0 else 'pool'))
                        k.op('dve' if j % 2 == 0 else 'pool', lambda j=j, wst=wst: (V if j % 2 == 0 else G).tensor_copy(wout[:, :, j * 256:(j + 1) * 256], wst[:]), R=[wk], W=['wout'])
                    for ci, nm in enumerate(('sgu_g', 'sgu_b', 'pool_scale')):
                        k.dma(colv[:, 2 * ci:2 * ci + 2], prm[nm][l].rearrange("(c p) -> p c", p=128), W=['colv'], allow_slow_non_contiguous=True)
                    k.dma(colv[:, 6:7], prm['g_q'][l][0:128].rearrange("(p o) -> p o", o=1), W=['colv'])
                    k.dma(colv[0:64, 7:8], prm['g_q'][l][128:192].rearrange("(p o) -> p o", o=1), W=['colv'])
                    k.op('pool', lambda: G.memset(winv[0:64, 0:1], 1.0 / 2), W=['winv'])
                    k.op('pool', lambda: G.memset(winv[64:128, 0:1], 1.0 / 4), W=['winv'])
                    k.op('pool', lambda: G.memset(winv[0:64, 1:2], 1.0 / 8), W=['winv'])
                    k.op('pool', lambda: G.memset(winv[64:128, 1:2], 1.0 / 16), W=['winv'])
                    k.dma(bsb[:].rearrange("p h t -> p (h t)"), prm['b_s'][l].rearrange("h t -> (h t)").partition_broadcast(128), W=['bsb'])
                    k.dma(lng_bc[:], prm['ln_g'][l].partition_broadcast(128), W=['lng_bc'])
                    k.dma(lnb_bc[:], prm['ln_b'][l].partition_broadcast(128), W=['lnb_bc'], q='pool')
                    k.op('pool', lambda: G.memset(wq32[:], 0.0), W=['wq32'])
                    k.dma(wq32[:, 0, :], prm['w_uq'][l][0:128, :], W=['wq32'])
                    k.dma(wq32[0:64, 1, :], prm['w_uq'][l][128:192, :], W=['wq32'])
                    k.op('pool', lambda: G.memset(wqr[:], 0.0), W=['wqr'])
                    for kt in range(2):
                        rows = slice(0, 128) if kt == 0 else slice(0, 64)
                        k.op('dve', lambda kt=kt, rows=rows: V.tensor_scalar_mul(wq32[rows, kt, :], wq32[rows, kt, :], colv[rows, 6 + kt:7 + kt]), R=['wq32', 'colv'], W=['wq32'])
                    k.op('dve', lambda: V.tensor_copy(wqh[:].rearrange("p a h c -> p a (h c)"), wq32[:]), R=['wq32'], W=['wqh'])
                    wq4 = wq32[:].rearrange("p a (h c) -> p a h c", h=4)
                    for (d0, s0_) in ((0, 8), (8, 0), (16, 24), (24, 16)):
                        k.op('dve', lambda d0=d0, s0_=s0_: V.tensor_copy(wqr[:, :, :, 64 + d0:72 + d0], wq4[:, :, :, 64 + s0_:72 + s0_]), R=['wq32'], W=['wqr'])
                    k.op('pool', lambda: G.memset(wp32[:], 0.0), W=['wp32'])
                    for g in range(4):
                        r0 = (g % 2) * 64
                        k.dma(wp32[r0:r0 + 64, g // 2, r0:r0 + 64], prm['w_pool'][l][g], W=['wp32'])
                    k.op('dve', lambda: V.tensor_copy(wpl[:], wp32[:]), R=['wp32'], W=['wpl'])
                    for h in range(4):
                        k.dma(ws32[:], prm['w_s'][l][h], W=['ws32'])
                        k.op('pe', lambda: PE.transpose(psw[:], ws32[:], ident_f[:]), R=['ws32', 'ident_f'], W=['psw'])
                        k.op('act', lambda h=h: A.copy(wsT[:, h, :], psw[:]), R=['psw'], W=['wsT'])
                    psd2 = P(sw, 'psd2', (128, 128), BF16)
                    k.op('pe', lambda: PE.transpose(psd2[:], ident_b[:], ident_b[:]), R=['ident_b', 'psw'], W=['psd2'])
                k.barrier()

                with ExitStack() as sc:
                    xt = [T(sc, 'xtC0', (128, D), F32)] * 2
                    st6 = T(sc, 'st6C', (128, 2, 6), F32)
                    mv = T(sc, 'mvC', (128, 2), F32)
                    rstd = T(sc, 'rstdC', (128, 1), F32)
                    xn = T(sc, 'xnC', (128, D), BF16)
                    hT = T(sc, 'hTC', (128, 8, QB), BF16)
                    sq = T(sc, 'sqC', (128, 2, QB), BF16)
                    rms = T(sc, 'rmsC', (128, QB), F32)
                    QT = T(sc, 'QT', (128, 4, QB), BF16)
                    rc_t = T(sc, 'rc_tC', (128, QB), F32)
                    rs_t = T(sc, 'rs_tC', (128, QB), F32)
                    su = T(sc, 'su', (128, 2, QB), BF16)
                    svb = T(sc, 'svb', (128, 2, QB), BF16)
                    cqn = svb
                    mean = T(sc, 'mean', (128, QB), F32)
                    var = T(sc, 'var', (128, QB), F32)
                    q1t, q2t = mean, var
                    vnb = T(sc, 'vnb', (128, 2, QB), BF16)
                    vc = T(sc, 'vc', (128, 256), BF16)
                    mx = T(sc, 'mx', (128, 128), F32)
                    sg = T(sc, 'sg', (128, 8, QB), BF16)
                    PT = [T(sc, 'PT%d' % i, (128, 2 * QB), BF16) for i in range(3)]
                    osb = T(sc, 'osb', (128, 4, 65), F32)
                    rec = T(sc, 'rec', (128, 4, 1), F32)
                    att = T(sc, 'att', (128, 4, 256), BF16)
                    catg = sg
                    pin = T(sc, 'pin', (128, 2, QB + 16), BF16)
                    pa = T(sc, 'pa', (128, 2, QB + 16), F32)
                    svf = pa
                    pb = T(sc, 'pb', (128, 2, QB + 16), F32)
                    pld = T(sc, 'pld', (128, 2, QB), BF16)
                    s5t = T(sc, 's5t', (128, 2, QB), BF16)
                    rr = T(sc, 'rr', (128, D), F32)
                    hf32 = rr[:].rearrange("p (a b) -> p a b", a=8)
                    pt = P(sc, 'ptC', (128, 8, 128), BF16)
                    psC = [P(sc, 'psC%d' % i, (128, 512)) for i in range(4)]
                    psS_ = [P(sc, 'psSc%d' % i, (128, 512)) for i in range(2)]
                    psO = P(sc, 'psO', (128, 4, 65))
                    sm2c = [T(sc, 'sm2c_%d' % i, (128, 16), F32) for i in range(2)]
                    ptc2 = psC[3][:].bitcast(BF16).rearrange("p (a b) -> p a b", a=8)
                    LNC = {'xt': [(xt[0], 'xtC0'), (rr, 'rr')], 'xn': [(xn, 'xn'), (xn, 'xn')],
                           'pt': [(pt, 'ptC'), (ptc2, 'psC3')], 'sm': [(sm2c[0], 'sm2c_0'), (sm2c[1], 'sm2c_1')], 'hTk': 'hT', 'dve_mod': True}
                    assert QB <= 512 and NCTX <= QB
                    SCALE = 96 ** -0.5

                    qblocks = [('lat', QB * i, QB) for i in range(HALF // QB)]
                    if not last:
                        qblocks = [('ctx', 0, NCTX)] + qblocks
                    NQB = int(os.environ.get('DBG_NQB', len(qblocks)))
                    xcnt = 0

                    def do_ln(bi_):
                        kind_, t0_, Tn_ = qblocks[bi_]
                        src_ = cin if kind_ == 'ctx' else (xq_in if l == 0 else x1h_scr)
                        ln_block([src_[t0_ + sb_ * 128: t0_ + (sb_ + 1) * 128, :] for sb_ in range(Tn_ // 128)], ['x1w'],
                                 1 if kind_ == 'ctx' else 0, hT, LNC)

                    gcol = [None]
                    do_ln(0)
                    for bidx, (kind, t0, Tn) in enumerate(qblocks[:NQB]):
                        col = 1 if kind == 'ctx' else 0
                        src = cin if kind == 'ctx' else (xq_in if l == 0 else x1h_scr)
                        dst = (ctx1_scr if kind == 'ctx' else (out_ap if last else x1h_scr))
                        u0 = t0 if kind == 'ctx' else NCTX + t0
                        nsub = Tn // 128
                        nkt = (NCTX // 128) if kind == 'ctx' else (NU // 128)
                        if gcol[0] != col:
                            k.dma(gate_bc[:], gate_scr[l, col].partition_broadcast(128), R=['gate_scr'], W=['gate_bc'])
                            gcol[0] = col

                        def proj(pst, key, c0, ncol):
                            for dc in range(8):
                                k.op('pe', lambda dc=dc: PE.matmul(pst[0:ncol, 0:Tn], lhsT=wC[:, dc, c0:c0 + ncol], rhs=hT[:, dc, 0:Tn],
                                                                  start=(dc == 0), stop=(dc == 7)), R=['wC', 'hT'], W=[key])
                        proj(psC[0], 'psC0', 0, 128)
                        proj(psC[1], 'psC1', 128, 64)
                        k.op('act', lambda: A.activation(sq[:, 0, 0:Tn], psC[0][:, 0:Tn], AF.Square), R=['psC0'], W=['sq'])
                        k.op('act', lambda: A.activation(sq[0:64, 1, 0:Tn], psC[1][0:64, 0:Tn], AF.Square), R=['psC1'], W=['sq'])
                        k.op('pe', lambda: PE.matmul(psC[2][:, 0:Tn], lhsT=ones_b[:, :], rhs=sq[:, 0, 0:Tn], start=True, stop=False), R=['ones_b', 'sq'], W=['psC2'])
                        k.op('pe', lambda: PE.matmul(psC[2][:, 0:Tn], lhsT=ones_b[0:64, :], rhs=sq[0:64, 1, 0:Tn], start=False, stop=True), R=['ones_b', 'sq'], W=['psC2'])
                        k.op('act', lambda: A.activation(rms[:, 0:Tn], psC[2][:, 0:Tn], AF.Sqrt, bias=epsb[:, 0:1], scale=1.0 / 192), R=['psC2', 'epsb'], W=['rms'])
                        k.op('dve', lambda: V.reciprocal(rms[:, 0:Tn], rms[:, 0:Tn]), R=['rms'], W=['rms'])
                        k.op('dve', lambda: V.tensor_tensor(cqn[:, 0, 0:Tn], psC[0][:, 0:Tn], rms[:, 0:Tn], op=ALU.mult), R=['psC0', 'rms'], W=['svb'])
                        k.op('dve', lambda: V.tensor_tensor(cqn[0:64, 1, 0:Tn], psC[1][0:64, 0:Tn], rms[0:64, 0:Tn], op=ALU.mult), R=['psC1', 'rms'], W=['svb'])
                        if kind == 'lat':
                            k.dma(rc_t[64:96, :], ropecq[:, t0:t0 + Tn], W=['rc_t'])
                            k.dma(rs_t[64:96, :], ropesq[:, t0:t0 + Tn], W=['rs_t'], q='pool')
                        for h in range(4):
                            pq, pqk = psC[h % 2], 'psC%d' % (h % 2)
                            pr, prk = psC[2 + h % 2], 'psC%d' % (2 + h % 2)
                            k.op('pe', lambda h=h: PE.matmul(pq[0:96, 0:Tn], lhsT=wqh[:, 0, h, :], rhs=cqn[:, 0, 0:Tn], start=True, stop=False), R=['wqh', 'svb'], W=[pqk])
                            k.op('pe', lambda h=h: PE.matmul(pq[0:96, 0:Tn], lhsT=wqh[0:64, 1, h, :], rhs=cqn[0:64, 1, 0:Tn], start=False, stop=True), R=['wqh', 'svb'], W=[pqk])
                            k.op('act', lambda h=h: A.copy(QT[0:64, h, 0:Tn], pq[0:64, 0:Tn]), R=[pqk], W=['QT'])
                            if kind == 'lat':
                                k.op('pe', lambda h=h: PE.matmul(pr[0:96, 0:Tn], lhsT=wqr[:, 0, h, :], rhs=cqn[:, 0, 0:Tn], start=True, stop=False), R=['wqr', 'svb'], W=[prk])
                                k.op('pe', lambda h=h: PE.matmul(pr[0:96, 0:Tn], lhsT=wqr[0:64, 1, h, :], rhs=cqn[0:64, 1, 0:Tn], start=False, stop=True), R=['wqr', 'svb'], W=[prk])
                                k.op('dve', lambda: V.tensor_tensor(q1t[64:96, 0:Tn], pq[64:96, 0:Tn], rc_t[64:96, 0:Tn], op=ALU.mult), R=[pqk, 'rc_t'], W=['mean'])
                                k.op('dve', lambda: V.tensor_tensor(q2t[64:96, 0:Tn], pr[64:96, 0:Tn], rs_t[64:96, 0:Tn], op=ALU.mult), R=[prk, 'rs_t'], W=['var'])
                                k.op('pool', lambda h=h: G.tensor_tensor(QT[64:96, h, 0:Tn], q1t[64:96, 0:Tn], q2t[64:96, 0:Tn], op=ALU.add), R=['mean', 'var'], W=['QT'])
                            else:
                                k.op('act', lambda h=h: A.copy(QT[64:96, h, 0:Tn], pq[64:96, 0:Tn]), R=[pqk], W=['QT'])
                        for ct in range(2):
                            proj(psC[ct], 'psC%d' % ct, 192 + ct * 128, 128)
                            k.op('act', lambda ct=ct: A.copy(su[:, ct, 0:Tn], psC[ct][:, 0:Tn]), R=['psC%d' % ct], W=['su'])
                        for ct in range(2):
                            proj(psC[ct], 'psC%d' % ct, 448 + ct * 128, 128)
                            k.op('act', lambda ct=ct: A.copy(svf[:, ct, 0:Tn], psC[ct][:, 0:Tn]), R=['psC%d' % ct], W=['pa'])
                            k.op('dve', lambda ct=ct: V.tensor_copy(svb[:, ct, 0:Tn], svf[:, ct, 0:Tn]), R=['pa'], W=['svb'])
                            k.op('pool', lambda ct=ct: G.tensor_tensor(sq[:, ct, 0:Tn], svf[:, ct, 0:Tn], svf[:, ct, 0:Tn], op=ALU.mult), R=['pa'], W=['sq'])
                        for ct in range(2):
                            k.op('pe', lambda ct=ct: PE.matmul(psC[2][:, 0:Tn], lhsT=ones_b[:], rhs=svb[:, ct, 0:Tn], start=(ct == 0), stop=(ct == 1)), R=['ones_b', 'svb'], W=['psC2'])
                        for ct in range(2):
                            k.op('pe', lambda ct=ct: PE.matmul(psC[3][:, 0:Tn], lhsT=ones_b[:], rhs=sq[:, ct, 0:Tn], start=(ct == 0), stop=(ct == 1)), R=['ones_b', 'sq'], W=['psC3'])
                        k.op('act', lambda: A.activation(mean[:, 0:Tn], psC[2][:, 0:Tn], AF.Copy, scale=1.0 / 256), R=['psC2'], W=['mean'])
                        k.op('dve', lambda: V.tensor_tensor(var[:, 0:Tn], mean[:, 0:Tn], mean[:, 0:Tn], op=ALU.mult), R=['mean'], W=['var'])
                        k.op('dve', lambda: V.scalar_tensor_tensor(var[:, 0:Tn], psC[3][:, 0:Tn], 1.0 / 256, var[:, 0:Tn], op0=ALU.mult, op1=ALU.subtract), R=['psC3', 'var'], W=['var'])
                        k.op('act', lambda: A.activation(var[:, 0:Tn], var[:, 0:Tn], AF.Sqrt, bias=epsb[:, 0:1]), R=['var', 'epsb'], W=['var'])
                        k.op('dve', lambda: V.reciprocal(var[:, 0:Tn], var[:, 0:Tn]), R=['var'], W=['var'])
                        for ct in range(2):
                            k.op('dve', lambda ct=ct: V.tensor_tensor(svf[:, ct, 0:Tn], svf[:, ct, 0:Tn], mean[:, 0:Tn], op=ALU.subtract), R=['pa', 'mean'], W=['pa'])
                            k.op('pool', lambda ct=ct: G.tensor_tensor(svf[:, ct, 0:Tn], svf[:, ct, 0:Tn], var[:, 0:Tn], op=ALU.mult), R=['pa', 'var'], W=['pa'])
                            k.op('dve', lambda ct=ct: V.tensor_scalar(vnb[:, ct, 0:Tn], svf[:, ct, 0:Tn], colv[:, ct:ct + 1], colv[:, 2 + ct:3 + ct], op0=ALU.mult, op1=ALU.add),
                                 R=['pa', 'colv'], W=['vnb'])
                        for gt in range(8):
                            pg, pgk = psC[gt % 4], 'psC%d' % (gt % 4)
                            proj(pg, pgk, 704 + gt * 128, 128)
                            k.op('act', lambda gt=gt, pg=pg: A.activation(sg[:, gt, 0:Tn], pg[:, 0:Tn], AF.Silu), R=[pgk], W=['sg'])
                        if bidx + 1 < min(NQB, len(qblocks)):
                            do_ln(bidx + 1)
                        sbufs = [(psS_[0], 'psSc0'), (psS_[1], 'psSc1'), (psC[0], 'psC0'), (psC[1], 'psC1')]
                        npair = nkt // 2

                        def score(h, kp):
                            pS, pSk = sbufs[kp % 4]
                            for j_ in range(2):
                                kt_ = 2 * kp + j_
                                k.op('pe', lambda j_=j_, kt_=kt_: PE.matmul(pS[:, j_ * Tn:(j_ + 1) * Tn], lhsT=KT[0:96, h, kt_ * 128:(kt_ + 1) * 128], rhs=QT[0:96, h, 0:Tn],
                                                                           start=True, stop=True, skip_group_check=True), R=['QT', 'KTall'], W=[pSk])
                        for h in range(4 if int(os.environ.get('DBG_ATT', 1)) else 0):
                            pO = psO if h % 2 == 0 else psC[3][:, 0:260].rearrange("p (a b) -> p a b", a=4)
                            pOk = 'psO' if h % 2 == 0 else 'psC3'
                            for kp in range(min(3, npair)):
                                score(h, kp)
                            for kp in range(npair):
                                pS, pSk = sbufs[kp % 4]
                                P_, Pk = PT[kp % 3], 'PT%d' % (kp % 3)
                                if kp + 3 < npair:
                                    score(h, kp + 3)
                                k.op('act', lambda pS=pS, P_=P_: A.activation(P_[:, 0:2 * Tn], pS[:, 0:2 * Tn], AF.Exp, scale=SCALE), R=[pSk], W=[Pk])
                                for j_ in range(2):
                                    kt_ = 2 * kp + j_
                                    for sb in range(nsub):
                                        first = (kp == 0 and j_ == 0 and sb == 0)
                                        k.op('pe', lambda h=h, kt_=kt_, j_=j_, sb=sb, P_=P_, first=first: PE.matmul(
                                            pO[:, sb, :], lhsT=P_[:, j_ * Tn + sb * 128:j_ * Tn + (sb + 1) * 128], rhs=Vt[:, kt_, h, :],
                                            start=first, stop=(kp == npair - 1 and j_ == 1), skip_group_check=True), R=[Pk, 'Vtall'], W=[pOk])
                            k.op('act', lambda: A.copy(osb[:, 0:nsub, :], pO[:, 0:nsub, :]), R=[pOk], W=['osb'])
                            k.op('dve', lambda: V.reciprocal(rec[:, 0:nsub, :], osb[:, 0:nsub, 64:65]), R=['osb'], W=['rec'])
                            k.op('dve', lambda h=h: V.tensor_tensor(att[:, 0:nsub, h * 64:(h + 1) * 64], osb[:, 0:nsub, 0:64], rec[:, 0:nsub, :].to_broadcast([128, nsub, 64]), op=ALU.mult),
                                 R=['osb', 'rec'], W=['att'])
                        if 'Catt' in dbg and l == 0 and kind == 'lat' and t0 == 0:
                            d_ = dbg_out('att', (128, 4 * 256), BF16)
                            k.dma(d_, att[:].rearrange("p a b -> p (a b)"), R=['att'], W=['dbgatt'])
                        for sb in range(nsub):
                            for ft in range(2):
                                k.op('pe', lambda sb=sb, ft=ft: PE.transpose(pt[:, ft, :], att[:, sb, ft * 128:(ft + 1) * 128], ident_b[:]), R=['att', 'ident_b'], W=['ptC'])
                            k.op('dve', lambda sb=sb: V.tensor_tensor(catg[:, 0:2, sb * 128:(sb + 1) * 128], pt[:, 0:2, :], sg[:, 0:2, sb * 128:(sb + 1) * 128], op=ALU.mult),
                                 R=['ptC', 'sg'], W=['sg'])
                        si = 0 if kind == 'ctx' else 1
                        nseq = NCTX if kind == 'ctx' else SEQ
                        L_ = Tn + 16
                        k.dma(pin[:, :, 0:L_], pool_scr[si][:, t0:t0 + L_].rearrange("(c p) n -> p c n", p=128), R=['pool_scr'], W=['pin'])
                        if kind == 'lat':
                            pinB = pb[:].bitcast(BF16)[:, :, 0:L_]
                            k.dma(pinB, pool_scr[si][:, HALF + t0:HALF + t0 + L_].rearrange("(c p) n -> p c n", p=128), R=['pool_scr'], W=['pb'], q='pool')
                            k.op('dve', lambda: V.tensor_scalar_mul(pin[:, :, 0:L_], pin[:, :, 0:L_], msb[:, 0:1]), R=['pin', 'msb'], W=['pin'])
                            k.op('dve', lambda: V.scalar_tensor_tensor(pin[:, :, 0:L_], pinB, msb[:, 1:2], pin[:, :, 0:L_], op0=ALU.mult, op1=ALU.add), R=['pin', 'pb', 'msb'], W=['pin'])
                        k.op('dve', lambda: V.tensor_tensor(pa[:, :, 0:L_ - 1], pin[:, :, 0:L_ - 1], pin[:, :, 1:L_], op=ALU.add), R=['pin'], W=['pa'])
                        o_ = slice(8, 8 + Tn)
                        k.op('pool', lambda: G.tensor_copy(pb[0:64, 0, o_], pa[0:64, 0, 7:7 + Tn]), R=['pa'], W=['pb'])
                        k.op('pool', lambda: G.tensor_tensor(pb[64:128, 0, o_], pa[64:128, 0, 6:6 + Tn], pa[64:128, 0, 8:8 + Tn], op=ALU.add), R=['pa'], W=['pb'])
                        k.op('dve', lambda: V.tensor_tensor(pb[:, 1, 0:L_ - 3], pa[:, 1, 0:L_ - 3], pa[:, 1, 2:L_ - 1], op=ALU.add), R=['pa'], W=['pb'])
                        k.op('dve', lambda: V.tensor_tensor(pa[64:128, 1, 0:L_ - 7], pb[64:128, 1, 0:L_ - 7], pb[64:128, 1, 4:L_ - 3], op=ALU.add), R=['pb'], W=['pa'])
                        k.op('pool', lambda: G.tensor_tensor(pa[0:64, 1, o_], pb[0:64, 1, 4:4 + Tn], pb[0:64, 1, 8:8 + Tn], op=ALU.add), R=['pb'], W=['pa'])
                        k.op('dve', lambda: V.tensor_tensor(pb[64:128, 1, o_], pa[64:128, 1, 0:Tn], pa[64:128, 1, 8:8 + Tn], op=ALU.add), R=['pa'], W=['pb'])
                        k.op('pool', lambda: G.tensor_copy(pb[0:64, 1, o_], pa[0:64, 1, o_]), R=['pa'], W=['pb'])
                        for ct in range(2):
                            k.op('dve', lambda ct=ct: V.scalar_tensor_tensor(pld[:, ct, 0:Tn], pb[:, ct, o_], winv[:, ct:ct + 1], pin[:, ct, o_], op0=ALU.mult, op1=ALU.subtract),
                                 R=['pb', 'pin', 'winv'], W=['pld'])
                        wins = ((0, 0, 2), (64, 0, 4), (0, 1, 8), (64, 1, 16))
                        if t0 == 0:
                            for (r0, ct, w_) in wins:
                                for p_ in range(w_ // 2):
                                    rcp = (1.0 / float(p_ + w_ // 2)) if kind == 'ctx' else recF[r0:r0 + 64, ct, p_:p_ + 1]
                                    k.op('dve', lambda r0=r0, ct=ct, p_=p_, rcp=rcp: V.scalar_tensor_tensor(pld[r0:r0 + 64, ct, p_:p_ + 1], pb[r0:r0 + 64, ct, 8 + p_:9 + p_], rcp,
                                                                                                       pin[r0:r0 + 64, ct, 8 + p_:9 + p_], op0=ALU.mult, op1=ALU.subtract),
                                         R=['pb', 'pin', 'recF'], W=['pld'])
                        if t0 + Tn == (NCTX if kind == 'ctx' else HALF):
                            for (r0, ct, w_) in wins:
                                for p_ in range(Tn - w_ // 2 + 1, Tn):
                                    rcp = (1.0 / float(Tn - p_ + w_ // 2)) if kind == 'ctx' else recL[r0:r0 + 64, ct, p_ - (Tn - 8):p_ - (Tn - 8) + 1]
                                    k.op('dve', lambda r0=r0, ct=ct, p_=p_, rcp=rcp: V.scalar_tensor_tensor(pld[r0:r0 + 64, ct, p_:p_ + 1], pb[r0:r0 + 64, ct, 8 + p_:9 + p_], rcp,
                                                                                                       pin[r0:r0 + 64, ct, 8 + p_:9 + p_], op0=ALU.mult, op1=ALU.subtract),
                                         R=['pb', 'pin', 'recL'], W=['pld'])
                        for ct in range(2):
                            k.op('pe', lambda ct=ct: PE.matmul(psC[ct][:, 0:Tn], lhsT=wpl[:, ct, :], rhs=pld[:, ct, 0:Tn], start=True, stop=True), R=['wpl', 'pld'], W=['psC%d' % ct])
                            k.op('dve', lambda ct=ct: V.scalar_tensor_tensor(catg[:, 2 + ct, 0:Tn], psC[ct][:, 0:Tn], colv[:, 4 + ct:5 + ct], sg[:, 2 + ct, 0:Tn], op0=ALU.mult, op1=ALU.mult),
                                 R=['psC%d' % ct, 'colv', 'sg'], W=['sg'])
                        k.dma(s5t[:, :, 0:Tn], s5o_scr[:, u0:u0 + Tn].rearrange("(c p) n -> p c n", p=128), R=['s5o_scr'], W=['s5t'], q='pool')
                        k.op('pool', lambda: G.tensor_tensor(catg[:, 4:6, 0:Tn], s5t[:, :, 0:Tn], sg[:, 4:6, 0:Tn], op=ALU.mult), R=['s5t', 'sg'], W=['sg'])
                        for sb in range(nsub):
                            cs = slice(sb * 128, (sb + 1) * 128)
                            for ft in range(2):
                                k.op('pe', lambda ft=ft, cs=cs: PE.transpose(pt[:, 2 + ft, :], vnb[:, ft, cs], ident_b[:]), R=['vnb', 'ident_b'], W=['ptC'])
                            k.op('act', lambda: A.copy(vc[:].rearrange("p (a b) -> p a b", a=2), pt[:, 2:4, :]), R=['ptC'], W=['vc'])
                            for h in range(4):
                                ft, r0 = h // 2, (h % 2) * 64
                                pm_, pmk = psC[2 + h % 2], 'psC%d' % (2 + h % 2)
                                k.op('pe', lambda h=h, ft=ft: PE.matmul(pm_[:, 0:128], lhsT=vc[:, ft * 128:(ft + 1) * 128], rhs=wsT[:, h, :], start=True, stop=True), R=['vc', 'wsT'], W=[pmk])
                                k.op('dve', lambda h=h, r0=r0: V.tensor_tensor(mx[r0:r0 + 64, :], pm_[r0:r0 + 64, 0:128], bsb[r0:r0 + 64, h, :], op=ALU.add), R=[pmk, 'bsb'], W=['mx'])
                                k.op('pool', lambda ft=ft, r0=r0, cs=cs: G.tensor_tensor(mx[r0:r0 + 64, :], mx[r0:r0 + 64, :], su[r0:r0 + 64, ft, cs], op=ALU.mult), R=['mx', 'su'], W=['mx'])
                                k.op('dve', lambda ft=ft, r0=r0, cs=cs: V.tensor_tensor(catg[r0:r0 + 64, 6 + ft, cs], mx[r0:r0 + 64, :], sg[r0:r0 + 64, 6 + ft, cs], op=ALU.mult),
                                     R=['mx', 'sg'], W=['sg'])
                        if 'Ccat' in dbg and l == 0 and kind == 'lat' and t0 == 0:
                            d_ = dbg_out('catg', (128, 8 * QB), BF16)
                            k.dma(d_, catg[:].rearrange("p a b -> p (a b)"), R=['sg'], W=['dbgcat'])
                        for sb in range(nsub):
                            cs = slice(sb * 128, (sb + 1) * 128)
                            xb = xt[xcnt % 2]
                            xk = 'xtC0'
                            xcnt += 1
                            k.dma(xb[:], src[t0 + sb * 128: t0 + (sb + 1) * 128, :], W=[xk], q=('sp' if sb % 2 == 0 else 'pool'))
                            for nh in range(2):
                                for kt in range(8):
                                    k.op('pe', lambda nh=nh, kt=kt, cs=cs: PE.matmul(psC[nh][:, :], lhsT=catg[:, kt, cs], rhs=wout[:, kt, nh * 512:(nh + 1) * 512],
                                                                                    start=(kt == 0), stop=(kt == 7)), R=['sg', 'wout'], W=['psC%d' % nh])
                                k.op('dve', lambda nh=nh: V.tensor_tensor(rr[:, nh * 512:(nh + 1) * 512], psC[nh][:, :], gate_bc[:, nh * 512:(nh + 1) * 512], op=ALU.mult),
                                     R=['psC%d' % nh, 'gate_bc'], W=['rr'])
                            k.op('dve', lambda xb=xb: V.scalar_tensor_tensor(rr[:], xb[:], float(ALPHA), rr[:], op0=ALU.mult, op1=ALU.add), R=[xk, 'rr'], W=['rr'])
                            for c2 in range(2):
                                k.op('dve', lambda c2=c2: V.bn_stats(st6[:, c2, :], rr[:, c2 * 512:(c2 + 1) * 512]), R=['rr'], W=['st6'])
                            k.op('dve', lambda: V.bn_aggr(mv[:], st6[:]), R=['st6'], W=['mv'])
                            k.op('act', lambda: A.activation(rstd[:], mv[:, 1:2], AF.Sqrt, bias=epsb[:, 0:1]), R=['mv', 'epsb'], W=['rstd'])
                            k.op('dve', lambda: V.reciprocal(rstd[:], rstd[:]), R=['rstd'], W=['rstd'])
                            k.op('dve', lambda: V.scalar_tensor_tensor(rr[:], rr[:], mv[:, 0:1], lng_bc[:], op0=ALU.subtract, op1=ALU.mult), R=['rr', 'mv', 'lng_bc'], W=['rr'])
                            k.op('dve', lambda xb=xb: V.scalar_tensor_tensor(xb[:], rr[:], rstd[:, 0:1], lnb_bc[:], op0=ALU.mult, op1=ALU.add), R=['rr', 'rstd', 'lnb_bc'], W=[xk])
                            k.dma(dst[t0 + sb * 128: t0 + (sb + 1) * 128, :], xb[:], R=[xk], W=['x1w'])

            if l == 0 and nlayers > 1:
                k._sync('pool', ['x1w'], ['x1full'])
                for c_ in range(HALF // CCR):
                    G.collective_compute("AllGather", ALU.bypass, replica_groups=[[0, 1], [2, 3], [4, 5], [6, 7]],
                                         ins=[x1h_scr[c_ * CCR:(c_ + 1) * CCR, :]], outs=[x1g[c_]]).then_inc(ccsem)
                ccn[0] += HALF // CCR
                x1_ready = ccn[0]
            if 'x1' in dbg and l == 0:
                d_ = dbg_out('x1', (SEQ, D))
                k.dma(d_[0:HALF, :], x1h_scr, R=['x1w'], W=['dbgx1'])
                d_ = dbg_out('ctx1', (NCTX, D))
                k.dma(d_, ctx1_scr, R=['x1w'], W=['dbgc1'])
                break

        k.wait_all('sp')
    return nc, dout


def _in_maps(inputs, cores):
    cst = _consts()
    shared = {n: np.ascontiguousarray(inputs[n], dtype=np.float32) for n in PARAMS}
    shared.update(cst)
    maps = []
    for i in cores:
        b, hf = i // 2, i % 2
        sl = slice(hf * HALF, (hf + 1) * HALF)
        m = dict(shared)
        m['x'] = np.ascontiguousarray(inputs['x'][b], dtype=np.float32)
        m['xq'] = np.ascontiguousarray(inputs['x'][b][sl], dtype=np.float32)
        m['ctx'] = np.ascontiguousarray(inputs['ctx'][b], dtype=np.float32)
        m['cvec'] = np.ascontiguousarray(np.stack([inputs['c'][b], inputs['c_ctx']], 0), dtype=np.float32)
        m['msel'] = np.array([1.0 - hf, float(hf)], np.float32)
        if hf:
            for n_ in ('lam_re', 'lam_im', 'log_dt', 's5_b_re', 's5_b_im', 's5_c_re', 's5_c_im'):
                m[n_] = np.ascontiguousarray(np.roll(shared[n_], -8, axis=2))
            w_in = shared['w_in'].copy()
            w_in[:, :, 160:416] = np.roll(w_in[:, :, 160:416], -128, axis=2)
            w_in[:, :, 1888:2144] = np.roll(w_in[:, :, 1888:2144], -128, axis=2)
            m['w_in'] = w_in
            m['s5_d'] = np.ascontiguousarray(np.roll(shared['s5_d'], -128, axis=1))
            m['b_glu'] = np.ascontiguousarray(np.roll(shared['b_glu'], -128, axis=1))
            m['w_glu'] = np.ascontiguousarray(np.roll(np.roll(shared['w_glu'], -128, axis=1), -128, axis=2))
            w_out = shared['w_out'].copy()
            w_out[:, 512:768, :] = np.roll(w_out[:, 512:768, :], -128, axis=1)
            m['w_out'] = w_out
        m['ropec_q'] = np.ascontiguousarray(cst['ropec'][:, sl])
        m['ropes_q'] = np.ascontiguousarray(cst['ropes'][:, sl])
        maps.append(m)
    return maps


def kernel(**inputs):
    nc, _ = build()
    cores = list(range(8))
    res = run_bass_kernel_spmd(nc, _in_maps(inputs, cores), core_ids=cores)
    out = np.empty((4, SEQ, D), np.float32)
    for i in cores:
        b, hf = i // 2, i % 2
        out[b, hf * HALF:(hf + 1) * HALF] = np.asarray(res.results[i]['out'], dtype=np.float32)
    return out
```

```python
import os
from contextlib import ExitStack
import numpy as np
import concourse.bass as bass
import concourse.mybir as mybir
from concourse.bass_utils import run_bass_kernel_spmd

F32 = mybir.dt.float32
BF16 = mybir.dt.bfloat16
AF = mybir.ActivationFunctionType
ALU = mybir.AluOpType

D = 1024
SEQ = 8192
NCTX = 256
NU = NCTX + SEQ
NE = NU + NCTX
NSC = NE // 8
HALF = SEQ // 2
DEPTH = 2
QB = 256
IN_DIM = 2400
LN_EPS = 1e-6
ALPHA = (2 * DEPTH) ** 0.25
PARAMS = ['w_mod', 'b_mod', 'w_in', 'g_q', 'w_uq', 'g_kv', 'w_ukv', 'w_pool', 'pool_scale', 'lam_re', 'lam_im',
          'log_dt', 's5_b_re', 's5_b_im', 's5_c_re', 's5_c_im', 's5_d', 'w_glu', 'b_glu', 'sgu_g', 'sgu_b', 'w_s',
          'b_s', 'w_out', 'ln_g', 'ln_b']
PSHAPES = {'w_mod': (2, 1024, 3072), 'b_mod': (2, 3072), 'w_in': (2, 1024, 2400), 'g_q': (2, 192), 'w_uq': (2, 192, 384),
           'g_kv': (2, 128), 'w_ukv': (2, 128, 512), 'w_pool': (2, 4, 64, 64), 'pool_scale': (2, 256),
           'lam_re': (2, 2, 16, 64), 'lam_im': (2, 2, 16, 64), 'log_dt': (2, 2, 16), 's5_b_re': (2, 2, 16, 64, 16),
           's5_b_im': (2, 2, 16, 64, 16), 's5_c_re': (2, 2, 16, 16, 64), 's5_c_im': (2, 2, 16, 16, 64), 's5_d': (2, 256),
           'w_glu': (2, 256, 256), 'b_glu': (2, 256), 'sgu_g': (2, 256), 'sgu_b': (2, 256), 'w_s': (2, 4, 128, 128),
           'b_s': (2, 4, 128), 'w_out': (2, 1024, 1024), 'ln_g': (2, 1024), 'ln_b': (2, 1024)}

SAME_ENGINE_SYNC = True


class KB:
    NSLOT = 8

    def __init__(self, nc, stack):
        self.nc = nc
        self.eng = {'pe': nc.tensor, 'act': nc.scalar, 'dve': nc.vector, 'pool': nc.gpsimd, 'sp': nc.sync}
        self.sem = {e: stack.enter_context(nc.semaphore('s_' + e)) for e in self.eng}
        self.cnt = {e: 0 for e in self.eng}
        self.dsem, self.duse = {}, {}
        for q in ('sp', 'pool', 'act'):
            for j in range(self.NSLOT):
                self.dsem[(q, j)] = stack.enter_context(nc.semaphore('d_%s%d' % (q, j)))
                self.duse[(q, j)] = 0
        self.dnext = {'sp': 0, 'pool': 0, 'act': 0}
        self.known = {e: {} for e in self.eng}
        self.lastw, self.readers = {}, {}
        self.ninstr = 0

    def _need(self, E, tok, waits):
        if tok is None:
            return
        if tok[0] == 'c':
            _, e2, n = tok
            if e2 == E and (not SAME_ENGINE_SYNC or E == 'pe'):
                return
            key, val = e2, n
        else:
            _, q, j, n = tok
            key, val = (q, j), n * 16
        if self.known[E].get(key, 0) >= val:
            return
        if waits.get(key, 0) < val:
            waits[key] = val

    def _sync(self, E, R, W, waits=None):
        waits = {} if waits is None else waits
        for r in R:
            self._need(E, self.lastw.get(r), waits)
        for w in W:
            self._need(E, self.lastw.get(w), waits)
            for t in self.readers.get(w, ()):
                self._need(E, t, waits)
        eng = self.eng[E]
        for key, val in waits.items():
            eng.wait_ge(self.sem[key] if isinstance(key, str) else self.dsem[key], val)
            self.known[E][key] = val

    def _commit(self, tok, R, W):
        for r in R:
            self.readers.setdefault(r, []).append(tok)
        for w in W:
            self.lastw[w] = tok
            self.readers[w] = []

    def op(self, E, fn, R=(), W=()):
        W = list(W) + [r for r in R if r.startswith('ps') and r not in W]
        self._sync(E, R, W)
        ins = fn()
        self.cnt[E] += 1
        ins.then_inc(self.sem[E], 1)
        self._commit(('c', E, self.cnt[E]), R, W)
        self.ninstr += 1
        return ins

    def dma(self, out, in_, R=(), W=(), q='sp', **kw):
        j = self.dnext[q]
        self.dnext[q] = (j + 1) % self.NSLOT
        slot = (q, j)
        waits = {}
        if self.duse[slot] > 0:
            self._need(q, ('d', q, j, self.duse[slot]), waits)
        self._sync(q, R, W, waits)
        ins = self.eng[q].dma_start(out=out, in_=in_, **kw)
        ins.then_inc(self.dsem[slot], 16)
        self.duse[slot] += 1
        self._commit(('d', q, j, self.duse[slot]), R, W)
        self.ninstr += 1

    def barrier(self):
        for E, eng in self.eng.items():
            for e2 in self.eng:
                if e2 != E and self.cnt[e2] > self.known[E].get(e2, 0):
                    eng.wait_ge(self.sem[e2], self.cnt[e2])
                    self.known[E][e2] = self.cnt[e2]
            for slot, n in self.duse.items():
                if n > 0 and 16 * n > self.known[E].get(slot, 0):
                    eng.wait_ge(self.dsem[slot], 16 * n)
                    self.known[E][slot] = 16 * n

    def wait_all(self, E='sp'):
        eng = self.eng[E]
        for e2 in self.eng:
            if e2 != E and self.cnt[e2] > 0:
                eng.wait_ge(self.sem[e2], self.cnt[e2])
        for slot, n in self.duse.items():
            if n > 0:
                eng.wait_ge(self.dsem[slot], 16 * n)


def _consts():
    c = {}
    c['ident'] = np.eye(128, dtype=np.float32)
    sw = np.zeros((128, 128), np.float32)
    for p in range(64):
        sw[p, p + 64] = 1.0
        sw[p + 64, p] = 1.0
    c['swapm'] = sw
    rows = SEQ // 64
    t = np.arange(SEQ)
    pos = np.stack([t // 64, t % 64], 0).astype(np.float32)
    inv = (10000.0 ** (-np.arange(8, dtype=np.float32) / 8)).astype(np.float32)
    ang = pos[:, None, :] * inv[None, :, None]
    cos, sin = np.cos(ang).astype(np.float32), np.sin(ang).astype(np.float32)
    rc = np.zeros((32, SEQ), np.float32)
    rs = np.zeros((32, SEQ), np.float32)
    for a in range(2):
        for hf in range(2):
            rc[a * 16 + hf * 8: a * 16 + hf * 8 + 8] = cos[a]
            rs[a * 16 + hf * 8: a * 16 + hf * 8 + 8] = sin[a] * (-1.0 if hf == 0 else 1.0)
    r_ = np.arange(128) // 16
    c['mlow'] = (r_[None, :] >= r_[:, None]).astype(np.float32)
    c['mup'] = (r_[None, :] <= r_[:, None]).astype(np.float32)
    c['ropec'] = rc
    c['ropes'] = rs
    return c


def build(nlayers=DEPTH, dbg=()):
    nc = bass.Bass("TRN2", target_bir_lowering=False)
    din = {}

    def inp(name, shape):
        din[name] = nc.dram_tensor(name, list(shape), F32, kind="ExternalInput").ap()
        return din[name]

    x_in = inp('x', (SEQ, D))
    xq_in = inp('xq', (HALF, D))
    ctx_in = inp('ctx', (NCTX, D))
    msel = inp('msel', (2,))
    ropecq = inp('ropec_q', (32, HALF))
    ropesq = inp('ropes_q', (32, HALF))
    cvec = inp('cvec', (2, D))
    prm = {n: inp(n, PSHAPES[n]) for n in PARAMS}
    cst = {n: inp(n, v.shape) for n, v in _consts().items()}
    out_ap = nc.dram_tensor('out', [HALF, D], F32, kind="ExternalOutput").ap()
    dout = {}

    def dbg_out(name, shape, dt=F32):
        dout[name] = nc.dram_tensor('dbg_' + name, list(shape), dt, kind="ExternalOutput").ap()
        return dout[name]

    scr = lambda name, shape, dt=F32: nc.dram_tensor(name, list(shape), dt, kind="Internal").ap()
    CCR = 512
    x1g = [scr('x1g%d' % c, (2 * CCR, D)) for c in range(HALF // CCR)]

    def x1rows(t):
        r_, w_ = t // HALF, t % HALF
        c_, o_ = w_ // CCR, w_ % CCR
        return x1g[c_][r_ * CCR + o_: r_ * CCR + o_ + 128, :]
    x1h_scr = scr('x1h_scr', (HALF, D))
    ctx1_scr = scr('ctx1_scr', (NCTX, D))
    gate_scr = scr('gate_scr', (2, 2, D))
    pool_scr = [scr('pool_scr_c', (256, NCTX + 16), BF16), scr('pool_scr_l', (256, SEQ + 16), BF16)]
    s5o_scr = scr('s5o_scr', (256, NU), BF16)
    xd_scr = scr('xd_scr', (256, 8, NSC), BF16)
    yd_scr = scr('yd_scr', (256, 8, NSC), BF16)
    ydg_scr = [scr('ydg%d' % c, (128, 8 * NSC), BF16) for c in range(2)]

    with ExitStack() as st:
        k = KB(nc, st)

        uid = [0]

        def T(stk, name, shape, dt):
            uid[0] += 1
            return stk.enter_context(nc.sbuf_tensor('%s_t%d' % (name, uid[0]), list(shape), dt))

        def P(stk, name, shape, dt=F32):
            uid[0] += 1
            return stk.enter_context(nc.psum_tensor('%s_p%d' % (name, uid[0]), list(shape), dt))

        V = nc.vector
        A = nc.scalar
        G = nc.gpsimd
        PE = nc.tensor

        def ln_block(xsrcs, rkeys, col, hT_, B):
            n = len(xsrcs)

            def stage1(s_):
                xb, xk = B['xt'][s_ % 2]
                xn_, xnk = B['xn'][s_ % 2]
                pt_, ptk = B['pt'][s_ % 2]
                sm, smk = B['sm'][s_ % 2]
                st6_ = sm[:, 0:12].rearrange("p (a b) -> p a b", a=2)
                k.dma(xb[:], xsrcs[s_], R=rkeys, W=[xk], q=('sp' if s_ % 2 == 0 else 'pool'))
                for c2 in range(2):
                    k.op('dve', lambda c2=c2: V.bn_stats(st6_[:, c2, :], xb[:, c2 * 512:(c2 + 1) * 512]), R=[xk], W=[smk])
                k.op('dve', lambda: V.bn_aggr(sm[:, 12:14], st6_), R=[smk], W=[smk])
                k.op('act', lambda: A.activation(sm[:, 14:15], sm[:, 13:14], AF.Sqrt, bias=epsb[:, 0:1]), R=[smk, 'epsb'], W=[smk])
                k.op('dve', lambda: V.reciprocal(sm[:, 14:15], sm[:, 14:15]), R=[smk], W=[smk])
                k.op('dve', lambda: V.tensor_scalar(sm[:, 15:16], sm[:, 12:13], sm[:, 14:15], -1.0, op0=ALU.mult, op1=ALU.mult), R=[smk], W=[smk])
                if B.get('dve_mod'):
                    k.op('dve', lambda: V.tensor_scalar(xn_[:], xb[:], sm[:, 12:13], sm[:, 14:15], op0=ALU.subtract, op1=ALU.mult), R=[xk, smk], W=[xnk])
                else:
                    k.op('act', lambda: A.activation(xn_[:], xb[:], AF.Identity, scale=sm[:, 14:15], bias=sm[:, 15:16]), R=[xk, smk], W=[xnk])
                for dc in range(8):
                    k.op('pe', lambda dc=dc: PE.transpose(pt_[:, dc, :], xn_[:, dc * 128:(dc + 1) * 128], ident_b[:]), R=[xnk, 'ident_b'], W=[ptk])

            def stage2(s_):
                pt_, ptk = B['pt'][s_ % 2]
                if B.get('dve_mod'):
                    hv = hT_[:, :, s_ * 128:(s_ + 1) * 128]
                    k.op('dve', lambda: V.tensor_tensor(hv, pt_[:, :, :], sc1T[:, :, col].unsqueeze(2).to_broadcast([128, 8, 128]), op=ALU.mult),
                         R=[ptk, 'sc1T'], W=[B['hTk']])
                    k.op('pool', lambda: G.tensor_tensor(hv, hv, modT[:, 0:8, col].unsqueeze(2).to_broadcast([128, 8, 128]), op=ALU.add),
                         R=[B['hTk'], 'modT'], W=[B['hTk']])
                    return
                for dc in range(8):
                    k.op('act', lambda dc=dc: A.activation(hT_[:, dc, s_ * 128:(s_ + 1) * 128], pt_[:, dc, :], AF.Identity,
                                                          scale=sc1T[:, dc, col:col + 1], bias=modT[:, dc, col:col + 1]),
                         R=[ptk, 'sc1T', 'modT'], W=[B['hTk']])

            stage1(0)
            for s_ in range(n):
                if s_ + 1 < n:
                    stage1(s_ + 1)
                stage2(s_)

        ident_f = T(st, 'ident_f', (128, 128), F32)
        ident_b = T(st, 'ident_b', (128, 128), BF16)
        ones_b = T(st, 'ones_b', (128, 128), BF16)
        epsb = T(st, 'epsb', (128, 1), F32)
        KT = T(st, 'KT', (128, 4, NU), BF16)
        Vt = T(st, 'Vt', (128, NU // 128, 4, 65), BF16)
        modT = T(st, 'modT', (128, 24, 2), F32)
        sc1T = T(st, 'sc1T', (128, 8, 2), F32)
        wukv = T(st, 'wukv', (128, 512), BF16)
        msb = T(st, 'msb', (128, 2), F32)
        recF = T(st, 'recF', (128, 2, 8), F32)
        recL = T(st, 'recL', (128, 2, 8), F32)
        ccsem = st.enter_context(nc.semaphore('ccsem'))
        ccn = [0]

        k.dma(ident_f[:], cst['ident'], W=['ident_f'])
        k.op('dve', lambda: V.tensor_copy(ident_b[:], ident_f[:]), R=['ident_f'], W=['ident_b'])
        k.op('pool', lambda: G.memset(ones_b[:], 1.0), W=['ones_b'])
        k.op('pool', lambda: G.memset(epsb[:], LN_EPS), W=['epsb'])
        k.op('pool', lambda: G.memset(Vt[:, :, :, 64:65], 1.0), W=['Vt'])
        k.dma(msb[:], msel.partition_broadcast(128), W=['msb'])
        k.op('pool', lambda: G.memset(recF[:], 1.0), W=['recF'])
        k.op('pool', lambda: G.memset(recL[:], 1.0), W=['recL'])
        WINS = ((0, 0, 2), (64, 0, 4), (0, 1, 8), (64, 1, 16))
        for (r0, ct, w_) in WINS:
            for p_ in range(w_ // 2):
                c1_ = 1.0 / (p_ + w_ // 2) - 1.0 / w_
                k.op('dve', lambda r0=r0, ct=ct, p_=p_, c1_=c1_, w_=w_: V.tensor_scalar(recF[r0:r0 + 64, ct, p_:p_ + 1], msb[r0:r0 + 64, 0:1], c1_, 1.0 / w_, op0=ALU.mult, op1=ALU.add),
                     R=['msb'], W=['recF'])
            for q_ in range(8 - w_ // 2 + 1, 8):
                c1_ = 1.0 / (8 - q_ + w_ // 2) - 1.0 / w_
                k.op('dve', lambda r0=r0, ct=ct, q_=q_, c1_=c1_, w_=w_: V.tensor_scalar(recL[r0:r0 + 64, ct, q_:q_ + 1], msb[r0:r0 + 64, 1:2], c1_, 1.0 / w_, op0=ALU.mult, op1=ALU.add),
                     R=['msb'], W=['recL'])

        for l in range(nlayers):
            last = (l == DEPTH - 1)
            xin = x_in
            cin = ctx_in if l == 0 else ctx1_scr
            k.barrier()
            with ExitStack() as s0:
                cc = T(s0, 'cc', (128, 8, 2), F32)
                scc = T(s0, 'scc', (128, 8, 2), F32)
                bmodT = T(s0, 'bmodT', (128, 24), F32)
                wm = [T(s0, 'wm%d' % i, (128, 8, 512), F32) for i in range(2)]
                pm = P(s0, 'pm', (128, 24, 2))
                wtmp = T(s0, 'wtmp', (128, 512), F32)
                gkv = T(s0, 'gkv', (128, 1), F32)
                for col in range(2):
                    k.dma(cc[:, :, col], cvec[col].rearrange("(c p) -> p c", p=128), W=['cc'], allow_slow_non_contiguous=True)
                k.dma(bmodT[:], prm['b_mod'][l].rearrange("(c p) -> p c", p=128), W=['bmodT'], allow_slow_non_contiguous=True)
                k.op('act', lambda: A.activation(scc[:], cc[:], AF.Silu), R=['cc'], W=['scc'])
                for blk in range(6):
                    w_ = wm[blk % 2]
                    k.dma(w_[:], prm['w_mod'][l][:, blk * 512:(blk + 1) * 512].rearrange("(c p) n -> p c n", p=128),
                          W=['wm%d' % (blk % 2)], q=('sp' if blk % 2 == 0 else 'pool'))
                    for jj in range(4):
                        jt = blk * 4 + jj
                        for dc in range(8):
                            k.op('pe', lambda w_=w_, jj=jj, jt=jt, dc=dc: PE.matmul(
                                pm[:, jt, :], lhsT=w_[:, dc, jj * 128:(jj + 1) * 128], rhs=scc[:, dc, :],
                                start=(dc == 0), stop=(dc == 7)), R=['wm%d' % (blk % 2), 'scc'], W=['pm'])
                k.op('dve', lambda: V.tensor_tensor(modT[:], pm[:], bmodT[:].unsqueeze(2).to_broadcast([128, 24, 2]), op=ALU.add),
                     R=['pm', 'bmodT'], W=['modT'])
                k.op('dve', lambda: V.tensor_scalar_add(sc1T[:], modT[:, 8:16, :], 1.0), R=['modT'], W=['sc1T'])
                for col in range(2):
                    k.dma(gate_scr[l, col].rearrange("(c p) -> p c", p=128), modT[:, 16:24, col], R=['modT'], W=['gate_scr'],
                          allow_slow_non_contiguous=True)
                k.dma(wtmp[:], prm['w_ukv'][l], W=['wtmp'])
                k.dma(gkv[:], prm['g_kv'][l].rearrange("(p o) -> p o", o=1), W=['gkv'])
                k.op('dve', lambda: V.tensor_scalar_mul(wukv[:], wtmp[:], gkv[:, 0:1]), R=['wtmp', 'gkv'], W=['wukv'])

            if 'mod' in dbg and l == 0:
                d_ = dbg_out('mod', (128, 48))
                k.dma(d_, modT[:].rearrange("p a b -> p (a b)"), R=['modT'], W=['dbgmod'])

            k.barrier()
            if l > 0:
                for q_ in ('sp', 'pool'):
                    k.eng[q_].wait_ge(ccsem, x1_ready)
            with ExitStack() as sA:
                with ExitStack() as sa:
                    wA = T(sa, 'wA', (128, 8, 832), BF16)
                    wst = T(sa, 'wst', (128, 8, 256), F32)
                    xt = [T(sa, 'xt%d' % i, (128, D), F32) for i in range(2)]
                    xn2 = [T(sa, 'xn2_%d' % i, (128, D), BF16) for i in range(2)]
                    sm2 = [T(sa, 'sm2_%d' % i, (128, 16), F32) for i in range(2)]
                    st6 = T(sa, 'st6', (128, 2, 6), F32)
                    mv = T(sa, 'mv', (128, 2), F32)
                    rstd = T(sa, 'rstd', (128, 1), F32)
                    xn = T(sa, 'xn', (128, D), BF16)
                    hf32 = T(sa, 'hf32', (128, 8, 128), F32)
                    hT2 = [T(sa, 'hT%d' % i, (128, 8, 512), BF16) for i in range(2)]
                    sq = T(sa, 'sq', (128, 512), BF16)
                    rms = T(sa, 'rms', (128, 512), F32)
                    ckvn = T(sa, 'ckvn', (128, 512), BF16)
                    rc_t = T(sa, 'rc_t', (128, 512), F32)
                    rs_t = T(sa, 'rs_t', (128, 512), F32)
                    kr1 = T(sa, 'kr1', (128, 512), F32)
                    kr2 = T(sa, 'kr2', (128, 512), F32)
                    plo = T(sa, 'plo', (128, 2, 512), BF16)
                    zpad = T(sa, 'zpad', (128, 2, 8), BF16)
                    xds = T(sa, 'xds', (128, 2, 8, 64), BF16)
                    pt = P(sa, 'pt', (128, 8, 128), BF16)
                    ptb = P(sa, 'ptb', (128, 8, 128), BF16)
                    LNB = {'xt': [(xt[0], 'xt0'), (xt[1], 'xt1')], 'xn': [(xn2[0], 'xn2_0'), (xn2[1], 'xn2_1')],
                           'pt': [(pt, 'pt'), (ptb, 'ptb')], 'sm': [(sm2[0], 'sm2_0'), (sm2[1], 'sm2_1')], 'hTk': 'hT'}
                    ps = [P(sa, 'psA%d' % i, (128, 512)) for i in range(6)]

                    k.op('pool', lambda: G.memset(wA[:, :, 128:320], 0.0), W=['wA'])
                    k.op('pool', lambda: G.memset(zpad[:], 0.0), W=['zpad'])
                    win = prm['w_in'][l].rearrange("(c p) n -> p c n", p=128)
                    k.dma(wst[:, :, 0:160], win[:, :, 0:160], W=['wst'])
                    k.op('dve', lambda: V.tensor_copy(wA[:, :, 0:128], wst[:, :, 0:128]), R=['wst'], W=['wA'])
                    k.op('dve', lambda: V.tensor_copy(wA[:, :, 192:224], wst[:, :, 128:160]), R=['wst'], W=['wA'])
                    for (d0, s0_) in ((0, 8), (8, 0), (16, 24), (24, 16)):
                        k.op('dve', lambda d0=d0, s0_=s0_: V.tensor_copy(wA[:, :, 288 + d0:296 + d0], wst[:, :, 128 + s0_:136 + s0_]),
                             R=['wst'], W=['wA'])
                    k.dma(wst[:], win[:, :, 160:416], R=[], W=['wst'])
                    k.op('dve', lambda: V.tensor_copy(wA[:, :, 320:576], wst[:]), R=['wst'], W=['wA'])
                    k.dma(wst[:], win[:, :, 608:864], R=[], W=['wst'])
                    k.op('dve', lambda: V.tensor_copy(wA[:, :, 576:832], wst[:]), R=['wst'], W=['wA'])
                    for si in range(2):
                        k.dma(pool_scr[si][:, 0:8].rearrange("(c p) n -> p c n", p=128), zpad[:], R=['zpad'], W=['pool_scr'])
                        n_ = NCTX if si == 0 else SEQ
                        k.dma(pool_scr[si][:, 8 + n_:16 + n_].rearrange("(c p) n -> p c n", p=128), zpad[:], R=['zpad'], W=['pool_scr'])

                    blocks = [('ctx', 0, NCTX)] + [('lat', 512 * i, 512) for i in range(SEQ // 512)]
                    xcnt = 0
                    for bidx, (kind, t0, Tn) in enumerate(blocks):
                        hT, hTk = hT2[bidx % 2], 'hT%d' % (bidx % 2)
                        LNB['hTk'] = hTk
                        col = 1 if kind == 'ctx' else 0
                        src = cin if kind == 'ctx' else xin
                        u0 = t0 if kind == 'ctx' else NCTX + t0
                        nsub = Tn // 128
                        xsrcs = [(x1rows(t0 + sb * 128) if (l > 0 and kind == 'lat') else src[t0 + sb * 128: t0 + (sb + 1) * 128, :]) for sb in range(nsub)]
                        ln_block(xsrcs, ['x1w'] if l > 0 else [], col, hT, LNB)
                        def proj(pst, c0, ncol, key):
                            for dc in range(8):
                                k.op('pe', lambda dc=dc: PE.matmul(pst[0:ncol, 0:Tn], lhsT=wA[:, dc, c0:c0 + ncol], rhs=hT[:, dc, 0:Tn],
                                                                  start=(dc == 0), stop=(dc == 7)), R=['wA', hTk], W=[key])
                        proj(ps[0], 0, 128, 'psA0')
                        proj(ps[1], 128, 96, 'psA1')
                        if kind == 'lat':
                            proj(ps[2], 224, 96, 'psA2')
                        k.op('act', lambda: A.activation(sq[:, 0:Tn], ps[0][:, 0:Tn], AF.Square), R=['psA0'], W=['sq'])
                        k.op('pe', lambda: PE.matmul(ps[3][:, 0:Tn], lhsT=ones_b[:], rhs=sq[:, 0:Tn], start=True, stop=True),
                             R=['ones_b', 'sq'], W=['psA3'])
                        k.op('act', lambda: A.activation(rms[:, 0:Tn], ps[3][:, 0:Tn], AF.Sqrt, bias=epsb[:, 0:1], scale=1.0 / 128),
                             R=['psA3', 'epsb'], W=['rms'])
                        k.op('dve', lambda: V.reciprocal(rms[:, 0:Tn], rms[:, 0:Tn]), R=['rms'], W=['rms'])
                        k.op('dve', lambda: V.tensor_tensor(ckvn[:, 0:Tn], ps[0][:, 0:Tn], rms[:, 0:Tn], op=ALU.mult), R=['psA0', 'rms'], W=['ckvn'])
                        for h in range(4):
                            k.op('pe', lambda h=h: PE.matmul(ps[3][0:64, 0:Tn], lhsT=wukv[:, h * 128:h * 128 + 64], rhs=ckvn[:, 0:Tn],
                                                            start=True, stop=True), R=['wukv', 'ckvn'], W=['psA3'])
                            k.op('act', lambda h=h: A.copy(KT[0:64, h, u0:u0 + Tn], ps[3][0:64, 0:Tn]), R=['psA3'], W=['KT%d' % (u0 // 512)])
                        wv = wukv[:].rearrange("p (h t d) -> p h t d", h=4, t=2)[:, :, 1, :]
                        for sb in range(nsub):
                            k.op('pe', lambda sb=sb: PE.matmul(ps[4][:, 0:256].rearrange("p (h d) -> p h d", h=4), lhsT=ckvn[:, sb * 128:(sb + 1) * 128],
                                                              rhs=wv, start=True, stop=True), R=['wukv', 'ckvn'], W=['psA4'])
                            k.op('dve', lambda sb=sb: V.tensor_copy(Vt[:, u0 // 128 + sb, :, 0:64], ps[4][:, 0:256].rearrange("p (h d) -> p h d", h=4)),
                                 R=['psA4'], W=['Vt%d' % (u0 // 512)])
                        if kind == 'lat':
                            k.dma(rc_t[64:96, :], cst['ropec'][:, t0:t0 + 512], W=['rc_t'])
                            k.dma(rs_t[64:96, :], cst['ropes'][:, t0:t0 + 512], W=['rs_t'], q='pool')
                            k.op('dve', lambda: V.tensor_tensor(kr1[64:96, :], ps[1][64:96, :], rc_t[64:96, :], op=ALU.mult), R=['psA1', 'rc_t'], W=['kr1'])
                            k.op('dve', lambda: V.tensor_tensor(kr2[64:96, :], ps[2][64:96, :], rs_t[64:96, :], op=ALU.mult), R=['psA2', 'rs_t'], W=['kr2'])
                            for h in range(4):
                                k.op('pool', lambda h=h: G.tensor_tensor(KT[64:96, h, u0:u0 + Tn], kr1[64:96, 0:Tn], kr2[64:96, 0:Tn], op=ALU.add),
                                     R=['kr1', 'kr2'], W=['KT%d' % (u0 // 512)])
                        else:
                            for h in range(4):
                                k.op('act', lambda h=h: A.copy(KT[64:96, h, u0:u0 + Tn], ps[1][64:96, 0:Tn]), R=['psA1'], W=['KT%d' % (u0 // 512)])
                        for ct in range(2):
                            proj(ps[ct % 2 + 4], 320 + ct * 128, 128, 'psA%d' % (ct % 2 + 4))
                            srcv = ps[ct % 2 + 4][:, 0:Tn].rearrange("p (c i) -> p i c", i=8)
                            k.op('act', lambda ct=ct, srcv=srcv: A.copy(xds[:, ct, :, 0:Tn // 8], srcv), R=['psA%d' % (ct % 2 + 4)], W=['xds'])
                        xdv = xd_scr.rearrange("(c p) i n -> p c i n", p=128)
                        for ct in range(2):
                            k.dma(xdv[:, ct, :, u0 // 8:(u0 + Tn) // 8], xds[:, ct, :, 0:Tn // 8], R=['xds'], W=['xd_scr'])
                            if kind == 'ctx':
                                k.dma(xdv[:, ct, :, NU // 8:NE // 8], xds[:, ct, :, 0:Tn // 8], R=['xds'], W=['xd_scr'], q='pool')
                        for ct in range(2):
                            proj(ps[ct % 2 + 4], 576 + ct * 128, 128, 'psA%d' % (ct % 2 + 4))
                            k.op('act', lambda ct=ct: A.copy(plo[:, ct, 0:Tn], ps[ct % 2 + 4][:, 0:Tn]), R=['psA%d' % (ct % 2 + 4)], W=['plo'])
                        si = 0 if kind == 'ctx' else 1
                        k.dma(pool_scr[si][:, 8 + t0:8 + t0 + Tn].rearrange("(c p) n -> p c n", p=128), plo[:, :, 0:Tn], R=['plo'], W=['pool_scr'])

                if 'A' in dbg and l == 0:
                    d1 = dbg_out('KT', (128, 4 * NU), BF16)
                    k.dma(d1, KT[:].rearrange("p a b -> p (a b)"), R=['KT%d' % i for i in range(17)], W=['dbg1'])
                    d2 = dbg_out('Vt', (128, (NU // 128) * 4 * 65), BF16)
                    k.dma(d2, Vt[:].rearrange("p a b c -> p (a b c)"), R=['Vt%d' % i for i in range(17)], W=['dbg2'])
                    d3 = dbg_out('Xd', (256, 8 * NSC), BF16)
                    k.dma(d3, xd_scr.rearrange("a b c -> a (b c)"), R=['xd_scr'], W=['dbg3'])
                    d4 = dbg_out('poolL', (256, SEQ + 16), BF16)
                    k.dma(d4, pool_scr[1], R=['pool_scr'], W=['dbg4'])
                    break

            k.barrier()
            with ExitStack() as sB:
                Bin8T = T(sB, 'Bin8T', (128, 32, 128), BF16)
                Cout8T = T(sB, 'Cout8T', (128, 32, 128), BF16)
                D8T = T(sB, 'D8T', (128, 32, 128), BF16)
                LPr = T(sB, 'LPr', (128, 32), F32)
                LPi = T(sB, 'LPi', (128, 32), F32)
                sgn = T(sB, 'sgn', (128, 1), F32)
                swap_f = T(sB, 'swap_f', (128, 128), F32)
                s5dT = T(sB, 's5dT', (128, 2), F32)
                bgluT = T(sB, 'bgluT', (128, 2), F32)
                wglu = T(sB, 'wglu', (128, 2, 256), BF16)
                k.dma(swap_f[:], cst['swapm'], W=['swap_f'])
                k.op('pool', lambda: G.memset(sgn[0:64, :], 1.0), W=['sgn'])
                k.op('pool', lambda: G.memset(sgn[64:128, :], -1.0), W=['sgn'])
                k.dma(s5dT[:], prm['s5_d'][l].rearrange("(c p) -> p c", p=128), W=['s5dT'], allow_slow_non_contiguous=True)
                k.dma(bgluT[:], prm['b_glu'][l].rearrange("(c p) -> p c", p=128), W=['bgluT'], allow_slow_non_contiguous=True)
                TK = ['s5t']
                with ExitStack() as sT:
                    def t32(name):
                        return T(sT, name, (128, 32), F32)
                    LR, LI, DT_, ar_, ai_ = t32('LR'), t32('LI'), t32('DT_'), t32('ar_'), t32('ai_')
                    t1, t2, t3, t4 = t32('t1'), t32('t2'), t32('t3'), t32('t4')
                    er, ei, kr_, ki_, den = t32('er'), t32('ei'), t32('kr_'), t32('ki_'), t32('den')
                    lnr, lni = t32('lnr'), t32('lni')
                    halfpi = T(sT, 'halfpi', (128, 1), F32)
                    PWr = T(sT, 'PWr', (128, 16, 32), F32)
                    PWi = T(sT, 'PWi', (128, 16, 32), F32)
                    BR = T(sT, 'BR', (128, 32, 16), F32)
                    BI = T(sT, 'BI', (128, 32, 16), F32)
                    CR = T(sT, 'CR', (128, 32, 16), F32)
                    CI = T(sT, 'CI', (128, 32, 16), F32)
                    bbr = T(sT, 'bbr', (128, 32, 16), F32)
                    bbi = T(sT, 'bbi', (128, 32, 16), F32)
                    u1 = T(sT, 'u1', (128, 16, 16), F32)
                    u2 = T(sT, 'u2', (128, 16, 16), F32)
                    CRn = T(sT, 'CRn', (128, 2, 64), F32)
                    Bin8 = T(sT, 'Bin8', (128, 32, 128), F32)
                    CoutN = T(sT, 'CoutN', (128, 32, 128), F32)
                    wg32 = T(sT, 'wg32', (128, 2, 256), F32)
                    mlow = T(sT, 'mlow', (128, 128), F32)
                    mup = T(sT, 'mup', (128, 128), F32)
                    psT = P(sT, 'psT', (128, 128))

                    def dv(fn):
                        k.op('dve', fn, R=TK, W=TK)

                    def ac(fn):
                        k.op('act', fn, R=TK, W=TK)

                    k.dma(mlow[:], cst['mlow'], W=TK)
                    k.dma(mup[:], cst['mup'], W=TK)
                    k.dma(wg32[:], prm['w_glu'][l].rearrange("(c p) n -> p c n", p=128), W=TK)
                    dv(lambda: V.tensor_copy(wglu[:], wg32[:]))
                    for hf in range(2):
                        sl = slice(hf * 64, hf * 64 + 64)
                        k.dma(LR[sl, :], prm['lam_re'][l].rearrange("d g p -> p (d g)"), W=TK, allow_slow_non_contiguous=True)
                        k.dma(LI[sl, :], prm['lam_im'][l].rearrange("d g p -> p (d g)"), W=TK, allow_slow_non_contiguous=True, q='pool')
                        k.dma(BR[sl], prm['s5_b_re'][l].rearrange("d g p h -> p (d g) h"), W=TK)
                        k.dma(BI[sl], prm['s5_b_im'][l].rearrange("d g p h -> p (d g) h"), W=TK, q='pool')
                    k.dma(DT_[:], prm['log_dt'][l].rearrange("d g -> (d g)").partition_broadcast(128), W=TK)
                    k.op('pool', lambda: G.memset(halfpi[:], float(np.pi / 2)), W=TK)
                    for ci, (cn, Ct) in enumerate((('s5_c_re', CR), ('s5_c_im', CI))):
                        crow = prm[cn][l].rearrange("d g h p -> (d g h) p")
                        for j in range(4):
                            k.dma(CRn[:, 0, :], crow[j * 128:(j + 1) * 128, :], W=TK)
                            k.dma(CRn[:, 1, :], crow[j * 128:(j + 1) * 128, :], W=TK, q='pool')
                            k.op('pe', lambda: PE.transpose(psT[:], CRn[:].rearrange("r a p -> r (a p)"), ident_f[:]), R=TK + ['ident_f'], W=['psT'])
                            k.op('dve', lambda j=j, Ct=Ct: V.tensor_copy(Ct[:, j * 8:(j + 1) * 8, :].rearrange("q a h -> q (a h)"), psT[:]), R=['psT'] + TK, W=TK)
                    ac(lambda: A.activation(DT_[:], DT_[:], AF.Exp))
                    dv(lambda: V.tensor_tensor(ar_[:], LR[:], DT_[:], op=ALU.mult))
                    dv(lambda: V.tensor_tensor(ai_[:], LI[:], DT_[:], op=ALU.mult))
                    ac(lambda: A.activation(t1[:], ar_[:], AF.Exp, scale=1.0 / 16))
                    ac(lambda: A.activation(t2[:], ai_[:], AF.Sin, scale=1.0 / 16, bias=halfpi[:, 0:1]))
                    ac(lambda: A.activation(t3[:], ai_[:], AF.Sin, scale=1.0 / 16))
                    dv(lambda: V.tensor_tensor(er[:], t1[:], t2[:], op=ALU.mult))
                    dv(lambda: V.tensor_tensor(ei[:], t1[:], t3[:], op=ALU.mult))

                    def cmul(or_, oi_, xr, xi, yr, yi, a1=None, a2=None, a3=None, a4=None):
                        a1, a2, a3, a4 = t1[:], t2[:], t3[:], t4[:]
                        dv(lambda: V.tensor_tensor(a1, xr, yr, op=ALU.mult))
                        dv(lambda: V.tensor_tensor(a2, xi, yi, op=ALU.mult))
                        dv(lambda: V.tensor_tensor(a3, a1, a2, op=ALU.subtract))
                        dv(lambda: V.tensor_tensor(a1, xr, yi, op=ALU.mult))
                        dv(lambda: V.tensor_tensor(a2, xi, yr, op=ALU.mult))
                        dv(lambda: V.tensor_tensor(a4, a1, a2, op=ALU.add))
                        dv(lambda: V.tensor_copy(or_, a3))
                        dv(lambda: V.tensor_copy(oi_, a4))

                    for _ in range(4):
                        cmul(er[:], ei[:], er[:], ei[:], er[:], ei[:], t1[:], t2[:], t3[:], t4[:])
                    dv(lambda: V.tensor_scalar_add(lnr[:], er[:], -1.0))
                    dv(lambda: V.tensor_tensor(t1[:], LR[:], LR[:], op=ALU.mult))
                    dv(lambda: V.tensor_tensor(t2[:], LI[:], LI[:], op=ALU.mult))
                    dv(lambda: V.tensor_tensor(den[:], t1[:], t2[:], op=ALU.add))
                    dv(lambda: V.reciprocal(den[:], den[:]))
                    dv(lambda: V.tensor_tensor(t1[:], lnr[:], LR[:], op=ALU.mult))
                    dv(lambda: V.tensor_tensor(t2[:], ei[:], LI[:], op=ALU.mult))
                    dv(lambda: V.tensor_tensor(t1[:], t1[:], t2[:], op=ALU.add))
                    dv(lambda: V.tensor_tensor(kr_[:], t1[:], den[:], op=ALU.mult))
                    dv(lambda: V.tensor_tensor(t1[:], ei[:], LR[:], op=ALU.mult))
                    dv(lambda: V.tensor_tensor(t2[:], lnr[:], LI[:], op=ALU.mult))
                    dv(lambda: V.tensor_tensor(t1[:], t1[:], t2[:], op=ALU.subtract))
                    dv(lambda: V.tensor_tensor(ki_[:], t1[:], den[:], op=ALU.mult))
                    bk = lambda a: a[:].unsqueeze(2).to_broadcast([128, 32, 16])
                    w1 = Bin8[:, :, 0:16]
                    w2 = Bin8[:, :, 16:32]
                    dv(lambda: V.tensor_tensor(w1, BR[:], bk(kr_), op=ALU.mult))
                    dv(lambda: V.tensor_tensor(w2, BI[:], bk(ki_), op=ALU.mult))
                    dv(lambda: V.tensor_tensor(bbr[:], w1, w2, op=ALU.subtract))
                    dv(lambda: V.tensor_tensor(w1, BI[:], bk(kr_), op=ALU.mult))
                    dv(lambda: V.tensor_tensor(w2, BR[:], bk(ki_), op=ALU.mult))
                    dv(lambda: V.tensor_tensor(bbi[:], w1, w2, op=ALU.add))
                    dv(lambda: V.memset(PWr[:, 7, :], 1.0))
                    dv(lambda: V.memset(PWi[:, 7, :], 0.0))
                    for e in range(1, 9):
                        cmul(PWr[:, 7 + e, :], PWi[:, 7 + e, :], PWr[:, 6 + e, :], PWi[:, 6 + e, :], er[:], ei[:])
                    ac(lambda: A.activation(den[:], ar_[:], AF.Exp, scale=-2.0))
                    dv(lambda: V.tensor_tensor(lnr[:], er[:], den[:], op=ALU.mult))
                    dv(lambda: V.tensor_tensor(lni[:], ei[:], den[:], op=ALU.mult))
                    dv(lambda: V.tensor_scalar_mul(lni[:], lni[:], -1.0))
                    for e in range(1, 8):
                        cmul(PWr[:, 7 - e, :], PWi[:, 7 - e, :], PWr[:, 8 - e, :], PWi[:, 8 - e, :], lnr[:], lni[:])
                    dv(lambda: V.tensor_copy(LPr[:], PWr[:, 15, :]))
                    dv(lambda: V.tensor_copy(LPi[:], PWi[:, 15, :]))

                    def ctab(dst, Xr, Xi, d, slot, e, im_sign):
                        gs = slice(d * 16, d * 16 + 8)
                        for hf in range(2):
                            ps_ = slice(hf * 64, hf * 64 + 64)
                            pr = PWr[ps_, 7 + e, gs].unsqueeze(2).to_broadcast([64, 8, 16])
                            pi = PWi[ps_, 7 + e, gs].unsqueeze(2).to_broadcast([64, 8, 16])
                            o = dst[ps_, gs, slot * 16:(slot + 1) * 16]
                            if hf == 0:
                                dv(lambda: V.tensor_tensor(u1[ps_, 0:8], Xr[ps_, gs, :], pr, op=ALU.mult))
                                dv(lambda: V.tensor_tensor(u2[ps_, 0:8], Xi[ps_, gs, :], pi, op=ALU.mult))
                                dv(lambda: V.tensor_tensor(o, u1[ps_, 0:8], u2[ps_, 0:8], op=ALU.subtract))
                            else:
                                dv(lambda: V.tensor_tensor(u1[ps_, 0:8], Xr[ps_, gs, :], pi, op=ALU.mult))
                                dv(lambda: V.tensor_tensor(u2[ps_, 0:8], Xi[ps_, gs, :], pr, op=ALU.mult))
                                dv(lambda: V.tensor_tensor(o, u1[ps_, 0:8], u2[ps_, 0:8], op=ALU.add))
                                if im_sign < 0:
                                    dv(lambda: V.tensor_scalar_mul(o, o, -1.0))

                    for d in range(2):
                        for i in range(8):
                            ctab(Bin8, bbr, bbi, d, i, (7 - i) if d == 0 else i, +1)
                            ctab(Cout8T, CR, CI, d, i, (i + 1) if d == 0 else (8 - i), -1)
                            ctab(CoutN, CR, CI, d, i, (i - 7) if d == 0 else (-i), -1)
                    for dg in [d_ * 16 + g_ for d_ in range(2) for g_ in range(8)]:
                        k.op('pe', lambda dg=dg: PE.transpose(psT[:], Bin8[:, dg, :], ident_f[:]), R=TK + ['ident_f'], W=['psT'])
                        k.op('act', lambda dg=dg: A.copy(Bin8T[:, dg, :], psT[:]), R=['psT'], W=['Bin8T'])
                        k.op('pe', lambda dg=dg: PE.matmul(psT[:], lhsT=Bin8[:, dg, :], rhs=CoutN[:, dg, :], start=True, stop=True), R=TK, W=['psT'])
                        mk = mlow if dg < 16 else mup
                        k.op('dve', lambda dg=dg, mk=mk: V.tensor_tensor(D8T[:, dg, :], psT[:], mk[:], op=ALU.mult), R=['psT'] + TK, W=['D8T'])

                    psd = P(sT, 'psd', (128, 128), BF16)
                    k.op('pe', lambda: PE.transpose(psd[:], ident_b[:], ident_b[:]), R=['ident_b', 'psT'], W=['psd'])

                if 'B0' in dbg and l == 0:
                    for nm, tl in (('Bin8T', Bin8T), ('Cout8T', Cout8T), ('D8T', D8T)):
                        d_ = dbg_out(nm, (128, 32 * 128), BF16)
                        k.dma(d_, tl[:].rearrange("p a b -> p (a b)"), R=[nm], W=['dbg' + nm])
                    d_ = dbg_out('LP', (128, 64))
                    k.dma(d_[:, 0:32], LPr[:], R=TK, W=['dbgLP'])
                    k.dma(d_[:, 32:64], LPi[:], R=TK, W=['dbgLP'])
                    break

                k.barrier()
                with ExitStack() as sS:
                    IM = T(sS, 'IM', (128, 2, NSC), BF16)
                    H = T(sS, 'H', (128, 4, NSC + 4), F32)
                    Hb = T(sS, 'Hb', (128, 4, NSC + 4), BF16)
                    Yo = T(sS, 'Yo', (128, 2, NSC), BF16)
                    ATb = [T(sS, 'AT%d' % i, (128, 11, 4, 128), BF16) for i in range(2)]
                    atmp = T(sS, 'atmp', (128, 128), F32)
                    v2 = T(sS, 'v2', (128, 32), F32)
                    LQr = T(sS, 'LQr', (128, 11, 32), F32)
                    LQi = T(sS, 'LQi', (128, 11, 32), F32)
                    q1, q2 = T(sS, 'q1', (128, 32), F32), T(sS, 'q2', (128, 32), F32)
                    psS = [[P(sS, 'psS%d_%d' % (i, j), (128, 512)) for j in range(4)] for i in range(2)]
                    GO = 2
                    NLEV = int(os.environ.get('DBG_NLEV', 11))
                    NRND = int(os.environ.get('DBG_NRND', 4))
                    DOREAD = int(os.environ.get('DBG_READ', 1))
                    k.op('dve', lambda: V.tensor_copy(LQr[:, 0, :], LPr[:]), R=TK, W=['LQ'])
                    k.op('dve', lambda: V.tensor_copy(LQi[:, 0, :], LPi[:]), R=TK, W=['LQ'])
                    for lev in range(1, NLEV):
                        a_r, a_i = LQr[:, lev - 1, :], LQi[:, lev - 1, :]
                        k.op('dve', lambda: V.tensor_tensor(q1[:], a_r, a_r, op=ALU.mult), R=['LQ'], W=['q1'])
                        k.op('dve', lambda: V.tensor_tensor(q2[:], a_i, a_i, op=ALU.mult), R=['LQ'], W=['q2'])
                        k.op('dve', lambda lev=lev: V.tensor_tensor(LQr[:, lev, :], q1[:], q2[:], op=ALU.subtract), R=['q1', 'q2'], W=['LQ'])
                        k.op('dve', lambda: V.tensor_tensor(q1[:], a_r, a_i, op=ALU.mult), R=['LQ'], W=['q1'])
                        k.op('dve', lambda lev=lev: V.tensor_scalar_mul(LQi[:, lev, :], q1[:], 2.0), R=['q1'], W=['LQ'])
                    if int(os.environ.get('DBG_MS', 1)):
                        k.op('dve', lambda: V.memset(H[:], 0.0), W=['H%d' % j for j in range(4)])
                        k.op('pool', lambda: G.memset(Hb[:], 0.0), W=['Hb%d' % j for j in range(4)])
                    xdg = xd_scr.rearrange("(g h) i n -> g h i n", h=16)
                    ydg = yd_scr.rearrange("(g h) i n -> g h i n", h=16)
                    CB = [(0, 512), (512, 1024), (1024, NSC)]
                    for rnd in range(NRND):
                        gl_ = [2 * rnd, 2 * rnd + 1]
                        combos = [(d, g) for g in gl_ for d in range(2)]
                        if int(os.environ.get('DBG_IMZ', 0)):
                            k.op('pool', lambda: G.memset(IM[:], 0.5), W=['IM'])
                        for gi, g in enumerate(gl_ if int(os.environ.get('DBG_IM', 1)) else []):
                            for i in range(8):
                                k.dma(IM[16 * i:16 * i + 16, gi, :], xdg[g, :, i, :], R=['xd_scr'], W=['IM'], q=('sp' if i % 2 == 0 else 'pool'))
                        ATc, ATk = ATb[rnd % 2], 'AT%d' % (rnd % 2)
                        for lev in range(NLEV):
                            k.op('dve', lambda lev=lev: V.tensor_scalar_mul(v2[:], LQi[:, lev, :], sgn[:, 0:1]), R=['LQ', 'sgn'], W=['v2'])
                            for j, (d, g) in enumerate(combos):
                                dg = d * 16 + g
                                k.op('dve', lambda lev=lev, dg=dg: V.tensor_scalar_mul(atmp[:], ident_f[:], LQr[:, lev, dg:dg + 1]), R=['LQ', 'ident_f'], W=['atmp'])
                                k.op('dve', lambda lev=lev, dg=dg, j=j: V.scalar_tensor_tensor(ATc[:, lev, j, :], swap_f[:], v2[:, dg:dg + 1], atmp[:],
                                                                                              op0=ALU.mult, op1=ALU.add), R=['swap_f', 'v2', 'atmp'], W=[ATk])
                        for bi_, (c0, c1) in enumerate(CB):
                            pss = psS[bi_ % 2]
                            for j, (d, g) in enumerate(combos):
                                pk = 'psS%d_%d' % (bi_ % 2, j)
                                k.op('pe', lambda j=j, d=d, g=g: PE.matmul(pss[j][:, 0:c1 - c0], lhsT=Bin8T[:, d * 16 + g, :], rhs=IM[:, j // 2, c0:c1],
                                                                            start=True, stop=True), R=['Bin8T', 'IM'], W=[pk])
                                k.op('dve', lambda j=j: V.tensor_copy(H[:, j, GO + c0:GO + c1], pss[j][:, 0:c1 - c0]), R=[pk], W=['H%d' % j])
                                k.op('act', lambda j=j: A.copy(Hb[:, j, GO + c0:GO + c1], H[:, j, GO + c0:GO + c1]), R=['H%d' % j], W=['Hb%d' % j])
                        for lev in range(NLEV):
                            s_ = 1 << lev
                            nb = (NSC - s_ + 511) // 512
                            for bi_ in range(nb):
                                pss = psS[bi_ % 2]
                                hi_f = NSC - bi_ * 512
                                lo_f = max(s_, hi_f - 512)
                                lo_b = bi_ * 512
                                hi_b = min(NSC - s_, lo_b + 512)
                                for j, (d, g) in enumerate(combos):
                                    pk = 'psS%d_%d' % (bi_ % 2, j)
                                    lo_, hi_ = (lo_f, hi_f) if d == 0 else (lo_b, hi_b)
                                    sh_ = -s_ if d == 0 else s_
                                    k.op('pe', lambda j=j: PE.matmul(pss[j][:, 0:hi_ - lo_], lhsT=ATc[:, lev, j, :], rhs=Hb[:, j, GO + lo_ + sh_:GO + hi_ + sh_],
                                                                    start=True, stop=True), R=[ATk, 'Hb%d' % j], W=[pk])
                                    k.op('dve', lambda j=j, lo_=lo_, hi_=hi_: V.tensor_tensor(H[:, j, GO + lo_:GO + hi_], H[:, j, GO + lo_:GO + hi_], pss[j][:, 0:hi_ - lo_], op=ALU.add),
                                         R=[pk, 'H%d' % j], W=['H%d' % j])
                                    if j % 2 == 0:
                                        k.op('act', lambda j=j, lo_=lo_, hi_=hi_: A.copy(Hb[:, j, GO + lo_:GO + hi_], H[:, j, GO + lo_:GO + hi_]), R=['H%d' % j], W=['Hb%d' % j])
                                    else:
                                        k.op('pool', lambda j=j, lo_=lo_, hi_=hi_: G.tensor_copy(Hb[:, j, GO + lo_:GO + hi_], H[:, j, GO + lo_:GO + hi_]), R=['H%d' % j], W=['Hb%d' % j])
                        RB = [(0, 32, (0,)), (32, 544, (0, 1)), (544, 1056, (0, 1)), (1056, NSC, (1,))]
                        for bi_, (c0, c1, dirs) in enumerate(RB):
                            pss = psS[bi_ % 2]
                            for gi, g in enumerate(gl_):
                                pk = 'psS%d_%d' % (bi_ % 2, gi)
                                nmm = 2 * len(dirs)
                                mi = 0
                                for d in dirs:
                                    j = 2 * gi + d
                                    sh = -1 if d == 0 else 1
                                    k.op('pe', lambda gi=gi, d=d, g=g, mi=mi: PE.matmul(pss[gi][:, 0:c1 - c0], lhsT=D8T[:, d * 16 + g, :], rhs=IM[:, gi, c0:c1],
                                                                                    start=(mi == 0), stop=False), R=['D8T', 'IM'], W=[pk])
                                    mi += 1
                                    k.op('pe', lambda gi=gi, d=d, g=g, j=j, sh=sh, mi=mi: PE.matmul(pss[gi][:, 0:c1 - c0], lhsT=Cout8T[:, d * 16 + g, :],
                                                                                                rhs=Hb[:, j, GO + c0 + sh:GO + c1 + sh],
                                                                                                start=False, stop=(mi == nmm - 1)), R=['Cout8T', 'Hb%d' % j], W=[pk])
                                    mi += 1
                                k.op('act', lambda gi=gi: A.copy(Yo[:, gi, c0:c1], pss[gi][:, 0:c1 - c0]), R=[pk], W=['Yo'])
                        for gi, g in enumerate(gl_ if int(os.environ.get('DBG_YD', 1)) else []):
                            for i in range(8):
                                k.dma(ydg[g, :, i, :], Yo[16 * i:16 * i + 16, gi, :], R=['Yo'], W=['yd_scr'], q=('sp' if i % 2 == 0 else 'pool'))

                if nlayers > 1 or int(os.environ.get('DBG_CC', 0)):
                    k.barrier()
                    k._sync('pool', ['yd_scr'], ['ydg'])
                    ydf = yd_scr.rearrange("c i n -> c (i n)")
                    for c_ in range(2):
                        G.collective_compute("AllGather", ALU.bypass, replica_groups=[[0, 1], [2, 3], [4, 5], [6, 7]],
                                             ins=[ydf[c_ * 64:(c_ + 1) * 64, :]], outs=[ydg_scr[c_]]).then_inc(ccsem)
                    ccn[0] += 2
                    for eng in k.eng.values():
                        eng.wait_ge(ccsem, ccn[0])
                    with ExitStack() as sX:
                        ya = T(sX, 'ya', (128, 8 * NSC), BF16)
                        yb = T(sX, 'yb', (128, 8 * NSC), BF16)
                        for c_ in range(2):
                            k.dma(ya[c_ * 64:(c_ + 1) * 64, :], ydg_scr[c_][64:128, :], W=['ya'], q=('sp' if c_ == 0 else 'pool'))
                            k.dma(yb[c_ * 64:(c_ + 1) * 64, :], ydg_scr[c_][0:64, :], W=['yb'], q=('sp' if c_ == 0 else 'pool'))
                        k.op('dve', lambda: V.tensor_scalar_mul(ya[:], ya[:], msb[:, 0:1]), R=['ya', 'msb'], W=['ya'])
                        k.op('dve', lambda: V.scalar_tensor_tensor(ya[:], yb[:], msb[:, 1:2], ya[:], op0=ALU.mult, op1=ALU.add), R=['ya', 'yb', 'msb'], W=['ya'])
                        k.dma(ydf[128:256, :], ya[:], R=['ya'], W=['yd_scr'])
                    k.barrier()

                if 'B1' in dbg and l == 0:
                    d_ = dbg_out('yd', (256, 8 * NSC), BF16)
                    if int(os.environ.get('DBG_YDD', 1)):
                        k.dma(d_, yd_scr.rearrange("a b c -> a (b c)"), R=['yd_scr'], W=['dbgyd'])
                    break

                k.barrier()
                with ExitStack() as sG:
                    xdt = T(sG, 'xdt', (128, 2, 8, 64), BF16)
                    ydt = T(sG, 'ydt', (128, 2, 8, 64), BF16)
                    yd2 = T(sG, 'yd2', (128, 2, 8, 64), BF16)
                    yf = T(sG, 'yf', (128, 2, 512), F32)
                    g1 = T(sG, 'g1', (128, 2, 512), F32)
                    g2 = T(sG, 'g2', (128, 2, 512), F32)
                    glb = T(sG, 'glb', (128, 2, 512), BF16)
                    sgm = T(sG, 'sgm', (128, 2, 512), F32)
                    s5ob = T(sG, 's5ob', (128, 2, 512), BF16)
                    psG = [P(sG, 'psG%d' % i, (128, 512)) for i in range(2)]
                    xdv = xd_scr.rearrange("(c p) i n -> p c i n", p=128)
                    ydv = yd_scr.rearrange("(c p) i n -> p c i n", p=128)
                    xdtB = T(sG, 'xdtB', (128, 2, 8, 64), BF16)
                    ydtB = T(sG, 'ydtB', (128, 2, 8, 64), BF16)
                    blocks = [('ctx', 0, NCTX)] + [('lat', 512 * i, 512) for i in range(HALF // 512)]
                    for (kind, t0, Tn) in blocks:
                        u0 = t0 if kind == 'ctx' else NCTX + t0
                        n8 = Tn // 8
                        for ct in range(2):
                            k.dma(xdt[:, ct, :, 0:n8], xdv[:, ct, :, u0 // 8:u0 // 8 + n8], R=['xd_scr'], W=['xdt'])
                            k.dma(ydt[:, ct, :, 0:n8], ydv[:, ct, :, u0 // 8:u0 // 8 + n8], R=['yd_scr'], W=['ydt'], q='pool')
                            if kind == 'lat':
                                uB = u0 + HALF
                                k.dma(xdtB[:, ct, :, 0:n8], xdv[:, ct, :, uB // 8:uB // 8 + n8], R=['xd_scr'], W=['xdtB'], q='pool')
                                k.dma(ydtB[:, ct, :, 0:n8], ydv[:, ct, :, uB // 8:uB // 8 + n8], R=['yd_scr'], W=['ydtB'])
                            if kind == 'ctx':
                                k.dma(yd2[:, ct, :, 0:n8], ydv[:, ct, :, NU // 8:NU // 8 + n8], R=['yd_scr'], W=['yd2'])
                        if kind == 'ctx':
                            k.op('pool', lambda: G.tensor_tensor(ydt[:, :, :, 0:n8], ydt[:, :, :, 0:n8], yd2[:, :, :, 0:n8], op=ALU.add), R=['ydt', 'yd2'], W=['ydt'])
                        else:
                            for (ta_, tb_, ka_, kb_) in ((xdt, xdtB, 'xdt', 'xdtB'), (ydt, ydtB, 'ydt', 'ydtB')):
                                k.op('dve', lambda ta_=ta_: V.tensor_scalar_mul(ta_[:], ta_[:], msb[:, 0:1]), R=[ka_, 'msb'], W=[ka_])
                                k.op('dve', lambda ta_=ta_, tb_=tb_: V.scalar_tensor_tensor(ta_[:], tb_[:], msb[:, 1:2], ta_[:], op0=ALU.mult, op1=ALU.add), R=[ka_, kb_, 'msb'], W=[ka_])
                        for ct in range(2):
                            k.op('dve', lambda ct=ct: V.scalar_tensor_tensor(yf[:, ct, 0:Tn].rearrange("p (c i) -> p i c", i=8), xdt[:, ct, :, 0:n8], s5dT[:, ct:ct + 1],
                                                                             ydt[:, ct, :, 0:n8], op0=ALU.mult, op1=ALU.add), R=['xdt', 'ydt', 's5dT'], W=['yf'])
                        if 'B2y' in dbg and l == 0:
                            if kind == 'ctx':
                                dyl = dbg_out('yl', (256, NU))
                            k.dma(dyl[:, u0:u0 + Tn].rearrange("(c p) n -> p c n", p=128), yf[:, :, 0:Tn], R=['yf'], W=['dbgyl'])
                        k.op('pool', lambda: G.tensor_tensor(g1[:, :, 0:Tn], yf[:, :, 0:Tn], yf[:, :, 0:Tn], op=ALU.mult), R=['yf'], W=['g1'])
                        k.op('dve', lambda: V.tensor_scalar(g1[:, :, 0:Tn], g1[:, :, 0:Tn], 0.044715, 1.0, op0=ALU.mult, op1=ALU.add), R=['g1'], W=['g1'])
                        k.op('pool', lambda: G.tensor_tensor(g2[:, :, 0:Tn], g1[:, :, 0:Tn], yf[:, :, 0:Tn], op=ALU.mult), R=['g1', 'yf'], W=['g2'])
                        k.op('act', lambda: A.activation(g2[:, :, 0:Tn], g2[:, :, 0:Tn], AF.Sigmoid, scale=1.5957691216057308), R=['g2'], W=['g2'])
                        k.op('dve', lambda: V.tensor_tensor(glb[:, :, 0:Tn], yf[:, :, 0:Tn], g2[:, :, 0:Tn], op=ALU.mult), R=['yf', 'g2'], W=['glb'])
                        for m in range(2):
                            for kt in range(2):
                                k.op('pe', lambda m=m, kt=kt: PE.matmul(psG[m][:, 0:Tn], lhsT=wglu[:, kt, m * 128:(m + 1) * 128], rhs=glb[:, kt, 0:Tn],
                                                                        start=(kt == 0), stop=(kt == 1)), R=['wglu', 'glb'], W=['psG%d' % m])
                            k.op('act', lambda m=m: A.activation(sgm[:, m, 0:Tn], psG[m][:, 0:Tn], AF.Sigmoid, bias=bgluT[:, m:m + 1]), R=['psG%d' % m, 'bgluT'], W=['sgm'])
                        k.op('dve', lambda: V.tensor_tensor(s5ob[:, :, 0:Tn], glb[:, :, 0:Tn], sgm[:, :, 0:Tn], op=ALU.mult), R=['glb', 'sgm'], W=['s5ob'])
                        k.dma(s5o_scr[:, u0:u0 + Tn].rearrange("(c p) n -> p c n", p=128), s5ob[:, :, 0:Tn], R=['s5ob'], W=['s5o_scr'])

                if 'B2' in dbg and l == 0:
                    d_ = dbg_out('s5o', (256, NU), BF16)
                    k.dma(d_, s5o_scr, R=['s5o_scr'], W=['dbgs5o'])
                    break

            k.barrier()
            with ExitStack() as sC:
                wC = T(sC, 'wC', (128, 8, 1728), BF16)
                wout = T(sC, 'wout', (128, 8, 1024), BF16)
                wqh = T(sC, 'wqh', (128, 2, 4, 96), BF16)
                wqr = T(sC, 'wqr', (128, 2, 4, 96), BF16)
                wpl = T(sC, 'wpl', (128, 2, 128), BF16)
                wsT = T(sC, 'wsT', (128, 4, 128), BF16)
                bsb = T(sC, 'bsb', (128, 4, 128), F32)
                gate_bc = T(sC, 'gate_bc', (128, D), F32)
                lng_bc = T(sC, 'lng_bc', (128, D), F32)
                lnb_bc = T(sC, 'lnb_bc', (128, D), F32)
                colv = T(sC, 'colv', (128, 8), F32)
                winv = T(sC, 'winv', (128, 2), F32)
                with ExitStack() as sw:
                    wst2 = [T(sw, 'wstC%d' % i, (128, 8, 256), F32) for i in range(2)]
                    wq32 = T(sw, 'wq32', (128, 2, 384), F32)
                    wp32 = T(sw, 'wp32', (128, 2, 128), F32)
                    ws32 = T(sw, 'ws32', (128, 128), F32)
                    psw = P(sw, 'psw', (128, 128))
                    win = prm['w_in'][l].rearrange("(c p) n -> p c n", p=128)
                    segs = [(416, 192, 0), (864, 256, 192), (1120, 256, 448), (1376, 256, 704), (1632, 256, 960), (1888, 256, 1216), (2144, 256, 1472)]
                    for si_, (c0, n, d0) in enumerate(segs):
                        wst, wk = wst2[si_ % 2], 'wstC%d' % (si_ % 2)
                        k.dma(wst[:, :, 0:n], win[:, :, c0:c0 + n], W=[wk], q=('sp' if si_ % 2 == 0 else 'pool'))
                        k.op('dve' if si_ % 2 == 0 else 'pool', lambda n=n, d0=d0, wst=wst, si_=si_: (V if si_ % 2 == 0 else G).tensor_copy(wC[:, :, d0:d0 + n], wst[:, :, 0:n]), R=[wk], W=['wC'])
                    wo = prm['w_out'][l].rearrange("(c p) n -> p c n", p=128)
                    for j in range(4):
                        wst, wk = wst2[(j + 1) % 2], 'wstC%d' % ((j + 1) % 2)
                        k.dma(wst[:], wo[:, :, j * 256:(j + 1) * 256], W=[wk], q=('sp' if j % 2 == 0 else 'pool'))
                        k.op('dve' if j % 2 == 0 else 'pool', lambda j=j, wst=wst: (V if j % 2 == 0 else G).tensor_copy(wout[:, :, j * 256:(j + 1) * 256], wst[:]), R=[wk], W=['wout'])
                    for ci, nm in enumerate(('sgu_g', 'sgu_b', 'pool_scale')):
                        k.dma(colv[:, 2 * ci:2 * ci + 2], prm[nm][l].rearrange("(c p) -> p c", p=128), W=['colv'], allow_slow_non_contiguous=True)
                    k.dma(colv[:, 6:7], prm['g_q'][l][0:128].rearrange("(p o) -> p o", o=1), W=['colv'])
                    k.dma(colv[0:64, 7:8], prm['g_q'][l][128:192].rearrange("(p o) -> p o", o=1), W=['colv'])
                    k.op('pool', lambda: G.memset(winv[0:64, 0:1], 1.0 / 2), W=['winv'])
                    k.op('pool', lambda: G.memset(winv[64:128, 0:1], 1.0 / 4), W=['winv'])
                    k.op('pool', lambda: G.memset(winv[0:64, 1:2], 1.0 / 8), W=['winv'])
                    k.op('pool', lambda: G.memset(winv[64:128, 1:2], 1.0 / 16), W=['winv'])
                    k.dma(bsb[:].rearrange("p h t -> p (h t)"), prm['b_s'][l].rearrange("h t -> (h t)").partition_broadcast(128), W=['bsb'])
                    k.dma(lng_bc[:], prm['ln_g'][l].partition_broadcast(128), W=['lng_bc'])
                    k.dma(lnb_bc[:], prm['ln_b'][l].partition_broadcast(128), W=['lnb_bc'], q='pool')
                    k.op('pool', lambda: G.memset(wq32[:], 0.0), W=['wq32'])
                    k.dma(wq32[:, 0, :], prm['w_uq'][l][0:128, :], W=['wq32'])
                    k.dma(wq32[0:64, 1, :], prm['w_uq'][l][128:192, :], W=['wq32'])
                    k.op('pool', lambda: G.memset(wqr[:], 0.0), W=['wqr'])
                    for kt in range(2):
                        rows = slice(0, 128) if kt == 0 else slice(0, 64)
                        k.op('dve', lambda kt=kt, rows=rows: V.tensor_scalar_mul(wq32[rows, kt, :], wq32[rows, kt, :], colv[rows, 6 + kt:7 + kt]), R=['wq32', 'colv'], W=['wq32'])
                    k.op('dve', lambda: V.tensor_copy(wqh[:].rearrange("p a h c -> p a (h c)"), wq32[:]), R=['wq32'], W=['wqh'])
                    wq4 = wq32[:].rearrange("p a (h c) -> p a h c", h=4)
                    for (d0, s0_) in ((0, 8), (8, 0), (16, 24), (24, 16)):
                        k.op('dve', lambda d0=d0, s0_=s0_: V.tensor_copy(wqr[:, :, :, 64 + d0:72 + d0], wq4[:, :, :, 64 + s0_:72 + s0_]), R=['wq32'], W=['wqr'])
                    k.op('pool', lambda: G.memset(wp32[:], 0.0), W=['wp32'])
                    for g in range(4):
                        r0 = (g % 2) * 64
                        k.dma(wp32[r0:r0 + 64, g // 2, r0:r0 + 64], prm['w_pool'][l][g], W=['wp32'])
                    k.op('dve', lambda: V.tensor_copy(wpl[:], wp32[:]), R=['wp32'], W=['wpl'])
                    for h in range(4):
                        k.dma(ws32[:], prm['w_s'][l][h], W=['ws32'])
                        k.op('pe', lambda: PE.transpose(psw[:], ws32[:], ident_f[:]), R=['ws32', 'ident_f'], W=['psw'])
                        k.op('act', lambda h=h: A.copy(wsT[:, h, :], psw[:]), R=['psw'], W=['wsT'])
                    psd2 = P(sw, 'psd2', (128, 128), BF16)
                    k.op('pe', lambda: PE.transpose(psd2[:], ident_b[:], ident_b[:]), R=['ident_b', 'psw'], W=['psd2'])
                k.barrier()

                with ExitStack() as sc:
                    xt = [T(sc, 'xtC0', (128, D), F32)] * 2
                    st6 = T(sc, 'st6C', (128, 2, 6), F32)
                    mv = T(sc, 'mvC', (128, 2), F32)
                    rstd = T(sc, 'rstdC', (128, 1), F32)
                    xn = T(sc, 'xnC', (128, D), BF16)
                    hT = T(sc, 'hTC', (128, 8, QB), BF16)
                    sq = T(sc, 'sqC', (128, 2, QB), BF16)
                    rms = T(sc, 'rmsC', (128, QB), F32)
                    QT = T(sc, 'QT', (128, 4, QB), BF16)
                    rc_t = T(sc, 'rc_tC', (128, QB), F32)
                    rs_t = T(sc, 'rs_tC', (128, QB), F32)
                    su = T(sc, 'su', (128, 2, QB), BF16)
                    svb = T(sc, 'svb', (128, 2, QB), BF16)
                    cqn = svb
                    mean = T(sc, 'mean', (128, QB), F32)
                    var = T(sc, 'var', (128, QB), F32)
                    q1t, q2t = mean, var
                    vnb = T(sc, 'vnb', (128, 2, QB), BF16)
                    vc = T(sc, 'vc', (128, 256), BF16)
                    mx = T(sc, 'mx', (128, 128), F32)
                    sg = T(sc, 'sg', (128, 8, QB), BF16)
                    PT = [T(sc, 'PT%d' % i, (128, 2 * QB), BF16) for i in range(3)]
                    osb = T(sc, 'osb', (128, 4, 65), F32)
                    rec = T(sc, 'rec', (128, 4, 1), F32)
                    att = T(sc, 'att', (128, 4, 256), BF16)
                    catg = sg
                    pin = T(sc, 'pin', (128, 2, QB + 16), BF16)
                    pa = T(sc, 'pa', (128, 2, QB + 16), F32)
                    svf = pa
                    pb = T(sc, 'pb', (128, 2, QB + 16), F32)
                    pld = T(sc, 'pld', (128, 2, QB), BF16)
                    s5t = T(sc, 's5t', (128, 2, QB), BF16)
                    rr = T(sc, 'rr', (128, D), F32)
                    hf32 = rr[:].rearrange("p (a b) -> p a b", a=8)
                    pt = P(sc, 'ptC', (128, 8, 128), BF16)
                    psC = [P(sc, 'psC%d' % i, (128, 512)) for i in range(4)]
                    psS_ = [P(sc, 'psSc%d' % i, (128, 512)) for i in range(2)]
                    psO = P(sc, 'psO', (128, 4, 65))
                    sm2c = [T(sc, 'sm2c_%d' % i, (128, 16), F32) for i in range(2)]
                    ptc2 = psC[3][:].bitcast(BF16).rearrange("p (a b) -> p a b", a=8)
                    LNC = {'xt': [(xt[0], 'xtC0'), (rr, 'rr')], 'xn': [(xn, 'xn'), (xn, 'xn')],
                           'pt': [(pt, 'ptC'), (ptc2, 'psC3')], 'sm': [(sm2c[0], 'sm2c_0'), (sm2c[1], 'sm2c_1')], 'hTk': 'hT', 'dve_mod': True}
                    assert QB <= 512 and NCTX <= QB
                    SCALE = 96 ** -0.5

                    qblocks = [('lat', QB * i, QB) for i in range(HALF // QB)]
                    if not last:
                        qblocks = [('ctx', 0, NCTX)] + qblocks
                    NQB = int(os.environ.get('DBG_NQB', len(qblocks)))
                    xcnt = 0

                    def do_ln(bi_):
                        kind_, t0_, Tn_ = qblocks[bi_]
                        src_ = cin if kind_ == 'ctx' else (xq_in if l == 0 else x1h_scr)
                        ln_block([src_[t0_ + sb_ * 128: t0_ + (sb_ + 1) * 128, :] for sb_ in range(Tn_ // 128)], ['x1w'],
                                 1 if kind_ == 'ctx' else 0, hT, LNC)

                    gcol = [None]
                    do_ln(0)
                    for bidx, (kind, t0, Tn) in enumerate(qblocks[:NQB]):
                        col = 1 if kind == 'ctx' else 0
                        src = cin if kind == 'ctx' else (xq_in if l == 0 else x1h_scr)
                        dst = (ctx1_scr if kind == 'ctx' else (out_ap if last else x1h_scr))
                        u0 = t0 if kind == 'ctx' else NCTX + t0
                        nsub = Tn // 128
                        nkt = (NCTX // 128) if kind == 'ctx' else (NU // 128)
                        if gcol[0] != col:
                            k.dma(gate_bc[:], gate_scr[l, col].partition_broadcast(128), R=['gate_scr'], W=['gate_bc'])
                            gcol[0] = col

                        def proj(pst, key, c0, ncol):
                            for dc in range(8):
                                k.op('pe', lambda dc=dc: PE.matmul(pst[0:ncol, 0:Tn], lhsT=wC[:, dc, c0:c0 + ncol], rhs=hT[:, dc, 0:Tn],
                                                                  start=(dc == 0), stop=(dc == 7)), R=['wC', 'hT'], W=[key])
                        proj(psC[0], 'psC0', 0, 128)
                        proj(psC[1], 'psC1', 128, 64)
                        k.op('act', lambda: A.activation(sq[:, 0, 0:Tn], psC[0][:, 0:Tn], AF.Square), R=['psC0'], W=['sq'])
                        k.op('act', lambda: A.activation(sq[0:64, 1, 0:Tn], psC[1][0:64, 0:Tn], AF.Square), R=['psC1'], W=['sq'])
                        k.op('pe', lambda: PE.matmul(psC[2][:, 0:Tn], lhsT=ones_b[:, :], rhs=sq[:, 0, 0:Tn], start=True, stop=False), R=['ones_b', 'sq'], W=['psC2'])
                        k.op('pe', lambda: PE.matmul(psC[2][:, 0:Tn], lhsT=ones_b[0:64, :], rhs=sq[0:64, 1, 0:Tn], start=False, stop=True), R=['ones_b', 'sq'], W=['psC2'])
                        k.op('act', lambda: A.activation(rms[:, 0:Tn], psC[2][:, 0:Tn], AF.Sqrt, bias=epsb[:, 0:1], scale=1.0 / 192), R=['psC2', 'epsb'], W=['rms'])
                        k.op('dve', lambda: V.reciprocal(rms[:, 0:Tn], rms[:, 0:Tn]), R=['rms'], W=['rms'])
                        k.op('dve', lambda: V.tensor_tensor(cqn[:, 0, 0:Tn], psC[0][:, 0:Tn], rms[:, 0:Tn], op=ALU.mult), R=['psC0', 'rms'], W=['svb'])
                        k.op('dve', lambda: V.tensor_tensor(cqn[0:64, 1, 0:Tn], psC[1][0:64, 0:Tn], rms[0:64, 0:Tn], op=ALU.mult), R=['psC1', 'rms'], W=['svb'])
                        if kind == 'lat':
                            k.dma(rc_t[64:96, :], ropecq[:, t0:t0 + Tn], W=['rc_t'])
                            k.dma(rs_t[64:96, :], ropesq[:, t0:t0 + Tn], W=['rs_t'], q='pool')
                        for h in range(4):
                            pq, pqk = psC[h % 2], 'psC%d' % (h % 2)
                            pr, prk = psC[2 + h % 2], 'psC%d' % (2 + h % 2)
                            k.op('pe', lambda h=h: PE.matmul(pq[0:96, 0:Tn], lhsT=wqh[:, 0, h, :], rhs=cqn[:, 0, 0:Tn], start=True, stop=False), R=['wqh', 'svb'], W=[pqk])
                            k.op('pe', lambda h=h: PE.matmul(pq[0:96, 0:Tn], lhsT=wqh[0:64, 1, h, :], rhs=cqn[0:64, 1, 0:Tn], start=False, stop=True), R=['wqh', 'svb'], W=[pqk])
                            k.op('act', lambda h=h: A.copy(QT[0:64, h, 0:Tn], pq[0:64, 0:Tn]), R=[pqk], W=['QT'])
                            if kind == 'lat':
                                k.op('pe', lambda h=h: PE.matmul(pr[0:96, 0:Tn], lhsT=wqr[:, 0, h, :], rhs=cqn[:, 0, 0:Tn], start=True, stop=False), R=['wqr', 'svb'], W=[prk])
                                k.op('pe', lambda h=h: PE.matmul(pr[0:96, 0:Tn], lhsT=wqr[0:64, 1, h, :], rhs=cqn[0:64, 1, 0:Tn], start=False, stop=True), R=['wqr', 'svb'], W=[prk])
                                k.op('dve', lambda: V.tensor_tensor(q1t[64:96, 0:Tn], pq[64:96, 0:Tn], rc_t[64:96, 0:Tn], op=ALU.mult), R=[pqk, 'rc_t'], W=['mean'])
                                k.op('dve', lambda: V.tensor_tensor(q2t[64:96, 0:Tn], pr[64:96, 0:Tn], rs_t[64:96, 0:Tn], op=ALU.mult), R=[prk, 'rs_t'], W=['var'])
                                k.op('pool', lambda h=h: G.tensor_tensor(QT[64:96, h, 0:Tn], q1t[64:96, 0:Tn], q2t[64:96, 0:Tn], op=ALU.add), R=['mean', 'var'], W=['QT'])
                            else:
                                k.op('act', lambda h=h: A.copy(QT[64:96, h, 0:Tn], pq[64:96, 0:Tn]), R=[pqk], W=['QT'])
                        for ct in range(2):
                            proj(psC[ct], 'psC%d' % ct, 192 + ct * 128, 128)
                            k.op('act', lambda ct=ct: A.copy(su[:, ct, 0:Tn], psC[ct][:, 0:Tn]), R=['psC%d' % ct], W=['su'])
                        for ct in range(2):
                            proj(psC[ct], 'psC%d' % ct, 448 + ct * 128, 128)
                            k.op('act', lambda ct=ct: A.copy(svf[:, ct, 0:Tn], psC[ct][:, 0:Tn]), R=['psC%d' % ct], W=['pa'])
                            k.op('dve', lambda ct=ct: V.tensor_copy(svb[:, ct, 0:Tn], svf[:, ct, 0:Tn]), R=['pa'], W=['svb'])
                            k.op('pool', lambda ct=ct: G.tensor_tensor(sq[:, ct, 0:Tn], svf[:, ct, 0:Tn], svf[:, ct, 0:Tn], op=ALU.mult), R=['pa'], W=['sq'])
                        for ct in range(2):
                            k.op('pe', lambda ct=ct: PE.matmul(psC[2][:, 0:Tn], lhsT=ones_b[:], rhs=svb[:, ct, 0:Tn], start=(ct == 0), stop=(ct == 1)), R=['ones_b', 'svb'], W=['psC2'])
                        for ct in range(2):
                            k.op('pe', lambda ct=ct: PE.matmul(psC[3][:, 0:Tn], lhsT=ones_b[:], rhs=sq[:, ct, 0:Tn], start=(ct == 0), stop=(ct == 1)), R=['ones_b', 'sq'], W=['psC3'])
                        k.op('act', lambda: A.activation(mean[:, 0:Tn], psC[2][:, 0:Tn], AF.Copy, scale=1.0 / 256), R=['psC2'], W=['mean'])
                        k.op('dve', lambda: V.tensor_tensor(var[:, 0:Tn], mean[:, 0:Tn], mean[:, 0:Tn], op=ALU.mult), R=['mean'], W=['var'])
                        k.op('dve', lambda: V.scalar_tensor_tensor(var[:, 0:Tn], psC[3][:, 0:Tn], 1.0 / 256, var[:, 0:Tn], op0=ALU.mult, op1=ALU.subtract), R=['psC3', 'var'], W=['var'])
                        k.op('act', lambda: A.activation(var[:, 0:Tn], var[:, 0:Tn], AF.Sqrt, bias=epsb[:, 0:1]), R=['var', 'epsb'], W=['var'])
                        k.op('dve', lambda: V.reciprocal(var[:, 0:Tn], var[:, 0:Tn]), R=['var'], W=['var'])
                        for ct in range(2):
                            k.op('dve', lambda ct=ct: V.tensor_tensor(svf[:, ct, 0:Tn], svf[:, ct, 0:Tn], mean[:, 0:Tn], op=ALU.subtract), R=['pa', 'mean'], W=['pa'])
                            k.op('pool', lambda ct=ct: G.tensor_tensor(svf[:, ct, 0:Tn], svf[:, ct, 0:Tn], var[:, 0:Tn], op=ALU.mult), R=['pa', 'var'], W=['pa'])
                            k.op('dve', lambda ct=ct: V.tensor_scalar(vnb[:, ct, 0:Tn], svf[:, ct, 0:Tn], colv[:, ct:ct + 1], colv[:, 2 + ct:3 + ct], op0=ALU.mult, op1=ALU.add),
                                 R=['pa', 'colv'], W=['vnb'])
                        for gt in range(8):
                            pg, pgk = psC[gt % 4], 'psC%d' % (gt % 4)
                            proj(pg, pgk, 704 + gt * 128, 128)
                            k.op('act', lambda gt=gt, pg=pg: A.activation(sg[:, gt, 0:Tn], pg[:, 0:Tn], AF.Silu), R=[pgk], W=['sg'])
                        if bidx + 1 < min(NQB, len(qblocks)):
                            do_ln(bidx + 1)
                        sbufs = [(psS_[0], 'psSc0'), (psS_[1], 'psSc1'), (psC[0], 'psC0'), (psC[1], 'psC1')]
                        npair = nkt // 2

                        def score(h, kp):
                            pS, pSk = sbufs[kp % 4]
                            for j_ in range(2):
                                kt_ = 2 * kp + j_
                                k.op('pe', lambda j_=j_, kt_=kt_: PE.matmul(pS[:, j_ * Tn:(j_ + 1) * Tn], lhsT=KT[0:96, h, kt_ * 128:(kt_ + 1) * 128], rhs=QT[0:96, h, 0:Tn],
                                                                           start=True, stop=True, skip_group_check=True), R=['QT', 'KTall'], W=[pSk])
                        for h in range(4 if int(os.environ.get('DBG_ATT', 1)) else 0):
                            pO = psO if h % 2 == 0 else psC[3][:, 0:260].rearrange("p (a b) -> p a b", a=4)
                            pOk = 'psO' if h % 2 == 0 else 'psC3'
                            for kp in range(min(3, npair)):
                                score(h, kp)
                            for kp in range(npair):
                                pS, pSk = sbufs[kp % 4]
                                P_, Pk = PT[kp % 3], 'PT%d' % (kp % 3)
                                if kp + 3 < npair:
                                    score(h, kp + 3)
                                k.op('act', lambda pS=pS, P_=P_: A.activation(P_[:, 0:2 * Tn], pS[:, 0:2 * Tn], AF.Exp, scale=SCALE), R=[pSk], W=[Pk])
                                for j_ in range(2):
                                    kt_ = 2 * kp + j_
                                    for sb in range(nsub):
                                        first = (kp == 0 and j_ == 0 and sb == 0)
                                        k.op('pe', lambda h=h, kt_=kt_, j_=j_, sb=sb, P_=P_, first=first: PE.matmul(
                                            pO[:, sb, :], lhsT=P_[:, j_ * Tn + sb * 128:j_ * Tn + (sb + 1) * 128], rhs=Vt[:, kt_, h, :],
                                            start=first, stop=(kp == npair - 1 and j_ == 1), skip_group_check=True), R=[Pk, 'Vtall'], W=[pOk])
                            k.op('act', lambda: A.copy(osb[:, 0:nsub, :], pO[:, 0:nsub, :]), R=[pOk], W=['osb'])
                            k.op('dve', lambda: V.reciprocal(rec[:, 0:nsub, :], osb[:, 0:nsub, 64:65]), R=['osb'], W=['rec'])
                            k.op('dve', lambda h=h: V.tensor_tensor(att[:, 0:nsub, h * 64:(h + 1) * 64], osb[:, 0:nsub, 0:64], rec[:, 0:nsub, :].to_broadcast([128, nsub, 64]), op=ALU.mult),
                                 R=['osb', 'rec'], W=['att'])
                        if 'Catt' in dbg and l == 0 and kind == 'lat' and t0 == 0:
                            d_ = dbg_out('att', (128, 4 * 256), BF16)
                            k.dma(d_, att[:].rearrange("p a b -> p (a b)"), R=['att'], W=['dbgatt'])
                        for sb in range(nsub):
                            for ft in range(2):
                                k.op('pe', lambda sb=sb, ft=ft: PE.transpose(pt[:, ft, :], att[:, sb, ft * 128:(ft + 1) * 128], ident_b[:]), R=['att', 'ident_b'], W=['ptC'])
                            k.op('dve', lambda sb=sb: V.tensor_tensor(catg[:, 0:2, sb * 128:(sb + 1) * 128], pt[:, 0:2, :], sg[:, 0:2, sb * 128:(sb + 1) * 128], op=ALU.mult),
                                 R=['ptC', 'sg'], W=['sg'])
                        si = 0 if kind == 'ctx' else 1
                        nseq = NCTX if kind == 'ctx' else SEQ
                        L_ = Tn + 16
                        k.dma(pin[:, :, 0:L_], pool_scr[si][:, t0:t0 + L_].rearrange("(c p) n -> p c n", p=128), R=['pool_scr'], W=['pin'])
                        if kind == 'lat':
                            pinB = pb[:].bitcast(BF16)[:, :, 0:L_]
                            k.dma(pinB, pool_scr[si][:, HALF + t0:HALF + t0 + L_].rearrange("(c p) n -> p c n", p=128), R=['pool_scr'], W=['pb'], q='pool')
                            k.op('dve', lambda: V.tensor_scalar_mul(pin[:, :, 0:L_], pin[:, :, 0:L_], msb[:, 0:1]), R=['pin', 'msb'], W=['pin'])
                            k.op('dve', lambda: V.scalar_tensor_tensor(pin[:, :, 0:L_], pinB, msb[:, 1:2], pin[:, :, 0:L_], op0=ALU.mult, op1=ALU.add), R=['pin', 'pb', 'msb'], W=['pin'])
                        k.op('dve', lambda: V.tensor_tensor(pa[:, :, 0:L_ - 1], pin[:, :, 0:L_ - 1], pin[:, :, 1:L_], op=ALU.add), R=['pin'], W=['pa'])
                        o_ = slice(8, 8 + Tn)
                        k.op('pool', lambda: G.tensor_copy(pb[0:64, 0, o_], pa[0:64, 0, 7:7 + Tn]), R=['pa'], W=['pb'])
                        k.op('pool', lambda: G.tensor_tensor(pb[64:128, 0, o_], pa[64:128, 0, 6:6 + Tn], pa[64:128, 0, 8:8 + Tn], op=ALU.add), R=['pa'], W=['pb'])
                        k.op('dve', lambda: V.tensor_tensor(pb[:, 1, 0:L_ - 3], pa[:, 1, 0:L_ - 3], pa[:, 1, 2:L_ - 1], op=ALU.add), R=['pa'], W=['pb'])
                        k.op('dve', lambda: V.tensor_tensor(pa[64:128, 1, 0:L_ - 7], pb[64:128, 1, 0:L_ - 7], pb[64:128, 1, 4:L_ - 3], op=ALU.add), R=['pb'], W=['pa'])
                        k.op('pool', lambda: G.tensor_tensor(pa[0:64, 1, o_], pb[0:64, 1, 4:4 + Tn], pb[0:64, 1, 8:8 + Tn], op=ALU.add), R=['pb'], W=['pa'])
                        k.op('dve', lambda: V.tensor_tensor(pb[64:128, 1, o_], pa[64:128, 1, 0:Tn], pa[64:128, 1, 8:8 + Tn], op=ALU.add), R=['pa'], W=['pb'])
                        k.op('pool', lambda: G.tensor_copy(pb[0:64, 1, o_], pa[0:64, 1, o_]), R=['pa'], W=['pb'])
                        for ct in range(2):
                            k.op('dve', lambda ct=ct: V.scalar_tensor_tensor(pld[:, ct, 0:Tn], pb[:, ct, o_], winv[:, ct:ct + 1], pin[:, ct, o_], op0=ALU.mult, op1=ALU.subtract),
                                 R=['pb', 'pin', 'winv'], W=['pld'])
                        wins = ((0, 0, 2), (64, 0, 4), (0, 1, 8), (64, 1, 16))
                        if t0 == 0:
                            for (r0, ct, w_) in wins:
                                for p_ in range(w_ // 2):
                                    rcp = (1.0 / float(p_ + w_ // 2)) if kind == 'ctx' else recF[r0:r0 + 64, ct, p_:p_ + 1]
                                    k.op('dve', lambda r0=r0, ct=ct, p_=p_, rcp=rcp: V.scalar_tensor_tensor(pld[r0:r0 + 64, ct, p_:p_ + 1], pb[r0:r0 + 64, ct, 8 + p_:9 + p_], rcp,
                                                                                                       pin[r0:r0 + 64, ct, 8 + p_:9 + p_], op0=ALU.mult, op1=ALU.subtract),
                                         R=['pb', 'pin', 'recF'], W=['pld'])
                        if t0 + Tn == (NCTX if kind == 'ctx' else HALF):
                            for (r0, ct, w_) in wins:
                                for p_ in range(Tn - w_ // 2 + 1, Tn):
                                    rcp = (1.0 / float(Tn - p_ + w_ // 2)) if kind == 'ctx' else recL[r0:r0 + 64, ct, p_ - (Tn - 8):p_ - (Tn - 8) + 1]
                                    k.op('dve', lambda r0=r0, ct=ct, p_=p_, rcp=rcp: V.scalar_tensor_tensor(pld[r0:r0 + 64, ct, p_:p_ + 1], pb[r0:r0 + 64, ct, 8 + p_:9 + p_], rcp,
                                                                                                       pin[r0:r0 + 64, ct, 8 + p_:9 + p_], op0=ALU.mult, op1=ALU.subtract),
                                         R=['pb', 'pin', 'recL'], W=['pld'])
                        for ct in range(2):
                            k.op('pe', lambda ct=ct: PE.matmul(psC[ct][:, 0:Tn], lhsT=wpl[:, ct, :], rhs=pld[:, ct, 0:Tn], start=True, stop=True), R=['wpl', 'pld'], W=['psC%d' % ct])
                            k.op('dve', lambda ct=ct: V.scalar_tensor_tensor(catg[:, 2 + ct, 0:Tn], psC[ct][:, 0:Tn], colv[:, 4 + ct:5 + ct], sg[:, 2 + ct, 0:Tn], op0=ALU.mult, op1=ALU.mult),
                                 R=['psC%d' % ct, 'colv', 'sg'], W=['sg'])
                        k.dma(s5t[:, :, 0:Tn], s5o_scr[:, u0:u0 + Tn].rearrange("(c p) n -> p c n", p=128), R=['s5o_scr'], W=['s5t'], q='pool')
                        k.op('pool', lambda: G.tensor_tensor(catg[:, 4:6, 0:Tn], s5t[:, :, 0:Tn], sg[:, 4:6, 0:Tn], op=ALU.mult), R=['s5t', 'sg'], W=['sg'])
                        for sb in range(nsub):
                            cs = slice(sb * 128, (sb + 1) * 128)
                            for ft in range(2):
                                k.op('pe', lambda ft=ft, cs=cs: PE.transpose(pt[:, 2 + ft, :], vnb[:, ft, cs], ident_b[:]), R=['vnb', 'ident_b'], W=['ptC'])
                            k.op('act', lambda: A.copy(vc[:].rearrange("p (a b) -> p a b", a=2), pt[:, 2:4, :]), R=['ptC'], W=['vc'])
                            for h in range(4):
                                ft, r0 = h // 2, (h % 2) * 64
                                pm_, pmk = psC[2 + h % 2], 'psC%d' % (2 + h % 2)
                                k.op('pe', lambda h=h, ft=ft: PE.matmul(pm_[:, 0:128], lhsT=vc[:, ft * 128:(ft + 1) * 128], rhs=wsT[:, h, :], start=True, stop=True), R=['vc', 'wsT'], W=[pmk])
                                k.op('dve', lambda h=h, r0=r0: V.tensor_tensor(mx[r0:r0 + 64, :], pm_[r0:r0 + 64, 0:128], bsb[r0:r0 + 64, h, :], op=ALU.add), R=[pmk, 'bsb'], W=['mx'])
                                k.op('pool', lambda ft=ft, r0=r0, cs=cs: G.tensor_tensor(mx[r0:r0 + 64, :], mx[r0:r0 + 64, :], su[r0:r0 + 64, ft, cs], op=ALU.mult), R=['mx', 'su'], W=['mx'])
                                k.op('dve', lambda ft=ft, r0=r0, cs=cs: V.tensor_tensor(catg[r0:r0 + 64, 6 + ft, cs], mx[r0:r0 + 64, :], sg[r0:r0 + 64, 6 + ft, cs], op=ALU.mult),
                                     R=['mx', 'sg'], W=['sg'])
                        if 'Ccat' in dbg and l == 0 and kind == 'lat' and t0 == 0:
                            d_ = dbg_out('catg', (128, 8 * QB), BF16)
                            k.dma(d_, catg[:].rearrange("p a b -> p (a b)"), R=['sg'], W=['dbgcat'])
                        for sb in range(nsub):
                            cs = slice(sb * 128, (sb + 1) * 128)
                            xb = xt[xcnt % 2]
                            xk = 'xtC0'
                            xcnt += 1
                            k.dma(xb[:], src[t0 + sb * 128: t0 + (sb + 1) * 128, :], W=[xk], q=('sp' if sb % 2 == 0 else 'pool'))
                            for nh in range(2):
                                for kt in range(8):
                                    k.op('pe', lambda nh=nh, kt=kt, cs=cs: PE.matmul(psC[nh][:, :], lhsT=catg[:, kt, cs], rhs=wout[:, kt, nh * 512:(nh + 1) * 512],
                                                                                    start=(kt == 0), stop=(kt == 7)), R=['sg', 'wout'], W=['psC%d' % nh])
                                k.op('dve', lambda nh=nh: V.tensor_tensor(rr[:, nh * 512:(nh + 1) * 512], psC[nh][:, :], gate_bc[:, nh * 512:(nh + 1) * 512], op=ALU.mult),
                                     R=['psC%d' % nh, 'gate_bc'], W=['rr'])
                            k.op('dve', lambda xb=xb: V.scalar_tensor_tensor(rr[:], xb[:], float(ALPHA), rr[:], op0=ALU.mult, op1=ALU.add), R=[xk, 'rr'], W=['rr'])
                            for c2 in range(2):
                                k.op('dve', lambda c2=c2: V.bn_stats(st6[:, c2, :], rr[:, c2 * 512:(c2 + 1) * 512]), R=['rr'], W=['st6'])
                            k.op('dve', lambda: V.bn_aggr(mv[:], st6[:]), R=['st6'], W=['mv'])
                            k.op('act', lambda: A.activation(rstd[:], mv[:, 1:2], AF.Sqrt, bias=epsb[:, 0:1]), R=['mv', 'epsb'], W=['rstd'])
                            k.op('dve', lambda: V.reciprocal(rstd[:], rstd[:]), R=['rstd'], W=['rstd'])
                            k.op('dve', lambda: V.scalar_tensor_tensor(rr[:], rr[:], mv[:, 0:1], lng_bc[:], op0=ALU.subtract, op1=ALU.mult), R=['rr', 'mv', 'lng_bc'], W=['rr'])
                            k.op('dve', lambda xb=xb: V.scalar_tensor_tensor(xb[:], rr[:], rstd[:, 0:1], lnb_bc[:], op0=ALU.mult, op1=ALU.add), R=['rr', 'rstd', 'lnb_bc'], W=[xk])
                            k.dma(dst[t0 + sb * 128: t0 + (sb + 1) * 128, :], xb[:], R=[xk], W=['x1w'])

            if l == 0 and nlayers > 1:
                k._sync('pool', ['x1w'], ['x1full'])
                for c_ in range(HALF // CCR):
                    G.collective_compute("AllGather", ALU.bypass, replica_groups=[[0, 1], [2, 3], [4, 5], [6, 7]],
                                         ins=[x1h_scr[c_ * CCR:(c_ + 1) * CCR, :]], outs=[x1g[c_]]).then_inc(ccsem)
                ccn[0] += HALF // CCR
                x1_ready = ccn[0]
            if 'x1' in dbg and l == 0:
                d_ = dbg_out('x1', (SEQ, D))
                k.dma(d_[0:HALF, :], x1h_scr, R=['x1w'], W=['dbgx1'])
                d_ = dbg_out('ctx1', (NCTX, D))
                k.dma(d_, ctx1_scr, R=['x1w'], W=['dbgc1'])
                break

        k.wait_all('sp')
    return nc, dout


def _in_maps(inputs, cores):
    cst = _consts()
    shared = {n: np.ascontiguousarray(inputs[n], dtype=np.float32) for n in PARAMS}
    shared.update(cst)
    maps = []
    for i in cores:
        b, hf = i // 2, i % 2
        sl = slice(hf * HALF, (hf + 1) * HALF)
        m = dict(shared)
        m['x'] = np.ascontiguousarray(inputs['x'][b], dtype=np.float32)
        m['xq'] = np.ascontiguousarray(inputs['x'][b][sl], dtype=np.float32)
        m['ctx'] = np.ascontiguousarray(inputs['ctx'][b], dtype=np.float32)
        m['cvec'] = np.ascontiguousarray(np.stack([inputs['c'][b], inputs['c_ctx']], 0), dtype=np.float32)
        m['msel'] = np.array([1.0 - hf, float(hf)], np.float32)
        if hf:
            for n_ in ('lam_re', 'lam_im', 'log_dt', 's5_b_re', 's5_b_im', 's5_c_re', 's5_c_im'):
                m[n_] = np.ascontiguousarray(np.roll(shared[n_], -8, axis=2))
            w_in = shared['w_in'].copy()
            w_in[:, :, 160:416] = np.roll(w_in[:, :, 160:416], -128, axis=2)
            w_in[:, :, 1888:2144] = np.roll(w_in[:, :, 1888:2144], -128, axis=2)
            m['w_in'] = w_in
            m['s5_d'] = np.ascontiguousarray(np.roll(shared['s5_d'], -128, axis=1))
            m['b_glu'] = np.ascontiguousarray(np.roll(shared['b_glu'], -128, axis=1))
            m['w_glu'] = np.ascontiguousarray(np.roll(np.roll(shared['w_glu'], -128, axis=1), -128, axis=2))
            w_out = shared['w_out'].copy()
            w_out[:, 512:768, :] = np.roll(w_out[:, 512:768, :], -128, axis=1)
            m['w_out'] = w_out
        m['ropec_q'] = np.ascontiguousarray(cst['ropec'][:, sl])
        m['ropes_q'] = np.ascontiguousarray(cst['ropes'][:, sl])
        maps.append(m)
    return maps


def kernel(**inputs):
    nc, _ = build()
    cores = list(range(8))
    res = run_bass_kernel_spmd(nc, _in_maps(inputs, cores), core_ids=cores)
    out = np.empty((4, SEQ, D), np.float32)
    for i in cores:
        b, hf = i // 2, i % 2
        out[b, hf * HALF:(hf + 1) * HALF] = np.asarray(res.results[i]['out'], dtype=np.float32)
    return out
```

```python
import os
from contextlib import ExitStack
import numpy as np
import concourse.bass as bass
import concourse.mybir as mybir
from concourse.bass_utils import run_bass_kernel_spmd

F32 = mybir.dt.float32
BF16 = mybir.dt.bfloat16
AF = mybir.ActivationFunctionType
ALU = mybir.AluOpType

D = 1024
SEQ = 8192
NCTX = 256
NU = NCTX + SEQ
NE = NU + NCTX
NSC = NE // 8
HALF = SEQ // 2
DEPTH = 2
QB = 256
IN_DIM = 2400
LN_EPS = 1e-6
ALPHA = (2 * DEPTH) ** 0.25
PARAMS = ['w_mod', 'b_mod', 'w_in', 'g_q', 'w_uq', 'g_kv', 'w_ukv', 'w_pool', 'pool_scale', 'lam_re', 'lam_im',
          'log_dt', 's5_b_re', 's5_b_im', 's5_c_re', 's5_c_im', 's5_d', 'w_glu', 'b_glu', 'sgu_g', 'sgu_b', 'w_s',
          'b_s', 'w_out', 'ln_g', 'ln_b']
PSHAPES = {'w_mod': (2, 1024, 3072), 'b_mod': (2, 3072), 'w_in': (2, 1024, 2400), 'g_q': (2, 192), 'w_uq': (2, 192, 384),
           'g_kv': (2, 128), 'w_ukv': (2, 128, 512), 'w_pool': (2, 4, 64, 64), 'pool_scale': (2, 256),
           'lam_re': (2, 2, 16, 64), 'lam_im': (2, 2, 16, 64), 'log_dt': (2, 2, 16), 's5_b_re': (2, 2, 16, 64, 16),
           's5_b_im': (2, 2, 16, 64, 16), 's5_c_re': (2, 2, 16, 16, 64), 's5_c_im': (2, 2, 16, 16, 64), 's5_d': (2, 256),
           'w_glu': (2, 256, 256), 'b_glu': (2, 256), 'sgu_g': (2, 256), 'sgu_b': (2, 256), 'w_s': (2, 4, 128, 128),
           'b_s': (2, 4, 128), 'w_out': (2, 1024, 1024), 'ln_g': (2, 1024), 'ln_b': (2, 1024)}

SAME_ENGINE_SYNC = True


class KB:
    NSLOT = 8

    def __init__(self, nc, stack):
        self.nc = nc
        self.eng = {'pe': nc.tensor, 'act': nc.scalar, 'dve': nc.vector, 'pool': nc.gpsimd, 'sp': nc.sync}
        self.sem = {e: stack.enter_context(nc.semaphore('s_' + e)) for e in self.eng}
        self.cnt = {e: 0 for e in self.eng}
        self.dsem, self.duse = {}, {}
        for q in ('sp', 'pool', 'act'):
            for j in range(self.NSLOT):
                self.dsem[(q, j)] = stack.enter_context(nc.semaphore('d_%s%d' % (q, j)))
                self.duse[(q, j)] = 0
        self.dnext = {'sp': 0, 'pool': 0, 'act': 0}
        self.known = {e: {} for e in self.eng}
        self.lastw, self.readers = {}, {}
        self.ninstr = 0

    def _need(self, E, tok, waits):
        if tok is None:
            return
        if tok[0] == 'c':
            _, e2, n = tok
            if e2 == E and (not SAME_ENGINE_SYNC or E == 'pe'):
                return
            key, val = e2, n
        else:
            _, q, j, n = tok
            key, val = (q, j), n * 16
        if self.known[E].get(key, 0) >= val:
            return
        if waits.get(key, 0) < val:
            waits[key] = val

    def _sync(self, E, R, W, waits=None):
        waits = {} if waits is None else waits
        for r in R:
            self._need(E, self.lastw.get(r), waits)
        for w in W:
            self._need(E, self.lastw.get(w), waits)
            for t in self.readers.get(w, ()):
                self._need(E, t, waits)
        eng = self.eng[E]
        for key, val in waits.items():
            eng.wait_ge(self.sem[key] if isinstance(key, str) else self.dsem[key], val)
            self.known[E][key] = val

    def _commit(self, tok, R, W):
        for r in R:
            self.readers.setdefault(r, []).append(tok)
        for w in W:
            self.lastw[w] = tok
            self.readers[w] = []

    def op(self, E, fn, R=(), W=()):
        W = list(W) + [r for r in R if r.startswith('ps') and r not in W]
        self._sync(E, R, W)
        ins = fn()
        self.cnt[E] += 1
        ins.then_inc(self.sem[E], 1)
        self._commit(('c', E, self.cnt[E]), R, W)
        self.ninstr += 1
        return ins

    def dma(self, out, in_, R=(), W=(), q='sp', **kw):
        j = self.dnext[q]
        self.dnext[q] = (j + 1) % self.NSLOT
        slot = (q, j)
        waits = {}
        if self.duse[slot] > 0:
            self._need(q, ('d', q, j, self.duse[slot]), waits)
        self._sync(q, R, W, waits)
        ins = self.eng[q].dma_start(out=out, in_=in_, **kw)
        ins.then_inc(self.dsem[slot], 16)
        self.duse[slot] += 1
        self._commit(('d', q, j, self.duse[slot]), R, W)
        self.ninstr += 1

    def barrier(self):
        for E, eng in self.eng.items():
            for e2 in self.eng:
                if e2 != E and self.cnt[e2] > self.known[E].get(e2, 0):
                    eng.wait_ge(self.sem[e2], self.cnt[e2])
                    self.known[E][e2] = self.cnt[e2]
            for slot, n in self.duse.items():
                if n > 0 and 16 * n > self.known[E].get(slot, 0):
                    eng.wait_ge(self.dsem[slot], 16 * n)
                    self.known[E][slot] = 16 * n

    def wait_all(self, E='sp'):
        eng = self.eng[E]
        for e2 in self.eng:
            if e2 != E and self.cnt[e2] > 0:
                eng.wait_ge(self.sem[e2], self.cnt[e2])
        for slot, n in self.duse.items():
            if n > 0:
                eng.wait_ge(self.dsem[slot], 16 * n)


def _consts():
    c = {}
    c['ident'] = np.eye(128, dtype=np.float32)
    sw = np.zeros((128, 128), np.float32)
    for p in range(64):
        sw[p, p + 64] = 1.0
        sw[p + 64, p] = 1.0
    c['swapm'] = sw
    rows = SEQ // 64
    t = np.arange(SEQ)
    pos = np.stack([t // 64, t % 64], 0).astype(np.float32)
    inv = (10000.0 ** (-np.arange(8, dtype=np.float32) / 8)).astype(np.float32)
    ang = pos[:, None, :] * inv[None, :, None]
    cos, sin = np.cos(ang).astype(np.float32), np.sin(ang).astype(np.float32)
    rc = np.zeros((32, SEQ), np.float32)
    rs = np.zeros((32, SEQ), np.float32)
    for a in range(2):
        for hf in range(2):
            rc[a * 16 + hf * 8: a * 16 + hf * 8 + 8] = cos[a]
            rs[a * 16 + hf * 8: a * 16 + hf * 8 + 8] = sin[a] * (-1.0 if hf == 0 else 1.0)
    r_ = np.arange(128) // 16
    c['mlow'] = (r_[None, :] >= r_[:, None]).astype(np.float32)
    c['mup'] = (r_[None, :] <= r_[:, None]).astype(np.float32)
    c['ropec'] = rc
    c['ropes'] = rs
    return c


def build(nlayers=DEPTH, dbg=()):
    nc = bass.Bass("TRN2", target_bir_lowering=False)
    din = {}

    def inp(name, shape):
        din[name] = nc.dram_tensor(name, list(shape), F32, kind="ExternalInput").ap()
        return din[name]

    x_in = inp('x', (SEQ, D))
    xq_in = inp('xq', (HALF, D))
    ctx_in = inp('ctx', (NCTX, D))
    msel = inp('msel', (2,))
    ropecq = inp('ropec_q', (32, HALF))
    ropesq = inp('ropes_q', (32, HALF))
    cvec = inp('cvec', (2, D))
    prm = {n: inp(n, PSHAPES[n]) for n in PARAMS}
    cst = {n: inp(n, v.shape) for n, v in _consts().items()}
    out_ap = nc.dram_tensor('out', [HALF, D], F32, kind="ExternalOutput").ap()
    dout = {}

    def dbg_out(name, shape, dt=F32):
        dout[name] = nc.dram_tensor('dbg_' + name, list(shape), dt, kind="ExternalOutput").ap()
        return dout[name]

    scr = lambda name, shape, dt=F32: nc.dram_tensor(name, list(shape), dt, kind="Internal").ap()
    CCR = 512
    x1g = [scr('x1g%d' % c, (2 * CCR, D)) for c in range(HALF // CCR)]

    def x1rows(t):
        r_, w_ = t // HALF, t % HALF
        c_, o_ = w_ // CCR, w_ % CCR
        return x1g[c_][r_ * CCR + o_: r_ * CCR + o_ + 128, :]
    x1h_scr = scr('x1h_scr', (HALF, D))
    ctx1_scr = scr('ctx1_scr', (NCTX, D))
    gate_scr = scr('gate_scr', (2, 2, D))
    pool_scr = [scr('pool_scr_c', (256, NCTX + 16), BF16), scr('pool_scr_l', (256, SEQ + 16), BF16)]
    s5o_scr = scr('s5o_scr', (256, NU), BF16)
    xd_scr = scr('xd_scr', (256, 8, NSC), BF16)
    yd_scr = scr('yd_scr', (256, 8, NSC), BF16)
    ydg_scr = [scr('ydg%d' % c, (128, 8 * NSC), BF16) for c in range(2)]

    with ExitStack() as st:
        k = KB(nc, st)

        uid = [0]

        def T(stk, name, shape, dt):
            uid[0] += 1
            return stk.enter_context(nc.sbuf_tensor('%s_t%d' % (name, uid[0]), list(shape), dt))

        def P(stk, name, shape, dt=F32):
            uid[0] += 1
            return stk.enter_context(nc.psum_tensor('%s_p%d' % (name, uid[0]), list(shape), dt))

        V = nc.vector
        A = nc.scalar
        G = nc.gpsimd
        PE = nc.tensor

        def ln_block(xsrcs, rkeys, col, hT_, B):
            n = len(xsrcs)

            def stage1(s_):
                xb, xk = B['xt'][s_ % 2]
                xn_, xnk = B['xn'][s_ % 2]
                pt_, ptk = B['pt'][s_ % 2]
                sm, smk = B['sm'][s_ % 2]
                st6_ = sm[:, 0:12].rearrange("p (a b) -> p a b", a=2)
                k.dma(xb[:], xsrcs[s_], R=rkeys, W=[xk], q=('sp' if s_ % 2 == 0 else 'pool'))
                for c2 in range(2):
                    k.op('dve', lambda c2=c2: V.bn_stats(st6_[:, c2, :], xb[:, c2 * 512:(c2 + 1) * 512]), R=[xk], W=[smk])
                k.op('dve', lambda: V.bn_aggr(sm[:, 12:14], st6_), R=[smk], W=[smk])
                k.op('act', lambda: A.activation(sm[:, 14:15], sm[:, 13:14], AF.Sqrt, bias=epsb[:, 0:1]), R=[smk, 'epsb'], W=[smk])
                k.op('dve', lambda: V.reciprocal(sm[:, 14:15], sm[:, 14:15]), R=[smk], W=[smk])
                k.op('dve', lambda: V.tensor_scalar(sm[:, 15:16], sm[:, 12:13], sm[:, 14:15], -1.0, op0=ALU.mult, op1=ALU.mult), R=[smk], W=[smk])
                k.op('act', lambda: A.activation(xn_[:], xb[:], AF.Identity, scale=sm[:, 14:15], bias=sm[:, 15:16]), R=[xk, smk], W=[xnk])
                for dc in range(8):
                    k.op('pe', lambda dc=dc: PE.transpose(pt_[:, dc, :], xn_[:, dc * 128:(dc + 1) * 128], ident_b[:]), R=[xnk, 'ident_b'], W=[ptk])

            def stage2(s_):
                pt_, ptk = B['pt'][s_ % 2]
                if B.get('dve_mod'):
                    hv = hT_[:, :, s_ * 128:(s_ + 1) * 128]
                    k.op('dve', lambda: V.tensor_tensor(hv, pt_[:, :, :], sc1T[:, :, col].unsqueeze(2).to_broadcast([128, 8, 128]), op=ALU.mult),
                         R=[ptk, 'sc1T'], W=[B['hTk']])
                    k.op('pool', lambda: G.tensor_tensor(hv, hv, modT[:, 0:8, col].unsqueeze(2).to_broadcast([128, 8, 128]), op=ALU.add),
                         R=[B['hTk'], 'modT'], W=[B['hTk']])
                    return
                for dc in range(8):
                    k.op('act', lambda dc=dc: A.activation(hT_[:, dc, s_ * 128:(s_ + 1) * 128], pt_[:, dc, :], AF.Identity,
                                                          scale=sc1T[:, dc, col:col + 1], bias=modT[:, dc, col:col + 1]),
                         R=[ptk, 'sc1T', 'modT'], W=[B['hTk']])

            stage1(0)
            for s_ in range(n):
                if s_ + 1 < n:
                    stage1(s_ + 1)
                stage2(s_)

        ident_f = T(st, 'ident_f', (128, 128), F32)
        ident_b = T(st, 'ident_b', (128, 128), BF16)
        ones_b = T(st, 'ones_b', (128, 128), BF16)
        epsb = T(st, 'epsb', (128, 1), F32)
        KT = T(st, 'KT', (128, 4, NU), BF16)
        Vt = T(st, 'Vt', (128, NU // 128, 4, 65), BF16)
        modT = T(st, 'modT', (128, 24, 2), F32)
        sc1T = T(st, 'sc1T', (128, 8, 2), F32)
        wukv = T(st, 'wukv', (128, 512), BF16)
        msb = T(st, 'msb', (128, 2), F32)
        recF = T(st, 'recF', (128, 2, 8), F32)
        recL = T(st, 'recL', (128, 2, 8), F32)
        ccsem = st.enter_context(nc.semaphore('ccsem'))
        ccn = [0]

        k.dma(ident_f[:], cst['ident'], W=['ident_f'])
        k.op('dve', lambda: V.tensor_copy(ident_b[:], ident_f[:]), R=['ident_f'], W=['ident_b'])
        k.op('pool', lambda: G.memset(ones_b[:], 1.0), W=['ones_b'])
        k.op('pool', lambda: G.memset(epsb[:], LN_EPS), W=['epsb'])
        k.op('pool', lambda: G.memset(Vt[:, :, :, 64:65], 1.0), W=['Vt'])
        k.dma(msb[:], msel.partition_broadcast(128), W=['msb'])
        k.op('pool', lambda: G.memset(recF[:], 1.0), W=['recF'])
        k.op('pool', lambda: G.memset(recL[:], 1.0), W=['recL'])
        WINS = ((0, 0, 2), (64, 0, 4), (0, 1, 8), (64, 1, 16))
        for (r0, ct, w_) in WINS:
            for p_ in range(w_ // 2):
                c1_ = 1.0 / (p_ + w_ // 2) - 1.0 / w_
                k.op('dve', lambda r0=r0, ct=ct, p_=p_, c1_=c1_, w_=w_: V.tensor_scalar(recF[r0:r0 + 64, ct, p_:p_ + 1], msb[r0:r0 + 64, 0:1], c1_, 1.0 / w_, op0=ALU.mult, op1=ALU.add),
                     R=['msb'], W=['recF'])
            for q_ in range(8 - w_ // 2 + 1, 8):
                c1_ = 1.0 / (8 - q_ + w_ // 2) - 1.0 / w_
                k.op('dve', lambda r0=r0, ct=ct, q_=q_, c1_=c1_, w_=w_: V.tensor_scalar(recL[r0:r0 + 64, ct, q_:q_ + 1], msb[r0:r0 + 64, 1:2], c1_, 1.0 / w_, op0=ALU.mult, op1=ALU.add),
                     R=['msb'], W=['recL'])

        for l in range(nlayers):
            last = (l == DEPTH - 1)
            xin = x_in
            cin = ctx_in if l == 0 else ctx1_scr
            k.barrier()
            with ExitStack() as s0:
                cc = T(s0, 'cc', (128, 8, 2), F32)
                scc = T(s0, 'scc', (128, 8, 2), F32)
                bmodT = T(s0, 'bmodT', (128, 24), F32)
                wm = [T(s0, 'wm%d' % i, (128, 8, 512), F32) for i in range(2)]
                pm = P(s0, 'pm', (128, 24, 2))
                wtmp = T(s0, 'wtmp', (128, 512), F32)
                gkv = T(s0, 'gkv', (128, 1), F32)
                for col in range(2):
                    k.dma(cc[:, :, col], cvec[col].rearrange("(c p) -> p c", p=128), W=['cc'], allow_slow_non_contiguous=True)
                k.dma(bmodT[:], prm['b_mod'][l].rearrange("(c p) -> p c", p=128), W=['bmodT'], allow_slow_non_contiguous=True)
                k.op('act', lambda: A.activation(scc[:], cc[:], AF.Silu), R=['cc'], W=['scc'])
                for blk in range(6):
                    w_ = wm[blk % 2]
                    k.dma(w_[:], prm['w_mod'][l][:, blk * 512:(blk + 1) * 512].rearrange("(c p) n -> p c n", p=128),
                          W=['wm%d' % (blk % 2)], q=('sp' if blk % 2 == 0 else 'pool'))
                    for jj in range(4):
                        jt = blk * 4 + jj
                        for dc in range(8):
                            k.op('pe', lambda w_=w_, jj=jj, jt=jt, dc=dc: PE.matmul(
                                pm[:, jt, :], lhsT=w_[:, dc, jj * 128:(jj + 1) * 128], rhs=scc[:, dc, :],
                                start=(dc == 0), stop=(dc == 7)), R=['wm%d' % (blk % 2), 'scc'], W=['pm'])
                k.op('dve', lambda: V.tensor_tensor(modT[:], pm[:], bmodT[:].unsqueeze(2).to_broadcast([128, 24, 2]), op=ALU.add),
                     R=['pm', 'bmodT'], W=['modT'])
                k.op('dve', lambda: V.tensor_scalar_add(sc1T[:], modT[:, 8:16, :], 1.0), R=['modT'], W=['sc1T'])
                for col in range(2):
                    k.dma(gate_scr[l, col].rearrange("(c p) -> p c", p=128), modT[:, 16:24, col], R=['modT'], W=['gate_scr'],
                          allow_slow_non_contiguous=True)
                k.dma(wtmp[:], prm['w_ukv'][l], W=['wtmp'])
                k.dma(gkv[:], prm['g_kv'][l].rearrange("(p o) -> p o", o=1), W=['gkv'])
                k.op('dve', lambda: V.tensor_scalar_mul(wukv[:], wtmp[:], gkv[:, 0:1]), R=['wtmp', 'gkv'], W=['wukv'])

            if 'mod' in dbg and l == 0:
                d_ = dbg_out('mod', (128, 48))
                k.dma(d_, modT[:].rearrange("p a b -> p (a b)"), R=['modT'], W=['dbgmod'])

            k.barrier()
            if l > 0:
                for q_ in ('sp', 'pool'):
                    k.eng[q_].wait_ge(ccsem, x1_ready)
            with ExitStack() as sA:
                with ExitStack() as sa:
                    wA = T(sa, 'wA', (128, 8, 832), BF16)
                    wst = T(sa, 'wst', (128, 8, 256), F32)
                    xt = [T(sa, 'xt%d' % i, (128, D), F32) for i in range(2)]
                    xn2 = [T(sa, 'xn2_%d' % i, (128, D), BF16) for i in range(2)]
                    sm2 = [T(sa, 'sm2_%d' % i, (128, 16), F32) for i in range(2)]
                    st6 = T(sa, 'st6', (128, 2, 6), F32)
                    mv = T(sa, 'mv', (128, 2), F32)
                    rstd = T(sa, 'rstd', (128, 1), F32)
                    xn = T(sa, 'xn', (128, D), BF16)
                    hf32 = T(sa, 'hf32', (128, 8, 128), F32)
                    hT2 = [T(sa, 'hT%d' % i, (128, 8, 512), BF16) for i in range(2)]
                    sq = T(sa, 'sq', (128, 512), BF16)
                    rms = T(sa, 'rms', (128, 512), F32)
                    ckvn = T(sa, 'ckvn', (128, 512), BF16)
                    rc_t = T(sa, 'rc_t', (128, 512), F32)
                    rs_t = T(sa, 'rs_t', (128, 512), F32)
                    kr1 = T(sa, 'kr1', (128, 512), F32)
                    kr2 = T(sa, 'kr2', (128, 512), F32)
                    plo = T(sa, 'plo', (128, 2, 512), BF16)
                    zpad = T(sa, 'zpad', (128, 2, 8), BF16)
                    xds = T(sa, 'xds', (128, 2, 8, 64), BF16)
                    pt = P(sa, 'pt', (128, 8, 128), BF16)
                    ptb = P(sa, 'ptb', (128, 8, 128), BF16)
                    LNB = {'xt': [(xt[0], 'xt0'), (xt[1], 'xt1')], 'xn': [(xn2[0], 'xn2_0'), (xn2[1], 'xn2_1')],
                           'pt': [(pt, 'pt'), (ptb, 'ptb')], 'sm': [(sm2[0], 'sm2_0'), (sm2[1], 'sm2_1')], 'hTk': 'hT'}
                    ps = [P(sa, 'psA%d' % i, (128, 512)) for i in range(6)]

                    k.op('pool', lambda: G.memset(wA[:, :, 128:320], 0.0), W=['wA'])
                    k.op('pool', lambda: G.memset(zpad[:], 0.0), W=['zpad'])
                    win = prm['w_in'][l].rearrange("(c p) n -> p c n", p=128)
                    k.dma(wst[:, :, 0:160], win[:, :, 0:160], W=['wst'])
                    k.op('dve', lambda: V.tensor_copy(wA[:, :, 0:128], wst[:, :, 0:128]), R=['wst'], W=['wA'])
                    k.op('dve', lambda: V.tensor_copy(wA[:, :, 192:224], wst[:, :, 128:160]), R=['wst'], W=['wA'])
                    for (d0, s0_) in ((0, 8), (8, 0), (16, 24), (24, 16)):
                        k.op('dve', lambda d0=d0, s0_=s0_: V.tensor_copy(wA[:, :, 288 + d0:296 + d0], wst[:, :, 128 + s0_:136 + s0_]),
                             R=['wst'], W=['wA'])
                    k.dma(wst[:], win[:, :, 160:416], R=[], W=['wst'])
                    k.op('dve', lambda: V.tensor_copy(wA[:, :, 320:576], wst[:]), R=['wst'], W=['wA'])
                    k.dma(wst[:], win[:, :, 608:864], R=[], W=['wst'])
                    k.op('dve', lambda: V.tensor_copy(wA[:, :, 576:832], wst[:]), R=['wst'], W=['wA'])
                    for si in range(2):
                        k.dma(pool_scr[si][:, 0:8].rearrange("(c p) n -> p c n", p=128), zpad[:], R=['zpad'], W=['pool_scr'])
                        n_ = NCTX if si == 0 else SEQ
                        k.dma(pool_scr[si][:, 8 + n_:16 + n_].rearrange("(c p) n -> p c n", p=128), zpad[:], R=['zpad'], W=['pool_scr'])

                    blocks = [('ctx', 0, NCTX)] + [('lat', 512 * i, 512) for i in range(SEQ // 512)]
                    xcnt = 0
                    for bidx, (kind, t0, Tn) in enumerate(blocks):
                        hT, hTk = hT2[bidx % 2], 'hT%d' % (bidx % 2)
                        LNB['hTk'] = hTk
                        col = 1 if kind == 'ctx' else 0
                        src = cin if kind == 'ctx' else xin
                        u0 = t0 if kind == 'ctx' else NCTX + t0
                        nsub = Tn // 128
                        xsrcs = [(x1rows(t0 + sb * 128) if (l > 0 and kind == 'lat') else src[t0 + sb * 128: t0 + (sb + 1) * 128, :]) for sb in range(nsub)]
                        ln_block(xsrcs, ['x1w'] if l > 0 else [], col, hT, LNB)
                        def proj(pst, c0, ncol, key):
                            for dc in range(8):
                                k.op('pe', lambda dc=dc: PE.matmul(pst[0:ncol, 0:Tn], lhsT=wA[:, dc, c0:c0 + ncol], rhs=hT[:, dc, 0:Tn],
                                                                  start=(dc == 0), stop=(dc == 7)), R=['wA', hTk], W=[key])
                        proj(ps[0], 0, 128, 'psA0')
                        proj(ps[1], 128, 96, 'psA1')
                        if kind == 'lat':
                            proj(ps[2], 224, 96, 'psA2')
                        k.op('act', lambda: A.activation(sq[:, 0:Tn], ps[0][:, 0:Tn], AF.Square), R=['psA0'], W=['sq'])
                        k.op('pe', lambda: PE.matmul(ps[3][:, 0:Tn], lhsT=ones_b[:], rhs=sq[:, 0:Tn], start=True, stop=True),
                             R=['ones_b', 'sq'], W=['psA3'])
                        k.op('act', lambda: A.activation(rms[:, 0:Tn], ps[3][:, 0:Tn], AF.Sqrt, bias=epsb[:, 0:1], scale=1.0 / 128),
                             R=['psA3', 'epsb'], W=['rms'])
                        k.op('dve', lambda: V.reciprocal(rms[:, 0:Tn], rms[:, 0:Tn]), R=['rms'], W=['rms'])
                        k.op('dve', lambda: V.tensor_tensor(ckvn[:, 0:Tn], ps[0][:, 0:Tn], rms[:, 0:Tn], op=ALU.mult), R=['psA0', 'rms'], W=['ckvn'])
                        for h in range(4):
                            k.op('pe', lambda h=h: PE.matmul(ps[3][0:64, 0:Tn], lhsT=wukv[:, h * 128:h * 128 + 64], rhs=ckvn[:, 0:Tn],
                                                            start=True, stop=True), R=['wukv', 'ckvn'], W=['psA3'])
                            k.op('act', lambda h=h: A.copy(KT[0:64, h, u0:u0 + Tn], ps[3][0:64, 0:Tn]), R=['psA3'], W=['KT%d' % (u0 // 512)])
                        wv = wukv[:].rearrange("p (h t d) -> p h t d", h=4, t=2)[:, :, 1, :]
                        for sb in range(nsub):
                            k.op('pe', lambda sb=sb: PE.matmul(ps[4][:, 0:256].rearrange("p (h d) -> p h d", h=4), lhsT=ckvn[:, sb * 128:(sb + 1) * 128],
                                                              rhs=wv, start=True, stop=True), R=['wukv', 'ckvn'], W=['psA4'])
                            k.op('dve', lambda sb=sb: V.tensor_copy(Vt[:, u0 // 128 + sb, :, 0:64], ps[4][:, 0:256].rearrange("p (h d) -> p h d", h=4)),
                                 R=['psA4'], W=['Vt%d' % (u0 // 512)])
                        if kind == 'lat':
                            k.dma(rc_t[64:96, :], cst['ropec'][:, t0:t0 + 512], W=['rc_t'])
                            k.dma(rs_t[64:96, :], cst['ropes'][:, t0:t0 + 512], W=['rs_t'], q='pool')
                            k.op('dve', lambda: V.tensor_tensor(kr1[64:96, :], ps[1][64:96, :], rc_t[64:96, :], op=ALU.mult), R=['psA1', 'rc_t'], W=['kr1'])
                            k.op('dve', lambda: V.tensor_tensor(kr2[64:96, :], ps[2][64:96, :], rs_t[64:96, :], op=ALU.mult), R=['psA2', 'rs_t'], W=['kr2'])
                            for h in range(4):
                                k.op('pool', lambda h=h: G.tensor_tensor(KT[64:96, h, u0:u0 + Tn], kr1[64:96, 0:Tn], kr2[64:96, 0:Tn], op=ALU.add),
                                     R=['kr1', 'kr2'], W=['KT%d' % (u0 // 512)])
                        else:
                            for h in range(4):
                                k.op('act', lambda h=h: A.copy(KT[64:96, h, u0:u0 + Tn], ps[1][64:96, 0:Tn]), R=['psA1'], W=['KT%d' % (u0 // 512)])
                        for ct in range(2):
                            proj(ps[ct % 2 + 4], 320 + ct * 128, 128, 'psA%d' % (ct % 2 + 4))
                            srcv = ps[ct % 2 + 4][:, 0:Tn].rearrange("p (c i) -> p i c", i=8)
                            k.op('act', lambda ct=ct, srcv=srcv: A.copy(xds[:, ct, :, 0:Tn // 8], srcv), R=['psA%d' % (ct % 2 + 4)], W=['xds'])
                        xdv = xd_scr.rearrange("(c p) i n -> p c i n", p=128)
                        for ct in range(2):
                            k.dma(xdv[:, ct, :, u0 // 8:(u0 + Tn) // 8], xds[:, ct, :, 0:Tn // 8], R=['xds'], W=['xd_scr'])
                            if kind == 'ctx':
                                k.dma(xdv[:, ct, :, NU // 8:NE // 8], xds[:, ct, :, 0:Tn // 8], R=['xds'], W=['xd_scr'], q='pool')
                        for ct in range(2):
                            proj(ps[ct % 2 + 4], 576 + ct * 128, 128, 'psA%d' % (ct % 2 + 4))
                            k.op('act', lambda ct=ct: A.copy(plo[:, ct, 0:Tn], ps[ct % 2 + 4][:, 0:Tn]), R=['psA%d' % (ct % 2 + 4)], W=['plo'])
                        si = 0 if kind == 'ctx' else 1
                        k.dma(pool_scr[si][:, 8 + t0:8 + t0 + Tn].rearrange("(c p) n -> p c n", p=128), plo[:, :, 0:Tn], R=['plo'], W=['pool_scr'])

                if 'A' in dbg and l == 0:
                    d1 = dbg_out('KT', (128, 4 * NU), BF16)
                    k.dma(d1, KT[:].rearrange("p a b -> p (a b)"), R=['KT%d' % i for i in range(17)], W=['dbg1'])
                    d2 = dbg_out('Vt', (128, (NU // 128) * 4 * 65), BF16)
                    k.dma(d2, Vt[:].rearrange("p a b c -> p (a b c)"), R=['Vt%d' % i for i in range(17)], W=['dbg2'])
                    d3 = dbg_out('Xd', (256, 8 * NSC), BF16)
                    k.dma(d3, xd_scr.rearrange("a b c -> a (b c)"), R=['xd_scr'], W=['dbg3'])
                    d4 = dbg_out('poolL', (256, SEQ + 16), BF16)
                    k.dma(d4, pool_scr[1], R=['pool_scr'], W=['dbg4'])
                    break

            k.barrier()
            with ExitStack() as sB:
                Bin8T = T(sB, 'Bin8T', (128, 32, 128), BF16)
                Cout8T = T(sB, 'Cout8T', (128, 32, 128), BF16)
                D8T = T(sB, 'D8T', (128, 32, 128), BF16)
                LPr = T(sB, 'LPr', (128, 32), F32)
                LPi = T(sB, 'LPi', (128, 32), F32)
                sgn = T(sB, 'sgn', (128, 1), F32)
                swap_f = T(sB, 'swap_f', (128, 128), F32)
                s5dT = T(sB, 's5dT', (128, 2), F32)
                bgluT = T(sB, 'bgluT', (128, 2), F32)
                wglu = T(sB, 'wglu', (128, 2, 256), BF16)
                k.dma(swap_f[:], cst['swapm'], W=['swap_f'])
                k.op('pool', lambda: G.memset(sgn[0:64, :], 1.0), W=['sgn'])
                k.op('pool', lambda: G.memset(sgn[64:128, :], -1.0), W=['sgn'])
                k.dma(s5dT[:], prm['s5_d'][l].rearrange("(c p) -> p c", p=128), W=['s5dT'], allow_slow_non_contiguous=True)
                k.dma(bgluT[:], prm['b_glu'][l].rearrange("(c p) -> p c", p=128), W=['bgluT'], allow_slow_non_contiguous=True)
                TK = ['s5t']
                with ExitStack() as sT:
                    def t32(name):
                        return T(sT, name, (128, 32), F32)
                    LR, LI, DT_, ar_, ai_ = t32('LR'), t32('LI'), t32('DT_'), t32('ar_'), t32('ai_')
                    t1, t2, t3, t4 = t32('t1'), t32('t2'), t32('t3'), t32('t4')
                    er, ei, kr_, ki_, den = t32('er'), t32('ei'), t32('kr_'), t32('ki_'), t32('den')
                    lnr, lni = t32('lnr'), t32('lni')
                    halfpi = T(sT, 'halfpi', (128, 1), F32)
                    PWr = T(sT, 'PWr', (128, 16, 32), F32)
                    PWi = T(sT, 'PWi', (128, 16, 32), F32)
                    BR = T(sT, 'BR', (128, 32, 16), F32)
                    BI = T(sT, 'BI', (128, 32, 16), F32)
                    CR = T(sT, 'CR', (128, 32, 16), F32)
                    CI = T(sT, 'CI', (128, 32, 16), F32)
                    bbr = T(sT, 'bbr', (128, 32, 16), F32)
                    bbi = T(sT, 'bbi', (128, 32, 16), F32)
                    u1 = T(sT, 'u1', (128, 16, 16), F32)
                    u2 = T(sT, 'u2', (128, 16, 16), F32)
                    CRn = T(sT, 'CRn', (128, 2, 64), F32)
                    Bin8 = T(sT, 'Bin8', (128, 32, 128), F32)
                    CoutN = T(sT, 'CoutN', (128, 32, 128), F32)
                    wg32 = T(sT, 'wg32', (128, 2, 256), F32)
                    mlow = T(sT, 'mlow', (128, 128), F32)
                    mup = T(sT, 'mup', (128, 128), F32)
                    psT = P(sT, 'psT', (128, 128))

                    def dv(fn):
                        k.op('dve', fn, R=TK, W=TK)

                    def ac(fn):
                        k.op('act', fn, R=TK, W=TK)

                    k.dma(mlow[:], cst['mlow'], W=TK)
                    k.dma(mup[:], cst['mup'], W=TK)
                    k.dma(wg32[:], prm['w_glu'][l].rearrange("(c p) n -> p c n", p=128), W=TK)
                    dv(lambda: V.tensor_copy(wglu[:], wg32[:]))
                    for hf in range(2):
                        sl = slice(hf * 64, hf * 64 + 64)
                        k.dma(LR[sl, :], prm['lam_re'][l].rearrange("d g p -> p (d g)"), W=TK, allow_slow_non_contiguous=True)
                        k.dma(LI[sl, :], prm['lam_im'][l].rearrange("d g p -> p (d g)"), W=TK, allow_slow_non_contiguous=True, q='pool')
                        k.dma(BR[sl], prm['s5_b_re'][l].rearrange("d g p h -> p (d g) h"), W=TK)
                        k.dma(BI[sl], prm['s5_b_im'][l].rearrange("d g p h -> p (d g) h"), W=TK, q='pool')
                    k.dma(DT_[:], prm['log_dt'][l].rearrange("d g -> (d g)").partition_broadcast(128), W=TK)
                    k.op('pool', lambda: G.memset(halfpi[:], float(np.pi / 2)), W=TK)
                    for ci, (cn, Ct) in enumerate((('s5_c_re', CR), ('s5_c_im', CI))):
                        crow = prm[cn][l].rearrange("d g h p -> (d g h) p")
                        for j in range(4):
                            k.dma(CRn[:, 0, :], crow[j * 128:(j + 1) * 128, :], W=TK)
                            k.dma(CRn[:, 1, :], crow[j * 128:(j + 1) * 128, :], W=TK, q='pool')
                            k.op('pe', lambda: PE.transpose(psT[:], CRn[:].rearrange("r a p -> r (a p)"), ident_f[:]), R=TK + ['ident_f'], W=['psT'])
                            k.op('dve', lambda j=j, Ct=Ct: V.tensor_copy(Ct[:, j * 8:(j + 1) * 8, :].rearrange("q a h -> q (a h)"), psT[:]), R=['psT'] + TK, W=TK)
                    ac(lambda: A.activation(DT_[:], DT_[:], AF.Exp))
                    dv(lambda: V.tensor_tensor(ar_[:], LR[:], DT_[:], op=ALU.mult))
                    dv(lambda: V.tensor_tensor(ai_[:], LI[:], DT_[:], op=ALU.mult))
                    ac(lambda: A.activation(t1[:], ar_[:], AF.Exp, scale=1.0 / 16))
                    ac(lambda: A.activation(t2[:], ai_[:], AF.Sin, scale=1.0 / 16, bias=halfpi[:, 0:1]))
                    ac(lambda: A.activation(t3[:], ai_[:], AF.Sin, scale=1.0 / 16))
                    dv(lambda: V.tensor_tensor(er[:], t1[:], t2[:], op=ALU.mult))
                    dv(lambda: V.tensor_tensor(ei[:], t1[:], t3[:], op=ALU.mult))

                    def cmul(or_, oi_, xr, xi, yr, yi, a1=None, a2=None, a3=None, a4=None):
                        a1, a2, a3, a4 = t1[:], t2[:], t3[:], t4[:]
                        dv(lambda: V.tensor_tensor(a1, xr, yr, op=ALU.mult))
                        dv(lambda: V.tensor_tensor(a2, xi, yi, op=ALU.mult))
                        dv(lambda: V.tensor_tensor(a3, a1, a2, op=ALU.subtract))
                        dv(lambda: V.tensor_tensor(a1, xr, yi, op=ALU.mult))
                        dv(lambda: V.tensor_tensor(a2, xi, yr, op=ALU.mult))
                        dv(lambda: V.tensor_tensor(a4, a1, a2, op=ALU.add))
                        dv(lambda: V.tensor_copy(or_, a3))
                        dv(lambda: V.tensor_copy(oi_, a4))

                    for _ in range(4):
                        cmul(er[:], ei[:], er[:], ei[:], er[:], ei[:], t1[:], t2[:], t3[:], t4[:])
                    dv(lambda: V.tensor_scalar_add(lnr[:], er[:], -1.0))
                    dv(lambda: V.tensor_tensor(t1[:], LR[:], LR[:], op=ALU.mult))
                    dv(lambda: V.tensor_tensor(t2[:], LI[:], LI[:], op=ALU.mult))
                    dv(lambda: V.tensor_tensor(den[:], t1[:], t2[:], op=ALU.add))
                    dv(lambda: V.reciprocal(den[:], den[:]))
                    dv(lambda: V.tensor_tensor(t1[:], lnr[:], LR[:], op=ALU.mult))
                    dv(lambda: V.tensor_tensor(t2[:], ei[:], LI[:], op=ALU.mult))
                    dv(lambda: V.tensor_tensor(t1[:], t1[:], t2[:], op=ALU.add))
                    dv(lambda: V.tensor_tensor(kr_[:], t1[:], den[:], op=ALU.mult))
                    dv(lambda: V.tensor_tensor(t1[:], ei[:], LR[:], op=ALU.mult))
                    dv(lambda: V.tensor_tensor(t2[:], lnr[:], LI[:], op=ALU.mult))
                    dv(lambda: V.tensor_tensor(t1[:], t1[:], t2[:], op=ALU.subtract))
                    dv(lambda: V.tensor_tensor(ki_[:], t1[:], den[:], op=ALU.mult))
                    bk = lambda a: a[:].unsqueeze(2).to_broadcast([128, 32, 16])
                    w1 = Bin8[:, :, 0:16]
                    w2 = Bin8[:, :, 16:32]
                    dv(lambda: V.tensor_tensor(w1, BR[:], bk(kr_), op=ALU.mult))
                    dv(lambda: V.tensor_tensor(w2, BI[:], bk(ki_), op=ALU.mult))
                    dv(lambda: V.tensor_tensor(bbr[:], w1, w2, op=ALU.subtract))
                    dv(lambda: V.tensor_tensor(w1, BI[:], bk(kr_), op=ALU.mult))
                    dv(lambda: V.tensor_tensor(w2, BR[:], bk(ki_), op=ALU.mult))
                    dv(lambda: V.tensor_tensor(bbi[:], w1, w2, op=ALU.add))
                    dv(lambda: V.memset(PWr[:, 7, :], 1.0))
                    dv(lambda: V.memset(PWi[:, 7, :], 0.0))
                    for e in range(1, 9):
                        cmul(PWr[:, 7 + e, :], PWi[:, 7 + e, :], PWr[:, 6 + e, :], PWi[:, 6 + e, :], er[:], ei[:])
                    ac(lambda: A.activation(den[:], ar_[:], AF.Exp, scale=-2.0))
                    dv(lambda: V.tensor_tensor(lnr[:], er[:], den[:], op=ALU.mult))
                    dv(lambda: V.tensor_tensor(lni[:], ei[:], den[:], op=ALU.mult))
                    dv(lambda: V.tensor_scalar_mul(lni[:], lni[:], -1.0))
                    for e in range(1, 8):
                        cmul(PWr[:, 7 - e, :], PWi[:, 7 - e, :], PWr[:, 8 - e, :], PWi[:, 8 - e, :], lnr[:], lni[:])
                    dv(lambda: V.tensor_copy(LPr[:], PWr[:, 15, :]))
                    dv(lambda: V.tensor_copy(LPi[:], PWi[:, 15, :]))

                    def ctab(dst, Xr, Xi, d, slot, e, im_sign):
                        gs = slice(d * 16, d * 16 + 8)
                        for hf in range(2):
                            ps_ = slice(hf * 64, hf * 64 + 64)
                            pr = PWr[ps_, 7 + e, gs].unsqueeze(2).to_broadcast([64, 8, 16])
                            pi = PWi[ps_, 7 + e, gs].unsqueeze(2).to_broadcast([64, 8, 16])
                            o = dst[ps_, gs, slot * 16:(slot + 1) * 16]
                            if hf == 0:
                                dv(lambda: V.tensor_tensor(u1[ps_, 0:8], Xr[ps_, gs, :], pr, op=ALU.mult))
                                dv(lambda: V.tensor_tensor(u2[ps_, 0:8], Xi[ps_, gs, :], pi, op=ALU.mult))
                                dv(lambda: V.tensor_tensor(o, u1[ps_, 0:8], u2[ps_, 0:8], op=ALU.subtract))
                            else:
                                dv(lambda: V.tensor_tensor(u1[ps_, 0:8], Xr[ps_, gs, :], pi, op=ALU.mult))
                                dv(lambda: V.tensor_tensor(u2[ps_, 0:8], Xi[ps_, gs, :], pr, op=ALU.mult))
                                dv(lambda: V.tensor_tensor(o, u1[ps_, 0:8], u2[ps_, 0:8], op=ALU.add))
                                if im_sign < 0:
                                    dv(lambda: V.tensor_scalar_mul(o, o, -1.0))

                    for d in range(2):
                        for i in range(8):
                            ctab(Bin8, bbr, bbi, d, i, (7 - i) if d == 0 else i, +1)
                            ctab(Cout8T, CR, CI, d, i, (i + 1) if d == 0 else (8 - i), -1)
                            ctab(CoutN, CR, CI, d, i, (i - 7) if d == 0 else (-i), -1)
                    for dg in [d_ * 16 + g_ for d_ in range(2) for g_ in range(8)]:
                        k.op('pe', lambda dg=dg: PE.transpose(psT[:], Bin8[:, dg, :], ident_f[:]), R=TK + ['ident_f'], W=['psT'])
                        k.op('act', lambda dg=dg: A.copy(Bin8T[:, dg, :], psT[:]), R=['psT'], W=['Bin8T'])
                        k.op('pe', lambda dg=dg: PE.matmul(psT[:], lhsT=Bin8[:, dg, :], rhs=CoutN[:, dg, :], start=True, stop=True), R=TK, W=['psT'])
                        mk = mlow if dg < 16 else mup
                        k.op('dve', lambda dg=dg, mk=mk: V.tensor_tensor(D8T[:, dg, :], psT[:], mk[:], op=ALU.mult), R=['psT'] + TK, W=['D8T'])

                    psd = P(sT, 'psd', (128, 128), BF16)
                    k.op('pe', lambda: PE.transpose(psd[:], ident_b[:], ident_b[:]), R=['ident_b', 'psT'], W=['psd'])

                if 'B0' in dbg and l == 0:
                    for nm, tl in (('Bin8T', Bin8T), ('Cout8T', Cout8T), ('D8T', D8T)):
                        d_ = dbg_out(nm, (128, 32 * 128), BF16)
                        k.dma(d_, tl[:].rearrange("p a b -> p (a b)"), R=[nm], W=['dbg' + nm])
                    d_ = dbg_out('LP', (128, 64))
                    k.dma(d_[:, 0:32], LPr[:], R=TK, W=['dbgLP'])
                    k.dma(d_[:, 32:64], LPi[:], R=TK, W=['dbgLP'])
                    break

                k.barrier()
                with ExitStack() as sS:
                    IM = T(sS, 'IM', (128, 2, NSC), BF16)
                    H = T(sS, 'H', (128, 4, NSC + 4), F32)
                    Hb = T(sS, 'Hb', (128, 4, NSC + 4), BF16)
                    Yo = T(sS, 'Yo', (128, 2, NSC), BF16)
                    ATb = [T(sS, 'AT%d' % i, (128, 11, 4, 128), BF16) for i in range(2)]
                    atmp = T(sS, 'atmp', (128, 128), F32)
                    v2 = T(sS, 'v2', (128, 32), F32)
                    LQr = T(sS, 'LQr', (128, 11, 32), F32)
                    LQi = T(sS, 'LQi', (128, 11, 32), F32)
                    q1, q2 = T(sS, 'q1', (128, 32), F32), T(sS, 'q2', (128, 32), F32)
                    psS = [[P(sS, 'psS%d_%d' % (i, j), (128, 512)) for j in range(4)] for i in range(2)]
                    GO = 2
                    NLEV = int(os.environ.get('DBG_NLEV', 11))
                    NRND = int(os.environ.get('DBG_NRND', 4))
                    DOREAD = int(os.environ.get('DBG_READ', 1))
                    k.op('dve', lambda: V.tensor_copy(LQr[:, 0, :], LPr[:]), R=TK, W=['LQ'])
                    k.op('dve', lambda: V.tensor_copy(LQi[:, 0, :], LPi[:]), R=TK, W=['LQ'])
                    for lev in range(1, NLEV):
                        a_r, a_i = LQr[:, lev - 1, :], LQi[:, lev - 1, :]
                        k.op('dve', lambda: V.tensor_tensor(q1[:], a_r, a_r, op=ALU.mult), R=['LQ'], W=['q1'])
                        k.op('dve', lambda: V.tensor_tensor(q2[:], a_i, a_i, op=ALU.mult), R=['LQ'], W=['q2'])
                        k.op('dve', lambda lev=lev: V.tensor_tensor(LQr[:, lev, :], q1[:], q2[:], op=ALU.subtract), R=['q1', 'q2'], W=['LQ'])
                        k.op('dve', lambda: V.tensor_tensor(q1[:], a_r, a_i, op=ALU.mult), R=['LQ'], W=['q1'])
                        k.op('dve', lambda lev=lev: V.tensor_scalar_mul(LQi[:, lev, :], q1[:], 2.0), R=['q1'], W=['LQ'])
                    if int(os.environ.get('DBG_MS', 1)):
                        k.op('dve', lambda: V.memset(H[:], 0.0), W=['H%d' % j for j in range(4)])
                        k.op('pool', lambda: G.memset(Hb[:], 0.0), W=['Hb%d' % j for j in range(4)])
                    xdg = xd_scr.rearrange("(g h) i n -> g h i n", h=16)
                    ydg = yd_scr.rearrange("(g h) i n -> g h i n", h=16)
                    CB = [(0, 512), (512, 1024), (1024, NSC)]
                    for rnd in range(NRND):
                        gl_ = [2 * rnd, 2 * rnd + 1]
                        combos = [(d, g) for g in gl_ for d in range(2)]
                        if int(os.environ.get('DBG_IMZ', 0)):
                            k.op('pool', lambda: G.memset(IM[:], 0.5), W=['IM'])
                        for gi, g in enumerate(gl_ if int(os.environ.get('DBG_IM', 1)) else []):
                            for i in range(8):
                                k.dma(IM[16 * i:16 * i + 16, gi, :], xdg[g, :, i, :], R=['xd_scr'], W=['IM'], q=('sp' if i % 2 == 0 else 'pool'))
                        ATc, ATk = ATb[rnd % 2], 'AT%d' % (rnd % 2)
                        for lev in range(NLEV):
                            k.op('dve', lambda lev=lev: V.tensor_scalar_mul(v2[:], LQi[:, lev, :], sgn[:, 0:1]), R=['LQ', 'sgn'], W=['v2'])
                            for j, (d, g) in enumerate(combos):
                                dg = d * 16 + g
                                k.op('dve', lambda lev=lev, dg=dg: V.tensor_scalar_mul(atmp[:], ident_f[:], LQr[:, lev, dg:dg + 1]), R=['LQ', 'ident_f'], W=['atmp'])
                                k.op('dve', lambda lev=lev, dg=dg, j=j: V.scalar_tensor_tensor(ATc[:, lev, j, :], swap_f[:], v2[:, dg:dg + 1], atmp[:],
                                                                                              op0=ALU.mult, op1=ALU.add), R=['swap_f', 'v2', 'atmp'], W=[ATk])
                        for bi_, (c0, c1) in enumerate(CB):
                            pss = psS[bi_ % 2]
                            for j, (d, g) in enumerate(combos):
                                pk = 'psS%d_%d' % (bi_ % 2, j)
                                k.op('pe', lambda j=j, d=d, g=g: PE.matmul(pss[j][:, 0:c1 - c0], lhsT=Bin8T[:, d * 16 + g, :], rhs=IM[:, j // 2, c0:c1],
                                                                            start=True, stop=True), R=['Bin8T', 'IM'], W=[pk])
                                k.op('dve', lambda j=j: V.tensor_copy(H[:, j, GO + c0:GO + c1], pss[j][:, 0:c1 - c0]), R=[pk], W=['H%d' % j])
                                k.op('act', lambda j=j: A.copy(Hb[:, j, GO + c0:GO + c1], H[:, j, GO + c0:GO + c1]), R=['H%d' % j], W=['Hb%d' % j])
                        for lev in range(NLEV):
                            s_ = 1 << lev
                            nb = (NSC - s_ + 511) // 512
                            for bi_ in range(nb):
                                pss = psS[bi_ % 2]
                                hi_f = NSC - bi_ * 512
                                lo_f = max(s_, hi_f - 512)
                                lo_b = bi_ * 512
                                hi_b = min(NSC - s_, lo_b + 512)
                                for j, (d, g) in enumerate(combos):
                                    pk = 'psS%d_%d' % (bi_ % 2, j)
                                    lo_, hi_ = (lo_f, hi_f) if d == 0 else (lo_b, hi_b)
                                    sh_ = -s_ if d == 0 else s_
                                    k.op('pe', lambda j=j: PE.matmul(pss[j][:, 0:hi_ - lo_], lhsT=ATc[:, lev, j, :], rhs=Hb[:, j, GO + lo_ + sh_:GO + hi_ + sh_],
                                                                    start=True, stop=True), R=[ATk, 'Hb%d' % j], W=[pk])
                                    k.op('dve', lambda j=j, lo_=lo_, hi_=hi_: V.tensor_tensor(H[:, j, GO + lo_:GO + hi_], H[:, j, GO + lo_:GO + hi_], pss[j][:, 0:hi_ - lo_], op=ALU.add),
                                         R=[pk, 'H%d' % j], W=['H%d' % j])
                                    if True:
                                        k.op('act', lambda j=j, lo_=lo_, hi_=hi_: A.copy(Hb[:, j, GO + lo_:GO + hi_], H[:, j, GO + lo_:GO + hi_]), R=['H%d' % j], W=['Hb%d' % j])
                                    else:
                                        k.op('pool', lambda j=j, lo_=lo_, hi_=hi_: G.tensor_copy(Hb[:, j, GO + lo_:GO + hi_], H[:, j, GO + lo_:GO + hi_]), R=['H%d' % j], W=['Hb%d' % j])
                        RB = [(0, 32, (0,)), (32, 544, (0, 1)), (544, 1056, (0, 1)), (1056, NSC, (1,))]
                        for bi_, (c0, c1, dirs) in enumerate(RB):
                            pss = psS[bi_ % 2]
                            for gi, g in enumerate(gl_):
                                pk = 'psS%d_%d' % (bi_ % 2, gi)
                                nmm = 2 * len(dirs)
                                mi = 0
                                for d in dirs:
                                    j = 2 * gi + d
                                    sh = -1 if d == 0 else 1
                                    k.op('pe', lambda gi=gi, d=d, g=g, mi=mi: PE.matmul(pss[gi][:, 0:c1 - c0], lhsT=D8T[:, d * 16 + g, :], rhs=IM[:, gi, c0:c1],
                                                                                    start=(mi == 0), stop=False), R=['D8T', 'IM'], W=[pk])
                                    mi += 1
                                    k.op('pe', lambda gi=gi, d=d, g=g, j=j, sh=sh, mi=mi: PE.matmul(pss[gi][:, 0:c1 - c0], lhsT=Cout8T[:, d * 16 + g, :],
                                                                                                rhs=Hb[:, j, GO + c0 + sh:GO + c1 + sh],
                                                                                                start=False, stop=(mi == nmm - 1)), R=['Cout8T', 'Hb%d' % j], W=[pk])
                                    mi += 1
                                k.op('act', lambda gi=gi: A.copy(Yo[:, gi, c0:c1], pss[gi][:, 0:c1 - c0]), R=[pk], W=['Yo'])
                        for gi, g in enumerate(gl_ if int(os.environ.get('DBG_YD', 1)) else []):
                            for i in range(8):
                                k.dma(ydg[g, :, i, :], Yo[16 * i:16 * i + 16, gi, :], R=['Yo'], W=['yd_scr'], q=('sp' if i % 2 == 0 else 'pool'))

                if nlayers > 1 or int(os.environ.get('DBG_CC', 0)):
                    k.barrier()
                    k._sync('pool', ['yd_scr'], ['ydg'])
                    ydf = yd_scr.rearrange("c i n -> c (i n)")
                    for c_ in range(2):
                        G.collective_compute("AllGather", ALU.bypass, replica_groups=[[0, 1], [2, 3], [4, 5], [6, 7]],
                                             ins=[ydf[c_ * 64:(c_ + 1) * 64, :]], outs=[ydg_scr[c_]]).then_inc(ccsem)
                    ccn[0] += 2
                    for eng in k.eng.values():
                        eng.wait_ge(ccsem, ccn[0])
                    with ExitStack() as sX:
                        ya = T(sX, 'ya', (128, 8 * NSC), BF16)
                        yb = T(sX, 'yb', (128, 8 * NSC), BF16)
                        for c_ in range(2):
                            k.dma(ya[c_ * 64:(c_ + 1) * 64, :], ydg_scr[c_][64:128, :], W=['ya'], q=('sp' if c_ == 0 else 'pool'))
                            k.dma(yb[c_ * 64:(c_ + 1) * 64, :], ydg_scr[c_][0:64, :], W=['yb'], q=('sp' if c_ == 0 else 'pool'))
                        k.op('dve', lambda: V.tensor_scalar_mul(ya[:], ya[:], msb[:, 0:1]), R=['ya', 'msb'], W=['ya'])
                        k.op('dve', lambda: V.scalar_tensor_tensor(ya[:], yb[:], msb[:, 1:2], ya[:], op0=ALU.mult, op1=ALU.add), R=['ya', 'yb', 'msb'], W=['ya'])
                        k.dma(ydf[128:256, :], ya[:], R=['ya'], W=['yd_scr'])
                    k.barrier()

                if 'B1' in dbg and l == 0:
                    d_ = dbg_out('yd', (256, 8 * NSC), BF16)
                    if int(os.environ.get('DBG_YDD', 1)):
                        k.dma(d_, yd_scr.rearrange("a b c -> a (b c)"), R=['yd_scr'], W=['dbgyd'])
                    break

                k.barrier()
                with ExitStack() as sG:
                    xdt = T(sG, 'xdt', (128, 2, 8, 64), BF16)
                    ydt = T(sG, 'ydt', (128, 2, 8, 64), BF16)
                    yd2 = T(sG, 'yd2', (128, 2, 8, 64), BF16)
                    yf = T(sG, 'yf', (128, 2, 512), F32)
                    g1 = T(sG, 'g1', (128, 2, 512), F32)
                    g2 = T(sG, 'g2', (128, 2, 512), F32)
                    glb = T(sG, 'glb', (128, 2, 512), BF16)
                    sgm = T(sG, 'sgm', (128, 2, 512), F32)
                    s5ob = T(sG, 's5ob', (128, 2, 512), BF16)
                    psG = [P(sG, 'psG%d' % i, (128, 512)) for i in range(2)]
                    xdv = xd_scr.rearrange("(c p) i n -> p c i n", p=128)
                    ydv = yd_scr.rearrange("(c p) i n -> p c i n", p=128)
                    xdtB = T(sG, 'xdtB', (128, 2, 8, 64), BF16)
                    ydtB = T(sG, 'ydtB', (128, 2, 8, 64), BF16)
                    blocks = [('ctx', 0, NCTX)] + [('lat', 512 * i, 512) for i in range(HALF // 512)]
                    for (kind, t0, Tn) in blocks:
                        u0 = t0 if kind == 'ctx' else NCTX + t0
                        n8 = Tn // 8
                        for ct in range(2):
                            k.dma(xdt[:, ct, :, 0:n8], xdv[:, ct, :, u0 // 8:u0 // 8 + n8], R=['xd_scr'], W=['xdt'])
                            k.dma(ydt[:, ct, :, 0:n8], ydv[:, ct, :, u0 // 8:u0 // 8 + n8], R=['yd_scr'], W=['ydt'], q='pool')
                            if kind == 'lat':
                                uB = u0 + HALF
                                k.dma(xdtB[:, ct, :, 0:n8], xdv[:, ct, :, uB // 8:uB // 8 + n8], R=['xd_scr'], W=['xdtB'], q='pool')
                                k.dma(ydtB[:, ct, :, 0:n8], ydv[:, ct, :, uB // 8:uB // 8 + n8], R=['yd_scr'], W=['ydtB'])
                            if kind == 'ctx':
                                k.dma(yd2[:, ct, :, 0:n8], ydv[:, ct, :, NU // 8:NU // 8 + n8], R=['yd_scr'], W=['yd2'])
                        if kind == 'ctx':
                            k.op('pool', lambda: G.tensor_tensor(ydt[:, :, :, 0:n8], ydt[:, :, :, 0:n8], yd2[:, :, :, 0:n8], op=ALU.add), R=['ydt', 'yd2'], W=['ydt'])
                        else:
                            for (ta_, tb_, ka_, kb_) in ((xdt, xdtB, 'xdt', 'xdtB'), (ydt, ydtB, 'ydt', 'ydtB')):
                                k.op('dve', lambda ta_=ta_: V.tensor_scalar_mul(ta_[:], ta_[:], msb[:, 0:1]), R=[ka_, 'msb'], W=[ka_])
                                k.op('dve', lambda ta_=ta_, tb_=tb_: V.scalar_tensor_tensor(ta_[:], tb_[:], msb[:, 1:2], ta_[:], op0=ALU.mult, op1=ALU.add), R=[ka_, kb_, 'msb'], W=[ka_])
                        for ct in range(2):
                            k.op('dve', lambda ct=ct: V.scalar_tensor_tensor(yf[:, ct, 0:Tn].rearrange("p (c i) -> p i c", i=8), xdt[:, ct, :, 0:n8], s5dT[:, ct:ct + 1],
                                                                             ydt[:, ct, :, 0:n8], op0=ALU.mult, op1=ALU.add), R=['xdt', 'ydt', 's5dT'], W=['yf'])
                        if 'B2y' in dbg and l == 0:
                            if kind == 'ctx':
                                dyl = dbg_out('yl', (256, NU))
                            k.dma(dyl[:, u0:u0 + Tn].rearrange("(c p) n -> p c n", p=128), yf[:, :, 0:Tn], R=['yf'], W=['dbgyl'])
                        k.op('pool', lambda: G.tensor_tensor(g1[:, :, 0:Tn], yf[:, :, 0:Tn], yf[:, :, 0:Tn], op=ALU.mult), R=['yf'], W=['g1'])
                        k.op('dve', lambda: V.tensor_scalar(g1[:, :, 0:Tn], g1[:, :, 0:Tn], 0.044715, 1.0, op0=ALU.mult, op1=ALU.add), R=['g1'], W=['g1'])
                        k.op('pool', lambda: G.tensor_tensor(g2[:, :, 0:Tn], g1[:, :, 0:Tn], yf[:, :, 0:Tn], op=ALU.mult), R=['g1', 'yf'], W=['g2'])
                        k.op('act', lambda: A.activation(g2[:, :, 0:Tn], g2[:, :, 0:Tn], AF.Sigmoid, scale=1.5957691216057308), R=['g2'], W=['g2'])
                        k.op('dve', lambda: V.tensor_tensor(glb[:, :, 0:Tn], yf[:, :, 0:Tn], g2[:, :, 0:Tn], op=ALU.mult), R=['yf', 'g2'], W=['glb'])
                        for m in range(2):
                            for kt in range(2):
                                k.op('pe', lambda m=m, kt=kt: PE.matmul(psG[m][:, 0:Tn], lhsT=wglu[:, kt, m * 128:(m + 1) * 128], rhs=glb[:, kt, 0:Tn],
                                                                        start=(kt == 0), stop=(kt == 1)), R=['wglu', 'glb'], W=['psG%d' % m])
                            k.op('act', lambda m=m: A.activation(sgm[:, m, 0:Tn], psG[m][:, 0:Tn], AF.Sigmoid, bias=bgluT[:, m:m + 1]), R=['psG%d' % m, 'bgluT'], W=['sgm'])
                        k.op('dve', lambda: V.tensor_tensor(s5ob[:, :, 0:Tn], glb[:, :, 0:Tn], sgm[:, :, 0:Tn], op=ALU.mult), R=['glb', 'sgm'], W=['s5ob'])
                        k.dma(s5o_scr[:, u0:u0 + Tn].rearrange("(c p) n -> p c n", p=128), s5ob[:, :, 0:Tn], R=['s5ob'], W=['s5o_scr'])

                if 'B2' in dbg and l == 0:
                    d_ = dbg_out('s5o', (256, NU), BF16)
                    k.dma(d_, s5o_scr, R=['s5o_scr'], W=['dbgs5o'])
                    break

            k.barrier()
            with ExitStack() as sC:
                wC = T(sC, 'wC', (128, 8, 1728), BF16)
                wout = T(sC, 'wout', (128, 8, 1024), BF16)
                wqh = T(sC, 'wqh', (128, 2, 4, 96), BF16)
                wqr = T(sC, 'wqr', (128, 2, 4, 96), BF16)
                wpl = T(sC, 'wpl', (128, 2, 128), BF16)
                wsT = T(sC, 'wsT', (128, 4, 128), BF16)
                bsb = T(sC, 'bsb', (128, 4, 128), F32)
                gate_bc = T(sC, 'gate_bc', (128, D), F32)
                lng_bc = T(sC, 'lng_bc', (128, D), F32)
                lnb_bc = T(sC, 'lnb_bc', (128, D), F32)
                colv = T(sC, 'colv', (128, 8), F32)
                winv = T(sC, 'winv', (128, 2), F32)
                with ExitStack() as sw:
                    wst2 = [T(sw, 'wstC%d' % i, (128, 8, 256), F32) for i in range(2)]
                    wq32 = T(sw, 'wq32', (128, 2, 384), F32)
                    wp32 = T(sw, 'wp32', (128, 2, 128), F32)
                    ws32 = T(sw, 'ws32', (128, 128), F32)
                    psw = P(sw, 'psw', (128, 128))
                    win = prm['w_in'][l].rearrange("(c p) n -> p c n", p=128)
                    segs = [(416, 192, 0), (864, 256, 192), (1120, 256, 448), (1376, 256, 704), (1632, 256, 960), (1888, 256, 1216), (2144, 256, 1472)]
                    for si_, (c0, n, d0) in enumerate(segs):
                        wst, wk = wst2[si_ % 2], 'wstC%d' % (si_ % 2)
                        k.dma(wst[:, :, 0:n], win[:, :, c0:c0 + n], W=[wk], q=('sp' if si_ % 2 == 0 else 'pool'))
                        k.op('dve' if si_ % 2 == 0 else 'pool', lambda n=n, d0=d0, wst=wst, si_=si_: (V if si_ % 2 == 0 else G).tensor_copy(wC[:, :, d0:d0 + n], wst[:, :, 0:n]), R=[wk], W=['wC'])
                    wo = prm['w_out'][l].rearrange("(c p) n -> p c n", p=128)
                    for j in range(4):
                        wst, wk = wst2[(j + 1) % 2], 'wstC%d' % ((j + 1) % 2)
                        k.dma(wst[:], wo[:, :, j * 256:(j + 1) * 256], W=[wk], q=('sp' if j % 2 == 0 else 'pool'))
                        k.op('dve' if j % 2 == 0 else 'pool', lambda j=j, wst=wst: (V if j % 2 == 0 else G).tensor_copy(wout[:, :, j * 256:(j + 1) * 256], wst[:]), R=[wk], W=['wout'])
                    for ci, nm in enumerate(('sgu_g', 'sgu_b', 'pool_scale')):
                        k.dma(colv[:, 2 * ci:2 * ci + 2], prm[nm][l].rearrange("(c p) -> p c", p=128), W=['colv'], allow_slow_non_contiguous=True)
                    k.dma(colv[:, 6:7], prm['g_q'][l][0:128].rearrange("(p o) -> p o", o=1), W=['colv'])
                    k.dma(colv[0:64, 7:8], prm['g_q'][l][128:192].rearrange("(p o) -> p o", o=1), W=['colv'])
                    k.op('pool', lambda: G.memset(winv[0:64, 0:1], 1.0 / 2), W=['winv'])
                    k.op('pool', lambda: G.memset(winv[64:128, 0:1], 1.0 / 4), W=['winv'])
                    k.op('pool', lambda: G.memset(winv[0:64, 1:2], 1.0 / 8), W=['winv'])
                    k.op('pool', lambda: G.memset(winv[64:128, 1:2], 1.0 / 16), W=['winv'])
                    k.dma(bsb[:].rearrange("p h t -> p (h t)"), prm['b_s'][l].rearrange("h t -> (h t)").partition_broadcast(128), W=['bsb'])
                    k.dma(lng_bc[:], prm['ln_g'][l].partition_broadcast(128), W=['lng_bc'])
                    k.dma(lnb_bc[:], prm['ln_b'][l].partition_broadcast(128), W=['lnb_bc'], q='pool')
                    k.op('pool', lambda: G.memset(wq32[:], 0.0), W=['wq32'])
                    k.dma(wq32[:, 0, :], prm['w_uq'][l][0:128, :], W=['wq32'])
                    k.dma(wq32[0:64, 1, :], prm['w_uq'][l][128:192, :], W=['wq32'])
                    k.op('pool', lambda: G.memset(wqr[:], 0.0), W=['wqr'])
                    for kt in range(2):
                        rows = slice(0, 128) if kt == 0 else slice(0, 64)
                        k.op('dve', lambda kt=kt, rows=rows: V.tensor_scalar_mul(wq32[rows, kt, :], wq32[rows, kt, :], colv[rows, 6 + kt:7 + kt]), R=['wq32', 'colv'], W=['wq32'])
                    k.op('dve', lambda: V.tensor_copy(wqh[:].rearrange("p a h c -> p a (h c)"), wq32[:]), R=['wq32'], W=['wqh'])
                    wq4 = wq32[:].rearrange("p a (h c) -> p a h c", h=4)
                    for (d0, s0_) in ((0, 8), (8, 0), (16, 24), (24, 16)):
                        k.op('dve', lambda d0=d0, s0_=s0_: V.tensor_copy(wqr[:, :, :, 64 + d0:72 + d0], wq4[:, :, :, 64 + s0_:72 + s0_]), R=['wq32'], W=['wqr'])
                    k.op('pool', lambda: G.memset(wp32[:], 0.0), W=['wp32'])
                    for g in range(4):
                        r0 = (g % 2) * 64
                        k.dma(wp32[r0:r0 + 64, g // 2, r0:r0 + 64], prm['w_pool'][l][g], W=['wp32'])
                    k.op('dve', lambda: V.tensor_copy(wpl[:], wp32[:]), R=['wp32'], W=['wpl'])
                    for h in range(4):
                        k.dma(ws32[:], prm['w_s'][l][h], W=['ws32'])
                        k.op('pe', lambda: PE.transpose(psw[:], ws32[:], ident_f[:]), R=['ws32', 'ident_f'], W=['psw'])
                        k.op('act', lambda h=h: A.copy(wsT[:, h, :], psw[:]), R=['psw'], W=['wsT'])
                    psd2 = P(sw, 'psd2', (128, 128), BF16)
                    k.op('pe', lambda: PE.transpose(psd2[:], ident_b[:], ident_b[:]), R=['ident_b', 'psw'], W=['psd2'])
                k.barrier()

                with ExitStack() as sc:
                    xt = [T(sc, 'xtC0', (128, D), F32)] * 2
                    st6 = T(sc, 'st6C', (128, 2, 6), F32)
                    mv = T(sc, 'mvC', (128, 2), F32)
                    rstd = T(sc, 'rstdC', (128, 1), F32)
                    xn = T(sc, 'xnC', (128, D), BF16)
                    hT = T(sc, 'hTC', (128, 8, QB), BF16)
                    sq = T(sc, 'sqC', (128, 2, QB), BF16)
                    rms = T(sc, 'rmsC', (128, QB), F32)
                    QT = T(sc, 'QT', (128, 4, QB), BF16)
                    rc_t = T(sc, 'rc_tC', (128, QB), F32)
                    rs_t = T(sc, 'rs_tC', (128, QB), F32)
                    su = T(sc, 'su', (128, 2, QB), BF16)
                    svb = T(sc, 'svb', (128, 2, QB), BF16)
                    cqn = svb
                    mean = T(sc, 'mean', (128, QB), F32)
                    var = T(sc, 'var', (128, QB), F32)
                    q1t, q2t = mean, var
                    vnb = T(sc, 'vnb', (128, 2, QB), BF16)
                    vc = T(sc, 'vc', (128, 256), BF16)
                    mx = T(sc, 'mx', (128, 128), F32)
                    sg = T(sc, 'sg', (128, 8, QB), BF16)
                    PT = [T(sc, 'PT%d' % i, (128, 2 * QB), BF16) for i in range(3)]
                    osb = T(sc, 'osb', (128, 4, 65), F32)
                    rec = T(sc, 'rec', (128, 4, 1), F32)
                    att = T(sc, 'att', (128, 4, 256), BF16)
                    catg = sg
                    pin = T(sc, 'pin', (128, 2, QB + 16), BF16)
                    pa = T(sc, 'pa', (128, 2, QB + 16), F32)
                    svf = pa
                    pb = T(sc, 'pb', (128, 2, QB + 16), F32)
                    pld = T(sc, 'pld', (128, 2, QB), BF16)
                    s5t = T(sc, 's5t', (128, 2, QB), BF16)
                    rr = T(sc, 'rr', (128, D), F32)
                    hf32 = rr[:].rearrange("p (a b) -> p a b", a=8)
                    pt = P(sc, 'ptC', (128, 8, 128), BF16)
                    psC = [P(sc, 'psC%d' % i, (128, 512)) for i in range(4)]
                    psS_ = [P(sc, 'psSc%d' % i, (128, 512)) for i in range(2)]
                    psO = P(sc, 'psO', (128, 4, 65))
                    sm2c = [T(sc, 'sm2c_%d' % i, (128, 16), F32) for i in range(2)]
                    ptc2 = psC[3][:].bitcast(BF16).rearrange("p (a b) -> p a b", a=8)
                    LNC = {'xt': [(xt[0], 'xtC0'), (rr, 'rr')], 'xn': [(xn, 'xn'), (xn, 'xn')],
                           'pt': [(pt, 'ptC'), (ptc2, 'psC3')], 'sm': [(sm2c[0], 'sm2c_0'), (sm2c[1], 'sm2c_1')], 'hTk': 'hT', 'dve_mod': True}
                    assert QB <= 512 and NCTX <= QB
                    SCALE = 96 ** -0.5

                    qblocks = [('lat', QB * i, QB) for i in range(HALF // QB)]
                    if not last:
                        qblocks = [('ctx', 0, NCTX)] + qblocks
                    NQB = int(os.environ.get('DBG_NQB', len(qblocks)))
                    xcnt = 0

                    def do_ln(bi_):
                        kind_, t0_, Tn_ = qblocks[bi_]
                        src_ = cin if kind_ == 'ctx' else (xq_in if l == 0 else x1h_scr)
                        ln_block([src_[t0_ + sb_ * 128: t0_ + (sb_ + 1) * 128, :] for sb_ in range(Tn_ // 128)], ['x1w'],
                                 1 if kind_ == 'ctx' else 0, hT, LNC)

                    gcol = [None]
                    do_ln(0)
                    for bidx, (kind, t0, Tn) in enumerate(qblocks[:NQB]):
                        col = 1 if kind == 'ctx' else 0
                        src = cin if kind == 'ctx' else (xq_in if l == 0 else x1h_scr)
                        dst = (ctx1_scr if kind == 'ctx' else (out_ap if last else x1h_scr))
                        u0 = t0 if kind == 'ctx' else NCTX + t0
                        nsub = Tn // 128
                        nkt = (NCTX // 128) if kind == 'ctx' else (NU // 128)
                        if gcol[0] != col:
                            k.dma(gate_bc[:], gate_scr[l, col].partition_broadcast(128), R=['gate_scr'], W=['gate_bc'])
                            gcol[0] = col

                        def proj(pst, key, c0, ncol):
                            for dc in range(8):
                                k.op('pe', lambda dc=dc: PE.matmul(pst[0:ncol, 0:Tn], lhsT=wC[:, dc, c0:c0 + ncol], rhs=hT[:, dc, 0:Tn],
                                                                  start=(dc == 0), stop=(dc == 7)), R=['wC', 'hT'], W=[key])
                        proj(psC[0], 'psC0', 0, 128)
                        proj(psC[1], 'psC1', 128, 64)
                        k.op('act', lambda: A.activation(sq[:, 0, 0:Tn], psC[0][:, 0:Tn], AF.Square), R=['psC0'], W=['sq'])
                        k.op('act', lambda: A.activation(sq[0:64, 1, 0:Tn], psC[1][0:64, 0:Tn], AF.Square), R=['psC1'], W=['sq'])
                        k.op('pe', lambda: PE.matmul(psC[2][:, 0:Tn], lhsT=ones_b[:, :], rhs=sq[:, 0, 0:Tn], start=True, stop=False), R=['ones_b', 'sq'], W=['psC2'])
                        k.op('pe', lambda: PE.matmul(psC[2][:, 0:Tn], lhsT=ones_b[0:64, :], rhs=sq[0:64, 1, 0:Tn], start=False, stop=True), R=['ones_b', 'sq'], W=['psC2'])
                        k.op('act', lambda: A.activation(rms[:, 0:Tn], psC[2][:, 0:Tn], AF.Sqrt, bias=epsb[:, 0:1], scale=1.0 / 192), R=['psC2', 'epsb'], W=['rms'])
                        k.op('dve', lambda: V.reciprocal(rms[:, 0:Tn], rms[:, 0:Tn]), R=['rms'], W=['rms'])
                        k.op('dve', lambda: V.tensor_tensor(cqn[:, 0, 0:Tn], psC[0][:, 0:Tn], rms[:, 0:Tn], op=ALU.mult), R=['psC0', 'rms'], W=['svb'])
                        k.op('dve', lambda: V.tensor_tensor(cqn[0:64, 1, 0:Tn], psC[1][0:64, 0:Tn], rms[0:64, 0:Tn], op=ALU.mult), R=['psC1', 'rms'], W=['svb'])
                        if kind == 'lat':
                            k.dma(rc_t[64:96, :], ropecq[:, t0:t0 + Tn], W=['rc_t'])
                            k.dma(rs_t[64:96, :], ropesq[:, t0:t0 + Tn], W=['rs_t'], q='pool')
                        for h in range(4):
                            pq, pqk = psC[h % 2], 'psC%d' % (h % 2)
                            pr, prk = psC[2 + h % 2], 'psC%d' % (2 + h % 2)
                            k.op('pe', lambda h=h: PE.matmul(pq[0:96, 0:Tn], lhsT=wqh[:, 0, h, :], rhs=cqn[:, 0, 0:Tn], start=True, stop=False), R=['wqh', 'svb'], W=[pqk])
                            k.op('pe', lambda h=h: PE.matmul(pq[0:96, 0:Tn], lhsT=wqh[0:64, 1, h, :], rhs=cqn[0:64, 1, 0:Tn], start=False, stop=True), R=['wqh', 'svb'], W=[pqk])
                            k.op('act', lambda h=h: A.copy(QT[0:64, h, 0:Tn], pq[0:64, 0:Tn]), R=[pqk], W=['QT'])
                            if kind == 'lat':
                                k.op('pe', lambda h=h: PE.matmul(pr[0:96, 0:Tn], lhsT=wqr[:, 0, h, :], rhs=cqn[:, 0, 0:Tn], start=True, stop=False), R=['wqr', 'svb'], W=[prk])
                                k.op('pe', lambda h=h: PE.matmul(pr[0:96, 0:Tn], lhsT=wqr[0:64, 1, h, :], rhs=cqn[0:64, 1, 0:Tn], start=False, stop=True), R=['wqr', 'svb'], W=[prk])
                                k.op('dve', lambda: V.tensor_tensor(q1t[64:96, 0:Tn], pq[64:96, 0:Tn], rc_t[64:96, 0:Tn], op=ALU.mult), R=[pqk, 'rc_t'], W=['mean'])
                                k.op('dve', lambda: V.tensor_tensor(q2t[64:96, 0:Tn], pr[64:96, 0:Tn], rs_t[64:96, 0:Tn], op=ALU.mult), R=[prk, 'rs_t'], W=['var'])
                                k.op('pool', lambda h=h: G.tensor_tensor(QT[64:96, h, 0:Tn], q1t[64:96, 0:Tn], q2t[64:96, 0:Tn], op=ALU.add), R=['mean', 'var'], W=['QT'])
                            else:
                                k.op('act', lambda h=h: A.copy(QT[64:96, h, 0:Tn], pq[64:96, 0:Tn]), R=[pqk], W=['QT'])
                        for ct in range(2):
                            proj(psC[ct], 'psC%d' % ct, 192 + ct * 128, 128)
                            k.op('act', lambda ct=ct: A.copy(su[:, ct, 0:Tn], psC[ct][:, 0:Tn]), R=['psC%d' % ct], W=['su'])
                        for ct in range(2):
                            proj(psC[ct], 'psC%d' % ct, 448 + ct * 128, 128)
                            k.op('act', lambda ct=ct: A.copy(svf[:, ct, 0:Tn], psC[ct][:, 0:Tn]), R=['psC%d' % ct], W=['pa'])
                            k.op('dve', lambda ct=ct: V.tensor_copy(svb[:, ct, 0:Tn], svf[:, ct, 0:Tn]), R=['pa'], W=['svb'])
                            k.op('pool', lambda ct=ct: G.tensor_tensor(sq[:, ct, 0:Tn], svf[:, ct, 0:Tn], svf[:, ct, 0:Tn], op=ALU.mult), R=['pa'], W=['sq'])
                        for ct in range(2):
                            k.op('pe', lambda ct=ct: PE.matmul(psC[2][:, 0:Tn], lhsT=ones_b[:], rhs=svb[:, ct, 0:Tn], start=(ct == 0), stop=(ct == 1)), R=['ones_b', 'svb'], W=['psC2'])
                        for ct in range(2):
                            k.op('pe', lambda ct=ct: PE.matmul(psC[3][:, 0:Tn], lhsT=ones_b[:], rhs=sq[:, ct, 0:Tn], start=(ct == 0), stop=(ct == 1)), R=['ones_b', 'sq'], W=['psC3'])
                        k.op('act', lambda: A.activation(mean[:, 0:Tn], psC[2][:, 0:Tn], AF.Copy, scale=1.0 / 256), R=['psC2'], W=['mean'])
                        k.op('dve', lambda: V.tensor_tensor(var[:, 0:Tn], mean[:, 0:Tn], mean[:, 0:Tn], op=ALU.mult), R=['mean'], W=['var'])
                        k.op('dve', lambda: V.scalar_tensor_tensor(var[:, 0:Tn], psC[3][:, 0:Tn], 1.0 / 256, var[:, 0:Tn], op0=ALU.mult, op1=ALU.subtract), R=['psC3', 'var'], W=['var'])
                        k.op('act', lambda: A.activation(var[:, 0:Tn], var[:, 0:Tn], AF.Sqrt, bias=epsb[:, 0:1]), R=['var', 'epsb'], W=['var'])
                        k.op('dve', lambda: V.reciprocal(var[:, 0:Tn], var[:, 0:Tn]), R=['var'], W=['var'])
                        for ct in range(2):
                            k.op('dve', lambda ct=ct: V.tensor_tensor(svf[:, ct, 0:Tn], svf[:, ct, 0:Tn], mean[:, 0:Tn], op=ALU.subtract), R=['pa', 'mean'], W=['pa'])
                            k.op('pool', lambda ct=ct: G.tensor_tensor(svf[:, ct, 0:Tn], svf[:, ct, 0:Tn], var[:, 0:Tn], op=ALU.mult), R=['pa', 'var'], W=['pa'])
                            k.op('dve', lambda ct=ct: V.tensor_scalar(vnb[:, ct, 0:Tn], svf[:, ct, 0:Tn], colv[:, ct:ct + 1], colv[:, 2 + ct:3 + ct], op0=ALU.mult, op1=ALU.add),
                                 R=['pa', 'colv'], W=['vnb'])
                        for gt in range(8):
                            pg, pgk = psC[gt % 4], 'psC%d' % (gt % 4)
                            proj(pg, pgk, 704 + gt * 128, 128)
                            k.op('act', lambda gt=gt, pg=pg: A.activation(sg[:, gt, 0:Tn], pg[:, 0:Tn], AF.Silu), R=[pgk], W=['sg'])
                        if bidx + 1 < min(NQB, len(qblocks)):
                            do_ln(bidx + 1)
                        sbufs = [(psS_[0], 'psSc0'), (psS_[1], 'psSc1'), (psC[0], 'psC0'), (psC[1], 'psC1')]
                        npair = nkt // 2

                        def score(h, kp):
                            pS, pSk = sbufs[kp % 4]
                            for j_ in range(2):
                                kt_ = 2 * kp + j_
                                k.op('pe', lambda j_=j_, kt_=kt_: PE.matmul(pS[:, j_ * Tn:(j_ + 1) * Tn], lhsT=KT[0:96, h, kt_ * 128:(kt_ + 1) * 128], rhs=QT[0:96, h, 0:Tn],
                                                                           start=True, stop=True, skip_group_check=True), R=['QT', 'KTall'], W=[pSk])
                        for h in range(4 if int(os.environ.get('DBG_ATT', 1)) else 0):
                            pO = psO if h % 2 == 0 else psC[3][:, 0:260].rearrange("p (a b) -> p a b", a=4)
                            pOk = 'psO' if h % 2 == 0 else 'psC3'
                            for kp in range(min(3, npair)):
                                score(h, kp)
                            for kp in range(npair):
                                pS, pSk = sbufs[kp % 4]
                                P_, Pk = PT[kp % 3], 'PT%d' % (kp % 3)
                                if kp + 3 < npair:
                                    score(h, kp + 3)
                                k.op('act', lambda pS=pS, P_=P_: A.activation(P_[:, 0:2 * Tn], pS[:, 0:2 * Tn], AF.Exp, scale=SCALE), R=[pSk], W=[Pk])
                                for j_ in range(2):
                                    kt_ = 2 * kp + j_
                                    for sb in range(nsub):
                                        first = (kp == 0 and j_ == 0 and sb == 0)
                                        k.op('pe', lambda h=h, kt_=kt_, j_=j_, sb=sb, P_=P_, first=first: PE.matmul(
                                            pO[:, sb, :], lhsT=P_[:, j_ * Tn + sb * 128:j_ * Tn + (sb + 1) * 128], rhs=Vt[:, kt_, h, :],
                                            start=first, stop=(kp == npair - 1 and j_ == 1), skip_group_check=True), R=[Pk, 'Vtall'], W=[pOk])
                            k.op('act', lambda: A.copy(osb[:, 0:nsub, :], pO[:, 0:nsub, :]), R=[pOk], W=['osb'])
                            k.op('dve', lambda: V.reciprocal(rec[:, 0:nsub, :], osb[:, 0:nsub, 64:65]), R=['osb'], W=['rec'])
                            k.op('dve', lambda h=h: V.tensor_tensor(att[:, 0:nsub, h * 64:(h + 1) * 64], osb[:, 0:nsub, 0:64], rec[:, 0:nsub, :].to_broadcast([128, nsub, 64]), op=ALU.mult),
                                 R=['osb', 'rec'], W=['att'])
                        if 'Catt' in dbg and l == 0 and kind == 'lat' and t0 == 0:
                            d_ = dbg_out('att', (128, 4 * 256), BF16)
                            k.dma(d_, att[:].rearrange("p a b -> p (a b)"), R=['att'], W=['dbgatt'])
                        for sb in range(nsub):
                            for ft in range(2):
                                k.op('pe', lambda sb=sb, ft=ft: PE.transpose(pt[:, ft, :], att[:, sb, ft * 128:(ft + 1) * 128], ident_b[:]), R=['att', 'ident_b'], W=['ptC'])
                            k.op('dve', lambda sb=sb: V.tensor_tensor(catg[:, 0:2, sb * 128:(sb + 1) * 128], pt[:, 0:2, :], sg[:, 0:2, sb * 128:(sb + 1) * 128], op=ALU.mult),
                                 R=['ptC', 'sg'], W=['sg'])
                        si = 0 if kind == 'ctx' else 1
                        nseq = NCTX if kind == 'ctx' else SEQ
                        L_ = Tn + 16
                        k.dma(pin[:, :, 0:L_], pool_scr[si][:, t0:t0 + L_].rearrange("(c p) n -> p c n", p=128), R=['pool_scr'], W=['pin'])
                        if kind == 'lat':
                            pinB = pb[:].bitcast(BF16)[:, :, 0:L_]
                            k.dma(pinB, pool_scr[si][:, HALF + t0:HALF + t0 + L_].rearrange("(c p) n -> p c n", p=128), R=['pool_scr'], W=['pb'], q='pool')
                            k.op('dve', lambda: V.tensor_scalar_mul(pin[:, :, 0:L_], pin[:, :, 0:L_], msb[:, 0:1]), R=['pin', 'msb'], W=['pin'])
                            k.op('dve', lambda: V.scalar_tensor_tensor(pin[:, :, 0:L_], pinB, msb[:, 1:2], pin[:, :, 0:L_], op0=ALU.mult, op1=ALU.add), R=['pin', 'pb', 'msb'], W=['pin'])
                        k.op('dve', lambda: V.tensor_tensor(pa[:, :, 0:L_ - 1], pin[:, :, 0:L_ - 1], pin[:, :, 1:L_], op=ALU.add), R=['pin'], W=['pa'])
                        o_ = slice(8, 8 + Tn)
                        k.op('pool', lambda: G.tensor_copy(pb[0:64, 0, o_], pa[0:64, 0, 7:7 + Tn]), R=['pa'], W=['pb'])
                        k.op('pool', lambda: G.tensor_tensor(pb[64:128, 0, o_], pa[64:128, 0, 6:6 + Tn], pa[64:128, 0, 8:8 + Tn], op=ALU.add), R=['pa'], W=['pb'])
                        k.op('dve', lambda: V.tensor_tensor(pb[:, 1, 0:L_ - 3], pa[:, 1, 0:L_ - 3], pa[:, 1, 2:L_ - 1], op=ALU.add), R=['pa'], W=['pb'])
                        k.op('dve', lambda: V.tensor_tensor(pa[64:128, 1, 0:L_ - 7], pb[64:128, 1, 0:L_ - 7], pb[64:128, 1, 4:L_ - 3], op=ALU.add), R=['pb'], W=['pa'])
                        k.op('pool', lambda: G.tensor_tensor(pa[0:64, 1, o_], pb[0:64, 1, 4:4 + Tn], pb[0:64, 1, 8:8 + Tn], op=ALU.add), R=['pb'], W=['pa'])
                        k.op('dve', lambda: V.tensor_tensor(pb[64:128, 1, o_], pa[64:128, 1, 0:Tn], pa[64:128, 1, 8:8 + Tn], op=ALU.add), R=['pa'], W=['pb'])
                        k.op('pool', lambda: G.tensor_copy(pb[0:64, 1, o_], pa[0:64, 1, o_]), R=['pa'], W=['pb'])
                        for ct in range(2):
                            k.op('dve', lambda ct=ct: V.scalar_tensor_tensor(pld[:, ct, 0:Tn], pb[:, ct, o_], winv[:, ct:ct + 1], pin[:, ct, o_], op0=ALU.mult, op1=ALU.subtract),
                                 R=['pb', 'pin', 'winv'], W=['pld'])
                        wins = ((0, 0, 2), (64, 0, 4), (0, 1, 8), (64, 1, 16))
                        if t0 == 0:
                            for (r0, ct, w_) in wins:
                                for p_ in range(w_ // 2):
                                    rcp = (1.0 / float(p_ + w_ // 2)) if kind == 'ctx' else recF[r0:r0 + 64, ct, p_:p_ + 1]
                                    k.op('dve', lambda r0=r0, ct=ct, p_=p_, rcp=rcp: V.scalar_tensor_tensor(pld[r0:r0 + 64, ct, p_:p_ + 1], pb[r0:r0 + 64, ct, 8 + p_:9 + p_], rcp,
                                                                                                       pin[r0:r0 + 64, ct, 8 + p_:9 + p_], op0=ALU.mult, op1=ALU.subtract),
                                         R=['pb', 'pin', 'recF'], W=['pld'])
                        if t0 + Tn == (NCTX if kind == 'ctx' else HALF):
                            for (r0, ct, w_) in wins:
                                for p_ in range(Tn - w_ // 2 + 1, Tn):
                                    rcp = (1.0 / float(Tn - p_ + w_ // 2)) if kind == 'ctx' else recL[r0:r0 + 64, ct, p_ - (Tn - 8):p_ - (Tn - 8) + 1]
                                    k.op('dve', lambda r0=r0, ct=ct, p_=p_, rcp=rcp: V.scalar_tensor_tensor(pld[r0:r0 + 64, ct, p_:p_ + 1], pb[r0:r0 + 64, ct, 8 + p_:9 + p_], rcp,
                                                                                                       pin[r0:r0 + 64, ct, 8 + p_:9 + p_], op0=ALU.mult, op1=ALU.subtract),
                                         R=['pb', 'pin', 'recL'], W=['pld'])
                        for ct in range(2):
                            k.op('pe', lambda ct=ct: PE.matmul(psC[ct][:, 0:Tn], lhsT=wpl[:, ct, :], rhs=pld[:, ct, 0:Tn], start=True, stop=True), R=['wpl', 'pld'], W=['psC%d' % ct])
                            k.op('dve', lambda ct=ct: V.scalar_tensor_tensor(catg[:, 2 + ct, 0:Tn], psC[ct][:, 0:Tn], colv[:, 4 + ct:5 + ct], sg[:, 2 + ct, 0:Tn], op0=ALU.mult, op1=ALU.mult),
                                 R=['psC%d' % ct, 'colv', 'sg'], W=['sg'])
                        k.dma(s5t[:, :, 0:Tn], s5o_scr[:, u0:u0 + Tn].rearrange("(c p) n -> p c n", p=128), R=['s5o_scr'], W=['s5t'], q='pool')
                        k.op('pool', lambda: G.tensor_tensor(catg[:, 4:6, 0:Tn], s5t[:, :, 0:Tn], sg[:, 4:6, 0:Tn], op=ALU.mult), R=['s5t', 'sg'], W=['sg'])
                        for sb in range(nsub):
                            cs = slice(sb * 128, (sb + 1) * 128)
                            for ft in range(2):
                                k.op('pe', lambda ft=ft, cs=cs: PE.transpose(pt[:, 2 + ft, :], vnb[:, ft, cs], ident_b[:]), R=['vnb', 'ident_b'], W=['ptC'])
                            k.op('act', lambda: A.copy(vc[:].rearrange("p (a b) -> p a b", a=2), pt[:, 2:4, :]), R=['ptC'], W=['vc'])
                            for h in range(4):
                                ft, r0 = h // 2, (h % 2) * 64
                                pm_, pmk = psC[2 + h % 2], 'psC%d' % (2 + h % 2)
                                k.op('pe', lambda h=h, ft=ft: PE.matmul(pm_[:, 0:128], lhsT=vc[:, ft * 128:(ft + 1) * 128], rhs=wsT[:, h, :], start=True, stop=True), R=['vc', 'wsT'], W=[pmk])
                                k.op('dve', lambda h=h, r0=r0: V.tensor_tensor(mx[r0:r0 + 64, :], pm_[r0:r0 + 64, 0:128], bsb[r0:r0 + 64, h, :], op=ALU.add), R=[pmk, 'bsb'], W=['mx'])
                                k.op('pool', lambda ft=ft, r0=r0, cs=cs: G.tensor_tensor(mx[r0:r0 + 64, :], mx[r0:r0 + 64, :], su[r0:r0 + 64, ft, cs], op=ALU.mult), R=['mx', 'su'], W=['mx'])
                                k.op('dve', lambda ft=ft, r0=r0, cs=cs: V.tensor_tensor(catg[r0:r0 + 64, 6 + ft, cs], mx[r0:r0 + 64, :], sg[r0:r0 + 64, 6 + ft, cs], op=ALU.mult),
                                     R=['mx', 'sg'], W=['sg'])
                        if 'Ccat' in dbg and l == 0 and kind == 'lat' and t0 == 0:
                            d_ = dbg_out('catg', (128, 8 * QB), BF16)
                            k.dma(d_, catg[:].rearrange("p a b -> p (a b)"), R=['sg'], W=['dbgcat'])
                        for sb in range(nsub):
                            cs = slice(sb * 128, (sb + 1) * 128)
                            xb = xt[xcnt % 2]
                            xk = 'xtC0'
                            xcnt += 1
                            k.dma(xb[:], src[t0 + sb * 128: t0 + (sb + 1) * 128, :], W=[xk], q=('sp' if sb % 2 == 0 else 'pool'))
                            for nh in range(2):
                                for kt in range(8):
                                    k.op('pe', lambda nh=nh, kt=kt, cs=cs: PE.matmul(psC[nh][:, :], lhsT=catg[:, kt, cs], rhs=wout[:, kt, nh * 512:(nh + 1) * 512],
                                                                                    start=(kt == 0), stop=(kt == 7)), R=['sg', 'wout'], W=['psC%d' % nh])
                                k.op('dve', lambda nh=nh: V.tensor_tensor(rr[:, nh * 512:(nh + 1) * 512], psC[nh][:, :], gate_bc[:, nh * 512:(nh + 1) * 512], op=ALU.mult),
                                     R=['psC%d' % nh, 'gate_bc'], W=['rr'])
                            k.op('dve', lambda xb=xb: V.scalar_tensor_tensor(rr[:], xb[:], float(ALPHA), rr[:], op0=ALU.mult, op1=ALU.add), R=[xk, 'rr'], W=['rr'])
                            for c2 in range(2):
                                k.op('dve', lambda c2=c2: V.bn_stats(st6[:, c2, :], rr[:, c2 * 512:(c2 + 1) * 512]), R=['rr'], W=['st6'])
                            k.op('dve', lambda: V.bn_aggr(mv[:], st6[:]), R=['st6'], W=['mv'])
                            k.op('act', lambda: A.activation(rstd[:], mv[:, 1:2], AF.Sqrt, bias=epsb[:, 0:1]), R=['mv', 'epsb'], W=['rstd'])
                            k.op('dve', lambda: V.reciprocal(rstd[:], rstd[:]), R=['rstd'], W=['rstd'])
                            k.op('dve', lambda: V.scalar_tensor_tensor(rr[:], rr[:], mv[:, 0:1], lng_bc[:], op0=ALU.subtract, op1=ALU.mult), R=['rr', 'mv', 'lng_bc'], W=['rr'])
                            k.op('dve', lambda xb=xb: V.scalar_tensor_tensor(xb[:], rr[:], rstd[:, 0:1], lnb_bc[:], op0=ALU.mult, op1=ALU.add), R=['rr', 'rstd', 'lnb_bc'], W=[xk])
                            k.dma(dst[t0 + sb * 128: t0 + (sb + 1) * 128, :], xb[:], R=[xk], W=['x1w'])

            if l == 0 and nlayers > 1:
                k._sync('pool', ['x1w'], ['x1full'])
                for c_ in range(HALF // CCR):
                    G.collective_compute("AllGather", ALU.bypass, replica_groups=[[0, 1], [2, 3], [4, 5], [6, 7]],
                                         ins=[x1h_scr[c_ * CCR:(c_ + 1) * CCR, :]], outs=[x1g[c_]]).then_inc(ccsem)
                ccn[0] += HALF // CCR
                x1_ready = ccn[0]
            if 'x1' in dbg and l == 0:
                d_ = dbg_out('x1', (SEQ, D))
                k.dma(d_[0:HALF, :], x1h_scr, R=['x1w'], W=['dbgx1'])
                d_ = dbg_out('ctx1', (NCTX, D))
                k.dma(d_, ctx1_scr, R=['x1w'], W=['dbgc1'])
                break

        k.wait_all('sp')
    return nc, dout


def _in_maps(inputs, cores):
    cst = _consts()
    shared = {n: np.ascontiguousarray(inputs[n], dtype=np.float32) for n in PARAMS}
    shared.update(cst)
    maps = []
    for i in cores:
        b, hf = i // 2, i % 2
        sl = slice(hf * HALF, (hf + 1) * HALF)
        m = dict(shared)
        m['x'] = np.ascontiguousarray(inputs['x'][b], dtype=np.float32)
        m['xq'] = np.ascontiguousarray(inputs['x'][b][sl], dtype=np.float32)
        m['ctx'] = np.ascontiguousarray(inputs['ctx'][b], dtype=np.float32)
        m['cvec'] = np.ascontiguousarray(np.stack([inputs['c'][b], inputs['c_ctx']], 0), dtype=np.float32)
        m['msel'] = np.array([1.0 - hf, float(hf)], np.float32)
        if hf:
            for n_ in ('lam_re', 'lam_im', 'log_dt', 's5_b_re', 's5_b_im', 's5_c_re', 's5_c_im'):
                m[n_] = np.ascontiguousarray(np.roll(shared[n_], -8, axis=2))
            w_in = shared['w_in'].copy()
            w_in[:, :, 160:416] = np.roll(w_in[:, :, 160:416], -128, axis=2)
            w_in[:, :, 1888:2144] = np.roll(w_in[:, :, 1888:2144], -128, axis=2)
            m['w_in'] = w_in
            m['s5_d'] = np.ascontiguousarray(np.roll(shared['s5_d'], -128, axis=1))
            m['b_glu'] = np.ascontiguousarray(np.roll(shared['b_glu'], -128, axis=1))
            m['w_glu'] = np.ascontiguousarray(np.roll(np.roll(shared['w_glu'], -128, axis=1), -128, axis=2))
            w_out = shared['w_out'].copy()
            w_out[:, 512:768, :] = np.roll(w_out[:, 512:768, :], -128, axis=1)
            m['w_out'] = w_out
        m['ropec_q'] = np.ascontiguousarray(cst['ropec'][:, sl])
        m['ropes_q'] = np.ascontiguousarray(cst['ropes'][:, sl])
        maps.append(m)
    return maps


def kernel(**inputs):
    nc, _ = build()
    cores = list(range(8))
    res = run_bass_kernel_spmd(nc, _in_maps(inputs, cores), core_ids=cores)
    out = np.empty((4, SEQ, D), np.float32)
    for i in cores:
        b, hf = i // 2, i % 2
        out[b, hf * HALF:(hf + 1) * HALF] = np.asarray(res.results[i]['out'], dtype=np.float32)
    return out
```

```python
import os
from contextlib import ExitStack
import numpy as np
import concourse.bass as bass
import concourse.mybir as mybir
from concourse.bass_utils import run_bass_kernel_spmd

F32 = mybir.dt.float32
BF16 = mybir.dt.bfloat16
AF = mybir.ActivationFunctionType
ALU = mybir.AluOpType

D = 1024
SEQ = 8192
NCTX = 256
NU = NCTX + SEQ
NE = NU + NCTX
NSC = NE // 8
HALF = SEQ // 2
DEPTH = 2
QB = 256
IN_DIM = 2400
LN_EPS = 1e-6
ALPHA = (2 * DEPTH) ** 0.25
PARAMS = ['w_mod', 'b_mod', 'w_in', 'g_q', 'w_uq', 'g_kv', 'w_ukv', 'w_pool', 'pool_scale', 'lam_re', 'lam_im',
          'log_dt', 's5_b_re', 's5_b_im', 's5_c_re', 's5_c_im', 's5_d', 'w_glu', 'b_glu', 'sgu_g', 'sgu_b', 'w_s',
          'b_s', 'w_out', 'ln_g', 'ln_b']
PSHAPES = {'w_mod': (2, 1024, 3072), 'b_mod': (2, 3072), 'w_in': (2, 1024, 2400), 'g_q': (2, 192), 'w_uq': (2, 192, 384),
           'g_kv': (2, 128), 'w_ukv': (2, 128, 512), 'w_pool': (2, 4, 64, 64), 'pool_scale': (2, 256),
           'lam_re': (2, 2, 16, 64), 'lam_im': (2, 2, 16, 64), 'log_dt': (2, 2, 16), 's5_b_re': (2, 2, 16, 64, 16),
           's5_b_im': (2, 2, 16, 64, 16), 's5_c_re': (2, 2, 16, 16, 64), 's5_c_im': (2, 2, 16, 16, 64), 's5_d': (2, 256),
           'w_glu': (2, 256, 256), 'b_glu': (2, 256), 'sgu_g': (2, 256), 'sgu_b': (2, 256), 'w_s': (2, 4, 128, 128),
           'b_s': (2, 4, 128), 'w_out': (2, 1024, 1024), 'ln_g': (2, 1024), 'ln_b': (2, 1024)}

SAME_ENGINE_SYNC = True


class KB:
    NSLOT = 8

    def __init__(self, nc, stack):
        self.nc = nc
        self.eng = {'pe': nc.tensor, 'act': nc.scalar, 'dve': nc.vector, 'pool': nc.gpsimd, 'sp': nc.sync}
        self.sem = {e: stack.enter_context(nc.semaphore('s_' + e)) for e in self.eng}
        self.cnt = {e: 0 for e in self.eng}
        self.dsem, self.duse = {}, {}
        for q in ('sp', 'pool', 'act'):
            for j in range(self.NSLOT):
                self.dsem[(q, j)] = stack.enter_context(nc.semaphore('d_%s%d' % (q, j)))
                self.duse[(q, j)] = 0
        self.dnext = {'sp': 0, 'pool': 0, 'act': 0}
        self.known = {e: {} for e in self.eng}
        self.lastw, self.readers = {}, {}
        self.ninstr = 0

    def _need(self, E, tok, waits):
        if tok is None:
            return
        if tok[0] == 'c':
            _, e2, n = tok
            if e2 == E and (not SAME_ENGINE_SYNC or E == 'pe'):
                return
            key, val = e2, n
        else:
            _, q, j, n = tok
            key, val = (q, j), n * 16
        if self.known[E].get(key, 0) >= val:
            return
        if waits.get(key, 0) < val:
            waits[key] = val

    def _sync(self, E, R, W, waits=None):
        waits = {} if waits is None else waits
        for r in R:
            self._need(E, self.lastw.get(r), waits)
        for w in W:
            self._need(E, self.lastw.get(w), waits)
            for t in self.readers.get(w, ()):
                self._need(E, t, waits)
        eng = self.eng[E]
        for key, val in waits.items():
            eng.wait_ge(self.sem[key] if isinstance(key, str) else self.dsem[key], val)
            self.known[E][key] = val

    def _commit(self, tok, R, W):
        for r in R:
            self.readers.setdefault(r, []).append(tok)
        for w in W:
            self.lastw[w] = tok
            self.readers[w] = []

    def op(self, E, fn, R=(), W=()):
        W = list(W) + [r for r in R if r.startswith('ps') and r not in W]
        self._sync(E, R, W)
        ins = fn()
        self.cnt[E] += 1
        ins.then_inc(self.sem[E], 1)
        self._commit(('c', E, self.cnt[E]), R, W)
        self.ninstr += 1
        return ins

    def dma(self, out, in_, R=(), W=(), q='sp', **kw):
        j = self.dnext[q]
        self.dnext[q] = (j + 1) % self.NSLOT
        slot = (q, j)
        waits = {}
        if self.duse[slot] > 0:
            self._need(q, ('d', q, j, self.duse[slot]), waits)
        self._sync(q, R, W, waits)
        ins = self.eng[q].dma_start(out=out, in_=in_, **kw)
        ins.then_inc(self.dsem[slot], 16)
        self.duse[slot] += 1
        self._commit(('d', q, j, self.duse[slot]), R, W)
        self.ninstr += 1

    def barrier(self):
        for E, eng in self.eng.items():
            for e2 in self.eng:
                if e2 != E and self.cnt[e2] > self.known[E].get(e2, 0):
                    eng.wait_ge(self.sem[e2], self.cnt[e2])
                    self.known[E][e2] = self.cnt[e2]
            for slot, n in self.duse.items():
                if n > 0 and 16 * n > self.known[E].get(slot, 0):
                    eng.wait_ge(self.dsem[slot], 16 * n)
                    self.known[E][slot] = 16 * n

    def wait_all(self, E='sp'):
        eng = self.eng[E]
        for e2 in self.eng:
            if e2 != E and self.cnt[e2] > 0:
                eng.wait_ge(self.sem[e2], self.cnt[e2])
        for slot, n in self.duse.items():
            if n > 0:
                eng.wait_ge(self.dsem[slot], 16 * n)


def _consts():
    c = {}
    c['ident'] = np.eye(128, dtype=np.float32)
    sw = np.zeros((128, 128), np.float32)
    for p in range(64):
        sw[p, p + 64] = 1.0
        sw[p + 64, p] = 1.0
    c['swapm'] = sw
    rows = SEQ // 64
    t = np.arange(SEQ)
    pos = np.stack([t // 64, t % 64], 0).astype(np.float32)
    inv = (10000.0 ** (-np.arange(8, dtype=np.float32) / 8)).astype(np.float32)
    ang = pos[:, None, :] * inv[None, :, None]
    cos, sin = np.cos(ang).astype(np.float32), np.sin(ang).astype(np.float32)
    rc = np.zeros((32, SEQ), np.float32)
    rs = np.zeros((32, SEQ), np.float32)
    for a in range(2):
        for hf in range(2):
            rc[a * 16 + hf * 8: a * 16 + hf * 8 + 8] = cos[a]
            rs[a * 16 + hf * 8: a * 16 + hf * 8 + 8] = sin[a] * (-1.0 if hf == 0 else 1.0)
    r_ = np.arange(128) // 16
    c['mlow'] = (r_[None, :] >= r_[:, None]).astype(np.float32)
    c['mup'] = (r_[None, :] <= r_[:, None]).astype(np.float32)
    c['ropec'] = rc
    c['ropes'] = rs
    return c


def build(nlayers=DEPTH, dbg=()):
    nc = bass.Bass("TRN2", target_bir_lowering=False)
    din = {}

    def inp(name, shape):
        din[name] = nc.dram_tensor(name, list(shape), F32, kind="ExternalInput").ap()
        return din[name]

    x_in = inp('x', (SEQ, D))
    xq_in = inp('xq', (HALF, D))
    ctx_in = inp('ctx', (NCTX, D))
    msel = inp('msel', (2,))
    ropecq = inp('ropec_q', (32, HALF))
    ropesq = inp('ropes_q', (32, HALF))
    cvec = inp('cvec', (2, D))
    prm = {n: inp(n, PSHAPES[n]) for n in PARAMS}
    cst = {n: inp(n, v.shape) for n, v in _consts().items()}
    out_ap = nc.dram_tensor('out', [HALF, D], F32, kind="ExternalOutput").ap()
    dout = {}

    def dbg_out(name, shape, dt=F32):
        dout[name] = nc.dram_tensor('dbg_' + name, list(shape), dt, kind="ExternalOutput").ap()
        return dout[name]

    scr = lambda name, shape, dt=F32: nc.dram_tensor(name, list(shape), dt, kind="Internal").ap()
    CCR = 512
    x1g = [scr('x1g%d' % c, (2 * CCR, D)) for c in range(HALF // CCR)]

    def x1rows(t):
        r_, w_ = t // HALF, t % HALF
        c_, o_ = w_ // CCR, w_ % CCR
        return x1g[c_][r_ * CCR + o_: r_ * CCR + o_ + 128, :]
    x1h_scr = scr('x1h_scr', (HALF, D))
    ctx1_scr = scr('ctx1_scr', (NCTX, D))
    gate_scr = scr('gate_scr', (2, 2, D))
    pool_scr = [scr('pool_scr_c', (256, NCTX + 16), BF16), scr('pool_scr_l', (256, SEQ + 16), BF16)]
    s5o_scr = scr('s5o_scr', (256, NU), BF16)
    xd_scr = scr('xd_scr', (256, 8, NSC), BF16)
    yd_scr = scr('yd_scr', (256, 8, NSC), BF16)
    ydg_scr = [scr('ydg%d' % c, (128, 8 * NSC), BF16) for c in range(2)]

    with ExitStack() as st:
        k = KB(nc, st)

        uid = [0]

        def T(stk, name, shape, dt):
            uid[0] += 1
            return stk.enter_context(nc.sbuf_tensor('%s_t%d' % (name, uid[0]), list(shape), dt))

        def P(stk, name, shape, dt=F32):
            uid[0] += 1
            return stk.enter_context(nc.psum_tensor('%s_p%d' % (name, uid[0]), list(shape), dt))

        V = nc.vector
        A = nc.scalar
        G = nc.gpsimd
        PE = nc.tensor

        def ln_block(xsrcs, rkeys, col, hT_, B):
            n = len(xsrcs)

            def stage1(s_):
                xb, xk = B['xt'][s_ % 2]
                xn_, xnk = B['xn'][s_ % 2]
                pt_, ptk = B['pt'][s_ % 2]
                sm, smk = B['sm'][s_ % 2]
                st6_ = sm[:, 0:12].rearrange("p (a b) -> p a b", a=2)
                k.dma(xb[:], xsrcs[s_], R=rkeys, W=[xk], q=('sp' if s_ % 2 == 0 else 'pool'))
                for c2 in range(2):
                    k.op('dve', lambda c2=c2: V.bn_stats(st6_[:, c2, :], xb[:, c2 * 512:(c2 + 1) * 512]), R=[xk], W=[smk])
                k.op('dve', lambda: V.bn_aggr(sm[:, 12:14], st6_), R=[smk], W=[smk])
                k.op('act', lambda: A.activation(sm[:, 14:15], sm[:, 13:14], AF.Sqrt, bias=epsb[:, 0:1]), R=[smk, 'epsb'], W=[smk])
                k.op('dve', lambda: V.reciprocal(sm[:, 14:15], sm[:, 14:15]), R=[smk], W=[smk])
                k.op('dve', lambda: V.tensor_scalar(sm[:, 15:16], sm[:, 12:13], sm[:, 14:15], -1.0, op0=ALU.mult, op1=ALU.mult), R=[smk], W=[smk])
                k.op('act', lambda: A.activation(xn_[:], xb[:], AF.Identity, scale=sm[:, 14:15], bias=sm[:, 15:16]), R=[xk, smk], W=[xnk])
                for dc in range(8):
                    k.op('pe', lambda dc=dc: PE.transpose(pt_[:, dc, :], xn_[:, dc * 128:(dc + 1) * 128], ident_b[:]), R=[xnk, 'ident_b'], W=[ptk])

            def stage2(s_):
                pt_, ptk = B['pt'][s_ % 2]
                if B.get('dve_mod'):
                    hv = hT_[:, :, s_ * 128:(s_ + 1) * 128]
                    k.op('dve', lambda: V.tensor_tensor(hv, pt_[:, :, :], sc1T[:, :, col].unsqueeze(2).to_broadcast([128, 8, 128]), op=ALU.mult),
                         R=[ptk, 'sc1T'], W=[B['hTk']])
                    k.op('pool', lambda: G.tensor_tensor(hv, hv, modT[:, 0:8, col].unsqueeze(2).to_broadcast([128, 8, 128]), op=ALU.add),
                         R=[B['hTk'], 'modT'], W=[B['hTk']])
                    return
                for dc in range(8):
                    k.op('act', lambda dc=dc: A.activation(hT_[:, dc, s_ * 128:(s_ + 1) * 128], pt_[:, dc, :], AF.Identity,
                                                          scale=sc1T[:, dc, col:col + 1], bias=modT[:, dc, col:col + 1]),
                         R=[ptk, 'sc1T', 'modT'], W=[B['hTk']])

            stage1(0)
            for s_ in range(n):
                if s_ + 1 < n:
                    stage1(s_ + 1)
                stage2(s_)

        ident_f = T(st, 'ident_f', (128, 128), F32)
        ident_b = T(st, 'ident_b', (128, 128), BF16)
        ones_b = T(st, 'ones_b', (128, 128), BF16)
        epsb = T(st, 'epsb', (128, 1), F32)
        KT = T(st, 'KT', (128, 4, NU), BF16)
        Vt = T(st, 'Vt', (128, NU // 128, 4, 65), BF16)
        modT = T(st, 'modT', (128, 24, 2), F32)
        sc1T = T(st, 'sc1T', (128, 8, 2), F32)
        wukv = T(st, 'wukv', (128, 512), BF16)
        msb = T(st, 'msb', (128, 2), F32)
        recF = T(st, 'recF', (128, 2, 8), F32)
        recL = T(st, 'recL', (128, 2, 8), F32)
        ccsem = st.enter_context(nc.semaphore('ccsem'))
        ccn = [0]

        k.dma(ident_f[:], cst['ident'], W=['ident_f'])
        k.op('dve', lambda: V.tensor_copy(ident_b[:], ident_f[:]), R=['ident_f'], W=['ident_b'])
        k.op('pool', lambda: G.memset(ones_b[:], 1.0), W=['ones_b'])
        k.op('pool', lambda: G.memset(epsb[:], LN_EPS), W=['epsb'])
        k.op('pool', lambda: G.memset(Vt[:, :, :, 64:65], 1.0), W=['Vt'])
        k.dma(msb[:], msel.partition_broadcast(128), W=['msb'])
        k.op('pool', lambda: G.memset(recF[:], 1.0), W=['recF'])
        k.op('pool', lambda: G.memset(recL[:], 1.0), W=['recL'])
        WINS = ((0, 0, 2), (64, 0, 4), (0, 1, 8), (64, 1, 16))
        for (r0, ct, w_) in WINS:
            for p_ in range(w_ // 2):
                c1_ = 1.0 / (p_ + w_ // 2) - 1.0 / w_
                k.op('dve', lambda r0=r0, ct=ct, p_=p_, c1_=c1_, w_=w_: V.tensor_scalar(recF[r0:r0 + 64, ct, p_:p_ + 1], msb[r0:r0 + 64, 0:1], c1_, 1.0 / w_, op0=ALU.mult, op1=ALU.add),
                     R=['msb'], W=['recF'])
            for q_ in range(8 - w_ // 2 + 1, 8):
                c1_ = 1.0 / (8 - q_ + w_ // 2) - 1.0 / w_
                k.op('dve', lambda r0=r0, ct=ct, q_=q_, c1_=c1_, w_=w_: V.tensor_scalar(recL[r0:r0 + 64, ct, q_:q_ + 1], msb[r0:r0 + 64, 1:2], c1_, 1.0 / w_, op0=ALU.mult, op1=ALU.add),
                     R=['msb'], W=['recL'])

        for l in range(nlayers):
            last = (l == DEPTH - 1)
            xin = x_in
            cin = ctx_in if l == 0 else ctx1_scr
            k.barrier()
            with ExitStack() as s0:
                cc = T(s0, 'cc', (128, 8, 2), F32)
                scc = T(s0, 'scc', (128, 8, 2), F32)
                bmodT = T(s0, 'bmodT', (128, 24), F32)
                wm = [T(s0, 'wm%d' % i, (128, 8, 512), F32) for i in range(2)]
                pm = P(s0, 'pm', (128, 24, 2))
                wtmp = T(s0, 'wtmp', (128, 512), F32)
                gkv = T(s0, 'gkv', (128, 1), F32)
                for col in range(2):
                    k.dma(cc[:, :, col], cvec[col].rearrange("(c p) -> p c", p=128), W=['cc'], allow_slow_non_contiguous=True)
                k.dma(bmodT[:], prm['b_mod'][l].rearrange("(c p) -> p c", p=128), W=['bmodT'], allow_slow_non_contiguous=True)
                k.op('act', lambda: A.activation(scc[:], cc[:], AF.Silu), R=['cc'], W=['scc'])
                for blk in range(6):
                    w_ = wm[blk % 2]
                    k.dma(w_[:], prm['w_mod'][l][:, blk * 512:(blk + 1) * 512].rearrange("(c p) n -> p c n", p=128),
                          W=['wm%d' % (blk % 2)], q=('sp' if blk % 2 == 0 else 'pool'))
                    for jj in range(4):
                        jt = blk * 4 + jj
                        for dc in range(8):
                            k.op('pe', lambda w_=w_, jj=jj, jt=jt, dc=dc: PE.matmul(
                                pm[:, jt, :], lhsT=w_[:, dc, jj * 128:(jj + 1) * 128], rhs=scc[:, dc, :],
                                start=(dc == 0), stop=(dc == 7)), R=['wm%d' % (blk % 2), 'scc'], W=['pm'])
                k.op('dve', lambda: V.tensor_tensor(modT[:], pm[:], bmodT[:].unsqueeze(2).to_broadcast([128, 24, 2]), op=ALU.add),
                     R=['pm', 'bmodT'], W=['modT'])
                k.op('dve', lambda: V.tensor_scalar_add(sc1T[:], modT[:, 8:16, :], 1.0), R=['modT'], W=['sc1T'])
                for col in range(2):
                    k.dma(gate_scr[l, col].rearrange("(c p) -> p c", p=128), modT[:, 16:24, col], R=['modT'], W=['gate_scr'],
                          allow_slow_non_contiguous=True)
                k.dma(wtmp[:], prm['w_ukv'][l], W=['wtmp'])
                k.dma(gkv[:], prm['g_kv'][l].rearrange("(p o) -> p o", o=1), W=['gkv'])
                k.op('dve', lambda: V.tensor_scalar_mul(wukv[:], wtmp[:], gkv[:, 0:1]), R=['wtmp', 'gkv'], W=['wukv'])

            if 'mod' in dbg and l == 0:
                d_ = dbg_out('mod', (128, 48))
                k.dma(d_, modT[:].rearrange("p a b -> p (a b)"), R=['modT'], W=['dbgmod'])

            k.barrier()
            if l > 0:
                for q_ in ('sp', 'pool'):
                    k.eng[q_].wait_ge(ccsem, x1_ready)
            with ExitStack() as sA:
                with ExitStack() as sa:
                    wA = T(sa, 'wA', (128, 8, 832), BF16)
                    wst = T(sa, 'wst', (128, 8, 256), F32)
                    xt = [T(sa, 'xt%d' % i, (128, D), F32) for i in range(2)]
                    xn2 = [T(sa, 'xn2_%d' % i, (128, D), BF16) for i in range(2)]
                    sm2 = [T(sa, 'sm2_%d' % i, (128, 16), F32) for i in range(2)]
                    st6 = T(sa, 'st6', (128, 2, 6), F32)
                    mv = T(sa, 'mv', (128, 2), F32)
                    rstd = T(sa, 'rstd', (128, 1), F32)
                    xn = T(sa, 'xn', (128, D), BF16)
                    hf32 = T(sa, 'hf32', (128, 8, 128), F32)
                    hT2 = [T(sa, 'hT%d' % i, (128, 8, 512), BF16) for i in range(2)]
                    sq = T(sa, 'sq', (128, 512), BF16)
                    rms = T(sa, 'rms', (128, 512), F32)
                    ckvn = T(sa, 'ckvn', (128, 512), BF16)
                    rc_t = T(sa, 'rc_t', (128, 512), F32)
                    rs_t = T(sa, 'rs_t', (128, 512), F32)
                    kr1 = T(sa, 'kr1', (128, 512), F32)
                    kr2 = T(sa, 'kr2', (128, 512), F32)
                    plo = T(sa, 'plo', (128, 2, 512), BF16)
                    zpad = T(sa, 'zpad', (128, 2, 8), BF16)
                    xds = T(sa, 'xds', (128, 2, 8, 64), BF16)
                    pt = P(sa, 'pt', (128, 8, 128), BF16)
                    ptb = P(sa, 'ptb', (128, 8, 128), BF16)
                    LNB = {'xt': [(xt[0], 'xt0'), (xt[1], 'xt1')], 'xn': [(xn2[0], 'xn2_0'), (xn2[1], 'xn2_1')],
                           'pt': [(pt, 'pt'), (ptb, 'ptb')], 'sm': [(sm2[0], 'sm2_0'), (sm2[1], 'sm2_1')], 'hTk': 'hT'}
                    ps = [P(sa, 'psA%d' % i, (128, 512)) for i in range(6)]

                    k.op('pool', lambda: G.memset(wA[:, :, 128:320], 0.0), W=['wA'])
                    k.op('pool', lambda: G.memset(zpad[:], 0.0), W=['zpad'])
                    win = prm['w_in'][l].rearrange("(c p) n -> p c n", p=128)
                    k.dma(wst[:, :, 0:160], win[:, :, 0:160], W=['wst'])
                    k.op('dve', lambda: V.tensor_copy(wA[:, :, 0:128], wst[:, :, 0:128]), R=['wst'], W=['wA'])
                    k.op('dve', lambda: V.tensor_copy(wA[:, :, 192:224], wst[:, :, 128:160]), R=['wst'], W=['wA'])
                    for (d0, s0_) in ((0, 8), (8, 0), (16, 24), (24, 16)):
                        k.op('dve', lambda d0=d0, s0_=s0_: V.tensor_copy(wA[:, :, 288 + d0:296 + d0], wst[:, :, 128 + s0_:136 + s0_]),
                             R=['wst'], W=['wA'])
                    k.dma(wst[:], win[:, :, 160:416], R=[], W=['wst'])
                    k.op('dve', lambda: V.tensor_copy(wA[:, :, 320:576], wst[:]), R=['wst'], W=['wA'])
                    k.dma(wst[:], win[:, :, 608:864], R=[], W=['wst'])
                    k.op('dve', lambda: V.tensor_copy(wA[:, :, 576:832], wst[:]), R=['wst'], W=['wA'])
                    for si in range(2):
                        k.dma(pool_scr[si][:, 0:8].rearrange("(c p) n -> p c n", p=128), zpad[:], R=['zpad'], W=['pool_scr'])
                        n_ = NCTX if si == 0 else SEQ
                        k.dma(pool_scr[si][:, 8 + n_:16 + n_].rearrange("(c p) n -> p c n", p=128), zpad[:], R=['zpad'], W=['pool_scr'])

                    blocks = [('ctx', 0, NCTX)] + [('lat', 512 * i, 512) for i in range(SEQ // 512)]
                    xcnt = 0
                    for bidx, (kind, t0, Tn) in enumerate(blocks):
                        hT, hTk = hT2[bidx % 2], 'hT%d' % (bidx % 2)
                        LNB['hTk'] = hTk
                        col = 1 if kind == 'ctx' else 0
                        src = cin if kind == 'ctx' else xin
                        u0 = t0 if kind == 'ctx' else NCTX + t0
                        nsub = Tn // 128
                        xsrcs = [(x1rows(t0 + sb * 128) if (l > 0 and kind == 'lat') else src[t0 + sb * 128: t0 + (sb + 1) * 128, :]) for sb in range(nsub)]
                        ln_block(xsrcs, ['x1w'] if l > 0 else [], col, hT, LNB)
                        def proj(pst, c0, ncol, key):
                            for dc in range(8):
                                k.op('pe', lambda dc=dc: PE.matmul(pst[0:ncol, 0:Tn], lhsT=wA[:, dc, c0:c0 + ncol], rhs=hT[:, dc, 0:Tn],
                                                                  start=(dc == 0), stop=(dc == 7)), R=['wA', hTk], W=[key])
                        proj(ps[0], 0, 128, 'psA0')
                        proj(ps[1], 128, 96, 'psA1')
                        if kind == 'lat':
                            proj(ps[2], 224, 96, 'psA2')
                        k.op('act', lambda: A.activation(sq[:, 0:Tn], ps[0][:, 0:Tn], AF.Square), R=['psA0'], W=['sq'])
                        k.op('pe', lambda: PE.matmul(ps[3][:, 0:Tn], lhsT=ones_b[:], rhs=sq[:, 0:Tn], start=True, stop=True),
                             R=['ones_b', 'sq'], W=['psA3'])
                        k.op('act', lambda: A.activation(rms[:, 0:Tn], ps[3][:, 0:Tn], AF.Sqrt, bias=epsb[:, 0:1], scale=1.0 / 128),
                             R=['psA3', 'epsb'], W=['rms'])
                        k.op('dve', lambda: V.reciprocal(rms[:, 0:Tn], rms[:, 0:Tn]), R=['rms'], W=['rms'])
                        k.op('dve', lambda: V.tensor_tensor(ckvn[:, 0:Tn], ps[0][:, 0:Tn], rms[:, 0:Tn], op=ALU.mult), R=['psA0', 'rms'], W=['ckvn'])
                        for h in range(4):
                            k.op('pe', lambda h=h: PE.matmul(ps[3][0:64, 0:Tn], lhsT=wukv[:, h * 128:h * 128 + 64], rhs=ckvn[:, 0:Tn],
                                                            start=True, stop=True), R=['wukv', 'ckvn'], W=['psA3'])
                            k.op('act', lambda h=h: A.copy(KT[0:64, h, u0:u0 + Tn], ps[3][0:64, 0:Tn]), R=['psA3'], W=['KT%d' % (u0 // 512)])
                        wv = wukv[:].rearrange("p (h t d) -> p h t d", h=4, t=2)[:, :, 1, :]
                        for sb in range(nsub):
                            k.op('pe', lambda sb=sb: PE.matmul(ps[4][:, 0:256].rearrange("p (h d) -> p h d", h=4), lhsT=ckvn[:, sb * 128:(sb + 1) * 128],
                                                              rhs=wv, start=True, stop=True), R=['wukv', 'ckvn'], W=['psA4'])
                            k.op('dve', lambda sb=sb: V.tensor_copy(Vt[:, u0 // 128 + sb, :, 0:64], ps[4][:, 0:256].rearrange("p (h d) -> p h d", h=4)),
                                 R=['psA4'], W=['Vt%d' % (u0 // 512)])
                        if kind == 'lat':
                            k.dma(rc_t[64:96, :], cst['ropec'][:, t0:t0 + 512], W=['rc_t'])
                            k.dma(rs_t[64:96, :], cst['ropes'][:, t0:t0 + 512], W=['rs_t'], q='pool')
                            k.op('dve', lambda: V.tensor_tensor(kr1[64:96, :], ps[1][64:96, :], rc_t[64:96, :], op=ALU.mult), R=['psA1', 'rc_t'], W=['kr1'])
                            k.op('dve', lambda: V.tensor_tensor(kr2[64:96, :], ps[2][64:96, :], rs_t[64:96, :], op=ALU.mult), R=['psA2', 'rs_t'], W=['kr2'])
                            for h in range(4):
                                k.op('pool', lambda h=h: G.tensor_tensor(KT[64:96, h, u0:u0 + Tn], kr1[64:96, 0:Tn], kr2[64:96, 0:Tn], op=ALU.add),
                                     R=['kr1', 'kr2'], W=['KT%d' % (u0 // 512)])
                        else:
                            for h in range(4):
                                k.op('act', lambda h=h: A.copy(KT[64:96, h, u0:u0 + Tn], ps[1][64:96, 0:Tn]), R=['psA1'], W=['KT%d' % (u0 // 512)])
                        for ct in range(2):
                            proj(ps[ct % 2 + 4], 320 + ct * 128, 128, 'psA%d' % (ct % 2 + 4))
                            srcv = ps[ct % 2 + 4][:, 0:Tn].rearrange("p (c i) -> p i c", i=8)
                            k.op('act', lambda ct=ct, srcv=srcv: A.copy(xds[:, ct, :, 0:Tn // 8], srcv), R=['psA%d' % (ct % 2 + 4)], W=['xds'])
                        xdv = xd_scr.rearrange("(c p) i n -> p c i n", p=128)
                        for ct in range(2):
                            k.dma(xdv[:, ct, :, u0 // 8:(u0 + Tn) // 8], xds[:, ct, :, 0:Tn // 8], R=['xds'], W=['xd_scr'])
                            if kind == 'ctx':
                                k.dma(xdv[:, ct, :, NU // 8:NE // 8], xds[:, ct, :, 0:Tn // 8], R=['xds'], W=['xd_scr'], q='pool')
                        for ct in range(2):
                            proj(ps[ct % 2 + 4], 576 + ct * 128, 128, 'psA%d' % (ct % 2 + 4))
                            k.op('act', lambda ct=ct: A.copy(plo[:, ct, 0:Tn], ps[ct % 2 + 4][:, 0:Tn]), R=['psA%d' % (ct % 2 + 4)], W=['plo'])
                        si = 0 if kind == 'ctx' else 1
                        k.dma(pool_scr[si][:, 8 + t0:8 + t0 + Tn].rearrange("(c p) n -> p c n", p=128), plo[:, :, 0:Tn], R=['plo'], W=['pool_scr'])

                if 'A' in dbg and l == 0:
                    d1 = dbg_out('KT', (128, 4 * NU), BF16)
                    k.dma(d1, KT[:].rearrange("p a b -> p (a b)"), R=['KT%d' % i for i in range(17)], W=['dbg1'])
                    d2 = dbg_out('Vt', (128, (NU // 128) * 4 * 65), BF16)
                    k.dma(d2, Vt[:].rearrange("p a b c -> p (a b c)"), R=['Vt%d' % i for i in range(17)], W=['dbg2'])
                    d3 = dbg_out('Xd', (256, 8 * NSC), BF16)
                    k.dma(d3, xd_scr.rearrange("a b c -> a (b c)"), R=['xd_scr'], W=['dbg3'])
                    d4 = dbg_out('poolL', (256, SEQ + 16), BF16)
                    k.dma(d4, pool_scr[1], R=['pool_scr'], W=['dbg4'])
                    break

            k.barrier()
            with ExitStack() as sB:
                Bin8T = T(sB, 'Bin8T', (128, 32, 128), BF16)
                Cout8T = T(sB, 'Cout8T', (128, 32, 128), BF16)
                D8T = T(sB, 'D8T', (128, 32, 128), BF16)
                LPr = T(sB, 'LPr', (128, 32), F32)
                LPi = T(sB, 'LPi', (128, 32), F32)
                sgn = T(sB, 'sgn', (128, 1), F32)
                swap_f = T(sB, 'swap_f', (128, 128), F32)
                s5dT = T(sB, 's5dT', (128, 2), F32)
                bgluT = T(sB, 'bgluT', (128, 2), F32)
                wglu = T(sB, 'wglu', (128, 2, 256), BF16)
                k.dma(swap_f[:], cst['swapm'], W=['swap_f'])
                k.op('pool', lambda: G.memset(sgn[0:64, :], 1.0), W=['sgn'])
                k.op('pool', lambda: G.memset(sgn[64:128, :], -1.0), W=['sgn'])
                k.dma(s5dT[:], prm['s5_d'][l].rearrange("(c p) -> p c", p=128), W=['s5dT'], allow_slow_non_contiguous=True)
                k.dma(bgluT[:], prm['b_glu'][l].rearrange("(c p) -> p c", p=128), W=['bgluT'], allow_slow_non_contiguous=True)
                TK = ['s5t']
                with ExitStack() as sT:
                    def t32(name):
                        return T(sT, name, (128, 32), F32)
                    LR, LI, DT_, ar_, ai_ = t32('LR'), t32('LI'), t32('DT_'), t32('ar_'), t32('ai_')
                    t1, t2, t3, t4 = t32('t1'), t32('t2'), t32('t3'), t32('t4')
                    er, ei, kr_, ki_, den = t32('er'), t32('ei'), t32('kr_'), t32('ki_'), t32('den')
                    lnr, lni = t32('lnr'), t32('lni')
                    halfpi = T(sT, 'halfpi', (128, 1), F32)
                    PWr = T(sT, 'PWr', (128, 16, 32), F32)
                    PWi = T(sT, 'PWi', (128, 16, 32), F32)
                    BR = T(sT, 'BR', (128, 32, 16), F32)
                    BI = T(sT, 'BI', (128, 32, 16), F32)
                    CR = T(sT, 'CR', (128, 32, 16), F32)
                    CI = T(sT, 'CI', (128, 32, 16), F32)
                    bbr = T(sT, 'bbr', (128, 32, 16), F32)
                    bbi = T(sT, 'bbi', (128, 32, 16), F32)
                    u1 = T(sT, 'u1', (128, 16, 16), F32)
                    u2 = T(sT, 'u2', (128, 16, 16), F32)
                    CRn = T(sT, 'CRn', (128, 2, 64), F32)
                    Bin8 = T(sT, 'Bin8', (128, 32, 128), F32)
                    CoutN = T(sT, 'CoutN', (128, 32, 128), F32)
                    wg32 = T(sT, 'wg32', (128, 2, 256), F32)
                    mlow = T(sT, 'mlow', (128, 128), F32)
                    mup = T(sT, 'mup', (128, 128), F32)
                    psT = P(sT, 'psT', (128, 128))

                    def dv(fn):
                        k.op('dve', fn, R=TK, W=TK)

                    def ac(fn):
                        k.op('act', fn, R=TK, W=TK)

                    k.dma(mlow[:], cst['mlow'], W=TK)
                    k.dma(mup[:], cst['mup'], W=TK)
                    k.dma(wg32[:], prm['w_glu'][l].rearrange("(c p) n -> p c n", p=128), W=TK)
                    dv(lambda: V.tensor_copy(wglu[:], wg32[:]))
                    for hf in range(2):
                        sl = slice(hf * 64, hf * 64 + 64)
                        k.dma(LR[sl, :], prm['lam_re'][l].rearrange("d g p -> p (d g)"), W=TK, allow_slow_non_contiguous=True)
                        k.dma(LI[sl, :], prm['lam_im'][l].rearrange("d g p -> p (d g)"), W=TK, allow_slow_non_contiguous=True, q='pool')
                        k.dma(BR[sl], prm['s5_b_re'][l].rearrange("d g p h -> p (d g) h"), W=TK)
                        k.dma(BI[sl], prm['s5_b_im'][l].rearrange("d g p h -> p (d g) h"), W=TK, q='pool')
                    k.dma(DT_[:], prm['log_dt'][l].rearrange("d g -> (d g)").partition_broadcast(128), W=TK)
                    k.op('pool', lambda: G.memset(halfpi[:], float(np.pi / 2)), W=TK)
                    for ci, (cn, Ct) in enumerate((('s5_c_re', CR), ('s5_c_im', CI))):
                        crow = prm[cn][l].rearrange("d g h p -> (d g h) p")
                        for j in range(4):
                            k.dma(CRn[:, 0, :], crow[j * 128:(j + 1) * 128, :], W=TK)
                            k.dma(CRn[:, 1, :], crow[j * 128:(j + 1) * 128, :], W=TK, q='pool')
                            k.op('pe', lambda: PE.transpose(psT[:], CRn[:].rearrange("r a p -> r (a p)"), ident_f[:]), R=TK + ['ident_f'], W=['psT'])
                            k.op('dve', lambda j=j, Ct=Ct: V.tensor_copy(Ct[:, j * 8:(j + 1) * 8, :].rearrange("q a h -> q (a h)"), psT[:]), R=['psT'] + TK, W=TK)
                    ac(lambda: A.activation(DT_[:], DT_[:], AF.Exp))
                    dv(lambda: V.tensor_tensor(ar_[:], LR[:], DT_[:], op=ALU.mult))
                    dv(lambda: V.tensor_tensor(ai_[:], LI[:], DT_[:], op=ALU.mult))
                    ac(lambda: A.activation(t1[:], ar_[:], AF.Exp, scale=1.0 / 16))
                    ac(lambda: A.activation(t2[:], ai_[:], AF.Sin, scale=1.0 / 16, bias=halfpi[:, 0:1]))
                    ac(lambda: A.activation(t3[:], ai_[:], AF.Sin, scale=1.0 / 16))
                    dv(lambda: V.tensor_tensor(er[:], t1[:], t2[:], op=ALU.mult))
                    dv(lambda: V.tensor_tensor(ei[:], t1[:], t3[:], op=ALU.mult))

                    def cmul(or_, oi_, xr, xi, yr, yi, a1=None, a2=None, a3=None, a4=None):
                        a1, a2, a3, a4 = t1[:], t2[:], t3[:], t4[:]
                        dv(lambda: V.tensor_tensor(a1, xr, yr, op=ALU.mult))
                        dv(lambda: V.tensor_tensor(a2, xi, yi, op=ALU.mult))
                        dv(lambda: V.tensor_tensor(a3, a1, a2, op=ALU.subtract))
                        dv(lambda: V.tensor_tensor(a1, xr, yi, op=ALU.mult))
                        dv(lambda: V.tensor_tensor(a2, xi, yr, op=ALU.mult))
                        dv(lambda: V.tensor_tensor(a4, a1, a2, op=ALU.add))
                        dv(lambda: V.tensor_copy(or_, a3))
                        dv(lambda: V.tensor_copy(oi_, a4))

                    for _ in range(4):
                        cmul(er[:], ei[:], er[:], ei[:], er[:], ei[:], t1[:], t2[:], t3[:], t4[:])
                    dv(lambda: V.tensor_scalar_add(lnr[:], er[:], -1.0))
                    dv(lambda: V.tensor_tensor(t1[:], LR[:], LR[:], op=ALU.mult))
                    dv(lambda: V.tensor_tensor(t2[:], LI[:], LI[:], op=ALU.mult))
                    dv(lambda: V.tensor_tensor(den[:], t1[:], t2[:], op=ALU.add))
                    dv(lambda: V.reciprocal(den[:], den[:]))
                    dv(lambda: V.tensor_tensor(t1[:], lnr[:], LR[:], op=ALU.mult))
                    dv(lambda: V.tensor_tensor(t2[:], ei[:], LI[:], op=ALU.mult))
                    dv(lambda: V.tensor_tensor(t1[:], t1[:], t2[:], op=ALU.add))
                    dv(lambda: V.tensor_tensor(kr_[:], t1[:], den[:], op=ALU.mult))
                    dv(lambda: V.tensor_tensor(t1[:], ei[:], LR[:], op=ALU.mult))
                    dv(lambda: V.tensor_tensor(t2[:], lnr[:], LI[:], op=ALU.mult))
                    dv(lambda: V.tensor_tensor(t1[:], t1[:], t2[:], op=ALU.subtract))
                    dv(lambda: V.tensor_tensor(ki_[:], t1[:], den[:], op=ALU.mult))
                    bk = lambda a: a[:].unsqueeze(2).to_broadcast([128, 32, 16])
                    w1 = Bin8[:, :, 0:16]
                    w2 = Bin8[:, :, 16:32]
                    dv(lambda: V.tensor_tensor(w1, BR[:], bk(kr_), op=ALU.mult))
                    dv(lambda: V.tensor_tensor(w2, BI[:], bk(ki_), op=ALU.mult))
                    dv(lambda: V.tensor_tensor(bbr[:], w1, w2, op=ALU.subtract))
                    dv(lambda: V.tensor_tensor(w1, BI[:], bk(kr_), op=ALU.mult))
                    dv(lambda: V.tensor_tensor(w2, BR[:], bk(ki_), op=ALU.mult))
                    dv(lambda: V.tensor_tensor(bbi[:], w1, w2, op=ALU.add))
                    dv(lambda: V.memset(PWr[:, 7, :], 1.0))
                    dv(lambda: V.memset(PWi[:, 7, :], 0.0))
                    for e in range(1, 9):
                        cmul(PWr[:, 7 + e, :], PWi[:, 7 + e, :], PWr[:, 6 + e, :], PWi[:, 6 + e, :], er[:], ei[:])
                    ac(lambda: A.activation(den[:], ar_[:], AF.Exp, scale=-2.0))
                    dv(lambda: V.tensor_tensor(lnr[:], er[:], den[:], op=ALU.mult))
                    dv(lambda: V.tensor_tensor(lni[:], ei[:], den[:], op=ALU.mult))
                    dv(lambda: V.tensor_scalar_mul(lni[:], lni[:], -1.0))
                    for e in range(1, 8):
                        cmul(PWr[:, 7 - e, :], PWi[:, 7 - e, :], PWr[:, 8 - e, :], PWi[:, 8 - e, :], lnr[:], lni[:])
                    dv(lambda: V.tensor_copy(LPr[:], PWr[:, 15, :]))
                    dv(lambda: V.tensor_copy(LPi[:], PWi[:, 15, :]))

                    def ctab(dst, Xr, Xi, d, slot, e, im_sign):
                        gs = slice(d * 16, d * 16 + 8)
                        for hf in range(2):
                            ps_ = slice(hf * 64, hf * 64 + 64)
                            pr = PWr[ps_, 7 + e, gs].unsqueeze(2).to_broadcast([64, 8, 16])
                            pi = PWi[ps_, 7 + e, gs].unsqueeze(2).to_broadcast([64, 8, 16])
                            o = dst[ps_, gs, slot * 16:(slot + 1) * 16]
                            if hf == 0:
                                dv(lambda: V.tensor_tensor(u1[ps_, 0:8], Xr[ps_, gs, :], pr, op=ALU.mult))
                                dv(lambda: V.tensor_tensor(u2[ps_, 0:8], Xi[ps_, gs, :], pi, op=ALU.mult))
                                dv(lambda: V.tensor_tensor(o, u1[ps_, 0:8], u2[ps_, 0:8], op=ALU.subtract))
                            else:
                                dv(lambda: V.tensor_tensor(u1[ps_, 0:8], Xr[ps_, gs, :], pi, op=ALU.mult))
                                dv(lambda: V.tensor_tensor(u2[ps_, 0:8], Xi[ps_, gs, :], pr, op=ALU.mult))
                                dv(lambda: V.tensor_tensor(o, u1[ps_, 0:8], u2[ps_, 0:8], op=ALU.add))
                                if im_sign < 0:
                                    dv(lambda: V.tensor_scalar_mul(o, o, -1.0))

                    for d in range(2):
                        for i in range(8):
                            ctab(Bin8, bbr, bbi, d, i, (7 - i) if d == 0 else i, +1)
                            ctab(Cout8T, CR, CI, d, i, (i + 1) if d == 0 else (8 - i), -1)
                            ctab(CoutN, CR, CI, d, i, (i - 7) if d == 0 else (-i), -1)
                    for dg in [d_ * 16 + g_ for d_ in range(2) for g_ in range(8)]:
                        k.op('pe', lambda dg=dg: PE.transpose(psT[:], Bin8[:, dg, :], ident_f[:]), R=TK + ['ident_f'], W=['psT'])
                        k.op('act', lambda dg=dg: A.copy(Bin8T[:, dg, :], psT[:]), R=['psT'], W=['Bin8T'])
                        k.op('pe', lambda dg=dg: PE.matmul(psT[:], lhsT=Bin8[:, dg, :], rhs=CoutN[:, dg, :], start=True, stop=True), R=TK, W=['psT'])
                        mk = mlow if dg < 16 else mup
                        k.op('dve', lambda dg=dg, mk=mk: V.tensor_tensor(D8T[:, dg, :], psT[:], mk[:], op=ALU.mult), R=['psT'] + TK, W=['D8T'])

                    psd = P(sT, 'psd', (128, 128), BF16)
                    k.op('pe', lambda: PE.transpose(psd[:], ident_b[:], ident_b[:]), R=['ident_b', 'psT'], W=['psd'])

                if 'B0' in dbg and l == 0:
                    for nm, tl in (('Bin8T', Bin8T), ('Cout8T', Cout8T), ('D8T', D8T)):
                        d_ = dbg_out(nm, (128, 32 * 128), BF16)
                        k.dma(d_, tl[:].rearrange("p a b -> p (a b)"), R=[nm], W=['dbg' + nm])
                    d_ = dbg_out('LP', (128, 64))
                    k.dma(d_[:, 0:32], LPr[:], R=TK, W=['dbgLP'])
                    k.dma(d_[:, 32:64], LPi[:], R=TK, W=['dbgLP'])
                    break

                k.barrier()
                with ExitStack() as sS:
                    IM = T(sS, 'IM', (128, 2, NSC), BF16)
                    H = T(sS, 'H', (128, 4, NSC + 4), F32)
                    Hb = T(sS, 'Hb', (128, 4, NSC + 4), BF16)
                    Yo = T(sS, 'Yo', (128, 2, NSC), BF16)
                    ATb = [T(sS, 'AT%d' % i, (128, 11, 4, 128), BF16) for i in range(2)]
                    atmp = T(sS, 'atmp', (128, 128), F32)
                    v2 = T(sS, 'v2', (128, 32), F32)
                    LQr = T(sS, 'LQr', (128, 11, 32), F32)
                    LQi = T(sS, 'LQi', (128, 11, 32), F32)
                    q1, q2 = T(sS, 'q1', (128, 32), F32), T(sS, 'q2', (128, 32), F32)
                    psS = [[P(sS, 'psS%d_%d' % (i, j), (128, 512)) for j in range(4)] for i in range(2)]
                    GO = 2
                    NLEV = int(os.environ.get('DBG_NLEV', 11))
                    NRND = int(os.environ.get('DBG_NRND', 4))
                    DOREAD = int(os.environ.get('DBG_READ', 1))
                    k.op('dve', lambda: V.tensor_copy(LQr[:, 0, :], LPr[:]), R=TK, W=['LQ'])
                    k.op('dve', lambda: V.tensor_copy(LQi[:, 0, :], LPi[:]), R=TK, W=['LQ'])
                    for lev in range(1, NLEV):
                        a_r, a_i = LQr[:, lev - 1, :], LQi[:, lev - 1, :]
                        k.op('dve', lambda: V.tensor_tensor(q1[:], a_r, a_r, op=ALU.mult), R=['LQ'], W=['q1'])
                        k.op('dve', lambda: V.tensor_tensor(q2[:], a_i, a_i, op=ALU.mult), R=['LQ'], W=['q2'])
                        k.op('dve', lambda lev=lev: V.tensor_tensor(LQr[:, lev, :], q1[:], q2[:], op=ALU.subtract), R=['q1', 'q2'], W=['LQ'])
                        k.op('dve', lambda: V.tensor_tensor(q1[:], a_r, a_i, op=ALU.mult), R=['LQ'], W=['q1'])
                        k.op('dve', lambda lev=lev: V.tensor_scalar_mul(LQi[:, lev, :], q1[:], 2.0), R=['q1'], W=['LQ'])
                    if int(os.environ.get('DBG_MS', 1)):
                        k.op('dve', lambda: V.memset(H[:], 0.0), W=['H%d' % j for j in range(4)])
                        k.op('pool', lambda: G.memset(Hb[:], 0.0), W=['Hb%d' % j for j in range(4)])
                    xdg = xd_scr.rearrange("(g h) i n -> g h i n", h=16)
                    ydg = yd_scr.rearrange("(g h) i n -> g h i n", h=16)
                    CB = [(0, 512), (512, 1024), (1024, NSC)]
                    for rnd in range(NRND):
                        gl_ = [2 * rnd, 2 * rnd + 1]
                        combos = [(d, g) for g in gl_ for d in range(2)]
                        if int(os.environ.get('DBG_IMZ', 0)):
                            k.op('pool', lambda: G.memset(IM[:], 0.5), W=['IM'])
                        for gi, g in enumerate(gl_ if int(os.environ.get('DBG_IM', 1)) else []):
                            for i in range(8):
                                k.dma(IM[16 * i:16 * i + 16, gi, :], xdg[g, :, i, :], R=['xd_scr'], W=['IM'], q=('sp' if i % 2 == 0 else 'pool'))
                        ATc, ATk = ATb[rnd % 2], 'AT%d' % (rnd % 2)
                        for lev in range(NLEV):
                            k.op('dve', lambda lev=lev: V.tensor_scalar_mul(v2[:], LQi[:, lev, :], sgn[:, 0:1]), R=['LQ', 'sgn'], W=['v2'])
                            for j, (d, g) in enumerate(combos):
                                dg = d * 16 + g
                                k.op('dve', lambda lev=lev, dg=dg: V.tensor_scalar_mul(atmp[:], ident_f[:], LQr[:, lev, dg:dg + 1]), R=['LQ', 'ident_f'], W=['atmp'])
                                k.op('dve', lambda lev=lev, dg=dg, j=j: V.scalar_tensor_tensor(ATc[:, lev, j, :], swap_f[:], v2[:, dg:dg + 1], atmp[:],
                                                                                              op0=ALU.mult, op1=ALU.add), R=['swap_f', 'v2', 'atmp'], W=[ATk])
                        for bi_, (c0, c1) in enumerate(CB):
                            pss = psS[bi_ % 2]
                            for j, (d, g) in enumerate(combos):
                                pk = 'psS%d_%d' % (bi_ % 2, j)
                                k.op('pe', lambda j=j, d=d, g=g: PE.matmul(pss[j][:, 0:c1 - c0], lhsT=Bin8T[:, d * 16 + g, :], rhs=IM[:, j // 2, c0:c1],
                                                                            start=True, stop=True), R=['Bin8T', 'IM'], W=[pk])
                                k.op('dve', lambda j=j: V.tensor_copy(H[:, j, GO + c0:GO + c1], pss[j][:, 0:c1 - c0]), R=[pk], W=['H%d' % j])
                                k.op('act', lambda j=j: A.copy(Hb[:, j, GO + c0:GO + c1], H[:, j, GO + c0:GO + c1]), R=['H%d' % j], W=['Hb%d' % j])
                        for lev in range(NLEV):
                            s_ = 1 << lev
                            nb = (NSC - s_ + 511) // 512
                            for bi_ in range(nb):
                                pss = psS[bi_ % 2]
                                hi_f = NSC - bi_ * 512
                                lo_f = max(s_, hi_f - 512)
                                lo_b = bi_ * 512
                                hi_b = min(NSC - s_, lo_b + 512)
                                for j, (d, g) in enumerate(combos):
                                    pk = 'psS%d_%d' % (bi_ % 2, j)
                                    lo_, hi_ = (lo_f, hi_f) if d == 0 else (lo_b, hi_b)
                                    sh_ = -s_ if d == 0 else s_
                                    k.op('pe', lambda j=j: PE.matmul(pss[j][:, 0:hi_ - lo_], lhsT=ATc[:, lev, j, :], rhs=Hb[:, j, GO + lo_ + sh_:GO + hi_ + sh_],
                                                                    start=True, stop=True), R=[ATk, 'Hb%d' % j], W=[pk])
                                    k.op('dve', lambda j=j, lo_=lo_, hi_=hi_: V.tensor_tensor(H[:, j, GO + lo_:GO + hi_], H[:, j, GO + lo_:GO + hi_], pss[j][:, 0:hi_ - lo_], op=ALU.add),
                                         R=[pk, 'H%d' % j], W=['H%d' % j])
                                    if True:
                                        k.op('act', lambda j=j, lo_=lo_, hi_=hi_: A.copy(Hb[:, j, GO + lo_:GO + hi_], H[:, j, GO + lo_:GO + hi_]), R=['H%d' % j], W=['Hb%d' % j])
                                    else:
                                        k.op('pool', lambda j=j, lo_=lo_, hi_=hi_: G.tensor_copy(Hb[:, j, GO + lo_:GO + hi_], H[:, j, GO + lo_:GO + hi_]), R=['H%d' % j], W=['Hb%d' % j])
                        RB = [(0, 32, (0,)), (32, 544, (0, 1)), (544, 1056, (0, 1)), (1056, NSC, (1,))]
                        for bi_, (c0, c1, dirs) in enumerate(RB):
                            pss = psS[bi_ % 2]
                            for gi, g in enumerate(gl_):
                                pk = 'psS%d_%d' % (bi_ % 2, gi)
                                nmm = 2 * len(dirs)
                                mi = 0
                                for d in dirs:
                                    j = 2 * gi + d
                                    sh = -1 if d == 0 else 1
                                    k.op('pe', lambda gi=gi, d=d, g=g, mi=mi: PE.matmul(pss[gi][:, 0:c1 - c0], lhsT=D8T[:, d * 16 + g, :], rhs=IM[:, gi, c0:c1],
                                                                                    start=(mi == 0), stop=False), R=['D8T', 'IM'], W=[pk])
                                    mi += 1
                                    k.op('pe', lambda gi=gi, d=d, g=g, j=j, sh=sh, mi=mi: PE.matmul(pss[gi][:, 0:c1 - c0], lhsT=Cout8T[:, d * 16 + g, :],
                                                                                                rhs=Hb[:, j, GO + c0 + sh:GO + c1 + sh],
                                                                                                start=False, stop=(mi == nmm - 1)), R=['Cout8T', 'Hb%d' % j], W=[pk])
                                    mi += 1
                                k.op('act', lambda gi=gi: A.copy(Yo[:, gi, c0:c1], pss[gi][:, 0:c1 - c0]), R=[pk], W=['Yo'])
                        for gi, g in enumerate(gl_ if int(os.environ.get('DBG_YD', 1)) else []):
                            for i in range(8):
                                k.dma(ydg[g, :, i, :], Yo[16 * i:16 * i + 16, gi, :], R=['Yo'], W=['yd_scr'], q=('sp' if i % 2 == 0 else 'pool'))

                if nlayers > 1 or int(os.environ.get('DBG_CC', 0)):
                    k.barrier()
                    k._sync('pool', ['yd_scr'], ['ydg'])
                    ydf = yd_scr.rearrange("c i n -> c (i n)")
                    for c_ in range(2):
                        G.collective_compute("AllGather", ALU.bypass, replica_groups=[[0, 1], [2, 3], [4, 5], [6, 7]],
                                             ins=[ydf[c_ * 64:(c_ + 1) * 64, :]], outs=[ydg_scr[c_]]).then_inc(ccsem)
                    ccn[0] += 2
                    for eng in k.eng.values():
                        eng.wait_ge(ccsem, ccn[0])
                    with ExitStack() as sX:
                        ya = T(sX, 'ya', (128, 8 * NSC), BF16)
                        yb = T(sX, 'yb', (128, 8 * NSC), BF16)
                        for c_ in range(2):
                            k.dma(ya[c_ * 64:(c_ + 1) * 64, :], ydg_scr[c_][64:128, :], W=['ya'], q=('sp' if c_ == 0 else 'pool'))
                            k.dma(yb[c_ * 64:(c_ + 1) * 64, :], ydg_scr[c_][0:64, :], W=['yb'], q=('sp' if c_ == 0 else 'pool'))
                        k.op('dve', lambda: V.tensor_scalar_mul(ya[:], ya[:], msb[:, 0:1]), R=['ya', 'msb'], W=['ya'])
                        k.op('dve', lambda: V.scalar_tensor_tensor(ya[:], yb[:], msb[:, 1:2], ya[:], op0=ALU.mult, op1=ALU.add), R=['ya', 'yb', 'msb'], W=['ya'])
                        k.dma(ydf[128:256, :], ya[:], R=['ya'], W=['yd_scr'])
                    k.barrier()

                if 'B1' in dbg and l == 0:
                    d_ = dbg_out('yd', (256, 8 * NSC), BF16)
                    if int(os.environ.get('DBG_YDD', 1)):
                        k.dma(d_, yd_scr.rearrange("a b c -> a (b c)"), R=['yd_scr'], W=['dbgyd'])
                    break

                k.barrier()
                with ExitStack() as sG:
                    xdt = T(sG, 'xdt', (128, 2, 8, 64), BF16)
                    ydt = T(sG, 'ydt', (128, 2, 8, 64), BF16)
                    yd2 = T(sG, 'yd2', (128, 2, 8, 64), BF16)
                    yf = T(sG, 'yf', (128, 2, 512), F32)
                    g1 = T(sG, 'g1', (128, 2, 512), F32)
                    g2 = T(sG, 'g2', (128, 2, 512), F32)
                    glb = T(sG, 'glb', (128, 2, 512), BF16)
                    sgm = T(sG, 'sgm', (128, 2, 512), F32)
                    s5ob = T(sG, 's5ob', (128, 2, 512), BF16)
                    psG = [P(sG, 'psG%d' % i, (128, 512)) for i in range(2)]
                    xdv = xd_scr.rearrange("(c p) i n -> p c i n", p=128)
                    ydv = yd_scr.rearrange("(c p) i n -> p c i n", p=128)
                    xdtB = T(sG, 'xdtB', (128, 2, 8, 64), BF16)
                    ydtB = T(sG, 'ydtB', (128, 2, 8, 64), BF16)
                    blocks = [('ctx', 0, NCTX)] + [('lat', 512 * i, 512) for i in range(HALF // 512)]
                    for (kind, t0, Tn) in blocks:
                        u0 = t0 if kind == 'ctx' else NCTX + t0
                        n8 = Tn // 8
                        for ct in range(2):
                            k.dma(xdt[:, ct, :, 0:n8], xdv[:, ct, :, u0 // 8:u0 // 8 + n8], R=['xd_scr'], W=['xdt'])
                            k.dma(ydt[:, ct, :, 0:n8], ydv[:, ct, :, u0 // 8:u0 // 8 + n8], R=['yd_scr'], W=['ydt'], q='pool')
                            if kind == 'lat':
                                uB = u0 + HALF
                                k.dma(xdtB[:, ct, :, 0:n8], xdv[:, ct, :, uB // 8:uB // 8 + n8], R=['xd_scr'], W=['xdtB'], q='pool')
                                k.dma(ydtB[:, ct, :, 0:n8], ydv[:, ct, :, uB // 8:uB // 8 + n8], R=['yd_scr'], W=['ydtB'])
                            if kind == 'ctx':
                                k.dma(yd2[:, ct, :, 0:n8], ydv[:, ct, :, NU // 8:NU // 8 + n8], R=['yd_scr'], W=['yd2'])
                        if kind == 'ctx':
                            k.op('pool', lambda: G.tensor_tensor(ydt[:, :, :, 0:n8], ydt[:, :, :, 0:n8], yd2[:, :, :, 0:n8], op=ALU.add), R=['ydt', 'yd2'], W=['ydt'])
                        else:
                            for (ta_, tb_, ka_, kb_) in ((xdt, xdtB, 'xdt', 'xdtB'), (ydt, ydtB, 'ydt', 'ydtB')):
                                k.op('dve', lambda ta_=ta_: V.tensor_scalar_mul(ta_[:], ta_[:], msb[:, 0:1]), R=[ka_, 'msb'], W=[ka_])
                                k.op('dve', lambda ta_=ta_, tb_=tb_: V.scalar_tensor_tensor(ta_[:], tb_[:], msb[:, 1:2], ta_[:], op0=ALU.mult, op1=ALU.add), R=[ka_, kb_, 'msb'], W=[ka_])
                        for ct in range(2):
                            k.op('dve', lambda ct=ct: V.scalar_tensor_tensor(yf[:, ct, 0:Tn].rearrange("p (c i) -> p i c", i=8), xdt[:, ct, :, 0:n8], s5dT[:, ct:ct + 1],
                                                                             ydt[:, ct, :, 0:n8], op0=ALU.mult, op1=ALU.add), R=['xdt', 'ydt', 's5dT'], W=['yf'])
                        if 'B2y' in dbg and l == 0:
                            if kind == 'ctx':
                                dyl = dbg_out('yl', (256, NU))
                            k.dma(dyl[:, u0:u0 + Tn].rearrange("(c p) n -> p c n", p=128), yf[:, :, 0:Tn], R=['yf'], W=['dbgyl'])
                        k.op('pool', lambda: G.tensor_tensor(g1[:, :, 0:Tn], yf[:, :, 0:Tn], yf[:, :, 0:Tn], op=ALU.mult), R=['yf'], W=['g1'])
                        k.op('dve', lambda: V.tensor_scalar(g1[:, :, 0:Tn], g1[:, :, 0:Tn], 0.044715, 1.0, op0=ALU.mult, op1=ALU.add), R=['g1'], W=['g1'])
                        k.op('pool', lambda: G.tensor_tensor(g2[:, :, 0:Tn], g1[:, :, 0:Tn], yf[:, :, 0:Tn], op=ALU.mult), R=['g1', 'yf'], W=['g2'])
                        k.op('act', lambda: A.activation(g2[:, :, 0:Tn], g2[:, :, 0:Tn], AF.Sigmoid, scale=1.5957691216057308), R=['g2'], W=['g2'])
                        k.op('dve', lambda: V.tensor_tensor(glb[:, :, 0:Tn], yf[:, :, 0:Tn], g2[:, :, 0:Tn], op=ALU.mult), R=['yf', 'g2'], W=['glb'])
                        for m in range(2):
                            for kt in range(2):
                                k.op('pe', lambda m=m, kt=kt: PE.matmul(psG[m][:, 0:Tn], lhsT=wglu[:, kt, m * 128:(m + 1) * 128], rhs=glb[:, kt, 0:Tn],
                                                                        start=(kt == 0), stop=(kt == 1)), R=['wglu', 'glb'], W=['psG%d' % m])
                            k.op('act', lambda m=m: A.activation(sgm[:, m, 0:Tn], psG[m][:, 0:Tn], AF.Sigmoid, bias=bgluT[:, m:m + 1]), R=['psG%d' % m, 'bgluT'], W=['sgm'])
                        k.op('dve', lambda: V.tensor_tensor(s5ob[:, :, 0:Tn], glb[:, :, 0:Tn], sgm[:, :, 0:Tn], op=ALU.mult), R=['glb', 'sgm'], W=['s5ob'])
                        k.dma(s5o_scr[:, u0:u0 + Tn].rearrange("(c p) n -> p c n", p=128), s5ob[:, :, 0:Tn], R=['s5ob'], W=['s5o_scr'])

                if 'B2' in dbg and l == 0:
                    d_ = dbg_out('s5o', (256, NU), BF16)
                    k.dma(d_, s5o_scr, R=['s5o_scr'], W=['dbgs5o'])
                    break

            k.barrier()
            with ExitStack() as sC:
                wC = T(sC, 'wC', (128, 8, 1728), BF16)
                wout = T(sC, 'wout', (128, 8, 1024), BF16)
                wqh = T(sC, 'wqh', (128, 2, 4, 96), BF16)
                wqr = T(sC, 'wqr', (128, 2, 4, 96), BF16)
                wpl = T(sC, 'wpl', (128, 2, 128), BF16)
                wsT = T(sC, 'wsT', (128, 4, 128), BF16)
                bsb = T(sC, 'bsb', (128, 4, 128), F32)
                gate_bc = T(sC, 'gate_bc', (128, D), F32)
                lng_bc = T(sC, 'lng_bc', (128, D), F32)
                lnb_bc = T(sC, 'lnb_bc', (128, D), F32)
                colv = T(sC, 'colv', (128, 8), F32)
                winv = T(sC, 'winv', (128, 2), F32)
                with ExitStack() as sw:
                    wst2 = [T(sw, 'wstC%d' % i, (128, 8, 256), F32) for i in range(2)]
                    wq32 = T(sw, 'wq32', (128, 2, 384), F32)
                    wp32 = T(sw, 'wp32', (128, 2, 128), F32)
                    ws32 = T(sw, 'ws32', (128, 128), F32)
                    psw = P(sw, 'psw', (128, 128))
                    win = prm['w_in'][l].rearrange("(c p) n -> p c n", p=128)
                    segs = [(416, 192, 0), (864, 256, 192), (1120, 256, 448), (1376, 256, 704), (1632, 256, 960), (1888, 256, 1216), (2144, 256, 1472)]
                    for si_, (c0, n, d0) in enumerate(segs):
                        wst, wk = wst2[si_ % 2], 'wstC%d' % (si_ % 2)
                        k.dma(wst[:, :, 0:n], win[:, :, c0:c0 + n], W=[wk], q=('sp' if si_ % 2 == 0 else 'pool'))
                        k.op('dve' if si_ % 2 == 0 else 'pool', lambda n=n, d0=d0, wst=wst, si_=si_: (V if si_ % 2 == 0 else G).tensor_copy(wC[:, :, d0:d0 + n], wst[:, :, 0:n]), R=[wk], W=['wC'])
                    wo = prm['w_out'][l].rearrange("(c p) n -> p c n", p=128)
                    for j in range(4):
                        wst, wk = wst2[(j + 1) % 2], 'wstC%d' % ((j + 1) % 2)
                        k.dma(wst[:], wo[:, :, j * 256:(j + 1) * 256], W=[wk], q=('sp' if j % 2 == 0 else 'pool'))
                        k.op('dve' if j % 2 == 0 else 'pool', lambda j=j, wst=wst: (V if j % 2 == 0 else G).tensor_copy(wout[:, :, j * 256:(j + 1) * 256], wst[:]), R=[wk], W=['wout'])
                    for ci, nm in enumerate(('sgu_g', 'sgu_b', 'pool_scale')):
                        k.dma(colv[:, 2 * ci:2 * ci + 2], prm[nm][l].rearrange("(c p) -> p c", p=128), W=['colv'], allow_slow_non_contiguous=True)
                    k.dma(colv[:, 6:7], prm['g_q'][l][0:128].rearrange("(p o) -> p o", o=1), W=['colv'])
                    k.dma(colv[0:64, 7:8], prm['g_q'][l][128:192].rearrange("(p o) -> p o", o=1), W=['colv'])
                    k.op('pool', lambda: G.memset(winv[0:64, 0:1], 1.0 / 2), W=['winv'])
                    k.op('pool', lambda: G.memset(winv[64:128, 0:1], 1.0 / 4), W=['winv'])
                    k.op('pool', lambda: G.memset(winv[0:64, 1:2], 1.0 / 8), W=['winv'])
                    k.op('pool', lambda: G.memset(winv[64:128, 1:2], 1.0 / 16), W=['winv'])
                    k.dma(bsb[:].rearrange("p h t -> p (h t)"), prm['b_s'][l].rearrange("h t -> (h t)").partition_broadcast(128), W=['bsb'])
                    k.dma(lng_bc[:], prm['ln_g'][l].partition_broadcast(128), W=['lng_bc'])
                    k.dma(lnb_bc[:], prm['ln_b'][l].partition_broadcast(128), W=['lnb_bc'], q='pool')
                    k.op('pool', lambda: G.memset(wq32[:], 0.0), W=['wq32'])
                    k.dma(wq32[:, 0, :], prm['w_uq'][l][0:128, :], W=['wq32'])
                    k.dma(wq32[0:64, 1, :], prm['w_uq'][l][128:192, :], W=['wq32'])
                    k.op('pool', lambda: G.memset(wqr[:], 0.0), W=['wqr'])
                    for kt in range(2):
                        rows = slice(0, 128) if kt == 0 else slice(0, 64)
                        k.op('dve', lambda kt=kt, rows=rows: V.tensor_scalar_mul(wq32[rows, kt, :], wq32[rows, kt, :], colv[rows, 6 + kt:7 + kt]), R=['wq32', 'colv'], W=['wq32'])
                    k.op('dve', lambda: V.tensor_copy(wqh[:].rearrange("p a h c -> p a (h c)"), wq32[:]), R=['wq32'], W=['wqh'])
                    wq4 = wq32[:].rearrange("p a (h c) -> p a h c", h=4)
                    for (d0, s0_) in ((0, 8), (8, 0), (16, 24), (24, 16)):
                        k.op('dve', lambda d0=d0, s0_=s0_: V.tensor_copy(wqr[:, :, :, 64 + d0:72 + d0], wq4[:, :, :, 64 + s0_:72 + s0_]), R=['wq32'], W=['wqr'])
                    k.op('pool', lambda: G.memset(wp32[:], 0.0), W=['wp32'])
                    for g in range(4):
                        r0 = (g % 2) * 64
                        k.dma(wp32[r0:r0 + 64, g // 2, r0:r0 + 64], prm['w_pool'][l][g], W=['wp32'])
                    k.op('dve', lambda: V.tensor_copy(wpl[:], wp32[:]), R=['wp32'], W=['wpl'])
                    for h in range(4):
                        k.dma(ws32[:], prm['w_s'][l][h], W=['ws32'])
                        k.op('pe', lambda: PE.transpose(psw[:], ws32[:], ident_f[:]), R=['ws32', 'ident_f'], W=['psw'])
                        k.op('act', lambda h=h: A.copy(wsT[:, h, :], psw[:]), R=['psw'], W=['wsT'])
                    psd2 = P(sw, 'psd2', (128, 128), BF16)
                    k.op('pe', lambda: PE.transpose(psd2[:], ident_b[:], ident_b[:]), R=['ident_b', 'psw'], W=['psd2'])
                k.barrier()

                with ExitStack() as sc:
                    xt = [T(sc, 'xtC0', (128, D), F32)] * 2
                    st6 = T(sc, 'st6C', (128, 2, 6), F32)
                    mv = T(sc, 'mvC', (128, 2), F32)
                    rstd = T(sc, 'rstdC', (128, 1), F32)
                    xn = T(sc, 'xnC', (128, D), BF16)
                    hT = T(sc, 'hTC', (128, 8, QB), BF16)
                    sq = T(sc, 'sqC', (128, 2, QB), BF16)
                    rms = T(sc, 'rmsC', (128, QB), F32)
                    QT = T(sc, 'QT', (128, 4, QB), BF16)
                    rc_t = T(sc, 'rc_tC', (128, QB), F32)
                    rs_t = T(sc, 'rs_tC', (128, QB), F32)
                    su = T(sc, 'su', (128, 2, QB), BF16)
                    svb = T(sc, 'svb', (128, 2, QB), BF16)
                    cqn = svb
                    mean = T(sc, 'mean', (128, QB), F32)
                    var = T(sc, 'var', (128, QB), F32)
                    q1t, q2t = mean, var
                    vnb = T(sc, 'vnb', (128, 2, QB), BF16)
                    vc = T(sc, 'vc', (128, 256), BF16)
                    mx = T(sc, 'mx', (128, 128), F32)
                    sg = T(sc, 'sg', (128, 8, QB), BF16)
                    PT = [T(sc, 'PT%d' % i, (128, 2 * QB), BF16) for i in range(3)]
                    osb = T(sc, 'osb', (128, 4, 65), F32)
                    rec = T(sc, 'rec', (128, 4, 1), F32)
                    att = T(sc, 'att', (128, 4, 256), BF16)
                    catg = sg
                    pin = T(sc, 'pin', (128, 2, QB + 16), BF16)
                    pa = T(sc, 'pa', (128, 2, QB + 16), F32)
                    svf = pa
                    pb = T(sc, 'pb', (128, 2, QB + 16), F32)
                    pld = T(sc, 'pld', (128, 2, QB), BF16)
                    s5t = T(sc, 's5t', (128, 2, QB), BF16)
                    rr = T(sc, 'rr', (128, D), F32)
                    hf32 = rr[:].rearrange("p (a b) -> p a b", a=8)
                    pt = P(sc, 'ptC', (128, 8, 128), BF16)
                    psC = [P(sc, 'psC%d' % i, (128, 512)) for i in range(4)]
                    psS_ = [P(sc, 'psSc%d' % i, (128, 512)) for i in range(2)]
                    psO = P(sc, 'psO', (128, 4, 65))
                    sm2c = [T(sc, 'sm2c_%d' % i, (128, 16), F32) for i in range(2)]
                    ptc2 = psC[3][:].bitcast(BF16).rearrange("p (a b) -> p a b", a=8)
                    LNC = {'xt': [(xt[0], 'xtC0'), (rr, 'rr')], 'xn': [(xn, 'xn'), (xn, 'xn')],
                           'pt': [(pt, 'ptC'), (ptc2, 'psC3')], 'sm': [(sm2c[0], 'sm2c_0'), (sm2c[1], 'sm2c_1')], 'hTk': 'hT', 'dve_mod': True}
                    assert QB <= 512 and NCTX <= QB
                    SCALE = 96 ** -0.5

                    qblocks = [('lat', QB * i, QB) for i in range(HALF // QB)]
                    if not last:
                        qblocks = [('ctx', 0, NCTX)] + qblocks
                    NQB = int(os.environ.get('DBG_NQB', len(qblocks)))
                    xcnt = 0

                    def do_ln(bi_):
                        kind_, t0_, Tn_ = qblocks[bi_]
                        src_ = cin if kind_ == 'ctx' else (xq_in if l == 0 else x1h_scr)
                        ln_block([src_[t0_ + sb_ * 128: t0_ + (sb_ + 1) * 128, :] for sb_ in range(Tn_ // 128)], ['x1w'],
                                 1 if kind_ == 'ctx' else 0, hT, LNC)

                    gcol = [None]
                    do_ln(0)
                    for bidx, (kind, t0, Tn) in enumerate(qblocks[:NQB]):
                        col = 1 if kind == 'ctx' else 0
                        src = cin if kind == 'ctx' else (xq_in if l == 0 else x1h_scr)
                        dst = (ctx1_scr if kind == 'ctx' else (out_ap if last else x1h_scr))
                        u0 = t0 if kind == 'ctx' else NCTX + t0
                        nsub = Tn // 128
                        nkt = (NCTX // 128) if kind == 'ctx' else (NU // 128)
                        if gcol[0] != col:
                            k.dma(gate_bc[:], gate_scr[l, col].partition_broadcast(128), R=['gate_scr'], W=['gate_bc'])
                            gcol[0] = col

                        def proj(pst, key, c0, ncol):
                            for dc in range(8):
                                k.op('pe', lambda dc=dc: PE.matmul(pst[0:ncol, 0:Tn], lhsT=wC[:, dc, c0:c0 + ncol], rhs=hT[:, dc, 0:Tn],
                                                                  start=(dc == 0), stop=(dc == 7)), R=['wC', 'hT'], W=[key])
                        proj(psC[0], 'psC0', 0, 128)
                        proj(psC[1], 'psC1', 128, 64)
                        k.op('act', lambda: A.activation(sq[:, 0, 0:Tn], psC[0][:, 0:Tn], AF.Square), R=['psC0'], W=['sq'])
                        k.op('act', lambda: A.activation(sq[0:64, 1, 0:Tn], psC[1][0:64, 0:Tn], AF.Square), R=['psC1'], W=['sq'])
                        k.op('pe', lambda: PE.matmul(psC[2][:, 0:Tn], lhsT=ones_b[:, :], rhs=sq[:, 0, 0:Tn], start=True, stop=False), R=['ones_b', 'sq'], W=['psC2'])
                        k.op('pe', lambda: PE.matmul(psC[2][:, 0:Tn], lhsT=ones_b[0:64, :], rhs=sq[0:64, 1, 0:Tn], start=False, stop=True), R=['ones_b', 'sq'], W=['psC2'])
                        k.op('act', lambda: A.activation(rms[:, 0:Tn], psC[2][:, 0:Tn], AF.Sqrt, bias=epsb[:, 0:1], scale=1.0 / 192), R=['psC2', 'epsb'], W=['rms'])
                        k.op('dve', lambda: V.reciprocal(rms[:, 0:Tn], rms[:, 0:Tn]), R=['rms'], W=['rms'])
                        k.op('dve', lambda: V.tensor_tensor(cqn[:, 0, 0:Tn], psC[0][:, 0:Tn], rms[:, 0:Tn], op=ALU.mult), R=['psC0', 'rms'], W=['svb'])
                        k.op('dve', lambda: V.tensor_tensor(cqn[0:64, 1, 0:Tn], psC[1][0:64, 0:Tn], rms[0:64, 0:Tn], op=ALU.mult), R=['psC1', 'rms'], W=['svb'])
                        if kind == 'lat':
                            k.dma(rc_t[64:96, :], ropecq[:, t0:t0 + Tn], W=['rc_t'])
                            k.dma(rs_t[64:96, :], ropesq[:, t0:t0 + Tn], W=['rs_t'], q='pool')
                        for h in range(4):
                            pq, pqk = psC[h % 2], 'psC%d' % (h % 2)
                            pr, prk = psC[2 + h % 2], 'psC%d' % (2 + h % 2)
                            k.op('pe', lambda h=h: PE.matmul(pq[0:96, 0:Tn], lhsT=wqh[:, 0, h, :], rhs=cqn[:, 0, 0:Tn], start=True, stop=False), R=['wqh', 'svb'], W=[pqk])
                            k.op('pe', lambda h=h: PE.matmul(pq[0:96, 0:Tn], lhsT=wqh[0:64, 1, h, :], rhs=cqn[0:64, 1, 0:Tn], start=False, stop=True), R=['wqh', 'svb'], W=[pqk])
                            k.op('act', lambda h=h: A.copy(QT[0:64, h, 0:Tn], pq[0:64, 0:Tn]), R=[pqk], W=['QT'])
                            if kind == 'lat':
                                k.op('pe', lambda h=h: PE.matmul(pr[0:96, 0:Tn], lhsT=wqr[:, 0, h, :], rhs=cqn[:, 0, 0:Tn], start=True, stop=False), R=['wqr', 'svb'], W=[prk])
                                k.op('pe', lambda h=h: PE.matmul(pr[0:96, 0:Tn], lhsT=wqr[0:64, 1, h, :], rhs=cqn[0:64, 1, 0:Tn], start=False, stop=True), R=['wqr', 'svb'], W=[prk])
                                k.op('dve', lambda: V.tensor_tensor(q1t[64:96, 0:Tn], pq[64:96, 0:Tn], rc_t[64:96, 0:Tn], op=ALU.mult), R=[pqk, 'rc_t'], W=['mean'])
                                k.op('dve', lambda: V.tensor_tensor(q2t[64:96, 0:Tn], pr[64:96, 0:Tn], rs_t[64:96, 0:Tn], op=ALU.mult), R=[prk, 'rs_t'], W=['var'])
                                k.op('pool', lambda h=h: G.tensor_tensor(QT[64:96, h, 0:Tn], q1t[64:96, 0:Tn], q2t[64:96, 0:Tn], op=ALU.add), R=['mean', 'var'], W=['QT'])
                            else:
                                k.op('act', lambda h=h: A.copy(QT[64:96, h, 0:Tn], pq[64:96, 0:Tn]), R=[pqk], W=['QT'])
                        for ct in range(2):
                            proj(psC[ct], 'psC%d' % ct, 192 + ct * 128, 128)
                            k.op('act', lambda ct=ct: A.copy(su[:, ct, 0:Tn], psC[ct][:, 0:Tn]), R=['psC%d' % ct], W=['su'])
                        for ct in range(2):
                            proj(psC[ct], 'psC%d' % ct, 448 + ct * 128, 128)
                            k.op('act', lambda ct=ct: A.copy(svf[:, ct, 0:Tn], psC[ct][:, 0:Tn]), R=['psC%d' % ct], W=['pa'])
                            k.op('dve', lambda ct=ct: V.tensor_copy(svb[:, ct, 0:Tn], svf[:, ct, 0:Tn]), R=['pa'], W=['svb'])
                            k.op('pool', lambda ct=ct: G.tensor_tensor(sq[:, ct, 0:Tn], svf[:, ct, 0:Tn], svf[:, ct, 0:Tn], op=ALU.mult), R=['pa'], W=['sq'])
                        for ct in range(2):
                            k.op('pe', lambda ct=ct: PE.matmul(psC[2][:, 0:Tn], lhsT=ones_b[:], rhs=svb[:, ct, 0:Tn], start=(ct == 0), stop=(ct == 1)), R=['ones_b', 'svb'], W=['psC2'])
                        for ct in range(2):
                            k.op('pe', lambda ct=ct: PE.matmul(psC[3][:, 0:Tn], lhsT=ones_b[:], rhs=sq[:, ct, 0:Tn], start=(ct == 0), stop=(ct == 1)), R=['ones_b', 'sq'], W=['psC3'])
                        k.op('act', lambda: A.activation(mean[:, 0:Tn], psC[2][:, 0:Tn], AF.Copy, scale=1.0 / 256), R=['psC2'], W=['mean'])
                        k.op('dve', lambda: V.tensor_tensor(var[:, 0:Tn], mean[:, 0:Tn], mean[:, 0:Tn], op=ALU.mult), R=['mean'], W=['var'])
                        k.op('dve', lambda: V.scalar_tensor_tensor(var[:, 0:Tn], psC[3][:, 0:Tn], 1.0 / 256, var[:, 0:Tn], op0=ALU.mult, op1=ALU.subtract), R=['psC3', 'var'], W=['var'])
                        k.op('act', lambda: A.activation(var[:, 0:Tn], var[:, 0:Tn], AF.Sqrt, bias=epsb[:, 0:1]), R=['var', 'epsb'], W=['var'])
                        k.op('dve', lambda: V.reciprocal(var[:, 0:Tn], var[:, 0:Tn]), R=['var'], W=['var'])
                        for ct in range(2):
                            k.op('dve', lambda ct=ct: V.tensor_tensor(svf[:, ct, 0:Tn], svf[:, ct, 0:Tn], mean[:, 0:Tn], op=ALU.subtract), R=['pa', 'mean'], W=['pa'])
                            k.op('pool', lambda ct=ct: G.tensor_tensor(svf[:, ct, 0:Tn], svf[:, ct, 0:Tn], var[:, 0:Tn], op=ALU.mult), R=['pa', 'var'], W=['pa'])
                            k.op('dve', lambda ct=ct: V.tensor_scalar(vnb[:, ct, 0:Tn], svf[:, ct, 0:Tn], colv[:, ct:ct + 1], colv[:, 2 + ct:3 + ct], op0=ALU.mult, op1=ALU.add),
                                 R=['pa', 'colv'], W=['vnb'])
                        for gt in range(8):
                            pg, pgk = psC[gt % 4], 'psC%d' % (gt % 4)
                            proj(pg, pgk, 704 + gt * 128, 128)
                            k.op('act', lambda gt=gt, pg=pg: A.activation(sg[:, gt, 0:Tn], pg[:, 0:Tn], AF.Silu), R=[pgk], W=['sg'])
                        if bidx + 1 < min(NQB, len(qblocks)):
                            do_ln(bidx + 1)
                        sbufs = [(psS_[0], 'psSc0'), (psS_[1], 'psSc1'), (psC[0], 'psC0'), (psC[1], 'psC1')]
                        npair = nkt // 2

                        def score(h, kp):
                            pS, pSk = sbufs[kp % 4]
                            for j_ in range(2):
                                kt_ = 2 * kp + j_
                                k.op('pe', lambda j_=j_, kt_=kt_: PE.matmul(pS[:, j_ * Tn:(j_ + 1) * Tn], lhsT=KT[0:96, h, kt_ * 128:(kt_ + 1) * 128], rhs=QT[0:96, h, 0:Tn],
                                                                           start=True, stop=True, skip_group_check=True), R=['QT', 'KTall'], W=[pSk])
                        for h in range(4 if int(os.environ.get('DBG_ATT', 1)) else 0):
                            pO = psO if h % 2 == 0 else psC[3][:, 0:260].rearrange("p (a b) -> p a b", a=4)
                            pOk = 'psO' if h % 2 == 0 else 'psC3'
                            for kp in range(min(3, npair)):
                                score(h, kp)
                            for kp in range(npair):
                                pS, pSk = sbufs[kp % 4]
                                P_, Pk = PT[kp % 3], 'PT%d' % (kp % 3)
                                if kp + 3 < npair:
                                    score(h, kp + 3)
                                k.op('act', lambda pS=pS, P_=P_: A.activation(P_[:, 0:2 * Tn], pS[:, 0:2 * Tn], AF.Exp, scale=SCALE), R=[pSk], W=[Pk])
                                for j_ in range(2):
                                    kt_ = 2 * kp + j_
                                    for sb in range(nsub):
                                        first = (kp == 0 and j_ == 0 and sb == 0)
                                        k.op('pe', lambda h=h, kt_=kt_, j_=j_, sb=sb, P_=P_, first=first: PE.matmul(
                                            pO[:, sb, :], lhsT=P_[:, j_ * Tn + sb * 128:j_ * Tn + (sb + 1) * 128], rhs=Vt[:, kt_, h, :],
                                            start=first, stop=(kp == npair - 1 and j_ == 1), skip_group_check=True), R=[Pk, 'Vtall'], W=[pOk])
                            k.op('act', lambda: A.copy(osb[:, 0:nsub, :], pO[:, 0:nsub, :]), R=[pOk], W=['osb'])
                            k.op('dve', lambda: V.reciprocal(rec[:, 0:nsub, :], osb[:, 0:nsub, 64:65]), R=['osb'], W=['rec'])
                            k.op('dve', lambda h=h: V.tensor_tensor(att[:, 0:nsub, h * 64:(h + 1) * 64], osb[:, 0:nsub, 0:64], rec[:, 0:nsub, :].to_broadcast([128, nsub, 64]), op=ALU.mult),
                                 R=['osb', 'rec'], W=['att'])
                        if 'Catt' in dbg and l == 0 and kind == 'lat' and t0 == 0:
                            d_ = dbg_out('att', (128, 4 * 256), BF16)
                            k.dma(d_, att[:].rearrange("p a b -> p (a b)"), R=['att'], W=['dbgatt'])
                        for sb in range(nsub):
                            for ft in range(2):
                                k.op('pe', lambda sb=sb, ft=ft: PE.transpose(pt[:, ft, :], att[:, sb, ft * 128:(ft + 1) * 128], ident_b[:]), R=['att', 'ident_b'], W=['ptC'])
                            k.op('dve', lambda sb=sb: V.tensor_tensor(catg[:, 0:2, sb * 128:(sb + 1) * 128], pt[:, 0:2, :], sg[:, 0:2, sb * 128:(sb + 1) * 128], op=ALU.mult),
                                 R=['ptC', 'sg'], W=['sg'])
                        si = 0 if kind == 'ctx' else 1
                        nseq = NCTX if kind == 'ctx' else SEQ
                        L_ = Tn + 16
                        k.dma(pin[:, :, 0:L_], pool_scr[si][:, t0:t0 + L_].rearrange("(c p) n -> p c n", p=128), R=['pool_scr'], W=['pin'])
                        if kind == 'lat':
                            pinB = pb[:].bitcast(BF16)[:, :, 0:L_]
                            k.dma(pinB, pool_scr[si][:, HALF + t0:HALF + t0 + L_].rearrange("(c p) n -> p c n", p=128), R=['pool_scr'], W=['pb'], q='pool')
                            k.op('dve', lambda: V.tensor_scalar_mul(pin[:, :, 0:L_], pin[:, :, 0:L_], msb[:, 0:1]), R=['pin', 'msb'], W=['pin'])
                            k.op('dve', lambda: V.scalar_tensor_tensor(pin[:, :, 0:L_], pinB, msb[:, 1:2], pin[:, :, 0:L_], op0=ALU.mult, op1=ALU.add), R=['pin', 'pb', 'msb'], W=['pin'])
                        k.op('dve', lambda: V.tensor_tensor(pa[:, :, 0:L_ - 1], pin[:, :, 0:L_ - 1], pin[:, :, 1:L_], op=ALU.add), R=['pin'], W=['pa'])
                        o_ = slice(8, 8 + Tn)
                        k.op('pool', lambda: G.tensor_copy(pb[0:64, 0, o_], pa[0:64, 0, 7:7 + Tn]), R=['pa'], W=['pb'])
                        k.op('pool', lambda: G.tensor_tensor(pb[64:128, 0, o_], pa[64:128, 0, 6:6 + Tn], pa[64:128, 0, 8:8 + Tn], op=ALU.add), R=['pa'], W=['pb'])
                        k.op('dve', lambda: V.tensor_tensor(pb[:, 1, 0:L_ - 3], pa[:, 1, 0:L_ - 3], pa[:, 1, 2:L_ - 1], op=ALU.add), R=['pa'], W=['pb'])
                        k.op('dve', lambda: V.tensor_tensor(pa[64:128, 1, 0:L_ - 7], pb[64:128, 1, 0:L_ - 7], pb[64:128, 1, 4:L_ - 3], op=ALU.add), R=['pb'], W=['pa'])
                        k.op('pool', lambda: G.tensor_tensor(pa[0:64, 1, o_], pb[0:64, 1, 4:4 + Tn], pb[0:64, 1, 8:8 + Tn], op=ALU.add), R=['pb'], W=['pa'])
                        k.op('dve', lambda: V.tensor_tensor(pb[64:128, 1, o_], pa[64:128, 1, 0:Tn], pa[64:128, 1, 8:8 + Tn], op=ALU.add), R=['pa'], W=['pb'])
                        k.op('pool', lambda: G.tensor_copy(pb[0:64, 1, o_], pa[0:64, 1, o_]), R=['pa'], W=['pb'])
                        for ct in range(2):
                            k.op('dve', lambda ct=ct: V.scalar_tensor_tensor(pld[:, ct, 0:Tn], pb[:, ct, o_], winv[:, ct:ct + 1], pin[:, ct, o_], op0=ALU.mult, op1=ALU.subtract),
                                 R=['pb', 'pin', 'winv'], W=['pld'])
                        wins = ((0, 0, 2), (64, 0, 4), (0, 1, 8), (64, 1, 16))
                        if t0 == 0:
                            for (r0, ct, w_) in wins:
                                for p_ in range(w_ // 2):
                                    rcp = (1.0 / float(p_ + w_ // 2)) if kind == 'ctx' else recF[r0:r0 + 64, ct, p_:p_ + 1]
                                    k.op('dve', lambda r0=r0, ct=ct, p_=p_, rcp=rcp: V.scalar_tensor_tensor(pld[r0:r0 + 64, ct, p_:p_ + 1], pb[r0:r0 + 64, ct, 8 + p_:9 + p_], rcp,
                                                                                                       pin[r0:r0 + 64, ct, 8 + p_:9 + p_], op0=ALU.mult, op1=ALU.subtract),
                                         R=['pb', 'pin', 'recF'], W=['pld'])
                        if t0 + Tn == (NCTX if kind == 'ctx' else HALF):
                            for (r0, ct, w_) in wins:
                                for p_ in range(Tn - w_ // 2 + 1, Tn):
                                    rcp = (1.0 / float(Tn - p_ + w_ // 2)) if kind == 'ctx' else recL[r0:r0 + 64, ct, p_ - (Tn - 8):p_ - (Tn - 8) + 1]
                                    k.op('dve', lambda r0=r0, ct=ct, p_=p_, rcp=rcp: V.scalar_tensor_tensor(pld[r0:r0 + 64, ct, p_:p_ + 1], pb[r0:r0 + 64, ct, 8 + p_:9 + p_], rcp,
                                                                                                       pin[r0:r0 + 64, ct, 8 + p_:9 + p_], op0=ALU.mult, op1=ALU.subtract),
                                         R=['pb', 'pin', 'recL'], W=['pld'])
                        for ct in range(2):
                            k.op('pe', lambda ct=ct: PE.matmul(psC[ct][:, 0:Tn], lhsT=wpl[:, ct, :], rhs=pld[:, ct, 0:Tn], start=True, stop=True), R=['wpl', 'pld'], W=['psC%d' % ct])
                            k.op('dve', lambda ct=ct: V.scalar_tensor_tensor(catg[:, 2 + ct, 0:Tn], psC[ct][:, 0:Tn], colv[:, 4 + ct:5 + ct], sg[:, 2 + ct, 0:Tn], op0=ALU.mult, op1=ALU.mult),
                                 R=['psC%d' % ct, 'colv', 'sg'], W=['sg'])
                        k.dma(s5t[:, :, 0:Tn], s5o_scr[:, u0:u0 + Tn].rearrange("(c p) n -> p c n", p=128), R=['s5o_scr'], W=['s5t'], q='pool')
                        k.op('pool', lambda: G.tensor_tensor(catg[:, 4:6, 0:Tn], s5t[:, :, 0:Tn], sg[:, 4:6, 0:Tn], op=ALU.mult), R=['s5t', 'sg'], W=['sg'])
                        for sb in range(nsub):
                            cs = slice(sb * 128, (sb + 1) * 128)
                            for ft in range(2):
                                k.op('pe', lambda ft=ft, cs=cs: PE.transpose(pt[:, 2 + ft, :], vnb[:, ft, cs], ident_b[:]), R=['vnb', 'ident_b'], W=['ptC'])
                            k.op('act', lambda: A.copy(vc[:].rearrange("p (a b) -> p a b", a=2), pt[:, 2:4, :]), R=['ptC'], W=['vc'])
                            for h in range(4):
                                ft, r0 = h // 2, (h % 2) * 64
                                pm_, pmk = psC[2 + h % 2], 'psC%d' % (2 + h % 2)
                                k.op('pe', lambda h=h, ft=ft: PE.matmul(pm_[:, 0:128], lhsT=vc[:, ft * 128:(ft + 1) * 128], rhs=wsT[:, h, :], start=True, stop=True), R=['vc', 'wsT'], W=[pmk])
                                k.op('dve', lambda h=h, r0=r0: V.tensor_tensor(mx[r0:r0 + 64, :], pm_[r0:r0 + 64, 0:128], bsb[r0:r0 + 64, h, :], op=ALU.add), R=[pmk, 'bsb'], W=['mx'])
                                k.op('dve', lambda ft=ft, r0=r0, cs=cs: V.tensor_tensor(mx[r0:r0 + 64, :], mx[r0:r0 + 64, :], su[r0:r0 + 64, ft, cs], op=ALU.mult), R=['mx', 'su'], W=['mx'])
                                k.op('dve', lambda ft=ft, r0=r0, cs=cs: V.tensor_tensor(catg[r0:r0 + 64, 6 + ft, cs], mx[r0:r0 + 64, :], sg[r0:r0 + 64, 6 + ft, cs], op=ALU.mult),
                                     R=['mx', 'sg'], W=['sg'])
                        if 'Ccat' in dbg and l == 0 and kind == 'lat' and t0 == 0:
                            d_ = dbg_out('catg', (128, 8 * QB), BF16)
                            k.dma(d_, catg[:].rearrange("p a b -> p (a b)"), R=['sg'], W=['dbgcat'])
                        for sb in range(nsub):
                            cs = slice(sb * 128, (sb + 1) * 128)
                            xb = xt[xcnt % 2]
                            xk = 'xtC0'
                            xcnt += 1
                            k.dma(xb[:], src[t0 + sb * 128: t0 + (sb + 1) * 128, :], W=[xk], q=('sp' if sb % 2 == 0 else 'pool'))
                            for nh in range(2):
                                for kt in range(8):
                                    k.op('pe', lambda nh=nh, kt=kt, cs=cs: PE.matmul(psC[nh][:, :], lhsT=catg[:, kt, cs], rhs=wout[:, kt, nh * 512:(nh + 1) * 512],
                                                                                    start=(kt == 0), stop=(kt == 7)), R=['sg', 'wout'], W=['psC%d' % nh])
                                k.op('dve', lambda nh=nh: V.tensor_tensor(rr[:, nh * 512:(nh + 1) * 512], psC[nh][:, :], gate_bc[:, nh * 512:(nh + 1) * 512], op=ALU.mult),
                                     R=['psC%d' % nh, 'gate_bc'], W=['rr'])
                            k.op('dve', lambda xb=xb: V.scalar_tensor_tensor(rr[:], xb[:], float(ALPHA), rr[:], op0=ALU.mult, op1=ALU.add), R=[xk, 'rr'], W=['rr'])
                            for c2 in range(2):
                                k.op('dve', lambda c2=c2: V.bn_stats(st6[:, c2, :], rr[:, c2 * 512:(c2 + 1) * 512]), R=['rr'], W=['st6'])
                            k.op('dve', lambda: V.bn_aggr(mv[:], st6[:]), R=['st6'], W=['mv'])
                            k.op('act', lambda: A.activation(rstd[:], mv[:, 1:2], AF.Sqrt, bias=epsb[:, 0:1]), R=['mv', 'epsb'], W=['rstd'])
                            k.op('dve', lambda: V.reciprocal(rstd[:], rstd[:]), R=['rstd'], W=['rstd'])
                            k.op('dve', lambda: V.scalar_tensor_tensor(rr[:], rr[:], mv[:, 0:1], lng_bc[:], op0=ALU.subtract, op1=ALU.mult), R=['rr', 'mv', 'lng_bc'], W=['rr'])
                            k.op('dve', lambda xb=xb: V.scalar_tensor_tensor(xb[:], rr[:], rstd[:, 0:1], lnb_bc[:], op0=ALU.mult, op1=ALU.add), R=['rr', 'rstd', 'lnb_bc'], W=[xk])
                            k.dma(dst[t0 + sb * 128: t0 + (sb + 1) * 128, :], xb[:], R=[xk], W=['x1w'])

            if l == 0 and nlayers > 1:
                k._sync('pool', ['x1w'], ['x1full'])
                for c_ in range(HALF // CCR):
                    G.collective_compute("AllGather", ALU.bypass, replica_groups=[[0, 1], [2, 3], [4, 5], [6, 7]],
                                         ins=[x1h_scr[c_ * CCR:(c_ + 1) * CCR, :]], outs=[x1g[c_]]).then_inc(ccsem)
                ccn[0] += HALF // CCR
                x1_ready = ccn[0]
            if 'x1' in dbg and l == 0:
                d_ = dbg_out('x1', (SEQ, D))
                k.dma(d_[0:HALF, :], x1h_scr, R=['x1w'], W=['dbgx1'])
                d_ = dbg_out('ctx1', (NCTX, D))
                k.dma(d_, ctx1_scr, R=['x1w'], W=['dbgc1'])
                break

        k.wait_all('sp')
    return nc, dout


def _in_maps(inputs, cores):
    cst = _consts()
    shared = {n: np.ascontiguousarray(inputs[n], dtype=np.float32) for n in PARAMS}
    shared.update(cst)
    maps = []
    for i in cores:
        b, hf = i // 2, i % 2
        sl = slice(hf * HALF, (hf + 1) * HALF)
        m = dict(shared)
        m['x'] = np.ascontiguousarray(inputs['x'][b], dtype=np.float32)
        m['xq'] = np.ascontiguousarray(inputs['x'][b][sl], dtype=np.float32)
        m['ctx'] = np.ascontiguousarray(inputs['ctx'][b], dtype=np.float32)
        m['cvec'] = np.ascontiguousarray(np.stack([inputs['c'][b], inputs['c_ctx']], 0), dtype=np.float32)
        m['msel'] = np.array([1.0 - hf, float(hf)], np.float32)
        if hf:
            for n_ in ('lam_re', 'lam_im', 'log_dt', 's5_b_re', 's5_b_im', 's5_c_re', 's5_c_im'):
                m[n_] = np.ascontiguousarray(np.roll(shared[n_], -8, axis=2))
            w_in = shared['w_in'].copy()
            w_in[:, :, 160:416] = np.roll(w_in[:, :, 160:416], -128, axis=2)
            w_in[:, :, 1888:2144] = np.roll(w_in[:, :, 1888:2144], -128, axis=2)
            m['w_in'] = w_in
            m['s5_d'] = np.ascontiguousarray(np.roll(shared['s5_d'], -128, axis=1))
            m['b_glu'] = np.ascontiguousarray(np.roll(shared['b_glu'], -128, axis=1))
            m['w_glu'] = np.ascontiguousarray(np.roll(np.roll(shared['w_glu'], -128, axis=1), -128, axis=2))
            w_out = shared['w_out'].copy()
            w_out[:, 512:768, :] = np.roll(w_out[:, 512:768, :], -128, axis=1)
            m['w_out'] = w_out
        m['ropec_q'] = np.ascontiguousarray(cst['ropec'][:, sl])
        m['ropes_q'] = np.ascontiguousarray(cst['ropes'][:, sl])
        maps.append(m)
    return maps


def kernel(**inputs):
    nc, _ = build()
    cores = list(range(8))
    res = run_bass_kernel_spmd(nc, _in_maps(inputs, cores), core_ids=cores)
    out = np.empty((4, SEQ, D), np.float32)
    for i in cores:
        b, hf = i // 2, i % 2
        out[b, hf * HALF:(hf + 1) * HALF] = np.asarray(res.results[i]['out'], dtype=np.float32)
    return out
```
